# Optimizing a Trainium2 kernel written in Bass

```python
import jax, jax.numpy as jnp
from jax import lax
import numpy as np

D_MODEL = 1024
BATCH = 16
SEQ = 2048
DEPTH = 2
DEC_BATCH = 128
DEC_SEQ = 8
PAST_LEN = 16384
PAGE_SIZE = 128

N_META = 16
N_A_LAYERS = DEPTH // 2
N_B_LAYERS = DEPTH - N_A_LAYERS
EPS = 1e-5
SSM_EXPAND = 2
D_INNER = SSM_EXPAND * D_MODEL
SSM_HEAD_DIM = 64
SSM_HEADS = D_INNER // SSM_HEAD_DIM
SSM_GROUPS = 4
D_STATE = 128
D_CONV = 4
CONV_DIM = D_INNER + 2 * SSM_GROUPS * D_STATE
D_IN_PROJ = D_INNER + CONV_DIM + SSM_HEADS
CHUNK = 128
DT_MIN = 0.001
DT_MAX = 0.1
HEAD_DIM = 64
N_HEADS = D_MODEL // HEAD_DIM
N_KV_HEADS = 4
Q_PER_KV = N_HEADS // N_KV_HEADS
WINDOW = 128
ATTN_BLOCK = WINDOW
ROT_DIM = HEAD_DIM // 4
ROPE_THETA = 500000.0
D_FF = ((8 * D_MODEL + 3 * 256 - 1) // (3 * 256)) * 256

kernel_name = "yoco_mamba2_swa_sink_step"


def _rmsnorm(x, g):
    xf = x.astype(jnp.float32)
    r = lax.rsqrt(jnp.mean(xf * xf, axis=-1, keepdims=True) + EPS)
    return (xf * r).astype(x.dtype) * g


def _gated_group_rmsnorm(y, z, g):
    b, l, _ = y.shape
    u = (y * jax.nn.silu(z)).astype(jnp.float32).reshape(b, l, SSM_GROUPS, D_INNER // SSM_GROUPS)
    u = u * lax.rsqrt(jnp.mean(u * u, axis=-1, keepdims=True) + EPS)
    return u.reshape(b, l, D_INNER).astype(y.dtype) * g


def _segsum(a):
    t = a.shape[-1]
    cs = jnp.cumsum(a, axis=-1)
    diff = cs[..., :, None] - cs[..., None, :]
    return jnp.where(jnp.tril(jnp.ones((t, t), dtype=bool)), diff, -jnp.inf)


def _pad_front(t, n):
    return jnp.pad(t, ((0, 0), (n, 0)) + ((0, 0),) * (t.ndim - 2))


def _ssd_chunked(xdt, a, bm, cm, h0):
    b, l = xdt.shape[:2]
    nc = l // CHUNK
    r = SSM_HEADS // SSM_GROUPS
    x = xdt.reshape(b, nc, CHUNK, SSM_GROUPS, r, SSM_HEAD_DIM)
    a = a.reshape(b, nc, CHUNK, SSM_GROUPS, r).transpose(0, 3, 4, 1, 2)
    bm = bm.reshape(b, nc, CHUNK, SSM_GROUPS, D_STATE)
    cm = cm.reshape(b, nc, CHUNK, SSM_GROUPS, D_STATE)
    a_cs = jnp.cumsum(a, axis=-1)
    decay_in = jnp.exp(_segsum(a))
    cb = jnp.einsum('bclgn,bcsgn->bgcls', cm, bm)
    y_diag = jnp.einsum('bgrcls,bcsgrp->bclgrp', cb[:, :, None] * decay_in, x)
    decay_to_end = jnp.exp(a_cs[..., -1:] - a_cs)
    chunk_states = jnp.einsum('bcsgn,bgrcs,bcsgrp->bcgrpn', bm, decay_to_end, x)
    h0 = h0.reshape(b, SSM_GROUPS, r, SSM_HEAD_DIM, D_STATE)
    all_states = jnp.concatenate([h0[:, None], chunk_states], axis=1)
    chunk_tot = jnp.pad(a_cs[..., -1], ((0, 0), (0, 0), (0, 0), (1, 0)))
    decay_chunk = jnp.exp(_segsum(chunk_tot))
    states = jnp.einsum('bgrzc,bcgrpn->bzgrpn', decay_chunk, all_states)
    y_off = jnp.einsum('bclgn,bcgrpn,bgrcl->bclgrp', cm, states[:, :-1], jnp.exp(a_cs))
    y = (y_diag + y_off).reshape(b, l, SSM_HEADS, SSM_HEAD_DIM)
    return y, states[:, -1].reshape(b, SSM_HEADS, SSM_HEAD_DIM, D_STATE)


def _mamba2_mixer(u, conv_prev, h0, w_in, conv_w, conv_b, dt_bias, a_log, d_skip, gate_g, w_out):
    b, l, _ = u.shape
    zxbcdt = u @ w_in
    z = zxbcdt[..., :D_INNER]
    xbc = zxbcdt[..., D_INNER:D_INNER + CONV_DIM]
    dt_raw = zxbcdt[..., D_INNER + CONV_DIM:]
    xpad = jnp.concatenate([conv_prev.astype(xbc.dtype), xbc], axis=1)
    conv = xpad[:, :l] * conv_w[0]
    for k in range(1, D_CONV):
        conv = conv + xpad[:, k:k + l] * conv_w[k]
    xbc = jax.nn.silu(conv + conv_b)
    new_conv = xpad[:, l:]
    gn = SSM_GROUPS * D_STATE
    xs = xbc[..., :D_INNER].reshape(b, l, SSM_HEADS, SSM_HEAD_DIM).astype(jnp.float32)
    bm = xbc[..., D_INNER:D_INNER + gn].reshape(b, l, SSM_GROUPS, D_STATE).astype(jnp.float32)
    cm = xbc[..., D_INNER + gn:].reshape(b, l, SSM_GROUPS, D_STATE).astype(jnp.float32)
    dt = jax.nn.softplus(dt_raw.astype(jnp.float32) + dt_bias.astype(jnp.float32))
    a = dt * (-jnp.exp(a_log.astype(jnp.float32)))
    pad = (-l) % CHUNK
    y, h_new = _ssd_chunked(_pad_front(xs * dt[..., None], pad), _pad_front(a, pad),
                            _pad_front(bm, pad), _pad_front(cm, pad), h0.astype(jnp.float32))
    y = y[:, pad:] + xs * d_skip.astype(jnp.float32)[:, None]
    y = y.reshape(b, l, D_INNER).astype(u.dtype)
    out = _gated_group_rmsnorm(y, z, gate_g) @ w_out
    return out, new_conv, h_new.astype(h0.dtype)


def _rope_partial(x, pos):
    inv = jnp.power(jnp.float32(ROPE_THETA), -jnp.arange(0, ROT_DIM, 2, dtype=jnp.float32) / ROT_DIM)
    ang = pos.astype(jnp.float32)[:, None] * inv[None, :]
    cos = jnp.cos(ang)[None, :, None, :]
    sin = jnp.sin(ang)[None, :, None, :]
    xr = x[..., :ROT_DIM].astype(jnp.float32)
    x1, x2 = xr[..., :ROT_DIM // 2], xr[..., ROT_DIM // 2:]
    rot = jnp.concatenate([x1 * cos - x2 * sin, x2 * cos + x1 * sin], axis=-1).astype(x.dtype)
    return jnp.concatenate([rot, x[..., ROT_DIM:]], axis=-1)


def _sliding_sink_attention(q, k_all, v_all, pos0, sinks):
    b, l = q.shape[:2]
    nb = -(-l // ATTN_BLOCK)
    lp = nb * ATTN_BLOCK
    qb = jnp.pad(q, ((0, 0), (0, lp - l), (0, 0), (0, 0), (0, 0))).reshape(
        b, nb, ATTN_BLOCK, N_KV_HEADS, Q_PER_KV, HEAD_DIM)

    def band(t):
        t = jnp.pad(t, ((0, 0), (0, lp - l), (0, 0), (0, 0))).reshape(b, nb + 1, ATTN_BLOCK, N_KV_HEADS, HEAD_DIM)
        return jnp.concatenate([t[:, :-1], t[:, 1:]], axis=2)

    kb, vb = band(k_all), band(v_all)
    qpos = pos0 + jnp.arange(lp).reshape(nb, ATTN_BLOCK)
    kpos = pos0 - WINDOW + jnp.arange((nb + 1) * ATTN_BLOCK).reshape(nb + 1, ATTN_BLOCK)
    kpos = jnp.concatenate([kpos[:-1], kpos[1:]], axis=1)
    dist = qpos[:, :, None] - kpos[:, None, :]
    mask = (dist >= 0) & (dist < WINDOW) & (kpos[:, None, :] >= 0)
    s = jnp.einsum('bnqkgd,bnskd->bnkgqs', qb.astype(jnp.float32), kb.astype(jnp.float32)) * (HEAD_DIM ** -0.5)
    s = jnp.where(mask[None, :, None, None], s, -jnp.inf)
    sink = sinks.astype(jnp.float32).reshape(N_KV_HEADS, Q_PER_KV)[None, None, :, :, None, None]
    m = jnp.maximum(jnp.max(s, axis=-1, keepdims=True), sink)
    p = jnp.exp(s - m)
    p = p / (jnp.sum(p, axis=-1, keepdims=True) + jnp.exp(sink - m))
    o = jnp.einsum('bnkgqs,bnskd->bnqkgd', p, vb.astype(jnp.float32))
    return o.reshape(b, lp, N_HEADS * HEAD_DIM)[:, :l].astype(q.dtype)


def _trunk(h, pos0, conv_prev, ssm_prev, k_buf, v_buf, p):
    b, l, _ = h.shape
    pos = pos0 + jnp.arange(l)
    new_conv, new_ssm = [], []
    k_all = v_all = None
    for layer in range(DEPTH):
        if layer < N_A_LAYERS:
            i = layer
            mix, c, s = _mamba2_mixer(_rmsnorm(h, p['ssm_norm_g'][i]), conv_prev[i], ssm_prev[i],
                                      p['ssm_w_in'][i], p['ssm_conv_w'][i], p['ssm_conv_b'][i],
                                      p['ssm_dt_bias'][i], p['ssm_A_log'][i], p['ssm_D'][i],
                                      p['ssm_gate_norm_g'][i], p['ssm_w_out'][i])
            new_conv.append(c)
            new_ssm.append(s)
        else:
            i = layer - N_A_LAYERS
            if i == 0:
                kv_in = _rmsnorm(h, p['kv_norm_g'])
                k_new = _rope_partial((kv_in @ p['w_k']).reshape(b, l, N_KV_HEADS, HEAD_DIM), pos)
                v_new = (kv_in @ p['w_v']).reshape(b, l, N_KV_HEADS, HEAD_DIM)
                k_all = jnp.concatenate([k_buf.astype(k_new.dtype), k_new], axis=1)
                v_all = jnp.concatenate([v_buf.astype(v_new.dtype), v_new], axis=1)
            hn = _rmsnorm(h, p['attn_norm_g'][i])
            q = _rope_partial((hn @ p['w_q'][i]).reshape(b, l, N_HEADS, HEAD_DIM), pos)
            q = q.reshape(b, l, N_KV_HEADS, Q_PER_KV, HEAD_DIM)
            mix = _sliding_sink_attention(q, k_all, v_all, pos0, p['attn_sinks'][i]) @ p['w_o'][i]
        h = h + mix
        hn = _rmsnorm(h, p['ffn_norm_g'][layer])
        h = h + (jax.nn.silu(hn @ p['ffn_w_gate'][layer]) * (hn @ p['ffn_w_up'][layer])) @ p['ffn_w_down'][layer]
    y = _rmsnorm(h, p['final_norm_g'])
    return y, jnp.stack(new_conv), jnp.stack(new_ssm), k_all[:, -WINDOW:], v_all[:, -WINDOW:]


def setup_inputs(seed: int = 0) -> dict:
    key = jax.random.key(seed)
    ks = jax.random.split(key, 32)
    f32 = jnp.float32
    nrm = lambda k, shape, scale: jax.random.normal(k, shape, f32) * scale
    dt0 = jnp.exp(jax.random.uniform(ks[10], (N_A_LAYERS, SSM_HEADS), f32) * (np.log(DT_MAX) - np.log(DT_MIN)) + np.log(DT_MIN))
    return {
        'x_prompt': nrm(ks[0], (BATCH, SEQ, D_MODEL), 1.0),
        'x_sample': nrm(ks[1], (DEC_BATCH, DEC_SEQ, D_MODEL), 1.0),
        'state_ssm': nrm(ks[2], (N_A_LAYERS, DEC_BATCH, SSM_HEADS, SSM_HEAD_DIM, D_STATE), 0.1),
        'state_conv': nrm(ks[3], (N_A_LAYERS, DEC_BATCH, D_CONV - 1, CONV_DIM), 1.0),
        'state_k': nrm(ks[4], (DEC_BATCH, WINDOW, N_KV_HEADS, HEAD_DIM), 1.0),
        'state_v': nrm(ks[5], (DEC_BATCH, WINDOW, N_KV_HEADS, HEAD_DIM), 1.0),
        'meta_tokens': nrm(ks[6], (N_META, D_MODEL), 1.0),
        'ssm_norm_g': 1.0 + nrm(ks[7], (N_A_LAYERS, D_MODEL), 0.01),
        'ssm_w_in': nrm(ks[8], (N_A_LAYERS, D_MODEL, D_IN_PROJ), D_MODEL ** -0.5),
        'ssm_conv_w': nrm(ks[9], (N_A_LAYERS, D_CONV, CONV_DIM), D_CONV ** -0.5),
        'ssm_conv_b': nrm(ks[11], (N_A_LAYERS, CONV_DIM), 0.01),
        'ssm_dt_bias': dt0 + jnp.log(-jnp.expm1(-dt0)),
        'ssm_A_log': jnp.log(jax.random.uniform(ks[12], (N_A_LAYERS, SSM_HEADS), f32, 1.0, 16.0)),
        'ssm_D': 1.0 + nrm(ks[13], (N_A_LAYERS, SSM_HEADS), 0.1),
        'ssm_gate_norm_g': 1.0 + nrm(ks[14], (N_A_LAYERS, D_INNER), 0.01),
        'ssm_w_out': nrm(ks[15], (N_A_LAYERS, D_INNER, D_MODEL), D_INNER ** -0.5),
        'kv_norm_g': 1.0 + nrm(ks[16], (D_MODEL,), 0.01),
        'w_k': nrm(ks[17], (D_MODEL, N_KV_HEADS * HEAD_DIM), D_MODEL ** -0.5),
        'w_v': nrm(ks[18], (D_MODEL, N_KV_HEADS * HEAD_DIM), D_MODEL ** -0.5),
        'attn_norm_g': 1.0 + nrm(ks[19], (N_B_LAYERS, D_MODEL), 0.01),
        'w_q': nrm(ks[20], (N_B_LAYERS, D_MODEL, N_HEADS * HEAD_DIM), D_MODEL ** -0.5),
        'attn_sinks': nrm(ks[21], (N_B_LAYERS, N_HEADS), 0.5),
        'w_o': nrm(ks[22], (N_B_LAYERS, N_HEADS * HEAD_DIM, D_MODEL), (N_HEADS * HEAD_DIM) ** -0.5),
        'ffn_norm_g': 1.0 + nrm(ks[23], (DEPTH, D_MODEL), 0.01),
        'ffn_w_gate': nrm(ks[24], (DEPTH, D_MODEL, D_FF), D_MODEL ** -0.5),
        'ffn_w_up': nrm(ks[25], (DEPTH, D_MODEL, D_FF), D_MODEL ** -0.5),
        'ffn_w_down': nrm(ks[26], (DEPTH, D_FF, D_MODEL), D_FF ** -0.5),
        'final_norm_g': 1.0 + nrm(ks[27], (D_MODEL,), 0.01),
    }


def reference(x_prompt, x_sample, state_ssm, state_conv, state_k, state_v, meta_tokens,
              ssm_norm_g, ssm_w_in, ssm_conv_w, ssm_conv_b, ssm_dt_bias, ssm_A_log, ssm_D,
              ssm_gate_norm_g, ssm_w_out, kv_norm_g, w_k, w_v, attn_norm_g, w_q, attn_sinks, w_o,
              ffn_norm_g, ffn_w_gate, ffn_w_up, ffn_w_down, final_norm_g):
    p = dict(ssm_norm_g=ssm_norm_g, ssm_w_in=ssm_w_in, ssm_conv_w=ssm_conv_w, ssm_conv_b=ssm_conv_b,
             ssm_dt_bias=ssm_dt_bias, ssm_A_log=ssm_A_log, ssm_D=ssm_D, ssm_gate_norm_g=ssm_gate_norm_g,
             ssm_w_out=ssm_w_out, kv_norm_g=kv_norm_g, w_k=w_k, w_v=w_v, attn_norm_g=attn_norm_g,
             w_q=w_q, attn_sinks=attn_sinks, w_o=w_o, ffn_norm_g=ffn_norm_g, ffn_w_gate=ffn_w_gate,
             ffn_w_up=ffn_w_up, ffn_w_down=ffn_w_down, final_norm_g=final_norm_g)
    dt = x_prompt.dtype
    b = x_prompt.shape[0]
    h_p = jnp.concatenate([jnp.broadcast_to(meta_tokens.astype(dt)[None], (b, N_META, D_MODEL)), x_prompt], axis=1)
    y_p, conv_p, ssm_p, k_p, v_p = _trunk(
        h_p, 0,
        jnp.zeros((N_A_LAYERS, b, D_CONV - 1, CONV_DIM), dt),
        jnp.zeros((N_A_LAYERS, b, SSM_HEADS, SSM_HEAD_DIM, D_STATE), dt),
        jnp.zeros((b, WINDOW, N_KV_HEADS, HEAD_DIM), dt),
        jnp.zeros((b, WINDOW, N_KV_HEADS, HEAD_DIM), dt), p)
    y_prompt = y_p[:, N_META:]
    y_sample, conv_s, ssm_s, k_s, v_s = _trunk(x_sample, PAST_LEN, state_conv, state_ssm, state_k, state_v, p)
    return (y_prompt, y_sample, ssm_p, conv_p, k_p, v_p, ssm_s, conv_s, k_s, v_s)
```

```python
import numpy as np
from contextlib import ExitStack
import ml_dtypes
import concourse.bass as bass
import concourse.mybir as mybir
from concourse.bass_utils import run_bass_kernel_spmd

F32 = mybir.dt.float32
BF16 = mybir.dt.bfloat16
AF = mybir.ActivationFunctionType
ALU = mybir.AluOpType

SEM_CAP = 30000
NEG = -30000.0
NCORE = 8
NSQ = 16
NPB = 2
NCHUNK = 16
DFF = 2816
EPS = 1e-5


class Dep:
    __slots__ = ("name", "w", "rs", "excl")

    def __init__(self, name=""):
        self.name = name
        self.w = None
        self.rs = []
        self.excl = False


class EngS:
    def __init__(self, name, handle):
        self.name = name
        self.h = handle
        self.count = 0
        self.sems = []
        self.seen = {}
        self.ops = []


class DmaSem:
    def __init__(self, sem, name):
        self.sem = sem
        self.total = 0
        self.name = name


class Prog:
    def __init__(self, nc, stack):
        self.nc = nc
        self.stack = stack
        self.E = {
            "pe": EngS("pe", nc.tensor),
            "act": EngS("act", nc.scalar),
            "dve": EngS("dve", nc.vector),
            "pool": EngS("pool", nc.gpsimd),
            "sp": EngS("sp", nc.sync),
        }
        self.dsems = []

    def new_sem(self, name):
        return self.stack.enter_context(self.nc.semaphore(name))

    def dma_sem(self, name):
        d = DmaSem(self.new_sem("d_" + name), name)
        self.dsems.append(d)
        return d

    def _need(self, eng, tok, needs):
        if tok is None:
            return
        if tok[0] == "e":
            if tok[1] == eng.name and eng.name == "pe":
                return
            key = ("e", tok[1])
            if needs.get(key, 0) < tok[2]:
                needs[key] = tok[2]
        else:
            ds = tok[1]
            key = ("d", ds)
            v = ds.total
            if needs.get(key, 0) < v:
                needs[key] = v

    def _waits(self, eng, reads, writes):
        needs = {}
        for d in reads:
            self._need(eng, d.w, needs)
            if d.excl:
                for r in d.rs:
                    if r[0] == "e" and r[1] != eng.name:
                        self._need(eng, r, needs)
        for d in writes:
            self._need(eng, d.w, needs)
            for r in d.rs:
                self._need(eng, r, needs)
        out = []
        for key, v in needs.items():
            if eng.seen.get(key, 0) >= v:
                continue
            eng.seen[key] = v
            out.append((key, v))
        return out

    def _emit_waits(self, eng, waits):
        for key, v in waits:
            if key[0] == "e":
                src = self.E[key[1]]
                si = (v - 1) // SEM_CAP
                val = v - si * SEM_CAP
                sem = src.sems[si]
                eng.ops.append(lambda h=eng.h, sem=sem, val=val: h.wait_ge(sem, val))
            else:
                ds = key[1]
                eng.ops.append(lambda h=eng.h, sem=ds.sem, val=v: h.wait_ge(sem, val))

    def _mark(self, tok, reads, writes):
        for d in reads:
            d.rs.append(tok)
        for d in writes:
            d.w = tok
            d.rs = []

    cap = None

    def op(self, engname, fn, reads=(), writes=()):
        if self.cap is not None:
            self.cap.append(("op", engname, fn, tuple(reads), tuple(writes), None, None, None))
            return
        eng = self.E[engname]
        self._emit_waits(eng, self._waits(eng, reads, writes))
        eng.count += 1
        idx = eng.count
        si = (idx - 1) // SEM_CAP
        while len(eng.sems) <= si:
            eng.sems.append(self.new_sem(f"s_{engname}{len(eng.sems)}"))
        sem = eng.sems[si]
        eng.ops.append(lambda h=eng.h, sem=sem, fn=fn: fn(h).then_inc(sem, 1))
        self._mark(("e", engname, idx), reads, writes)

    def dma(self, qname, dsem, out, in_, reads=(), writes=(), **kw):
        if self.cap is not None:
            self.cap.append(("dma", qname, dsem, tuple(reads), tuple(writes), out, in_, kw))
            return
        eng = self.E[qname]
        self._emit_waits(eng, self._waits(eng, reads, writes))
        dsem.total += 16
        eng.ops.append(lambda h=eng.h, sem=dsem.sem, out=out, in_=in_, kw=kw:
                       h.dma_start(out=out, in_=in_, **kw).then_inc(sem, 16))
        self._mark(("d", dsem, dsem.total), reads, writes)

    def capture(self, gen):
        assert self.cap is None
        self.cap = []
        gen()
        lst, self.cap = self.cap, None
        return lst

    def replay(self, lst):
        for (kind, a, b, reads, writes, out, in_, kw) in lst:
            if kind == "op":
                self.op(a, b, reads, writes)
            else:
                self.dma(a, b, out, in_, reads, writes, **kw)

    def replay_merged(self, l1, l2):
        n1, n2 = len(l1), len(l2)
        i = j = 0
        while i < n1 or j < n2:
            if j >= n2 or (i < n1 and (i + 0.5) * n2 <= (j + 0.5) * n1):
                self.replay(l1[i:i + 1]); i += 1
            else:
                self.replay(l2[j:j + 1]); j += 1

    def finish(self):
        eng = self.E["sp"]
        for name, e in self.E.items():
            if e.count > 0:
                v = e.count
                si = (v - 1) // SEM_CAP
                eng.ops.append(lambda h=eng.h, sem=e.sems[si], val=v - si * SEM_CAP: h.wait_ge(sem, val))
        for ds in self.dsems:
            if ds.total > 0:
                eng.ops.append(lambda h=eng.h, sem=ds.sem, val=ds.total: h.wait_ge(sem, val))

    def emit(self):
        with self.nc.Block() as block:
            @block.tensor
            def _(e):
                for f in self.E["pe"].ops:
                    f()

            @block.scalar
            def _(e):
                for f in self.E["act"].ops:
                    f()

            @block.vector
            def _(e):
                for f in self.E["dve"].ops:
                    f()

            @block.gpsimd
            def _(e):
                for f in self.E["pool"].ops:
                    f()

            @block.sync
            def _(e):
                for f in self.E["sp"].ops:
                    f()


class Tl:
    __slots__ = ("t", "d")

    def __init__(self, t, name):
        self.t = t
        self.d = Dep(name)


def make_consts():
    bf = ml_dtypes.bfloat16
    i = np.arange(128)
    c = {}
    tri_std = (i[:, None] <= i[None, :]).astype(np.float32)
    same = (i[:, None] // 8 == i[None, :] // 8)
    tri_blk = tri_std * same
    c["tri"] = np.stack([tri_std, tri_blk]).astype(bf)
    mb_std = np.where(tri_std > 0, 0.0, NEG).astype(np.float32)
    mb_blk = np.where(tri_blk > 0, 0.0, NEG).astype(np.float32)
    c["mb"] = np.stack([mb_std, mb_blk]).astype(bf)
    mbp = np.where(i[:, None] > i[None, :], 0.0, NEG).astype(np.float32)
    mbp0 = np.where((i[:, None] > i[None, :]) & (i[:, None] >= 112), 0.0, NEG).astype(np.float32)
    c["mbprev"] = np.stack([mbp, mbp0]).astype(bf)
    t8 = np.arange(128) % 8
    c["mbstate"] = np.where(i[:, None] > t8[None, :], 0.0, NEG).astype(np.float32).astype(bf)
    c["ident_b"] = np.eye(128, dtype=np.float32).astype(bf)
    c["ident_f"] = np.eye(128, dtype=np.float32)
    c["ones_b"] = np.ones((128, 128), np.float32).astype(bf)
    sel = np.zeros((128, NSQ, 128), np.float32)
    mbq = np.full((128, NSQ), NEG, np.float32)
    rowm = np.zeros((128, NSQ, 128), np.float32)
    for q in range(NSQ):
        sel[8 * q:8 * q + 8, q, :] = 1.0
        mbq[8 * q:8 * q + 8, q] = 0.0
        rowm[:, q, 8 * q:8 * q + 8] = 1.0
    c["sel"] = sel.astype(bf)
    c["mbq"] = mbq
    c["rowm"] = rowm.astype(bf)
    tm = np.ones((128, 2), np.float32)
    tm[:112, 1] = 0.0
    c["tokmask"] = tm
    inv = (np.float32(500000.0) ** (-np.arange(0, 16, 2, dtype=np.float32) / np.float32(16))).astype(np.float32)
    rope = np.zeros((18, 128, 16), np.float32)
    for ty in range(18):
        if ty < 16:
            pos = 16 + 128 * ty + i
        elif ty == 16:
            pos = i - 112
        else:
            pos = 16384 + (i % 8)
        ang = pos.astype(np.float32)[:, None] * inv[None, :]
        rope[ty, :, 0:8] = np.cos(ang)
        rope[ty, :, 8:16] = np.sin(ang)
    c["rope"] = rope
    return c


class Kern:
    def __init__(self):
        self.nc = bass.Bass("TRN2", target_bir_lowering=False)
        self.in_names = []
        self.out_names = []

    def din(self, name, shape, dt=F32):
        self.in_names.append(name)
        return self.nc.dram_tensor(name, list(shape), dt, kind="ExternalInput").ap()

    def dout(self, name, shape, dt=F32):
        self.out_names.append(name)
        return self.nc.dram_tensor(name, list(shape), dt, kind="ExternalOutput").ap()

    def sb(self, name, shape, dt):
        return Tl(self.st.enter_context(self.nc.sbuf_tensor("sb_" + name, list(shape), dt)), name)

    def build(self):
        nc = self.nc
        A = {}
        A["xs"] = self.din("xs", [128, 1024])
        A["xp"] = self.din("xp", [NPB, 2048, 1024])
        A["meta"] = self.din("meta", [16, 1024])
        A["st_ssm"] = self.din("st_ssm", [NSQ, 32, 64, 128])
        A["st_conv"] = self.din("st_conv", [NSQ * 3, 3072])
        A["st_k"] = self.din("st_k", [NSQ, 128, 256])
        A["st_v"] = self.din("st_v", [NSQ, 128, 256])
        A["w_in"] = self.din("w_in", [1024, 5152])
        A["w_out"] = self.din("w_out", [2048, 1024])
        A["w_k"] = self.din("w_k", [1024, 256])
        A["w_v"] = self.din("w_v", [1024, 256])
        A["w_q"] = self.din("w_q", [1024, 1024])
        A["w_o"] = self.din("w_o", [1024, 1024])
        A["w_gate"] = self.din("w_gate", [2, 1024, DFF])
        A["w_up"] = self.din("w_up", [2, 1024, DFF])
        A["w_down"] = self.din("w_down", [2, DFF, 1024])
        A["gcols"] = self.din("gcols", [128, 6, 8])
        A["gate_g"] = self.din("gate_g", [128, 16])
        A["conv_w"] = self.din("conv_w", [128, 24, 4])
        A["conv_b"] = self.din("conv_b", [128, 24])
        A["hp"] = self.din("hp", [128, 4, 32])
        A["sinks"] = self.din("sinks", [128, 16])
        A["c_tri"] = self.din("c_tri", [2, 128, 128], BF16)
        A["c_mb"] = self.din("c_mb", [2, 128, 128], BF16)
        A["c_mbprev"] = self.din("c_mbprev", [2, 128, 128], BF16)
        A["c_mbstate"] = self.din("c_mbstate", [128, 128], BF16)
        A["c_ident_b"] = self.din("c_ident_b", [128, 128], BF16)
        A["c_ident_f"] = self.din("c_ident_f", [128, 128])
        A["c_ones_b"] = self.din("c_ones_b", [128, 128], BF16)
        A["c_sel"] = self.din("c_sel", [128, NSQ, 128], BF16)
        A["c_mbq"] = self.din("c_mbq", [128, NSQ])
        A["c_rowm"] = self.din("c_rowm", [128, NSQ, 128], BF16)
        A["c_tokmask"] = self.din("c_tokmask", [128, 2])
        A["c_rope"] = self.din("c_rope", [18, 128, 16])
        A["y_p"] = self.dout("y_p", [NPB, 2048, 1024])
        A["y_s"] = self.dout("y_s", [128, 1024])
        A["ssm_p"] = self.dout("ssm_p", [NPB, 32, 64, 128])
        A["conv_p"] = self.dout("conv_p", [NPB, 3, 3072])
        A["k_p"] = self.dout("k_p", [NPB, 128, 256])
        A["v_p"] = self.dout("v_p", [NPB, 128, 256])
        A["ssm_s"] = self.dout("ssm_s", [NSQ, 32, 64, 128])
        A["conv_s"] = self.dout("conv_s", [NSQ * 3, 3072])
        A["k_s"] = self.dout("k_s", [NSQ, 128, 256])
        A["v_s"] = self.dout("v_s", [NSQ, 128, 256])
        self.A = A
        self.WB = {}
        for wn in ["w_in", "w_out", "w_k", "w_v", "w_q", "w_o", "w_gate", "w_up", "w_down"]:
            shp = list(A[wn].shape)
            self.WB[wn] = self.nc.dram_tensor(wn + "_bf", shp, BF16, kind="Internal").ap()
        with ExitStack() as st:
            self.st = st
            self.P = Prog(nc, st)
            self.alloc()
            self.setup()
            import os
            plan = os.environ.get("KPLAN", "SMP")
            if "S" in plan:
                self.chunk("S", None, 0)
            if "M" in plan:
                self.chunk("M", None, 0)
            if "P" in plan:
                self.run_prompt(int(os.environ.get("KNCH", NCHUNK)))
            self.P.finish()
            self.P.emit()
        return nc

    def alloc(self):
        sb = self.sb
        nc = self.nc
        self.tri = sb("tri", [128, 2, 128], BF16)
        self.mb = sb("mb", [128, 2, 128], BF16)
        self.mbprev = sb("mbprev", [128, 2, 128], BF16)
        self.mbstate = sb("mbstate", [128, 128], BF16)
        self.ident_b = sb("ident_b", [128, 128], BF16)
        self.ident_f = sb("ident_f", [128, 128], F32)
        self.ones_b = sb("ones_b", [128, 128], BF16)
        self.sel = sb("sel", [128, NSQ, 128], BF16)
        self.mbq = sb("mbq", [128, NSQ], F32)
        self.rowm = sb("rowm", [128, NSQ, 128], BF16)
        self.tokmask = sb("tokmask", [128, 2], F32)
        self.zero1 = sb("zero1", [128, 1], F32)
        self.gcols = sb("gcols", [128, 6, 8], F32)
        self.gate_g = sb("gate_g", [128, 16], F32)
        self.conv_w = sb("conv_w", [128, 24, 4], F32)
        self.conv_b = sb("conv_b", [128, 24], F32)
        self.hp = sb("hp", [128, 4, 32], F32)
        self.negA = sb("negA", [128, 32], F32)
        self.esink = sb("esink", [128, 16], F32)
        self.sinks = sb("sinks", [128, 16], F32)
        self.ropes = [sb(f"rope{i}", [128, 16], F32) for i in range(2)]
        self.rope_i = 0
        self.cacc = [sb(f"cacc{i}", [128, 128], F32) for i in range(4)]
        self.cth = [sb(f"cth{i}", [128, 128], F32) for i in range(4)]
        self.hTs = [sb(f"hT{i}", [128, 8, 128], F32) for i in range(2)]
        self.hT = self.hTs[0]
        self.uT = sb("uT", [128, 8, 128], BF16)
        self.rstd = sb("rstd", [128, 128], F32)
        self.xin = sb("xin", [128, 1024], F32)
        self.z_tm = sb("z_tm", [128, 2048], BF16)
        self.xpre = sb("xpre", [128, 24, 176], BF16)
        self.uni = sb("uni", [128, 1408], F32)
        self.big1 = sb("big1", [128, 3072], F32)
        self.dtraw = sb("dtraw", [128, 32], F32)
        self.xbcT = sb("xbcT", [128, 24, 128], BF16)
        self.x_tm = sb("x_tm", [128, 2048], BF16)
        self.xdt = sb("xdt", [128, 2048], BF16)
        self.xw = sb("xw", [128, 2048], BF16)
        self.B_tm = sb("B_tm", [128, 512], BF16)
        self.dt = sb("dt", [128, 32], F32)
        self.a32 = sb("a32", [128, 32], F32)
        self.ahi = sb("ahi", [128, 32], BF16)
        self.alo = sb("alo", [128, 32], BF16)
        self.negcs = sb("negcs", [128, 32], F32)
        self.ecs = sb("ecs", [128, 32], F32)
        self.dA = sb("dA", [128, 32], F32)
        self.wdec = sb("wdec", [128, 32], F32)
        self.cbT = sb("cbT", [128, 4, 128], F32)
        self.dec = [sb(f"dec{i}", [128, 4, 128], BF16) for i in range(2)]
        self.MT = [sb(f"MT{i}", [128, 4, 128], BF16) for i in range(2)]
        self.tmpg = [sb(f"tmpg{i}", [128, 512], F32) for i in range(2)]
        self.CTq = sb("CTq", [128, 4, 128], BF16)
        self.ST = sb("ST", [128, 2048], F32)
        self.STb = sb("STb", [128, 2048], BF16)
        self.ST_meta = sb("ST_meta", [128, 2048], F32)
        self.big2 = sb("big2", [128, 2, 16, 128], F32)
        self.big2_d0 = Dep("big2_in")
        self.big2_d1 = Dep("big2_out")
        self.sz = sb("sz", [128, 512], F32)
        self.ssq = sb("ssq", [128, 4], F32)
        self.grs = sb("grs", [128, 4], F32)
        self.ynT = sb("ynT", [128, 16, 128], BF16)
        self.fth = [sb(f"fth{i}", [128, 256], F32) for i in range(2)]
        self.actT = sb("actT", [128, 22, 128], BF16)
        self.k_tm = sb("k_tm", [128, 256], F32)
        self.k_rot = sb("k_rot", [128, 256], F32)
        self.k_b = sb("k_b", [128, 256], BF16)
        self.v_tm = sb("v_tm", [128, 256], F32)
        self.v_b = [sb(f"v_b{i}", [128, 256], BF16) for i in range(2)]
        self.kT = [sb(f"kT{i}", [64, 4, 128], BF16) for i in range(2)]
        self.kT_meta = sb("kT_meta", [64, 4, 128], BF16)
        self.v_meta = sb("v_meta", [128, 256], BF16)
        self.cvtail_meta = sb("cvtail_meta", [128, 24, 3], BF16)
        self.q_tm = sb("q_tm", [128, 1024], F32)
        self.q_rot = sb("q_rot", [128, 1024], F32)
        self.q_b = sb("q_b", [128, 1024], BF16)
        self.qT = sb("qT", [64, 16, 128], BF16)
        self.rtmp = sb("rtmp", [128, 4, 16, 8], F32)
        self.PT = [sb(f"PT{i}", [128, 256], BF16) for i in range(2)]
        self.oT = sb("oT", [64, 16, 128], BF16)
        self.den = sb("den", [64, 4, 128], F32)
        self.sk32 = sb("sk32", [128, 256], F32)
        self.sv32 = sb("sv32", [128, 256], F32)
        self.skb = sb("skb", [128, 256], BF16)
        self.svb = sb("svb", [128, 256], BF16)
        self.kTs = sb("kTs", [64, 4, 128], BF16)
        self.PTs = sb("PTs", [128, 128], BF16)
        self.y_st = sb("y_st", [128, 1024], F32)
        import os
        self.NW = int(os.environ.get("KNW", "6"))
        self.wslot = [sb(f"wslot{i}", [128, 2048], BF16) for i in range(self.NW)]
        self.wsem = [self.P.dma_sem(f"w{i}") for i in range(self.NW)]
        self.wi = 0
        self.psf = [Tl(self.st.enter_context(nc.psum_tensor(f"psf{i}", [128, 512], F32)), f"psf{i}") for i in range(6)]
        self.psb = [Tl(self.st.enter_context(nc.psum_tensor(f"psb{i}", [128, 1024], BF16)), f"psb{i}") for i in range(2)]
        self.pfi = 0
        self.pbi = 0
        for t in self.psf + self.psb:
            t.d.excl = True
        self.ds_in = self.P.dma_sem("in")
        self.ds_x = self.P.dma_sem("x")
        self.ds_stg = self.P.dma_sem("stg")
        self.ds_stgo = self.P.dma_sem("stgo")
        self.ds_kv = self.P.dma_sem("kv")
        self.ds_out = self.P.dma_sem("out")
        self.ds_y = self.P.dma_sem("y")
        self.ds_cp = self.P.dma_sem("cp")
        self.ds_rope = self.P.dma_sem("rope")
        self.pbuf = 0

    def stage(self, n):
        if n > self.kstage:
            raise StopIteration

    ps_pool = "all"

    def PF(self):
        if self.ps_pool == "all":
            t = self.psf[self.pfi % 6]
        elif self.ps_pool == "lo":
            t = self.psf[self.pfi % 3]
        else:
            t = self.psf[3 + self.pfi % 3]
        self.pfi += 1
        return t

    def run_prompt(self, nch):
        P = self.P
        seq = [(b, c) for b in range(NPB) for c in range(nch)]
        prevE = None
        for i, (b, c) in enumerate(seq):
            hs = i % 2
            self.chunk("P", b, c, ph="A", hsel=hs)
            if prevE is None:
                self.chunk("P", b, c, ph="B", hsel=hs)
            else:
                pb_, pc_, phs = prevE
                self.ps_pool = "lo"
                lE = P.capture(lambda: self.chunk("P", pb_, pc_, ph="E", hsel=phs))
                self.ps_pool = "hi"
                lB = P.capture(lambda: self.chunk("P", b, c, ph="B", hsel=hs))
                self.ps_pool = "all"
                P.replay_merged(lE, lB)
            self.chunk("P", b, c, ph="C", hsel=hs)
            self.chunk("P", b, c, ph="D", hsel=hs)
            prevE = (b, c, hs)
        pb_, pc_, phs = prevE
        self.chunk("P", pb_, pc_, ph="E", hsel=phs)

    def PB(self):
        t = self.psb[self.pbi % 2]
        self.pbi += 1
        return t

    def setup(self):
        P, A = self.P, self.A
        ld = lambda tl, src: P.dma("sp", self.ds_in, tl.t[:], src, writes=[tl.d])
        ld(self.tri, A["c_tri"].rearrange("a p n -> p a n"))
        ld(self.mb, A["c_mb"].rearrange("a p n -> p a n"))
        ld(self.mbprev, A["c_mbprev"].rearrange("a p n -> p a n"))
        ld(self.mbstate, A["c_mbstate"])
        ld(self.ident_b, A["c_ident_b"])
        ld(self.ident_f, A["c_ident_f"])
        ld(self.ones_b, A["c_ones_b"])
        ld(self.sel, A["c_sel"])
        ld(self.mbq, A["c_mbq"])
        ld(self.rowm, A["c_rowm"])
        ld(self.tokmask, A["c_tokmask"])
        ld(self.gcols, A["gcols"])
        ld(self.gate_g, A["gate_g"])
        ld(self.conv_w, A["conv_w"])
        ld(self.conv_b, A["conv_b"])
        ld(self.hp, A["hp"])
        ld(self.sinks, A["sinks"])
        P.op("pool", lambda h: h.memset(self.zero1.t[:], 0.0), writes=[self.zero1.d])
        self.wdep = {}
        for wn in ["w_in", "w_out", "w_gate", "w_up", "w_down", "w_k", "w_v", "w_q", "w_o"]:
            self.wdep[wn] = Dep("wb_" + wn)
            self.ds_cast = P.dma_sem("cast_" + wn)
            src, dst = A[wn], self.WB[wn]
            if len(src.shape) == 3:
                for l in range(2):
                    P.dma("pool", self.ds_cast, dst[l].rearrange("(p r) n -> p r n", p=128), src[l].rearrange("(p r) n -> p r n", p=128), writes=[self.wdep[wn]])
            else:
                P.dma("pool", self.ds_cast, dst.rearrange("(p r) n -> p r n", p=128), src.rearrange("(p r) n -> p r n", p=128), writes=[self.wdep[wn]])
        P.op("act", lambda h: h.activation(self.negA.t[:], self.hp.t[:, 1, :], AF.Exp), reads=[self.hp.d], writes=[self.negA.d])
        P.op("dve", lambda h: h.tensor_scalar_mul(self.negA.t[:], self.negA.t[:], -1.0), reads=[self.negA.d], writes=[self.negA.d])
        P.op("act", lambda h: h.activation(self.esink.t[:], self.sinks.t[:], AF.Exp), reads=[self.sinks.d], writes=[self.esink.d])
        P.op("dve", lambda h: h.tensor_scalar_mul(self.conv_w.t[:], self.conv_w.t[:], 0.5), reads=[self.conv_w.d], writes=[self.conv_w.d])
        P.op("dve", lambda h: h.tensor_scalar_mul(self.conv_b.t[:], self.conv_b.t[:], 0.5), reads=[self.conv_b.d], writes=[self.conv_b.d])

    def wload(self, srcspec, r0, r1, ncols, part=128):
        P = self.P
        i = self.wi % self.NW
        self.wi += 1
        sl = self.wslot[i]
        kc = (r1 - r0) // part
        assert kc * ncols <= 2048
        view = sl.t[0:part, 0:kc * ncols].rearrange("p (c n) -> p c n", n=ncols)
        wn, sel = srcspec
        src = sel(self.WB[wn])[r0:r1, :]
        P.dma("sp", self.wsem[i], view, src.rearrange("(c p) n -> p c n", p=part), reads=[self.wdep[wn]], writes=[sl.d])
        return view, sl

    def rms(self, gi, out_tl, view3=False):
        P = self.P
        hT, sq, rstd = self.hT, self.actT, self.rstd
        P.op("act", lambda h: h.activation(sq.t[:, 0:8, :], hT.t[:], AF.Square), reads=[hT.d], writes=[sq.d])
        ps = self.PF()
        for c in range(8):
            P.op("pe", lambda h, c=c: h.matmul(ps.t[:, 0:128], lhsT=self.ones_b.t[:], rhs=sq.t[:, c, :], start=(c == 0), stop=(c == 7)),
                 reads=[sq.d, self.ones_b.d], writes=[ps.d])
        P.op("act", lambda h: h.activation(rstd.t[:], ps.t[:, 0:128], AF.Ln, bias=EPS, scale=1.0 / 1024.0), reads=[ps.d], writes=[rstd.d])
        P.op("act", lambda h: h.activation(rstd.t[:], rstd.t[:], AF.Exp, scale=-0.5), reads=[rstd.d], writes=[rstd.d])
        for c in range(8):
            oc = out_tl.t[:, c * 128:(c + 1) * 128] if view3 else out_tl.t[:, c, :]
            P.op("dve", lambda h, c=c, oc=oc: h.scalar_tensor_tensor(oc, hT.t[:, c, :], self.gcols.t[:, gi, c:c + 1], rstd.t[:], ALU.mult, ALU.mult),
                 reads=[hT.d, rstd.d, self.gcols.d], writes=[out_tl.d])

    def dense_fm(self, src, K, ncols_total, xT_tl, evac, blk=None, part=128):
        P = self.P
        kc = K // part
        if kc <= 16:
            nb0 = (2048 // kc) // 128 * 128
            ksubs = [(0, kc)]
        else:
            nb0 = 128
            half = (kc + 1) // 2
            ksubs = [(0, half), (half, kc)]
        for c0 in range(0, ncols_total, nb0):
            nb = min(nb0, ncols_total - c0)
            loaded = []
            for (k0, k1) in ksubs:
                wv, wsl = self.wload((src[0], lambda w, c0=c0, nb=nb, f=src[1]: f(w)[:, c0:c0 + nb]), k0 * part, k1 * part, nb, part=part)
                loaded.append((k0, k1, wv, wsl))
            for m in range(nb // 128):
                ps = self.PF()
                for (k0, k1, wv, wsl) in loaded:
                    for k in range(k0, k1):
                        rhs = xT_tl.t[0:part, k, :]
                        P.op("pe", lambda h, k=k, k0=k0, m=m, rhs=rhs, wv=wv, ps=ps: h.matmul(ps.t[:, 0:128], lhsT=wv[:, k - k0, m * 128:(m + 1) * 128], rhs=rhs, start=(k == 0), stop=(k == kc - 1)),
                             reads=[wsl.d, xT_tl.d], writes=[ps.d])
                evac((c0 // 128) + m, ps)

    def dense_tm(self, src, ncols, xT_tl, evac):
        P = self.P
        ps = self.PF()
        for c0 in range(0, ncols, 256):
            nb = min(256, ncols - c0)
            wv, wsl = self.wload((src[0], lambda w, c0=c0, nb=nb, f=src[1]: f(w)[:, c0:c0 + nb]), 0, 1024, nb)
            for k in range(8):
                P.op("pe", lambda h, k=k, c0=c0, nb=nb, wv=wv: h.matmul(ps.t[:, c0:c0 + nb], lhsT=xT_tl.t[:, k, :], rhs=wv[:, k, :], start=(k == 0), stop=(k == 7)),
                     reads=[wsl.d, xT_tl.d], writes=[ps.d])
        evac(ps)

    def resid_add(self, m, ps):
        hT = self.hT
        self.P.op("dve", lambda h: h.tensor_tensor(hT.t[:, m, :], hT.t[:, m, :], ps.t[:, 0:128], ALU.add), reads=[ps.d, hT.d], writes=[hT.d])

    def ffn(self, layer):
        P, A = self.P, self.A
        self.rms(1 if layer == 0 else 4, self.uT)
        wd = ("w_down", lambda w: w[layer])
        act_tm = self.uni.t[:].bitcast(BF16)
        uT = self.uT
        for bi, c0 in enumerate(range(0, DFF, 256)):
            nb = min(256, DFF - c0)
            gv, gsl = self.wload(("w_gate", lambda w, c0=c0, nb=nb: w[layer][:, c0:c0 + nb]), 0, 1024, nb)
            uv, usl = self.wload(("w_up", lambda w, c0=c0, nb=nb: w[layer][:, c0:c0 + nb]), 0, 1024, nb)
            psg = self.PF()
            psu = self.PF()
            for k in range(8):
                P.op("pe", lambda h, k=k, gv=gv, psg=psg, nb=nb: h.matmul(psg.t[:, 0:nb], lhsT=uT.t[:, k, :], rhs=gv[:, k, :], start=(k == 0), stop=(k == 7)),
                     reads=[gsl.d, uT.d], writes=[psg.d])
            for k in range(8):
                P.op("pe", lambda h, k=k, uv=uv, psu=psu, nb=nb: h.matmul(psu.t[:, 0:nb], lhsT=uT.t[:, k, :], rhs=uv[:, k, :], start=(k == 0), stop=(k == 7)),
                     reads=[usl.d, uT.d], writes=[psu.d])
            th = self.fth[bi % 2]
            P.op("act", lambda h, th=th, psg=psg, nb=nb: h.activation(th.t[:, 0:nb], psg.t[:, 0:nb], AF.Tanh, scale=0.5), reads=[psg.d], writes=[th.d])
            P.op("dve", lambda h, th=th, psg=psg, nb=nb: h.scalar_tensor_tensor(th.t[:, 0:nb], th.t[:, 0:nb], 1.0, psg.t[:, 0:nb], ALU.add, ALU.mult), reads=[th.d, psg.d], writes=[th.d])
            P.op("dve", lambda h, th=th, psu=psu, nb=nb, c0=c0: h.scalar_tensor_tensor(act_tm[:, c0:c0 + nb], th.t[:, 0:nb], 0.5, psu.t[:, 0:nb], ALU.mult, ALU.mult),
                 reads=[th.d, psu.d], writes=[self.uni.d])
        for gi_, (m0, n) in enumerate([(0, 8), (8, 8), (16, 6)]):
            ps = self.PF()
            pbv = ps.t[:, :].bitcast(BF16)
            for j in range(n):
                m = m0 + j
                P.op("pe", lambda h, j=j, m=m, pbv=pbv: h.transpose(pbv[:, j * 128:(j + 1) * 128], act_tm[:, m * 128:(m + 1) * 128], self.ident_b.t[:]),
                     reads=[self.uni.d, self.ident_b.d], writes=[ps.d])
            src = pbv[:, 0:n * 128].rearrange("p (m t) -> p m t", t=128)
            if gi_ == 1:
                P.op("act", lambda h, m0=m0, n=n, src=src: h.copy(self.actT.t[:, m0:m0 + n, :], src), reads=[ps.d], writes=[self.actT.d])
            else:
                P.op("dve", lambda h, m0=m0, n=n, src=src: h.tensor_copy(self.actT.t[:, m0:m0 + n, :], src), reads=[ps.d], writes=[self.actT.d])
        self.dense_fm(wd, DFF, 1024, self.actT, self.resid_add)

    def rope_apply(self, src, dst, nh):
        P = self.P
        s3 = src.t[:, :].rearrange("p (h d) -> p h d", d=64)
        d3 = dst.t[:, :].rearrange("p (h d) -> p h d", d=64)
        cos = self.rope.t[:, 0:8].unsqueeze(1).to_broadcast([128, nh, 8])
        sin = self.rope.t[:, 8:16].unsqueeze(1).to_broadcast([128, nh, 8])
        x1, x2 = s3[:, :, 0:8], s3[:, :, 8:16]
        t = [self.rtmp.t[:, i, 0:nh, :] for i in range(4)]
        rd = [src.d, self.rope.d]
        P.op("dve", lambda h: h.tensor_tensor(t[0], x1, cos, ALU.mult), reads=rd, writes=[self.rtmp.d])
        P.op("dve", lambda h: h.tensor_tensor(t[1], x2, sin, ALU.mult), reads=rd, writes=[self.rtmp.d])
        P.op("dve", lambda h: h.tensor_tensor(t[2], x2, cos, ALU.mult), reads=rd, writes=[self.rtmp.d])
        P.op("dve", lambda h: h.tensor_tensor(t[3], x1, sin, ALU.mult), reads=rd, writes=[self.rtmp.d])
        P.op("dve", lambda h: h.tensor_tensor(d3[:, :, 0:8], t[0], t[1], ALU.subtract), reads=[self.rtmp.d], writes=[dst.d])
        P.op("dve", lambda h: h.tensor_tensor(d3[:, :, 8:16], t[2], t[3], ALU.add), reads=[self.rtmp.d], writes=[dst.d])

    def chunk(self, ty, b, c, ph="ABCDE", hsel=0):
        P, A = self.P, self.A
        self.hT = self.hTs[hsel]
        ti = 1 if ty == "S" else 0
        nseq = NSQ if ty == "S" else 1
        first = (ty == "P" and c == 0)
        last = (ty == "P" and c == NCHUNK - 1)
        if "A" in ph:
            xin = self.xin
            if ty == "S":
                P.dma("sp", self.ds_x, xin.t[:], A["xs"], writes=[xin.d])
            elif ty == "M":
                P.op("pool", lambda h: h.memset(xin.t[:], 0.0), writes=[xin.d])
                P.dma("sp", self.ds_x, xin.t[112:128, :], A["meta"], writes=[xin.d])
            else:
                P.dma("sp", self.ds_x, xin.t[:], A["xp"][b, c * 128:(c + 1) * 128, :], writes=[xin.d])
            rty = 17 if ty == "S" else (16 if ty == "M" else c)
            self.rope = self.ropes[self.rope_i % 2]
            self.rope_i += 1
            P.dma("sp", self.ds_rope, self.rope.t[:], A["c_rope"][rty], writes=[self.rope.d])
            for half in range(2):
                ps = self.PF()
                for m in range(4):
                    mm = half * 4 + m
                    P.op("pe", lambda h, m=m, mm=mm, ps=ps: h.transpose(ps.t[:, m * 128:(m + 1) * 128], xin.t[:, mm * 128:(mm + 1) * 128], self.ident_f.t[:]),
                         reads=[xin.d, self.ident_f.d], writes=[ps.d])
                hTc = self.hT
                P.op("act", lambda h, half=half, ps=ps, hTc=hTc: h.copy(hTc.t[:, half * 4:(half + 1) * 4, :], ps.t[:, :].rearrange("p (m t) -> p m t", t=128)),
                     reads=[ps.d], writes=[hTc.d])

            if ty == "M":
                P.op("pool", lambda h: h.memset(self.xpre.t[:, :, 0:3], 0.0), writes=[self.xpre.d])
            if first:
                P.op("pool", lambda h: h.tensor_copy(self.xpre.t[:, :, 0:3], self.cvtail_meta.t[:]), reads=[self.cvtail_meta.d], writes=[self.xpre.d])
            elif ty == "P":
                P.op("pool", lambda h: h.tensor_copy(self.xpre.t[:, :, 0:3], self.xpre.t[:, :, 128:131]), reads=[self.xpre.d], writes=[self.xpre.d])
            if ty == "S":
                P.dma("sp", self.ds_stg, self.big1.t[0:48, :], A["st_conv"], writes=[self.big1.d])
                for m in range(24):
                    ps = self.PF()
                    P.op("pe", lambda h, m=m, ps=ps: h.transpose(ps.t[:, 0:48], self.big1.t[0:48, m * 128:(m + 1) * 128], self.ident_f.t[0:48, 0:48]),
                         reads=[self.big1.d, self.ident_f.d], writes=[ps.d])
                    P.op("dve", lambda h, m=m, ps=ps: h.tensor_copy(self.xpre.t[:, m, :].rearrange("p (q t) -> p q t", t=11)[:, :, 0:3],
                                                                      ps.t[:, 0:48].rearrange("p (q r) -> p q r", r=3)),
                         reads=[ps.d], writes=[self.xpre.d])

            self.rms(0, self.uT)
            w_in = A["w_in"]
            for blk in range(4):
                def ev(ps, blk=blk):
                    P.op("act", lambda h: h.copy(self.z_tm.t[:, blk * 512:(blk + 1) * 512], ps.t[:, :]), reads=[ps.d], writes=[self.z_tm.d])
                self.dense_tm(("w_in", lambda w, blk=blk: w[:, blk * 512:(blk + 1) * 512]), 512, self.uT, ev)
            if ty == "S":
                def ev_x(m, ps):
                    P.op("dve", lambda h: h.tensor_copy(self.xpre.t[:, m, :].rearrange("p (q t) -> p q t", t=11)[:, :, 3:11],
                                                 ps.t[:, 0:128].rearrange("p (q t) -> p q t", t=8)), reads=[ps.d], writes=[self.xpre.d])
                    P.op("dve", lambda h: h.tensor_copy(self.uni.t[:, 0:1152].rearrange("p (m r) -> p m r", r=48)[:, m, :].rearrange("p (q r) -> p q r", r=3),
                                                        ps.t[:, 0:128].rearrange("p (q t) -> p q t", t=8)[:, :, 5:8]), reads=[ps.d], writes=[self.uni.d])
            else:
                def ev_x(m, ps):
                    P.op("dve", lambda h: h.tensor_copy(self.xpre.t[:, m, 3:131], ps.t[:, 0:128]), reads=[ps.d], writes=[self.xpre.d])
                    if last:
                        P.op("dve", lambda h: h.tensor_copy(self.uni.t[:, 0:1152].rearrange("p (m r) -> p m r", r=48)[:, m, 0:3], ps.t[:, 125:128]), reads=[ps.d], writes=[self.uni.d])
            self.dense_fm(("w_in", lambda w: w[:, 2048:5120]), 1024, 3072, self.uT, ev_x)
            def ev_dt(ps):
                P.op("dve", lambda h: h.tensor_copy(self.dtraw.t[:], ps.t[:, 0:32]), reads=[ps.d], writes=[self.dtraw.d])
            self.dense_tm(("w_in", lambda w: w[:, 5120:5152]), 32, self.uT, ev_dt)
            if ty == "M":
                P.op("pool", lambda h: h.tensor_copy(self.cvtail_meta.t[:], self.xpre.t[:, :, 128:131]), reads=[self.xpre.d], writes=[self.cvtail_meta.d])
            if ty == "S" or last:
                nr = 48 if ty == "S" else 3
                for m in range(24):
                    ps = self.PF()
                    P.op("pe", lambda h, m=m, ps=ps: h.transpose(ps.t[0:nr, 0:128], self.uni.t[:, 0:1152].rearrange("p (m r) -> p m r", r=48)[:, m, 0:nr], self.ident_f.t[:]),
                         reads=[self.uni.d, self.ident_f.d], writes=[ps.d])
                    P.op("dve", lambda h, m=m, ps=ps: h.tensor_copy(self.big1.t[0:nr, m * 128:(m + 1) * 128], ps.t[0:nr, 0:128]), reads=[ps.d], writes=[self.big1.d])
                dst = A["conv_s"] if ty == "S" else A["conv_p"][b]
                P.dma("pool", self.ds_out, dst, self.big1.t[0:nr, :], reads=[self.big1.d])

        if "B" in ph:
            if ty == "M":
                P.op("pool", lambda h: h.memset(self.ST.t[:], 0.0), writes=[self.ST.d])
                P.op("pool", lambda h: h.memset(self.STb.t[:], 0.0), writes=[self.STb.d])
            if first:
                P.op("dve", lambda h: h.tensor_copy(self.ST.t[:], self.ST_meta.t[:]), reads=[self.ST_meta.d], writes=[self.ST.d])
                P.op("act", lambda h: h.copy(self.STb.t[:], self.ST_meta.t[:]), reads=[self.ST_meta.d], writes=[self.STb.d])
            for mg in range(6):
                for k in range(4):
                    for mi in range(4):
                        m = mg * 4 + mi
                        acc = self.cacc[mi]
                        if ty == "S":
                            src = self.xpre.t[:, m, :].rearrange("p (q t) -> p q t", t=11)[:, :, k:k + 8]
                            out = acc.t[:, :].rearrange("p (q t) -> p q t", t=8)
                        else:
                            src = self.xpre.t[:, m, k:k + 128]
                            out = acc.t[:, :]
                        wk = self.conv_w.t[:, m, k:k + 1]
                        if k == 0:
                            bk = self.conv_b.t[:, m:m + 1]
                            P.op("dve", lambda h, src=src, out=out, wk=wk, bk=bk: h.tensor_scalar(out, src, wk, bk, ALU.mult, ALU.add), reads=[self.xpre.d, self.conv_w.d, self.conv_b.d], writes=[acc.d])
                        else:
                            P.op("dve", lambda h, src=src, out=out, wk=wk: h.scalar_tensor_tensor(out, src, wk, out, ALU.mult, ALU.add), reads=[self.xpre.d, self.conv_w.d, acc.d], writes=[acc.d])
                for mi in range(4):
                    m = mg * 4 + mi
                    acc = self.cacc[mi]
                    th = self.cth[mi]
                    P.op("act", lambda h, acc=acc, th=th: h.activation(th.t[:, :], acc.t[:, :], AF.Tanh), reads=[acc.d], writes=[th.d])
                    P.op("dve", lambda h, m=m, acc=acc, th=th: h.scalar_tensor_tensor(self.xbcT.t[:, m, :], th.t[:, :], 1.0, acc.t[:, :], ALU.add, ALU.mult),
                         reads=[acc.d, th.d], writes=[self.xbcT.d])
            for half in range(2):
                pb = self.PB()
                for m in range(8):
                    mm = half * 8 + m
                    P.op("pe", lambda h, m=m, mm=mm, pb=pb: h.transpose(pb.t[:, m * 128:(m + 1) * 128], self.xbcT.t[:, mm, :], self.ident_b.t[:]),
                         reads=[self.xbcT.d, self.ident_b.d], writes=[pb.d])
                P.op("act", lambda h, half=half, pb=pb: h.copy(self.x_tm.t[:, half * 1024:(half + 1) * 1024], pb.t[:, :]), reads=[pb.d], writes=[self.x_tm.d])
            pb = self.PB()
            for m in range(4):
                P.op("pe", lambda h, m=m, pb=pb: h.transpose(pb.t[:, m * 128:(m + 1) * 128], self.xbcT.t[:, 16 + m, :], self.ident_b.t[:]),
                     reads=[self.xbcT.d, self.ident_b.d], writes=[pb.d])
            P.op("dve", lambda h, pb=pb: h.tensor_copy(self.B_tm.t[:], pb.t[:, 0:512]), reads=[pb.d], writes=[self.B_tm.d])
            dt, a32 = self.dt, self.a32
            P.op("dve", lambda h: h.tensor_tensor(dt.t[:], self.dtraw.t[:], self.hp.t[:, 0, :], ALU.add), reads=[self.dtraw.d, self.hp.d], writes=[dt.d])
            P.op("act", lambda h: h.activation(dt.t[:], dt.t[:], AF.Exp), reads=[dt.d], writes=[dt.d])
            P.op("act", lambda h: h.activation(dt.t[:], dt.t[:], AF.Ln, bias=1.0), reads=[dt.d], writes=[dt.d])
            tmcol = self.tokmask.t[:, 1:2] if ty == "M" else self.tokmask.t[:, 0:1]
            P.op("dve", lambda h: h.tensor_scalar_mul(dt.t[:], dt.t[:], tmcol), reads=[dt.d, self.tokmask.d], writes=[dt.d])
            P.op("dve", lambda h: h.tensor_tensor(a32.t[:], dt.t[:], self.negA.t[:], ALU.mult), reads=[dt.d, self.negA.d], writes=[a32.d])
            P.op("dve", lambda h: h.tensor_copy(self.ahi.t[:], a32.t[:]), reads=[a32.d], writes=[self.ahi.d])
            P.op("dve", lambda h: h.tensor_tensor(self.alo.t[:], a32.t[:], self.ahi.t[:], ALU.subtract), reads=[a32.d, self.ahi.d], writes=[self.alo.d])
            ps = self.PF()
            P.op("pe", lambda h, ps=ps: h.matmul(ps.t[:, 0:32], lhsT=self.tri.t[:, ti, :], rhs=self.ahi.t[:], start=True, stop=False), reads=[self.tri.d, self.ahi.d], writes=[ps.d])
            P.op("pe", lambda h, ps=ps: h.matmul(ps.t[:, 0:32], lhsT=self.tri.t[:, ti, :], rhs=self.alo.t[:], start=False, stop=True), reads=[self.tri.d, self.alo.d], writes=[ps.d])
            P.op("dve", lambda h, ps=ps: h.tensor_scalar_mul(self.negcs.t[:], ps.t[:, 0:32], -1.0), reads=[ps.d], writes=[self.negcs.d])
            P.op("act", lambda h, ps=ps: h.activation(self.ecs.t[:], ps.t[:, 0:32], AF.Exp), reads=[ps.d], writes=[self.ecs.d])
            x3 = self.x_tm.t[:, :].rearrange("p (h d) -> p h d", d=64)
            P.op("dve", lambda h: h.tensor_tensor(self.xdt.t[:, :].rearrange("p (h d) -> p h d", d=64), x3, dt.t[:].unsqueeze(2).to_broadcast([128, 32, 64]), ALU.mult),
                 reads=[self.x_tm.d, dt.d], writes=[self.xdt.d])
            ps = self.PF()
            for g in range(4):
                P.op("pe", lambda h, g=g, ps=ps: h.matmul(ps.t[:, g * 128:(g + 1) * 128], lhsT=self.xbcT.t[:, 16 + g, :], rhs=self.xbcT.t[:, 20 + g, :], start=True, stop=True),
                     reads=[self.xbcT.d], writes=[ps.d])
            P.op("act", lambda h, ps=ps: h.copy(self.cbT.t[:, :, :], ps.t[:, :].rearrange("p (g l) -> p g l", l=128)), reads=[ps.d], writes=[self.cbT.d])
            psYs = {}

            def emit_decay(hb):
                g = hb // 2
                dec, MT = self.dec[hb % 2], self.MT[hb % 2]
                ps = self.PF()
                for hh in range(4):
                    hd = hb * 4 + hh
                    o = ps.t[:, hh * 128:(hh + 1) * 128]
                    P.op("pe", lambda h, o=o, hd=hd: h.matmul(o, lhsT=self.ahi.t[:, hd:hd + 1].to_broadcast([128, 128]), rhs=self.tri.t[:, ti, :], start=True, stop=False),
                         reads=[self.ahi.d, self.tri.d], writes=[ps.d])
                    P.op("pe", lambda h, o=o, hd=hd: h.matmul(o, lhsT=self.alo.t[:, hd:hd + 1].to_broadcast([128, 128]), rhs=self.tri.t[:, ti, :], start=False, stop=False),
                         reads=[self.alo.d, self.tri.d], writes=[ps.d])
                    P.op("pe", lambda h, o=o: h.matmul(o, lhsT=self.ident_b.t[:], rhs=self.mb.t[:, ti, :], start=False, stop=True),
                         reads=[self.ident_b.d, self.mb.d], writes=[ps.d])
                for hh in range(4):
                    hd = hb * 4 + hh
                    P.op("act", lambda h, hh=hh, hd=hd, ps=ps, dec=dec: h.activation(dec.t[:, hh, :], ps.t[:, hh * 128:(hh + 1) * 128], AF.Exp, bias=self.negcs.t[:, hd:hd + 1]),
                         reads=[ps.d, self.negcs.d], writes=[dec.d])
                P.op("dve", lambda h, g=g, dec=dec, MT=MT: h.tensor_tensor(MT.t[:, :, :], dec.t[:, :, :], self.cbT.t[:, g, :].unsqueeze(1).to_broadcast([128, 4, 128]), ALU.mult),
                     reads=[dec.d, self.cbT.d], writes=[MT.d])

            def emit_ydiag(hb):
                g = hb // 2
                MT = self.MT[hb % 2]
                if hb % 2 == 0:
                    psYs[g] = self.PF()
                psY = psYs[g]
                for hh in range(4):
                    hd = hb * 4 + hh
                    col = (hd % 8) * 64
                    P.op("pe", lambda h, hh=hh, hd=hd, col=col, MT=MT, psY=psY: h.matmul(psY.t[:, col:col + 64], lhsT=MT.t[:, hh, :], rhs=self.xdt.t[:, hd * 64:(hd + 1) * 64], start=True, stop=True),
                         reads=[MT.d, self.xdt.d], writes=[psY.d])
                if hb % 2 == 1:
                    tg = self.tmpg[g % 2]
                    P.op("dve", lambda h, g=g, tg=tg: h.tensor_tensor(tg.t[:, :].rearrange("p (h d) -> p h d", d=64), self.x_tm.t[:, g * 512:(g + 1) * 512].rearrange("p (h d) -> p h d", d=64),
                                                                     self.hp.t[:, 2, g * 8:(g + 1) * 8].unsqueeze(2).to_broadcast([128, 8, 64]), ALU.mult),
                         reads=[self.x_tm.d, self.hp.d], writes=[tg.d])
                    P.op("dve", lambda h, g=g, tg=tg, psY=psY: h.tensor_tensor(self.big1.t[:, g * 512:(g + 1) * 512], psY.t[:, :], tg.t[:], ALU.add),
                         reads=[psY.d, tg.d], writes=[self.big1.d])

            emit_decay(0)
            for hb in range(8):
                if hb + 1 < 8:
                    emit_decay(hb + 1)
                emit_ydiag(hb)
            for q in range(nseq):
                if ty == "S":
                    selq = self.sel.t[:, q, :]
                    mbcol = self.mbq.t[:, q:q + 1]
                else:
                    selq = self.ones_b.t[:]
                    mbcol = self.zero1.t[:, 0:1]
                ps = self.PF()
                P.op("pe", lambda h, ps=ps, selq=selq: h.matmul(ps.t[:, 0:32], lhsT=selq, rhs=self.ahi.t[:], start=True, stop=False), reads=[self.sel.d, self.ones_b.d, self.ahi.d], writes=[ps.d])
                P.op("pe", lambda h, ps=ps, selq=selq: h.matmul(ps.t[:, 0:32], lhsT=selq, rhs=self.alo.t[:], start=False, stop=True), reads=[self.sel.d, self.ones_b.d, self.alo.d], writes=[ps.d])
                P.op("act", lambda h, ps=ps: h.activation(self.dA.t[:], ps.t[:, 0:32], AF.Exp), reads=[ps.d], writes=[self.dA.d])
                P.op("dve", lambda h, ps=ps: h.tensor_tensor(self.wdec.t[:], ps.t[:, 0:32], self.negcs.t[:], ALU.add), reads=[ps.d, self.negcs.d], writes=[self.wdec.d])
                P.op("act", lambda h, mbcol=mbcol: h.activation(self.wdec.t[:], self.wdec.t[:], AF.Exp, bias=mbcol), reads=[self.wdec.d, self.mbq.d, self.zero1.d], writes=[self.wdec.d])
                P.op("dve", lambda h: h.tensor_tensor(self.xw.t[:, :].rearrange("p (h d) -> p h d", d=64), self.xdt.t[:, :].rearrange("p (h d) -> p h d", d=64),
                                                      self.wdec.t[:].unsqueeze(2).to_broadcast([128, 32, 64]), ALU.mult),
                     reads=[self.xdt.d, self.wdec.d], writes=[self.xw.d])
                if ty == "S":
                    P.op("dve", lambda h, q=q: h.tensor_tensor(self.CTq.t[:, :, :], self.xbcT.t[:, 20:24, :], self.rowm.t[:, q, :].unsqueeze(1).to_broadcast([128, 4, 128]), ALU.mult),
                         reads=[self.xbcT.d, self.rowm.d], writes=[self.CTq.d])
                    CT = lambda g: self.CTq.t[:, g, :]
                    ctd = self.CTq.d
                else:
                    CT = lambda g: self.xbcT.t[:, 20 + g, :]
                    ctd = self.xbcT.d
                if ty == "S":
                    for j in range(16):
                        P.dma("sp", self.ds_stg, self.big2.t[:, 0, j, :], A["st_ssm"][q, 2 * j:2 * j + 2].rearrange("h2 p n -> (h2 p) n"), writes=[self.big2_d0])
                    for jb in range(4):
                        ps = self.PF()
                        for jj in range(4):
                            j = jb * 4 + jj
                            P.op("pe", lambda h, j=j, jj=jj, ps=ps: h.transpose(ps.t[:, jj * 128:(jj + 1) * 128], self.big2.t[:, 0, j, :], self.ident_f.t[:]),
                                 reads=[self.big2_d0, self.ident_f.d], writes=[ps.d])
                        P.op("dve", lambda h, jb=jb, ps=ps: h.tensor_copy(self.ST.t[:, jb * 512:(jb + 1) * 512], ps.t[:, :]), reads=[ps.d], writes=[self.ST.d])
                        P.op("act", lambda h, jb=jb, ps=ps: h.copy(self.STb.t[:, jb * 512:(jb + 1) * 512], ps.t[:, :]), reads=[ps.d], writes=[self.STb.d])
                for g in range(4):
                    ps = self.PF()
                    P.op("pe", lambda h, g=g, ps=ps, CT=CT: h.matmul(ps.t[:, :], lhsT=CT(g), rhs=self.STb.t[:, g * 512:(g + 1) * 512], start=True, stop=True),
                         reads=[ctd, self.STb.d], writes=[ps.d])
                    tg = self.tmpg[g % 2]
                    P.op("dve", lambda h, g=g, ps=ps, tg=tg: h.tensor_tensor(tg.t[:, :].rearrange("p (h d) -> p h d", d=64), ps.t[:, :].rearrange("p (h d) -> p h d", d=64),
                                                                           self.ecs.t[:, g * 8:(g + 1) * 8].unsqueeze(2).to_broadcast([128, 8, 64]), ALU.mult),
                         reads=[ps.d, self.ecs.d], writes=[tg.d])
                    P.op("dve", lambda h, g=g, tg=tg: h.tensor_tensor(self.big1.t[:, g * 512:(g + 1) * 512], self.big1.t[:, g * 512:(g + 1) * 512], tg.t[:], ALU.add),
                         reads=[tg.d, self.big1.d], writes=[self.big1.d])
                for g in range(4):
                    ps = self.PF()
                    P.op("pe", lambda h, g=g, ps=ps: h.matmul(ps.t[:, :], lhsT=self.B_tm.t[:, g * 128:(g + 1) * 128], rhs=self.xw.t[:, g * 512:(g + 1) * 512], start=True, stop=True),
                         reads=[self.B_tm.d, self.xw.d], writes=[ps.d])
                    sg3 = self.ST.t[:, g * 512:(g + 1) * 512].rearrange("p (h d) -> p h d", d=64)
                    P.op("dve", lambda h, g=g, sg3=sg3: h.tensor_tensor(sg3, sg3, self.dA.t[:, g * 8:(g + 1) * 8].unsqueeze(2).to_broadcast([128, 8, 64]), ALU.mult),
                         reads=[self.ST.d, self.dA.d], writes=[self.ST.d])
                    P.op("dve", lambda h, g=g, ps=ps: h.tensor_tensor(self.ST.t[:, g * 512:(g + 1) * 512], self.ST.t[:, g * 512:(g + 1) * 512], ps.t[:, :], ALU.add),
                         reads=[self.ST.d, ps.d], writes=[self.ST.d])
                if ty != "S":
                    P.op("act", lambda h: h.copy(self.STb.t[:], self.ST.t[:]), reads=[self.ST.d], writes=[self.STb.d])
                if ty == "M":
                    P.op("pool", lambda h: h.tensor_copy(self.ST_meta.t[:], self.ST.t[:]), reads=[self.ST.d], writes=[self.ST_meta.d])
                if ty == "S" or last:
                    for jb in range(4):
                        ps = self.PF()
                        for jj in range(4):
                            j = jb * 4 + jj
                            P.op("pe", lambda h, j=j, jj=jj, ps=ps: h.transpose(ps.t[:, jj * 128:(jj + 1) * 128], self.ST.t[:, j * 128:(j + 1) * 128], self.ident_f.t[:]),
                                 reads=[self.ST.d, self.ident_f.d], writes=[ps.d])
                        P.op("act", lambda h, jb=jb, ps=ps: h.copy(self.big2.t[:, 1, jb * 4:(jb + 1) * 4, :], ps.t[:, :].rearrange("p (j n) -> p j n", n=128)), reads=[ps.d], writes=[self.big2_d1])
                    dst = A["ssm_s"][q] if ty == "S" else A["ssm_p"][b]
                    for j in range(16):
                        P.dma("pool", self.ds_stgo, dst[2 * j:2 * j + 2].rearrange("h2 p n -> (h2 p) n"), self.big2.t[:, 1, j, :], reads=[self.big2_d1])
            P.op("pool", lambda h: h.memset(self.ssq.t[:], 0.0), writes=[self.ssq.d])
            P.op("act", lambda h: h.activation(self.xw.t[:], self.z_tm.t[:], AF.Tanh, scale=0.5), reads=[self.z_tm.d], writes=[self.xw.d])
            P.op("dve", lambda h: h.scalar_tensor_tensor(self.z_tm.t[:], self.xw.t[:], 1.0, self.z_tm.t[:], ALU.add, ALU.mult), reads=[self.xw.d, self.z_tm.d], writes=[self.z_tm.d])
            P.op("dve", lambda h: h.scalar_tensor_tensor(self.big1.t[:, 0:2048], self.big1.t[:, 0:2048], 0.5, self.z_tm.t[:], ALU.mult, ALU.mult), reads=[self.big1.d, self.z_tm.d], writes=[self.big1.d])
            for g in range(4):
                P.op("act", lambda h, g=g: h.activation(self.sz.t[:], self.big1.t[:, g * 512:(g + 1) * 512], AF.Square, accum_out=self.ssq.t[:, g:g + 1]), reads=[self.big1.d], writes=[self.sz.d, self.ssq.d])
            P.op("act", lambda h: h.activation(self.grs.t[:], self.ssq.t[:], AF.Ln, bias=EPS, scale=1.0 / 512.0), reads=[self.ssq.d], writes=[self.grs.d])
            P.op("act", lambda h: h.activation(self.grs.t[:], self.grs.t[:], AF.Exp, scale=-0.5), reads=[self.grs.d], writes=[self.grs.d])
            for g in range(4):
                P.op("dve", lambda h, g=g: h.tensor_scalar_mul(self.x_tm.t[:, g * 512:(g + 1) * 512], self.big1.t[:, g * 512:(g + 1) * 512], self.grs.t[:, g:g + 1]), reads=[self.big1.d, self.grs.d], writes=[self.x_tm.d])
            for g in range(4):
                pb = self.PB()
                for m in range(4):
                    mm = g * 4 + m
                    P.op("pe", lambda h, m=m, mm=mm, pb=pb: h.transpose(pb.t[:, m * 128:(m + 1) * 128], self.x_tm.t[:, mm * 128:(mm + 1) * 128], self.ident_b.t[:]),
                         reads=[self.x_tm.d, self.ident_b.d], writes=[pb.d])
                P.op("dve", lambda h, g=g, pb=pb: h.tensor_tensor(self.ynT.t[:, g * 4:(g + 1) * 4, :], pb.t[:, 0:512].rearrange("p (m t) -> p m t", t=128),
                                                                   self.gate_g.t[:, g * 4:(g + 1) * 4].unsqueeze(2).to_broadcast([128, 4, 128]), ALU.mult),
                     reads=[pb.d, self.gate_g.d], writes=[self.ynT.d])
        if "C" in ph:
            self.dense_fm(("w_out", lambda w: w), 2048, 1024, self.ynT, self.resid_add)
            self.ffn(0)

        if "D" in ph:
            cur, prv = self.pbuf, 1 - self.pbuf
            kTo, vbo = self.kT[cur], self.v_b[cur]
            self.rms(2, self.uT)
            def ev_k(ps):
                P.op("act", lambda h: h.copy(self.k_tm.t[:], ps.t[:, 0:256]), reads=[ps.d], writes=[self.k_tm.d])
                P.op("dve", lambda h: h.tensor_copy(self.k_rot.t[:], ps.t[:, 0:256]), reads=[ps.d], writes=[self.k_rot.d])
            self.dense_tm(("w_k", lambda w: w), 256, self.uT, ev_k)
            self.rope_apply(self.k_tm, self.k_rot, 4)
            P.op("act", lambda h: h.copy(self.k_b.t[:], self.k_rot.t[:]), reads=[self.k_rot.d], writes=[self.k_b.d])
            pb = self.PB()
            for k in range(4):
                P.op("pe", lambda h, k=k, pb=pb: h.transpose(pb.t[0:64, k * 128:(k + 1) * 128], self.k_b.t[:, k * 64:(k + 1) * 64], self.ident_b.t[:]),
                     reads=[self.k_b.d, self.ident_b.d], writes=[pb.d])
            P.op("dve", lambda h, pb=pb: h.tensor_copy(kTo.t[:, :, :], pb.t[0:64, 0:512].rearrange("p (k t) -> p k t", t=128)), reads=[pb.d], writes=[kTo.d])
            def ev_v(ps):
                P.op("act", lambda h: h.copy(self.v_tm.t[:], ps.t[:, 0:256]), reads=[ps.d], writes=[self.v_tm.d])
                P.op("dve", lambda h: h.tensor_copy(vbo.t[:], ps.t[:, 0:256]), reads=[ps.d], writes=[vbo.d])
            self.dense_tm(("w_v", lambda w: w), 256, self.uT, ev_v)
            if ty == "S":
                for q in range(NSQ):
                    P.dma("pool", self.ds_out, A["k_s"][q, 120:128, :], self.k_rot.t[8 * q:8 * q + 8, :], reads=[self.k_rot.d])
                    P.dma("pool", self.ds_out, A["v_s"][q, 120:128, :], self.v_tm.t[8 * q:8 * q + 8, :], reads=[self.v_tm.d])
                P.dma("pool", self.ds_cp, A["k_s"][:, 0:120, :], A["st_k"][:, 8:128, :])
                P.dma("pool", self.ds_cp, A["v_s"][:, 0:120, :], A["st_v"][:, 8:128, :])
            if last:
                P.dma("pool", self.ds_out, A["k_p"][b], self.k_rot.t[:], reads=[self.k_rot.d])
                P.dma("pool", self.ds_out, A["v_p"][b], self.v_tm.t[:], reads=[self.v_tm.d])
            if ty == "M":
                P.op("pool", lambda h: h.tensor_copy(self.kT_meta.t[:], kTo.t[:]), reads=[kTo.d], writes=[self.kT_meta.d])
                P.op("pool", lambda h: h.tensor_copy(self.v_meta.t[:], vbo.t[:]), reads=[vbo.d], writes=[self.v_meta.d])
            self.rms(3, self.uT)
            for blk in range(2):
                def ev_q(ps, blk=blk):
                    P.op("act", lambda h: h.copy(self.q_tm.t[:, blk * 512:(blk + 1) * 512], ps.t[:, :]), reads=[ps.d], writes=[self.q_tm.d])
                    P.op("dve", lambda h: h.tensor_copy(self.q_rot.t[:, blk * 512:(blk + 1) * 512], ps.t[:, :]), reads=[ps.d], writes=[self.q_rot.d])
                self.dense_tm(("w_q", lambda w, blk=blk: w[:, blk * 512:(blk + 1) * 512]), 512, self.uT, ev_q)
            self.rope_apply(self.q_tm, self.q_rot, 16)
            P.op("act", lambda h: h.copy(self.q_b.t[:], self.q_rot.t[:]), reads=[self.q_rot.d], writes=[self.q_b.d])
            for half in range(2):
                pb = self.PB()
                for hh in range(8):
                    hd = half * 8 + hh
                    P.op("pe", lambda h, hh=hh, hd=hd, pb=pb: h.transpose(pb.t[0:64, hh * 128:(hh + 1) * 128], self.q_b.t[:, hd * 64:(hd + 1) * 64], self.ident_b.t[:]),
                         reads=[self.q_b.d, self.ident_b.d], writes=[pb.d])
                P.op("dve", lambda h, half=half, pb=pb: h.tensor_copy(self.qT.t[:, half * 8:(half + 1) * 8, :], pb.t[0:64, :].rearrange("p (k t) -> p k t", t=128)),
                     reads=[pb.d], writes=[self.qT.d])
            if ty == "S":
                for q in range(NSQ):
                    P.dma("sp", self.ds_kv, self.sk32.t[:], A["st_k"][q], writes=[self.sk32.d])
                    P.dma("sp", self.ds_kv, self.sv32.t[:], A["st_v"][q], writes=[self.sv32.d])
                    P.op("act", lambda h: h.copy(self.skb.t[:], self.sk32.t[:]), reads=[self.sk32.d], writes=[self.skb.d])
                    P.op("dve", lambda h: h.tensor_copy(self.svb.t[:], self.sv32.t[:]), reads=[self.sv32.d], writes=[self.svb.d])
                    pb = self.PB()
                    for k in range(4):
                        P.op("pe", lambda h, k=k, pb=pb: h.transpose(pb.t[0:64, k * 128:(k + 1) * 128], self.skb.t[:, k * 64:(k + 1) * 64], self.ident_b.t[:]),
                             reads=[self.skb.d, self.ident_b.d], writes=[pb.d])
                    P.op("dve", lambda h, pb=pb: h.tensor_copy(self.kTs.t[:, :, :], pb.t[0:64, 0:512].rearrange("p (k t) -> p k t", t=128)), reads=[pb.d], writes=[self.kTs.d])
                    ps = self.PF()
                    for k in range(4):
                        o = ps.t[:, k * 32:(k + 1) * 32]
                        P.op("pe", lambda h, k=k, o=o, q=q: h.matmul(o.rearrange("p (j t) -> p j t", t=8), lhsT=self.kTs.t[:, k, :], rhs=self.qT.t[:, 4 * k:4 * k + 4, 8 * q:8 * q + 8], start=True, stop=False),
                             reads=[self.kTs.d, self.qT.d], writes=[ps.d])
                        P.op("pe", lambda h, k=k, o=o: h.matmul(o, lhsT=self.ident_b.t[:], rhs=self.mbstate.t[:, k * 32:(k + 1) * 32], start=False, stop=True),
                             reads=[self.ident_b.d, self.mbstate.d], writes=[ps.d])
                    P.op("act", lambda h, ps=ps: h.activation(self.PTs.t[:], ps.t[:, 0:128], AF.Exp, scale=0.125), reads=[ps.d], writes=[self.PTs.d])
                    ps2 = self.PF()
                    for k in range(4):
                        P.op("pe", lambda h, k=k, ps2=ps2: h.matmul(ps2.t[0:64, k * 32:(k + 1) * 32], lhsT=self.svb.t[:, k * 64:(k + 1) * 64], rhs=self.PTs.t[:, k * 32:(k + 1) * 32], start=True, stop=True),
                             reads=[self.svb.d, self.PTs.d], writes=[ps2.d])
                    P.op("pe", lambda h, ps2=ps2: h.matmul(ps2.t[0:64, 128:256], lhsT=self.ones_b.t[:, 0:64], rhs=self.PTs.t[:], start=True, stop=True),
                         reads=[self.ones_b.d, self.PTs.d], writes=[ps2.d])
                    P.op("dve", lambda h, q=q, ps2=ps2: h.tensor_copy(self.big2.t[0:64, 0, :, 8 * q:8 * q + 8], ps2.t[0:64, 0:128].rearrange("p (h t) -> p h t", t=8)), reads=[ps2.d], writes=[self.big2_d0])
                    P.op("dve", lambda h, q=q, ps2=ps2: h.tensor_copy(self.big2.t[0:64, 1, :, 8 * q:8 * q + 8], ps2.t[0:64, 128:256].rearrange("p (h t) -> p h t", t=8)), reads=[ps2.d], writes=[self.big2_d1])
            use_prev = ty == "P"
            if first:
                kTp, vbp = self.kT_meta, self.v_meta
            else:
                kTp, vbp = self.kT[prv], self.v_b[prv]
            mpi = 1 if first else 0
            sc = {}

            def emit_scores(hd):
                k = hd // 4
                PT = self.PT[hd % 2]
                ps = self.PF()
                if use_prev:
                    P.op("pe", lambda h, ps=ps, k=k, hd=hd: h.matmul(ps.t[:, 0:128], lhsT=kTp.t[:, k, :], rhs=self.qT.t[:, hd, :], start=True, stop=False),
                         reads=[kTp.d, self.qT.d], writes=[ps.d])
                    P.op("pe", lambda h, ps=ps: h.matmul(ps.t[:, 0:128], lhsT=self.ident_b.t[:], rhs=self.mbprev.t[:, mpi, :], start=False, stop=True),
                         reads=[self.ident_b.d, self.mbprev.d], writes=[ps.d])
                P.op("pe", lambda h, ps=ps, k=k, hd=hd: h.matmul(ps.t[:, 128:256], lhsT=kTo.t[:, k, :], rhs=self.qT.t[:, hd, :], start=True, stop=False),
                     reads=[kTo.d, self.qT.d], writes=[ps.d])
                P.op("pe", lambda h, ps=ps: h.matmul(ps.t[:, 128:256], lhsT=self.ident_b.t[:], rhs=self.mb.t[:, ti, :], start=False, stop=True),
                     reads=[self.ident_b.d, self.mb.d], writes=[ps.d])
                lo = 0 if use_prev else 128
                P.op("act", lambda h, ps=ps, PT=PT, lo=lo: h.activation(PT.t[:, lo:256], ps.t[:, lo:256], AF.Exp, scale=0.125), reads=[ps.d], writes=[PT.d])

            def emit_pv(hd, psO, psD):
                k = hd // 4
                hh = hd % 4
                PT = self.PT[hd % 2]
                oo = psO.t[0:64, hh * 128:(hh + 1) * 128]
                od = psD.t[0:64, hh * 128:(hh + 1) * 128]
                if use_prev:
                    P.op("pe", lambda h, oo=oo, k=k, PT=PT: h.matmul(oo, lhsT=vbp.t[:, k * 64:(k + 1) * 64], rhs=PT.t[:, 0:128], start=True, stop=False), reads=[vbp.d, PT.d], writes=[psO.d])
                P.op("pe", lambda h, oo=oo, k=k, PT=PT: h.matmul(oo, lhsT=vbo.t[:, k * 64:(k + 1) * 64], rhs=PT.t[:, 128:256], start=(not use_prev), stop=True), reads=[vbo.d, PT.d], writes=[psO.d])
                if use_prev:
                    P.op("pe", lambda h, od=od, PT=PT: h.matmul(od, lhsT=self.ones_b.t[:, 0:64], rhs=PT.t[:, 0:128], start=True, stop=False), reads=[self.ones_b.d, PT.d], writes=[psD.d])
                P.op("pe", lambda h, od=od, PT=PT: h.matmul(od, lhsT=self.ones_b.t[:, 0:64], rhs=PT.t[:, 128:256], start=(not use_prev), stop=True), reads=[self.ones_b.d, PT.d], writes=[psD.d])

            emit_scores(0)
            for hq in range(4):
                psO = self.PF()
                psD = self.PF()
                for hh in range(4):
                    hd = hq * 4 + hh
                    if hd + 1 < 16:
                        emit_scores(hd + 1)
                    emit_pv(hd, psO, psD)
                h0 = hq * 4
                den = self.den
                P.op("dve", lambda h, psD=psD, h0=h0: h.tensor_tensor(den.t[:, :, :], psD.t[0:64, :].rearrange("p (h t) -> p h t", t=128),
                                                                      self.esink.t[0:64, h0:h0 + 4].unsqueeze(2).to_broadcast([64, 4, 128]), ALU.add),
                     reads=[psD.d, self.esink.d], writes=[den.d])
                if ty == "S":
                    P.op("dve", lambda h, h0=h0: h.tensor_tensor(den.t[:, :, :], den.t[:, :, :], self.big2.t[0:64, 1, h0:h0 + 4, :], ALU.add), reads=[den.d, self.big2_d1], writes=[den.d])
                    P.op("dve", lambda h, h0=h0, psO=psO: h.tensor_tensor(self.big2.t[0:64, 0, h0:h0 + 4, :], self.big2.t[0:64, 0, h0:h0 + 4, :], psO.t[0:64, :].rearrange("p (h t) -> p h t", t=128), ALU.add),
                         reads=[psO.d, self.big2_d0], writes=[self.big2_d0])
                P.op("dve", lambda h: h.reciprocal(den.t[:, :, :], den.t[:, :, :]), reads=[den.d], writes=[den.d])
                if ty == "S":
                    P.op("dve", lambda h, h0=h0: h.tensor_tensor(self.oT.t[:, h0:h0 + 4, :], self.big2.t[0:64, 0, h0:h0 + 4, :], den.t[:, :, :], ALU.mult),
                         reads=[self.big2_d0, den.d], writes=[self.oT.d])
                else:
                    P.op("dve", lambda h, h0=h0, psO=psO: h.tensor_tensor(self.oT.t[:, h0:h0 + 4, :], psO.t[0:64, :].rearrange("p (h t) -> p h t", t=128), den.t[:, :, :], ALU.mult),
                         reads=[psO.d, den.d], writes=[self.oT.d])
            self.pbuf = 1 - self.pbuf
        if "E" in ph:
            self.dense_fm(("w_o", lambda w: w), 1024, 1024, self.oT, self.resid_add, part=64)
            self.ffn(1)
            if ty != "M":
                self.rms(5, self.q_tm, view3=True)
                for half in range(2):
                    ps = self.PF()
                    for m in range(4):
                        mm = half * 4 + m
                        P.op("pe", lambda h, m=m, mm=mm, ps=ps: h.transpose(ps.t[:, m * 128:(m + 1) * 128], self.q_tm.t[:, mm * 128:(mm + 1) * 128], self.ident_f.t[:]),
                             reads=[self.q_tm.d, self.ident_f.d], writes=[ps.d])
                    P.op("act", lambda h, half=half, ps=ps: h.copy(self.y_st.t[:, half * 512:(half + 1) * 512], ps.t[:, :]), reads=[ps.d], writes=[self.y_st.d])
                dst = A["y_s"] if ty == "S" else A["y_p"][b, c * 128:(c + 1) * 128, :]
                P.dma("pool", self.ds_y, dst, self.y_st.t[:], reads=[self.y_st.d])


_CACHE = {}


def _get_nc():
    if "nc" not in _CACHE:
        k = Kern()
        _CACHE["nc"] = k.build()
        _CACHE["outs"] = k.out_names
    return _CACHE["nc"]


def _col(v, nchunk):
    return np.ascontiguousarray(np.asarray(v, np.float32).reshape(nchunk, 128).T)


def kernel(x_prompt, x_sample, state_ssm, state_conv, state_k, state_v, meta_tokens,
           ssm_norm_g, ssm_w_in, ssm_conv_w, ssm_conv_b, ssm_dt_bias, ssm_A_log, ssm_D,
           ssm_gate_norm_g, ssm_w_out, kv_norm_g, w_k, w_v, attn_norm_g, w_q, attn_sinks, w_o,
           ffn_norm_g, ffn_w_gate, ffn_w_up, ffn_w_down, final_norm_g):
    f = lambda a: np.ascontiguousarray(np.asarray(a, dtype=np.float32))
    nc = _get_nc()
    cst = make_consts()
    gcols = np.stack([_col(ssm_norm_g[0], 8), _col(ffn_norm_g[0], 8), _col(kv_norm_g, 8), _col(attn_norm_g[0], 8),
                      _col(ffn_norm_g[1], 8), _col(final_norm_g, 8)], axis=1)
    gate_g = _col(ssm_gate_norm_g[0], 16)
    conv_w = np.ascontiguousarray(f(ssm_conv_w[0]).reshape(4, 24, 128).transpose(2, 1, 0))
    conv_b = _col(ssm_conv_b[0], 24)
    hp = np.zeros((128, 4, 32), np.float32)
    hp[:, 0, :] = f(ssm_dt_bias[0])[None, :]
    hp[:, 1, :] = f(ssm_A_log[0])[None, :]
    hp[:, 2, :] = f(ssm_D[0])[None, :]
    sinks = np.ascontiguousarray(np.broadcast_to(f(attn_sinks[0])[None, :], (128, 16)))
    shared = {
        "meta": f(meta_tokens), "w_in": f(ssm_w_in[0]), "w_out": f(ssm_w_out[0]), "w_k": f(w_k), "w_v": f(w_v),
        "w_q": f(w_q[0]), "w_o": f(w_o[0]), "w_gate": f(ffn_w_gate), "w_up": f(ffn_w_up), "w_down": f(ffn_w_down),
        "gcols": np.ascontiguousarray(gcols), "gate_g": gate_g, "conv_w": conv_w, "conv_b": conv_b, "hp": hp, "sinks": sinks,
        "c_tri": cst["tri"], "c_mb": cst["mb"], "c_mbprev": cst["mbprev"], "c_mbstate": cst["mbstate"],
        "c_ident_b": cst["ident_b"], "c_ident_f": cst["ident_f"], "c_ones_b": cst["ones_b"], "c_sel": cst["sel"],
        "c_mbq": cst["mbq"], "c_rowm": cst["rowm"], "c_tokmask": cst["tokmask"], "c_rope": cst["rope"],
    }
    xp, xs = f(x_prompt), f(x_sample)
    sssm, sconv, sk, sv = f(state_ssm[0]), f(state_conv[0]), f(state_k), f(state_v)
    in_maps = []
    for i in range(NCORE):
        m = dict(shared)
        m["xs"] = xs[NSQ * i:NSQ * (i + 1)].reshape(128, 1024)
        m["xp"] = xp[NPB * i:NPB * (i + 1)]
        m["st_ssm"] = sssm[NSQ * i:NSQ * (i + 1)]
        m["st_conv"] = sconv[NSQ * i:NSQ * (i + 1)].reshape(NSQ * 3, 3072)
        m["st_k"] = sk[NSQ * i:NSQ * (i + 1)].reshape(NSQ, 128, 256)
        m["st_v"] = sv[NSQ * i:NSQ * (i + 1)].reshape(NSQ, 128, 256)
        in_maps.append(m)
    res = run_bass_kernel_spmd(nc, in_maps, core_ids=list(range(NCORE)))
    R = res.results
    cat = lambda name: np.concatenate([np.asarray(r[name], np.float32) for r in R], axis=0)
    y_prompt = cat("y_p")
    y_sample = cat("y_s").reshape(128, 8, 1024)
    ssm_p = cat("ssm_p")[None]
    conv_p = cat("conv_p")[None]
    k_p = cat("k_p").reshape(16, 128, 4, 64)
    v_p = cat("v_p").reshape(16, 128, 4, 64)
    ssm_s = cat("ssm_s")[None]
    conv_s = cat("conv_s").reshape(128, 3, 3072)[None]
    k_s = cat("k_s").reshape(128, 128, 4, 64)
    v_s = cat("v_s").reshape(128, 128, 4, 64)
    return (y_prompt, y_sample, ssm_p, conv_p, k_p, v_p, ssm_s, conv_s, k_s, v_s)
```

```python
import numpy as np
from contextlib import ExitStack
import ml_dtypes
import concourse.bass as bass
import concourse.mybir as mybir
from concourse.bass_utils import run_bass_kernel_spmd

F32 = mybir.dt.float32
BF16 = mybir.dt.bfloat16
AF = mybir.ActivationFunctionType
ALU = mybir.AluOpType

SEM_CAP = 30000
NEG = -30000.0
NCORE = 8
NSQ = 16
NPB = 2
NCHUNK = 16
DFF = 2816
EPS = 1e-5


class Dep:
    __slots__ = ("name", "w", "rs", "excl")

    def __init__(self, name=""):
        self.name = name
        self.w = None
        self.rs = []
        self.excl = False


class EngS:
    def __init__(self, name, handle):
        self.name = name
        self.h = handle
        self.count = 0
        self.sems = []
        self.seen = {}
        self.ops = []


class DmaSem:
    def __init__(self, sem, name):
        self.sem = sem
        self.total = 0
        self.name = name


class Prog:
    def __init__(self, nc, stack):
        self.nc = nc
        self.stack = stack
        self.E = {
            "pe": EngS("pe", nc.tensor),
            "act": EngS("act", nc.scalar),
            "dve": EngS("dve", nc.vector),
            "pool": EngS("pool", nc.gpsimd),
            "sp": EngS("sp", nc.sync),
        }
        self.dsems = []

    def new_sem(self, name):
        return self.stack.enter_context(self.nc.semaphore(name))

    def dma_sem(self, name):
        d = DmaSem(self.new_sem("d_" + name), name)
        self.dsems.append(d)
        return d

    def _need(self, eng, tok, needs):
        if tok is None:
            return
        if tok[0] == "e":
            if tok[1] == eng.name and eng.name == "pe":
                return
            key = ("e", tok[1])
            if needs.get(key, 0) < tok[2]:
                needs[key] = tok[2]
        else:
            ds = tok[1]
            key = ("d", ds)
            v = ds.total
            if needs.get(key, 0) < v:
                needs[key] = v

    def _waits(self, eng, reads, writes):
        needs = {}
        for d in reads:
            self._need(eng, d.w, needs)
            if d.excl:
                for r in d.rs:
                    if r[0] == "e" and r[1] != eng.name:
                        self._need(eng, r, needs)
        for d in writes:
            self._need(eng, d.w, needs)
            for r in d.rs:
                self._need(eng, r, needs)
        out = []
        for key, v in needs.items():
            if eng.seen.get(key, 0) >= v:
                continue
            eng.seen[key] = v
            out.append((key, v))
        return out

    def _emit_waits(self, eng, waits):
        for key, v in waits:
            if key[0] == "e":
                src = self.E[key[1]]
                si = (v - 1) // SEM_CAP
                val = v - si * SEM_CAP
                sem = src.sems[si]
                eng.ops.append(lambda h=eng.h, sem=sem, val=val: h.wait_ge(sem, val))
            else:
                ds = key[1]
                eng.ops.append(lambda h=eng.h, sem=ds.sem, val=v: h.wait_ge(sem, val))

    def _mark(self, tok, reads, writes):
        for d in reads:
            d.rs.append(tok)
        for d in writes:
            d.w = tok
            d.rs = []

    cap = None

    def op(self, engname, fn, reads=(), writes=()):
        if self.cap is not None:
            self.cap.append(("op", engname, fn, tuple(reads), tuple(writes), None, None, None))
            return
        eng = self.E[engname]
        self._emit_waits(eng, self._waits(eng, reads, writes))
        eng.count += 1
        idx = eng.count
        si = (idx - 1) // SEM_CAP
        while len(eng.sems) <= si:
            eng.sems.append(self.new_sem(f"s_{engname}{len(eng.sems)}"))
        sem = eng.sems[si]
        eng.ops.append(lambda h=eng.h, sem=sem, fn=fn: fn(h).then_inc(sem, 1))
        self._mark(("e", engname, idx), reads, writes)

    def dma(self, qname, dsem, out, in_, reads=(), writes=(), **kw):
        if self.cap is not None:
            self.cap.append(("dma", qname, dsem, tuple(reads), tuple(writes), out, in_, kw))
            return
        eng = self.E[qname]
        self._emit_waits(eng, self._waits(eng, reads, writes))
        dsem.total += 16
        eng.ops.append(lambda h=eng.h, sem=dsem.sem, out=out, in_=in_, kw=kw:
                       h.dma_start(out=out, in_=in_, **kw).then_inc(sem, 16))
        self._mark(("d", dsem, dsem.total), reads, writes)

    def capture(self, gen):
        assert self.cap is None
        self.cap = []
        gen()
        lst, self.cap = self.cap, None
        return lst

    def replay(self, lst):
        for (kind, a, b, reads, writes, out, in_, kw) in lst:
            if kind == "op":
                self.op(a, b, reads, writes)
            else:
                self.dma(a, b, out, in_, reads, writes, **kw)

    def replay_merged(self, l1, l2):
        n1, n2 = len(l1), len(l2)
        i = j = 0
        while i < n1 or j < n2:
            if j >= n2 or (i < n1 and (i + 0.5) * n2 <= (j + 0.5) * n1):
                self.replay(l1[i:i + 1]); i += 1
            else:
                self.replay(l2[j:j + 1]); j += 1

    def finish(self):
        eng = self.E["sp"]
        for name, e in self.E.items():
            if e.count > 0:
                v = e.count
                si = (v - 1) // SEM_CAP
                eng.ops.append(lambda h=eng.h, sem=e.sems[si], val=v - si * SEM_CAP: h.wait_ge(sem, val))
        for ds in self.dsems:
            if ds.total > 0:
                eng.ops.append(lambda h=eng.h, sem=ds.sem, val=ds.total: h.wait_ge(sem, val))

    def emit(self):
        with self.nc.Block() as block:
            @block.tensor
            def _(e):
                for f in self.E["pe"].ops:
                    f()

            @block.scalar
            def _(e):
                for f in self.E["act"].ops:
                    f()

            @block.vector
            def _(e):
                for f in self.E["dve"].ops:
                    f()

            @block.gpsimd
            def _(e):
                for f in self.E["pool"].ops:
                    f()

            @block.sync
            def _(e):
                for f in self.E["sp"].ops:
                    f()


class Tl:
    __slots__ = ("t", "d")

    def __init__(self, t, name):
        self.t = t
        self.d = Dep(name)


def make_consts():
    bf = ml_dtypes.bfloat16
    i = np.arange(128)
    c = {}
    tri_std = (i[:, None] <= i[None, :]).astype(np.float32)
    same = (i[:, None] // 8 == i[None, :] // 8)
    tri_blk = tri_std * same
    c["tri"] = np.stack([tri_std, tri_blk]).astype(bf)
    mb_std = np.where(tri_std > 0, 0.0, NEG).astype(np.float32)
    mb_blk = np.where(tri_blk > 0, 0.0, NEG).astype(np.float32)
    c["mb"] = np.stack([mb_std, mb_blk]).astype(bf)
    mbp = np.where(i[:, None] > i[None, :], 0.0, NEG).astype(np.float32)
    mbp0 = np.where((i[:, None] > i[None, :]) & (i[:, None] >= 112), 0.0, NEG).astype(np.float32)
    c["mbprev"] = np.stack([mbp, mbp0]).astype(bf)
    t8 = np.arange(128) % 8
    c["mbstate"] = np.where(i[:, None] > t8[None, :], 0.0, NEG).astype(np.float32).astype(bf)
    c["ident_b"] = np.eye(128, dtype=np.float32).astype(bf)
    c["ident_f"] = np.eye(128, dtype=np.float32)
    c["ones_b"] = np.ones((128, 128), np.float32).astype(bf)
    sel = np.zeros((128, NSQ, 128), np.float32)
    mbq = np.full((128, NSQ), NEG, np.float32)
    rowm = np.zeros((128, NSQ, 128), np.float32)
    for q in range(NSQ):
        sel[8 * q:8 * q + 8, q, :] = 1.0
        mbq[8 * q:8 * q + 8, q] = 0.0
        rowm[:, q, 8 * q:8 * q + 8] = 1.0
    c["sel"] = sel.astype(bf)
    c["mbq"] = mbq
    c["rowm"] = rowm.astype(bf)
    tm = np.ones((128, 2), np.float32)
    tm[:112, 1] = 0.0
    c["tokmask"] = tm
    inv = (np.float32(500000.0) ** (-np.arange(0, 16, 2, dtype=np.float32) / np.float32(16))).astype(np.float32)
    rope = np.zeros((18, 128, 16), np.float32)
    for ty in range(18):
        if ty < 16:
            pos = 16 + 128 * ty + i
        elif ty == 16:
            pos = i - 112
        else:
            pos = 16384 + (i % 8)
        ang = pos.astype(np.float32)[:, None] * inv[None, :]
        rope[ty, :, 0:8] = np.cos(ang)
        rope[ty, :, 8:16] = np.sin(ang)
    c["rope"] = rope
    return c


class Kern:
    def __init__(self):
        self.nc = bass.Bass("TRN2", target_bir_lowering=False)
        self.in_names = []
        self.out_names = []

    def din(self, name, shape, dt=F32):
        self.in_names.append(name)
        return self.nc.dram_tensor(name, list(shape), dt, kind="ExternalInput").ap()

    def dout(self, name, shape, dt=F32):
        self.out_names.append(name)
        return self.nc.dram_tensor(name, list(shape), dt, kind="ExternalOutput").ap()

    def sb(self, name, shape, dt):
        return Tl(self.st.enter_context(self.nc.sbuf_tensor("sb_" + name, list(shape), dt)), name)

    def build(self):
        nc = self.nc
        A = {}
        A["xs"] = self.din("xs", [128, 1024])
        A["xp"] = self.din("xp", [NPB, 2048, 1024])
        A["meta"] = self.din("meta", [16, 1024])
        A["st_ssm"] = self.din("st_ssm", [NSQ, 32, 64, 128])
        A["st_conv"] = self.din("st_conv", [NSQ * 3, 3072])
        A["st_k"] = self.din("st_k", [NSQ, 128, 256])
        A["st_v"] = self.din("st_v", [NSQ, 128, 256])
        A["w_in"] = self.din("w_in", [1024, 5152])
        A["w_out"] = self.din("w_out", [2048, 1024])
        A["w_k"] = self.din("w_k", [1024, 256])
        A["w_v"] = self.din("w_v", [1024, 256])
        A["w_q"] = self.din("w_q", [1024, 1024])
        A["w_o"] = self.din("w_o", [1024, 1024])
        A["w_gate"] = self.din("w_gate", [2, 1024, DFF])
        A["w_up"] = self.din("w_up", [2, 1024, DFF])
        A["w_down"] = self.din("w_down", [2, DFF, 1024])
        A["gcols"] = self.din("gcols", [128, 6, 8])
        A["gate_g"] = self.din("gate_g", [128, 16])
        A["conv_w"] = self.din("conv_w", [128, 24, 4])
        A["conv_b"] = self.din("conv_b", [128, 24])
        A["hp"] = self.din("hp", [128, 4, 32])
        A["sinks"] = self.din("sinks", [128, 16])
        A["c_tri"] = self.din("c_tri", [2, 128, 128], BF16)
        A["c_mb"] = self.din("c_mb", [2, 128, 128], BF16)
        A["c_mbprev"] = self.din("c_mbprev", [2, 128, 128], BF16)
        A["c_mbstate"] = self.din("c_mbstate", [128, 128], BF16)
        A["c_ident_b"] = self.din("c_ident_b", [128, 128], BF16)
        A["c_ident_f"] = self.din("c_ident_f", [128, 128])
        A["c_ones_b"] = self.din("c_ones_b", [128, 128], BF16)
        A["c_sel"] = self.din("c_sel", [128, NSQ, 128], BF16)
        A["c_mbq"] = self.din("c_mbq", [128, NSQ])
        A["c_rowm"] = self.din("c_rowm", [128, NSQ, 128], BF16)
        A["c_tokmask"] = self.din("c_tokmask", [128, 2])
        A["c_rope"] = self.din("c_rope", [18, 128, 16])
        A["y_p"] = self.dout("y_p", [NPB, 2048, 1024])
        A["y_s"] = self.dout("y_s", [128, 1024])
        A["ssm_p"] = self.dout("ssm_p", [NPB, 32, 64, 128])
        A["conv_p"] = self.dout("conv_p", [NPB, 3, 3072])
        A["k_p"] = self.dout("k_p", [NPB, 128, 256])
        A["v_p"] = self.dout("v_p", [NPB, 128, 256])
        A["ssm_s"] = self.dout("ssm_s", [NSQ, 32, 64, 128])
        A["conv_s"] = self.dout("conv_s", [NSQ * 3, 3072])
        A["k_s"] = self.dout("k_s", [NSQ, 128, 256])
        A["v_s"] = self.dout("v_s", [NSQ, 128, 256])
        self.A = A
        self.WB = {}
        for wn in ["w_in", "w_out", "w_k", "w_v", "w_q", "w_o", "w_gate", "w_up", "w_down"]:
            shp = list(A[wn].shape)
            self.WB[wn] = self.nc.dram_tensor(wn + "_bf", shp, BF16, kind="Internal").ap()
        with ExitStack() as st:
            self.st = st
            self.P = Prog(nc, st)
            self.alloc()
            self.setup()
            import os
            plan = os.environ.get("KPLAN", "SMP")
            if "S" in plan:
                self.chunk("S", None, 0)
            if "M" in plan:
                self.chunk("M", None, 0)
            if "P" in plan:
                self.run_prompt(int(os.environ.get("KNCH", NCHUNK)))
            self.P.finish()
            self.P.emit()
        return nc

    def alloc(self):
        sb = self.sb
        nc = self.nc
        self.tri = sb("tri", [128, 2, 128], BF16)
        self.mb = sb("mb", [128, 2, 128], BF16)
        self.mbprev = sb("mbprev", [128, 2, 128], BF16)
        self.mbstate = sb("mbstate", [128, 128], BF16)
        self.ident_b = sb("ident_b", [128, 128], BF16)
        self.ident_f = sb("ident_f", [128, 128], F32)
        self.ones_b = sb("ones_b", [128, 128], BF16)
        self.sel = sb("sel", [128, NSQ, 128], BF16)
        self.mbq = sb("mbq", [128, NSQ], F32)
        self.rowm = sb("rowm", [128, NSQ, 128], BF16)
        self.tokmask = sb("tokmask", [128, 2], F32)
        self.zero1 = sb("zero1", [128, 1], F32)
        self.gcols = sb("gcols", [128, 6, 8], F32)
        self.gate_g = sb("gate_g", [128, 16], F32)
        self.conv_w = sb("conv_w", [128, 24, 4], F32)
        self.conv_b = sb("conv_b", [128, 24], F32)
        self.hp = sb("hp", [128, 4, 32], F32)
        self.negA = sb("negA", [128, 32], F32)
        self.esink = sb("esink", [128, 16], F32)
        self.sinks = sb("sinks", [128, 16], F32)
        self.ropes = [sb(f"rope{i}", [128, 16], F32) for i in range(2)]
        self.rope_i = 0
        self.cacc = [sb(f"cacc{i}", [128, 128], F32) for i in range(4)]
        self.cth = [sb(f"cth{i}", [128, 128], F32) for i in range(4)]
        self.hTs = [sb(f"hT{i}", [128, 8, 128], F32) for i in range(2)]
        self.hT = self.hTs[0]
        self.uT = sb("uT", [128, 8, 128], BF16)
        self.rstd = sb("rstd", [128, 128], F32)
        self.xin = sb("xin", [128, 1024], F32)
        self.z_tm = sb("z_tm", [128, 2048], BF16)
        self.xpre = sb("xpre", [128, 24, 176], BF16)
        self.uni = sb("uni", [128, 1408], F32)
        self.big1 = sb("big1", [128, 3072], F32)
        self.dtraw = sb("dtraw", [128, 32], F32)
        self.xbcT = sb("xbcT", [128, 24, 128], BF16)
        self.x_tm = sb("x_tm", [128, 2048], BF16)
        self.xdt = sb("xdt", [128, 2048], BF16)
        self.xw = sb("xw", [128, 2048], BF16)
        self.B_tm = sb("B_tm", [128, 512], BF16)
        self.dt = sb("dt", [128, 32], F32)
        self.a32 = sb("a32", [128, 32], F32)
        self.ahi = sb("ahi", [128, 32], BF16)
        self.alo = sb("alo", [128, 32], BF16)
        self.negcs = sb("negcs", [128, 32], F32)
        self.ecs = sb("ecs", [128, 32], F32)
        self.dA = sb("dA", [128, 32], F32)
        self.wdec = sb("wdec", [128, 32], F32)
        self.cbT = sb("cbT", [128, 4, 128], F32)
        self.dec = [sb(f"dec{i}", [128, 4, 128], BF16) for i in range(2)]
        self.MT = [sb(f"MT{i}", [128, 4, 128], BF16) for i in range(2)]
        self.tmpg = [sb(f"tmpg{i}", [128, 512], F32) for i in range(2)]
        self.CTq = sb("CTq", [128, 4, 128], BF16)
        self.ST = sb("ST", [128, 2048], F32)
        self.STb = sb("STb", [128, 2048], BF16)
        self.ST_meta = sb("ST_meta", [128, 2048], F32)
        self.big2 = sb("big2", [128, 2, 16, 128], F32)
        self.big2_d0 = Dep("big2_in")
        self.big2_d1 = Dep("big2_out")
        self.sz = sb("sz", [128, 512], F32)
        self.ssq = sb("ssq", [128, 4], F32)
        self.grs = sb("grs", [128, 4], F32)
        self.ynT = sb("ynT", [128, 16, 128], BF16)
        self.fth = [sb(f"fth{i}", [128, 256], F32) for i in range(2)]
        self.actT = sb("actT", [128, 22, 128], BF16)
        self.k_tm = sb("k_tm", [128, 256], F32)
        self.k_rot = sb("k_rot", [128, 256], F32)
        self.k_b = sb("k_b", [128, 256], BF16)
        self.v_tm = sb("v_tm", [128, 256], F32)
        self.v_b = [sb(f"v_b{i}", [128, 256], BF16) for i in range(2)]
        self.kT = [sb(f"kT{i}", [64, 4, 128], BF16) for i in range(2)]
        self.kT_meta = sb("kT_meta", [64, 4, 128], BF16)
        self.v_meta = sb("v_meta", [128, 256], BF16)
        self.cvtail_meta = sb("cvtail_meta", [128, 24, 3], BF16)
        self.q_tm = sb("q_tm", [128, 1024], F32)
        self.q_rot = sb("q_rot", [128, 1024], F32)
        self.q_b = sb("q_b", [128, 1024], BF16)
        self.qT = sb("qT", [64, 16, 128], BF16)
        self.rtmp = sb("rtmp", [128, 4, 16, 8], F32)
        self.PT = [sb(f"PT{i}", [128, 256], BF16) for i in range(2)]
        self.oT = sb("oT", [64, 16, 128], BF16)
        self.den = sb("den", [64, 4, 128], F32)
        self.sk32 = sb("sk32", [128, 256], F32)
        self.sv32 = sb("sv32", [128, 256], F32)
        self.skb = sb("skb", [128, 256], BF16)
        self.svb = sb("svb", [128, 256], BF16)
        self.kTs = sb("kTs", [64, 4, 128], BF16)
        self.PTs = sb("PTs", [128, 128], BF16)
        self.y_st = sb("y_st", [128, 1024], F32)
        import os
        self.NW = int(os.environ.get("KNW", "6"))
        self.wslot = [sb(f"wslot{i}", [128, 2048], BF16) for i in range(self.NW)]
        self.wsem = [self.P.dma_sem(f"w{i}") for i in range(self.NW)]
        self.wi = 0
        self.nw_cur = self.NW

        class _SV:
            def __init__(s_, ap, name, first):
                s_.t, s_.d, s_.first = ap, Dep(name), list(first)
        b2 = self.big2.t[:, :, :, :].rearrange("p a b c -> p (a b c)").bitcast(BF16)
        self.wslot_extra = [_SV(b2[:, i * 2048:(i + 1) * 2048], f"wx{i}", [self.big2_d0]) for i in range(2)]
        self.wslot_extra.append(_SV(self.sel.t[:, :, :].rearrange("p a b -> p (a b)"), "wx4", [self.sel.d]))
        self.wslot_extra.append(_SV(self.rowm.t[:, :, :].rearrange("p a b -> p (a b)"), "wx5", [self.rowm.d]))
        self.wsem_extra = [self.P.dma_sem(f"wx{i}") for i in range(6)]
        self.psf = [Tl(self.st.enter_context(nc.psum_tensor(f"psf{i}", [128, 512], F32)), f"psf{i}") for i in range(6)]
        self.psb = [Tl(self.st.enter_context(nc.psum_tensor(f"psb{i}", [128, 1024], BF16)), f"psb{i}") for i in range(2)]
        self.pfi = 0
        self.pbi = 0
        for t in self.psf + self.psb:
            t.d.excl = True
        self.ds_in = self.P.dma_sem("in")
        self.ds_x = self.P.dma_sem("x")
        self.ds_stg = self.P.dma_sem("stg")
        self.ds_stgo = self.P.dma_sem("stgo")
        self.ds_kv = self.P.dma_sem("kv")
        self.ds_out = self.P.dma_sem("out")
        self.ds_y = self.P.dma_sem("y")
        self.ds_cp = self.P.dma_sem("cp")
        self.ds_rope = self.P.dma_sem("rope")
        self.pbuf = 0

    def stage(self, n):
        if n > self.kstage:
            raise StopIteration

    ps_pool = "all"

    def PF(self):
        if self.ps_pool == "all":
            t = self.psf[self.pfi % 6]
        elif self.ps_pool == "lo":
            t = self.psf[self.pfi % 3]
        else:
            t = self.psf[3 + self.pfi % 3]
        self.pfi += 1
        return t

    def run_prompt(self, nch):
        self.nw_cur = self.NW + len(self.wslot_extra)
        P = self.P
        seq = [(b, c) for b in range(NPB) for c in range(nch)]
        prevE = None
        for i, (b, c) in enumerate(seq):
            hs = i % 2
            self.chunk("P", b, c, ph="A", hsel=hs)
            if prevE is None:
                self.chunk("P", b, c, ph="B", hsel=hs)
            else:
                pb_, pc_, phs = prevE
                self.ps_pool = "lo"
                lE = P.capture(lambda: self.chunk("P", pb_, pc_, ph="E", hsel=phs))
                self.ps_pool = "hi"
                lB = P.capture(lambda: self.chunk("P", b, c, ph="B", hsel=hs))
                self.ps_pool = "all"
                P.replay_merged(lE, lB)
            self.chunk("P", b, c, ph="C", hsel=hs)
            self.chunk("P", b, c, ph="D", hsel=hs)
            prevE = (b, c, hs)
        pb_, pc_, phs = prevE
        self.chunk("P", pb_, pc_, ph="E", hsel=phs)

    def PB(self):
        t = self.psb[self.pbi % 2]
        self.pbi += 1
        return t

    def setup(self):
        P, A = self.P, self.A
        ld = lambda tl, src: P.dma("sp", self.ds_in, tl.t[:], src, writes=[tl.d])
        ld(self.tri, A["c_tri"].rearrange("a p n -> p a n"))
        ld(self.mb, A["c_mb"].rearrange("a p n -> p a n"))
        ld(self.mbprev, A["c_mbprev"].rearrange("a p n -> p a n"))
        ld(self.mbstate, A["c_mbstate"])
        ld(self.ident_b, A["c_ident_b"])
        ld(self.ident_f, A["c_ident_f"])
        ld(self.ones_b, A["c_ones_b"])
        ld(self.sel, A["c_sel"])
        ld(self.mbq, A["c_mbq"])
        ld(self.rowm, A["c_rowm"])
        ld(self.tokmask, A["c_tokmask"])
        ld(self.gcols, A["gcols"])
        ld(self.gate_g, A["gate_g"])
        ld(self.conv_w, A["conv_w"])
        ld(self.conv_b, A["conv_b"])
        ld(self.hp, A["hp"])
        ld(self.sinks, A["sinks"])
        P.op("pool", lambda h: h.memset(self.zero1.t[:], 0.0), writes=[self.zero1.d])
        self.wdep = {}
        for wn in ["w_in", "w_out", "w_gate", "w_up", "w_down", "w_k", "w_v", "w_q", "w_o"]:
            self.wdep[wn] = Dep("wb_" + wn)
            self.ds_cast = P.dma_sem("cast_" + wn)
            src, dst = A[wn], self.WB[wn]
            if len(src.shape) == 3:
                for l in range(2):
                    P.dma("pool", self.ds_cast, dst[l].rearrange("(p r) n -> p r n", p=128), src[l].rearrange("(p r) n -> p r n", p=128), writes=[self.wdep[wn]])
            else:
                P.dma("pool", self.ds_cast, dst.rearrange("(p r) n -> p r n", p=128), src.rearrange("(p r) n -> p r n", p=128), writes=[self.wdep[wn]])
        P.op("act", lambda h: h.activation(self.negA.t[:], self.hp.t[:, 1, :], AF.Exp), reads=[self.hp.d], writes=[self.negA.d])
        P.op("dve", lambda h: h.tensor_scalar_mul(self.negA.t[:], self.negA.t[:], -1.0), reads=[self.negA.d], writes=[self.negA.d])
        P.op("act", lambda h: h.activation(self.esink.t[:], self.sinks.t[:], AF.Exp), reads=[self.sinks.d], writes=[self.esink.d])
        P.op("dve", lambda h: h.tensor_scalar_mul(self.conv_w.t[:], self.conv_w.t[:], 0.5), reads=[self.conv_w.d], writes=[self.conv_w.d])
        P.op("dve", lambda h: h.tensor_scalar_mul(self.conv_b.t[:], self.conv_b.t[:], 0.5), reads=[self.conv_b.d], writes=[self.conv_b.d])

    def wload(self, srcspec, r0, r1, ncols, part=128):
        P = self.P
        i = self.wi % self.nw_cur
        self.wi += 1
        if i < self.NW:
            sl, wsem, extra = self.wslot[i], self.wsem[i], []
        else:
            sl, wsem = self.wslot_extra[i - self.NW], self.wsem_extra[i - self.NW]
            extra, sl.first = sl.first, []
        kc = (r1 - r0) // part
        assert kc * ncols <= 2048
        view = sl.t[0:part, 0:kc * ncols].rearrange("p (c n) -> p c n", n=ncols)
        wn, sel = srcspec
        src = sel(self.WB[wn])[r0:r1, :]
        P.dma("sp", wsem, view, src.rearrange("(c p) n -> p c n", p=part), reads=[self.wdep[wn]], writes=[sl.d] + extra)
        return view, sl

    def rms(self, gi, out_tl, view3=False):
        P = self.P
        hT, sq, rstd = self.hT, self.actT, self.rstd
        P.op("act", lambda h: h.activation(sq.t[:, 0:8, :], hT.t[:], AF.Square), reads=[hT.d], writes=[sq.d])
        ps = self.PF()
        for c in range(8):
            P.op("pe", lambda h, c=c: h.matmul(ps.t[:, 0:128], lhsT=self.ones_b.t[:], rhs=sq.t[:, c, :], start=(c == 0), stop=(c == 7)),
                 reads=[sq.d, self.ones_b.d], writes=[ps.d])
        P.op("act", lambda h: h.activation(rstd.t[:], ps.t[:, 0:128], AF.Ln, bias=EPS, scale=1.0 / 1024.0), reads=[ps.d], writes=[rstd.d])
        P.op("act", lambda h: h.activation(rstd.t[:], rstd.t[:], AF.Exp, scale=-0.5), reads=[rstd.d], writes=[rstd.d])
        for c in range(8):
            oc = out_tl.t[:, c * 128:(c + 1) * 128] if view3 else out_tl.t[:, c, :]
            P.op("dve", lambda h, c=c, oc=oc: h.scalar_tensor_tensor(oc, hT.t[:, c, :], self.gcols.t[:, gi, c:c + 1], rstd.t[:], ALU.mult, ALU.mult),
                 reads=[hT.d, rstd.d, self.gcols.d], writes=[out_tl.d])

    def dense_fm(self, src, K, ncols_total, xT_tl, evac, blk=None, part=128):
        P = self.P
        kc = K // part
        nb0 = 256
        ksubs = [(k0, min(k0 + 8, kc)) for k0 in range(0, kc, 8)]
        for c0 in range(0, ncols_total, nb0):
            nb = min(nb0, ncols_total - c0)
            loaded = []
            for (k0, k1) in ksubs:
                wv, wsl = self.wload((src[0], lambda w, c0=c0, nb=nb, f=src[1]: f(w)[:, c0:c0 + nb]), k0 * part, k1 * part, nb, part=part)
                loaded.append((k0, k1, wv, wsl))
            for m in range(nb // 128):
                ps = self.PF()
                for (k0, k1, wv, wsl) in loaded:
                    for k in range(k0, k1):
                        rhs = xT_tl.t[0:part, k, :]
                        P.op("pe", lambda h, k=k, k0=k0, m=m, rhs=rhs, wv=wv, ps=ps: h.matmul(ps.t[:, 0:128], lhsT=wv[:, k - k0, m * 128:(m + 1) * 128], rhs=rhs, start=(k == 0), stop=(k == kc - 1)),
                             reads=[wsl.d, xT_tl.d], writes=[ps.d])
                evac((c0 // 128) + m, ps)

    def dense_tm(self, src, ncols, xT_tl, evac):
        P = self.P
        ps = self.PF()
        for c0 in range(0, ncols, 256):
            nb = min(256, ncols - c0)
            wv, wsl = self.wload((src[0], lambda w, c0=c0, nb=nb, f=src[1]: f(w)[:, c0:c0 + nb]), 0, 1024, nb)
            for k in range(8):
                P.op("pe", lambda h, k=k, c0=c0, nb=nb, wv=wv: h.matmul(ps.t[:, c0:c0 + nb], lhsT=xT_tl.t[:, k, :], rhs=wv[:, k, :], start=(k == 0), stop=(k == 7)),
                     reads=[wsl.d, xT_tl.d], writes=[ps.d])
        evac(ps)

    def resid_add(self, m, ps):
        hT = self.hT
        self.P.op("dve", lambda h: h.tensor_tensor(hT.t[:, m, :], hT.t[:, m, :], ps.t[:, 0:128], ALU.add), reads=[ps.d, hT.d], writes=[hT.d])

    def ffn(self, layer):
        P, A = self.P, self.A
        self.rms(1 if layer == 0 else 4, self.uT)
        wd = ("w_down", lambda w: w[layer])
        act_tm = self.uni.t[:].bitcast(BF16)
        uT = self.uT
        for bi, c0 in enumerate(range(0, DFF, 256)):
            nb = min(256, DFF - c0)
            gv, gsl = self.wload(("w_gate", lambda w, c0=c0, nb=nb: w[layer][:, c0:c0 + nb]), 0, 1024, nb)
            uv, usl = self.wload(("w_up", lambda w, c0=c0, nb=nb: w[layer][:, c0:c0 + nb]), 0, 1024, nb)
            psg = self.PF()
            psu = self.PF()
            for k in range(8):
                P.op("pe", lambda h, k=k, gv=gv, psg=psg, nb=nb: h.matmul(psg.t[:, 0:nb], lhsT=uT.t[:, k, :], rhs=gv[:, k, :], start=(k == 0), stop=(k == 7)),
                     reads=[gsl.d, uT.d], writes=[psg.d])
            for k in range(8):
                P.op("pe", lambda h, k=k, uv=uv, psu=psu, nb=nb: h.matmul(psu.t[:, 0:nb], lhsT=uT.t[:, k, :], rhs=uv[:, k, :], start=(k == 0), stop=(k == 7)),
                     reads=[usl.d, uT.d], writes=[psu.d])
            th = self.fth[bi % 2]
            P.op("act", lambda h, th=th, psg=psg, nb=nb: h.activation(th.t[:, 0:nb], psg.t[:, 0:nb], AF.Tanh, scale=0.5), reads=[psg.d], writes=[th.d])
            P.op("dve", lambda h, th=th, psg=psg, nb=nb: h.scalar_tensor_tensor(th.t[:, 0:nb], th.t[:, 0:nb], 1.0, psg.t[:, 0:nb], ALU.add, ALU.mult), reads=[th.d, psg.d], writes=[th.d])
            P.op("dve", lambda h, th=th, psu=psu, nb=nb, c0=c0: h.scalar_tensor_tensor(act_tm[:, c0:c0 + nb], th.t[:, 0:nb], 0.5, psu.t[:, 0:nb], ALU.mult, ALU.mult),
                 reads=[th.d, psu.d], writes=[self.uni.d])
        for gi_, (m0, n) in enumerate([(0, 8), (8, 8), (16, 6)]):
            ps = self.PF()
            pbv = ps.t[:, :].bitcast(BF16)
            for j in range(n):
                m = m0 + j
                P.op("pe", lambda h, j=j, m=m, pbv=pbv: h.transpose(pbv[:, j * 128:(j + 1) * 128], act_tm[:, m * 128:(m + 1) * 128], self.ident_b.t[:]),
                     reads=[self.uni.d, self.ident_b.d], writes=[ps.d])
            src = pbv[:, 0:n * 128].rearrange("p (m t) -> p m t", t=128)
            if gi_ == 1:
                P.op("act", lambda h, m0=m0, n=n, src=src: h.copy(self.actT.t[:, m0:m0 + n, :], src), reads=[ps.d], writes=[self.actT.d])
            else:
                P.op("dve", lambda h, m0=m0, n=n, src=src: h.tensor_copy(self.actT.t[:, m0:m0 + n, :], src), reads=[ps.d], writes=[self.actT.d])
        self.dense_fm(wd, DFF, 1024, self.actT, self.resid_add)

    def rope_apply(self, src, dst, nh):
        P = self.P
        s3 = src.t[:, :].rearrange("p (h d) -> p h d", d=64)
        d3 = dst.t[:, :].rearrange("p (h d) -> p h d", d=64)
        cos = self.rope.t[:, 0:8].unsqueeze(1).to_broadcast([128, nh, 8])
        sin = self.rope.t[:, 8:16].unsqueeze(1).to_broadcast([128, nh, 8])
        x1, x2 = s3[:, :, 0:8], s3[:, :, 8:16]
        t = [self.rtmp.t[:, i, 0:nh, :] for i in range(4)]
        rd = [src.d, self.rope.d]
        P.op("dve", lambda h: h.tensor_tensor(t[0], x1, cos, ALU.mult), reads=rd, writes=[self.rtmp.d])
        P.op("dve", lambda h: h.tensor_tensor(t[1], x2, sin, ALU.mult), reads=rd, writes=[self.rtmp.d])
        P.op("dve", lambda h: h.tensor_tensor(t[2], x2, cos, ALU.mult), reads=rd, writes=[self.rtmp.d])
        P.op("dve", lambda h: h.tensor_tensor(t[3], x1, sin, ALU.mult), reads=rd, writes=[self.rtmp.d])
        P.op("dve", lambda h: h.tensor_tensor(d3[:, :, 0:8], t[0], t[1], ALU.subtract), reads=[self.rtmp.d], writes=[dst.d])
        P.op("dve", lambda h: h.tensor_tensor(d3[:, :, 8:16], t[2], t[3], ALU.add), reads=[self.rtmp.d], writes=[dst.d])

    def chunk(self, ty, b, c, ph="ABCDE", hsel=0):
        P, A = self.P, self.A
        self.hT = self.hTs[hsel]
        ti = 1 if ty == "S" else 0
        nseq = NSQ if ty == "S" else 1
        first = (ty == "P" and c == 0)
        last = (ty == "P" and c == NCHUNK - 1)
        if "A" in ph:
            xin = self.xin
            if ty == "S":
                P.dma("sp", self.ds_x, xin.t[:], A["xs"], writes=[xin.d])
            elif ty == "M":
                P.op("pool", lambda h: h.memset(xin.t[:], 0.0), writes=[xin.d])
                P.dma("sp", self.ds_x, xin.t[112:128, :], A["meta"], writes=[xin.d])
            else:
                P.dma("sp", self.ds_x, xin.t[:], A["xp"][b, c * 128:(c + 1) * 128, :], writes=[xin.d])
            rty = 17 if ty == "S" else (16 if ty == "M" else c)
            self.rope = self.ropes[self.rope_i % 2]
            self.rope_i += 1
            P.dma("sp", self.ds_rope, self.rope.t[:], A["c_rope"][rty], writes=[self.rope.d])
            for half in range(2):
                ps = self.PF()
                for m in range(4):
                    mm = half * 4 + m
                    P.op("pe", lambda h, m=m, mm=mm, ps=ps: h.transpose(ps.t[:, m * 128:(m + 1) * 128], xin.t[:, mm * 128:(mm + 1) * 128], self.ident_f.t[:]),
                         reads=[xin.d, self.ident_f.d], writes=[ps.d])
                hTc = self.hT
                P.op("act", lambda h, half=half, ps=ps, hTc=hTc: h.copy(hTc.t[:, half * 4:(half + 1) * 4, :], ps.t[:, :].rearrange("p (m t) -> p m t", t=128)),
                     reads=[ps.d], writes=[hTc.d])

            if ty == "M":
                P.op("pool", lambda h: h.memset(self.xpre.t[:, :, 0:3], 0.0), writes=[self.xpre.d])
            if first:
                P.op("pool", lambda h: h.tensor_copy(self.xpre.t[:, :, 0:3], self.cvtail_meta.t[:]), reads=[self.cvtail_meta.d], writes=[self.xpre.d])
            elif ty == "P":
                P.op("pool", lambda h: h.tensor_copy(self.xpre.t[:, :, 0:3], self.xpre.t[:, :, 128:131]), reads=[self.xpre.d], writes=[self.xpre.d])
            if ty == "S":
                P.dma("sp", self.ds_stg, self.big1.t[0:48, :], A["st_conv"], writes=[self.big1.d])
                for m in range(24):
                    ps = self.PF()
                    P.op("pe", lambda h, m=m, ps=ps: h.transpose(ps.t[:, 0:48], self.big1.t[0:48, m * 128:(m + 1) * 128], self.ident_f.t[0:48, 0:48]),
                         reads=[self.big1.d, self.ident_f.d], writes=[ps.d])
                    P.op("dve", lambda h, m=m, ps=ps: h.tensor_copy(self.xpre.t[:, m, :].rearrange("p (q t) -> p q t", t=11)[:, :, 0:3],
                                                                      ps.t[:, 0:48].rearrange("p (q r) -> p q r", r=3)),
                         reads=[ps.d], writes=[self.xpre.d])

            self.rms(0, self.uT)
            w_in = A["w_in"]
            for blk in range(4):
                def ev(ps, blk=blk):
                    P.op("act", lambda h: h.copy(self.z_tm.t[:, blk * 512:(blk + 1) * 512], ps.t[:, :]), reads=[ps.d], writes=[self.z_tm.d])
                self.dense_tm(("w_in", lambda w, blk=blk: w[:, blk * 512:(blk + 1) * 512]), 512, self.uT, ev)
            if ty == "S":
                def ev_x(m, ps):
                    P.op("dve", lambda h: h.tensor_copy(self.xpre.t[:, m, :].rearrange("p (q t) -> p q t", t=11)[:, :, 3:11],
                                                 ps.t[:, 0:128].rearrange("p (q t) -> p q t", t=8)), reads=[ps.d], writes=[self.xpre.d])
                    P.op("dve", lambda h: h.tensor_copy(self.uni.t[:, 0:1152].rearrange("p (m r) -> p m r", r=48)[:, m, :].rearrange("p (q r) -> p q r", r=3),
                                                        ps.t[:, 0:128].rearrange("p (q t) -> p q t", t=8)[:, :, 5:8]), reads=[ps.d], writes=[self.uni.d])
            else:
                def ev_x(m, ps):
                    P.op("dve", lambda h: h.tensor_copy(self.xpre.t[:, m, 3:131], ps.t[:, 0:128]), reads=[ps.d], writes=[self.xpre.d])
                    if last:
                        P.op("dve", lambda h: h.tensor_copy(self.uni.t[:, 0:1152].rearrange("p (m r) -> p m r", r=48)[:, m, 0:3], ps.t[:, 125:128]), reads=[ps.d], writes=[self.uni.d])
            self.dense_fm(("w_in", lambda w: w[:, 2048:5120]), 1024, 3072, self.uT, ev_x)
            def ev_dt(ps):
                P.op("dve", lambda h: h.tensor_copy(self.dtraw.t[:], ps.t[:, 0:32]), reads=[ps.d], writes=[self.dtraw.d])
            self.dense_tm(("w_in", lambda w: w[:, 5120:5152]), 32, self.uT, ev_dt)
            if ty == "M":
                P.op("pool", lambda h: h.tensor_copy(self.cvtail_meta.t[:], self.xpre.t[:, :, 128:131]), reads=[self.xpre.d], writes=[self.cvtail_meta.d])
            if ty == "S" or last:
                nr = 48 if ty == "S" else 3
                for m in range(24):
                    ps = self.PF()
                    P.op("pe", lambda h, m=m, ps=ps: h.transpose(ps.t[0:nr, 0:128], self.uni.t[:, 0:1152].rearrange("p (m r) -> p m r", r=48)[:, m, 0:nr], self.ident_f.t[:]),
                         reads=[self.uni.d, self.ident_f.d], writes=[ps.d])
                    P.op("dve", lambda h, m=m, ps=ps: h.tensor_copy(self.big1.t[0:nr, m * 128:(m + 1) * 128], ps.t[0:nr, 0:128]), reads=[ps.d], writes=[self.big1.d])
                dst = A["conv_s"] if ty == "S" else A["conv_p"][b]
                P.dma("pool", self.ds_out, dst, self.big1.t[0:nr, :], reads=[self.big1.d])

        if "B" in ph:
            if ty == "M":
                P.op("pool", lambda h: h.memset(self.ST.t[:], 0.0), writes=[self.ST.d])
                P.op("pool", lambda h: h.memset(self.STb.t[:], 0.0), writes=[self.STb.d])
            if first:
                P.op("dve", lambda h: h.tensor_copy(self.ST.t[:], self.ST_meta.t[:]), reads=[self.ST_meta.d], writes=[self.ST.d])
                P.op("act", lambda h: h.copy(self.STb.t[:], self.ST_meta.t[:]), reads=[self.ST_meta.d], writes=[self.STb.d])
            for mg in range(6):
                for k in range(4):
                    for mi in range(4):
                        m = mg * 4 + mi
                        acc = self.cacc[mi]
                        if ty == "S":
                            src = self.xpre.t[:, m, :].rearrange("p (q t) -> p q t", t=11)[:, :, k:k + 8]
                            out = acc.t[:, :].rearrange("p (q t) -> p q t", t=8)
                        else:
                            src = self.xpre.t[:, m, k:k + 128]
                            out = acc.t[:, :]
                        wk = self.conv_w.t[:, m, k:k + 1]
                        if k == 0:
                            bk = self.conv_b.t[:, m:m + 1]
                            P.op("dve", lambda h, src=src, out=out, wk=wk, bk=bk: h.tensor_scalar(out, src, wk, bk, ALU.mult, ALU.add), reads=[self.xpre.d, self.conv_w.d, self.conv_b.d], writes=[acc.d])
                        else:
                            P.op("dve", lambda h, src=src, out=out, wk=wk: h.scalar_tensor_tensor(out, src, wk, out, ALU.mult, ALU.add), reads=[self.xpre.d, self.conv_w.d, acc.d], writes=[acc.d])
                for mi in range(4):
                    m = mg * 4 + mi
                    acc = self.cacc[mi]
                    th = self.cth[mi]
                    P.op("act", lambda h, acc=acc, th=th: h.activation(th.t[:, :], acc.t[:, :], AF.Tanh), reads=[acc.d], writes=[th.d])
                    P.op("dve", lambda h, m=m, acc=acc, th=th: h.scalar_tensor_tensor(self.xbcT.t[:, m, :], th.t[:, :], 1.0, acc.t[:, :], ALU.add, ALU.mult),
                         reads=[acc.d, th.d], writes=[self.xbcT.d])
            for half in range(2):
                pb = self.PB()
                for m in range(8):
                    mm = half * 8 + m
                    P.op("pe", lambda h, m=m, mm=mm, pb=pb: h.transpose(pb.t[:, m * 128:(m + 1) * 128], self.xbcT.t[:, mm, :], self.ident_b.t[:]),
                         reads=[self.xbcT.d, self.ident_b.d], writes=[pb.d])
                P.op("act", lambda h, half=half, pb=pb: h.copy(self.x_tm.t[:, half * 1024:(half + 1) * 1024], pb.t[:, :]), reads=[pb.d], writes=[self.x_tm.d])
            pb = self.PB()
            for m in range(4):
                P.op("pe", lambda h, m=m, pb=pb: h.transpose(pb.t[:, m * 128:(m + 1) * 128], self.xbcT.t[:, 16 + m, :], self.ident_b.t[:]),
                     reads=[self.xbcT.d, self.ident_b.d], writes=[pb.d])
            P.op("dve", lambda h, pb=pb: h.tensor_copy(self.B_tm.t[:], pb.t[:, 0:512]), reads=[pb.d], writes=[self.B_tm.d])
            dt, a32 = self.dt, self.a32
            P.op("dve", lambda h: h.tensor_tensor(dt.t[:], self.dtraw.t[:], self.hp.t[:, 0, :], ALU.add), reads=[self.dtraw.d, self.hp.d], writes=[dt.d])
            P.op("act", lambda h: h.activation(dt.t[:], dt.t[:], AF.Exp), reads=[dt.d], writes=[dt.d])
            P.op("act", lambda h: h.activation(dt.t[:], dt.t[:], AF.Ln, bias=1.0), reads=[dt.d], writes=[dt.d])
            tmcol = self.tokmask.t[:, 1:2] if ty == "M" else self.tokmask.t[:, 0:1]
            P.op("dve", lambda h: h.tensor_scalar_mul(dt.t[:], dt.t[:], tmcol), reads=[dt.d, self.tokmask.d], writes=[dt.d])
            P.op("dve", lambda h: h.tensor_tensor(a32.t[:], dt.t[:], self.negA.t[:], ALU.mult), reads=[dt.d, self.negA.d], writes=[a32.d])
            P.op("dve", lambda h: h.tensor_copy(self.ahi.t[:], a32.t[:]), reads=[a32.d], writes=[self.ahi.d])
            P.op("dve", lambda h: h.tensor_tensor(self.alo.t[:], a32.t[:], self.ahi.t[:], ALU.subtract), reads=[a32.d, self.ahi.d], writes=[self.alo.d])
            ps = self.PF()
            P.op("pe", lambda h, ps=ps: h.matmul(ps.t[:, 0:32], lhsT=self.tri.t[:, ti, :], rhs=self.ahi.t[:], start=True, stop=False), reads=[self.tri.d, self.ahi.d], writes=[ps.d])
            P.op("pe", lambda h, ps=ps: h.matmul(ps.t[:, 0:32], lhsT=self.tri.t[:, ti, :], rhs=self.alo.t[:], start=False, stop=True), reads=[self.tri.d, self.alo.d], writes=[ps.d])
            P.op("dve", lambda h, ps=ps: h.tensor_scalar_mul(self.negcs.t[:], ps.t[:, 0:32], -1.0), reads=[ps.d], writes=[self.negcs.d])
            P.op("act", lambda h, ps=ps: h.activation(self.ecs.t[:], ps.t[:, 0:32], AF.Exp), reads=[ps.d], writes=[self.ecs.d])
            x3 = self.x_tm.t[:, :].rearrange("p (h d) -> p h d", d=64)
            P.op("dve", lambda h: h.tensor_tensor(self.xdt.t[:, :].rearrange("p (h d) -> p h d", d=64), x3, dt.t[:].unsqueeze(2).to_broadcast([128, 32, 64]), ALU.mult),
                 reads=[self.x_tm.d, dt.d], writes=[self.xdt.d])
            ps = self.PF()
            for g in range(4):
                P.op("pe", lambda h, g=g, ps=ps: h.matmul(ps.t[:, g * 128:(g + 1) * 128], lhsT=self.xbcT.t[:, 16 + g, :], rhs=self.xbcT.t[:, 20 + g, :], start=True, stop=True),
                     reads=[self.xbcT.d], writes=[ps.d])
            P.op("act", lambda h, ps=ps: h.copy(self.cbT.t[:, :, :], ps.t[:, :].rearrange("p (g l) -> p g l", l=128)), reads=[ps.d], writes=[self.cbT.d])
            psYs = {}

            def emit_decay(hb):
                g = hb // 2
                dec, MT = self.dec[hb % 2], self.MT[hb % 2]
                ps = self.PF()
                for hh in range(4):
                    hd = hb * 4 + hh
                    o = ps.t[:, hh * 128:(hh + 1) * 128]
                    P.op("pe", lambda h, o=o, hd=hd: h.matmul(o, lhsT=self.ahi.t[:, hd:hd + 1].to_broadcast([128, 128]), rhs=self.tri.t[:, ti, :], start=True, stop=False),
                         reads=[self.ahi.d, self.tri.d], writes=[ps.d])
                    P.op("pe", lambda h, o=o, hd=hd: h.matmul(o, lhsT=self.alo.t[:, hd:hd + 1].to_broadcast([128, 128]), rhs=self.tri.t[:, ti, :], start=False, stop=False),
                         reads=[self.alo.d, self.tri.d], writes=[ps.d])
                    P.op("pe", lambda h, o=o: h.matmul(o, lhsT=self.ident_b.t[:], rhs=self.mb.t[:, ti, :], start=False, stop=True),
                         reads=[self.ident_b.d, self.mb.d], writes=[ps.d])
                for hh in range(4):
                    hd = hb * 4 + hh
                    P.op("act", lambda h, hh=hh, hd=hd, ps=ps, dec=dec: h.activation(dec.t[:, hh, :], ps.t[:, hh * 128:(hh + 1) * 128], AF.Exp, bias=self.negcs.t[:, hd:hd + 1]),
                         reads=[ps.d, self.negcs.d], writes=[dec.d])
                P.op("dve", lambda h, g=g, dec=dec, MT=MT: h.tensor_tensor(MT.t[:, :, :], dec.t[:, :, :], self.cbT.t[:, g, :].unsqueeze(1).to_broadcast([128, 4, 128]), ALU.mult),
                     reads=[dec.d, self.cbT.d], writes=[MT.d])

            def emit_ydiag(hb):
                g = hb // 2
                MT = self.MT[hb % 2]
                if hb % 2 == 0:
                    psYs[g] = self.PF()
                psY = psYs[g]
                for hh in range(4):
                    hd = hb * 4 + hh
                    col = (hd % 8) * 64
                    P.op("pe", lambda h, hh=hh, hd=hd, col=col, MT=MT, psY=psY: h.matmul(psY.t[:, col:col + 64], lhsT=MT.t[:, hh, :], rhs=self.xdt.t[:, hd * 64:(hd + 1) * 64], start=True, stop=True),
                         reads=[MT.d, self.xdt.d], writes=[psY.d])
                if hb % 2 == 1:
                    tg = self.tmpg[g % 2]
                    P.op("dve", lambda h, g=g, tg=tg: h.tensor_tensor(tg.t[:, :].rearrange("p (h d) -> p h d", d=64), self.x_tm.t[:, g * 512:(g + 1) * 512].rearrange("p (h d) -> p h d", d=64),
                                                                     self.hp.t[:, 2, g * 8:(g + 1) * 8].unsqueeze(2).to_broadcast([128, 8, 64]), ALU.mult),
                         reads=[self.x_tm.d, self.hp.d], writes=[tg.d])
                    P.op("dve", lambda h, g=g, tg=tg, psY=psY: h.tensor_tensor(self.big1.t[:, g * 512:(g + 1) * 512], psY.t[:, :], tg.t[:], ALU.add),
                         reads=[psY.d, tg.d], writes=[self.big1.d])

            emit_decay(0)
            for hb in range(8):
                if hb + 1 < 8:
                    emit_decay(hb + 1)
                emit_ydiag(hb)
            for q in range(nseq):
                if ty == "S":
                    selq = self.sel.t[:, q, :]
                    mbcol = self.mbq.t[:, q:q + 1]
                else:
                    selq = self.ones_b.t[:]
                    mbcol = self.zero1.t[:, 0:1]
                ps = self.PF()
                P.op("pe", lambda h, ps=ps, selq=selq: h.matmul(ps.t[:, 0:32], lhsT=selq, rhs=self.ahi.t[:], start=True, stop=False), reads=[self.sel.d, self.ones_b.d, self.ahi.d], writes=[ps.d])
                P.op("pe", lambda h, ps=ps, selq=selq: h.matmul(ps.t[:, 0:32], lhsT=selq, rhs=self.alo.t[:], start=False, stop=True), reads=[self.sel.d, self.ones_b.d, self.alo.d], writes=[ps.d])
                P.op("act", lambda h, ps=ps: h.activation(self.dA.t[:], ps.t[:, 0:32], AF.Exp), reads=[ps.d], writes=[self.dA.d])
                P.op("dve", lambda h, ps=ps: h.tensor_tensor(self.wdec.t[:], ps.t[:, 0:32], self.negcs.t[:], ALU.add), reads=[ps.d, self.negcs.d], writes=[self.wdec.d])
                P.op("act", lambda h, mbcol=mbcol: h.activation(self.wdec.t[:], self.wdec.t[:], AF.Exp, bias=mbcol), reads=[self.wdec.d, self.mbq.d, self.zero1.d], writes=[self.wdec.d])
                P.op("dve", lambda h: h.tensor_tensor(self.xw.t[:, :].rearrange("p (h d) -> p h d", d=64), self.xdt.t[:, :].rearrange("p (h d) -> p h d", d=64),
                                                      self.wdec.t[:].unsqueeze(2).to_broadcast([128, 32, 64]), ALU.mult),
                     reads=[self.xdt.d, self.wdec.d], writes=[self.xw.d])
                if ty == "S":
                    P.op("dve", lambda h, q=q: h.tensor_tensor(self.CTq.t[:, :, :], self.xbcT.t[:, 20:24, :], self.rowm.t[:, q, :].unsqueeze(1).to_broadcast([128, 4, 128]), ALU.mult),
                         reads=[self.xbcT.d, self.rowm.d], writes=[self.CTq.d])
                    CT = lambda g: self.CTq.t[:, g, :]
                    ctd = self.CTq.d
                else:
                    CT = lambda g: self.xbcT.t[:, 20 + g, :]
                    ctd = self.xbcT.d
                if ty == "S":
                    for j in range(16):
                        P.dma("sp", self.ds_stg, self.big2.t[:, 0, j, :], A["st_ssm"][q, 2 * j:2 * j + 2].rearrange("h2 p n -> (h2 p) n"), writes=[self.big2_d0])
                    for jb in range(4):
                        ps = self.PF()
                        for jj in range(4):
                            j = jb * 4 + jj
                            P.op("pe", lambda h, j=j, jj=jj, ps=ps: h.transpose(ps.t[:, jj * 128:(jj + 1) * 128], self.big2.t[:, 0, j, :], self.ident_f.t[:]),
                                 reads=[self.big2_d0, self.ident_f.d], writes=[ps.d])
                        P.op("dve", lambda h, jb=jb, ps=ps: h.tensor_copy(self.ST.t[:, jb * 512:(jb + 1) * 512], ps.t[:, :]), reads=[ps.d], writes=[self.ST.d])
                        P.op("act", lambda h, jb=jb, ps=ps: h.copy(self.STb.t[:, jb * 512:(jb + 1) * 512], ps.t[:, :]), reads=[ps.d], writes=[self.STb.d])
                for g in range(4):
                    ps = self.PF()
                    P.op("pe", lambda h, g=g, ps=ps, CT=CT: h.matmul(ps.t[:, :], lhsT=CT(g), rhs=self.STb.t[:, g * 512:(g + 1) * 512], start=True, stop=True),
                         reads=[ctd, self.STb.d], writes=[ps.d])
                    tg = self.tmpg[g % 2]
                    P.op("dve", lambda h, g=g, ps=ps, tg=tg: h.tensor_tensor(tg.t[:, :].rearrange("p (h d) -> p h d", d=64), ps.t[:, :].rearrange("p (h d) -> p h d", d=64),
                                                                           self.ecs.t[:, g * 8:(g + 1) * 8].unsqueeze(2).to_broadcast([128, 8, 64]), ALU.mult),
                         reads=[ps.d, self.ecs.d], writes=[tg.d])
                    P.op("dve", lambda h, g=g, tg=tg: h.tensor_tensor(self.big1.t[:, g * 512:(g + 1) * 512], self.big1.t[:, g * 512:(g + 1) * 512], tg.t[:], ALU.add),
                         reads=[tg.d, self.big1.d], writes=[self.big1.d])
                for g in range(4):
                    ps = self.PF()
                    P.op("pe", lambda h, g=g, ps=ps: h.matmul(ps.t[:, :], lhsT=self.B_tm.t[:, g * 128:(g + 1) * 128], rhs=self.xw.t[:, g * 512:(g + 1) * 512], start=True, stop=True),
                         reads=[self.B_tm.d, self.xw.d], writes=[ps.d])
                    sg3 = self.ST.t[:, g * 512:(g + 1) * 512].rearrange("p (h d) -> p h d", d=64)
                    P.op("dve", lambda h, g=g, sg3=sg3: h.tensor_tensor(sg3, sg3, self.dA.t[:, g * 8:(g + 1) * 8].unsqueeze(2).to_broadcast([128, 8, 64]), ALU.mult),
                         reads=[self.ST.d, self.dA.d], writes=[self.ST.d])
                    P.op("dve", lambda h, g=g, ps=ps: h.tensor_tensor(self.ST.t[:, g * 512:(g + 1) * 512], self.ST.t[:, g * 512:(g + 1) * 512], ps.t[:, :], ALU.add),
                         reads=[self.ST.d, ps.d], writes=[self.ST.d])
                if ty != "S":
                    P.op("act", lambda h: h.copy(self.STb.t[:], self.ST.t[:]), reads=[self.ST.d], writes=[self.STb.d])
                if ty == "M":
                    P.op("pool", lambda h: h.tensor_copy(self.ST_meta.t[:], self.ST.t[:]), reads=[self.ST.d], writes=[self.ST_meta.d])
                if ty == "S" or last:
                    for jb in range(4):
                        ps = self.PF()
                        for jj in range(4):
                            j = jb * 4 + jj
                            P.op("pe", lambda h, j=j, jj=jj, ps=ps: h.transpose(ps.t[:, jj * 128:(jj + 1) * 128], self.ST.t[:, j * 128:(j + 1) * 128], self.ident_f.t[:]),
                                 reads=[self.ST.d, self.ident_f.d], writes=[ps.d])
                        P.op("act", lambda h, jb=jb, ps=ps: h.copy(self.big2.t[:, 1, jb * 4:(jb + 1) * 4, :], ps.t[:, :].rearrange("p (j n) -> p j n", n=128)), reads=[ps.d], writes=[self.big2_d1])
                    dst = A["ssm_s"][q] if ty == "S" else A["ssm_p"][b]
                    for j in range(16):
                        P.dma("pool", self.ds_stgo, dst[2 * j:2 * j + 2].rearrange("h2 p n -> (h2 p) n"), self.big2.t[:, 1, j, :], reads=[self.big2_d1])
            P.op("pool", lambda h: h.memset(self.ssq.t[:], 0.0), writes=[self.ssq.d])
            P.op("act", lambda h: h.activation(self.xw.t[:], self.z_tm.t[:], AF.Tanh, scale=0.5), reads=[self.z_tm.d], writes=[self.xw.d])
            P.op("dve", lambda h: h.scalar_tensor_tensor(self.z_tm.t[:], self.xw.t[:], 1.0, self.z_tm.t[:], ALU.add, ALU.mult), reads=[self.xw.d, self.z_tm.d], writes=[self.z_tm.d])
            P.op("dve", lambda h: h.scalar_tensor_tensor(self.big1.t[:, 0:2048], self.big1.t[:, 0:2048], 0.5, self.z_tm.t[:], ALU.mult, ALU.mult), reads=[self.big1.d, self.z_tm.d], writes=[self.big1.d])
            for g in range(4):
                P.op("act", lambda h, g=g: h.activation(self.sz.t[:], self.big1.t[:, g * 512:(g + 1) * 512], AF.Square, accum_out=self.ssq.t[:, g:g + 1]), reads=[self.big1.d], writes=[self.sz.d, self.ssq.d])
            P.op("act", lambda h: h.activation(self.grs.t[:], self.ssq.t[:], AF.Ln, bias=EPS, scale=1.0 / 512.0), reads=[self.ssq.d], writes=[self.grs.d])
            P.op("act", lambda h: h.activation(self.grs.t[:], self.grs.t[:], AF.Exp, scale=-0.5), reads=[self.grs.d], writes=[self.grs.d])
            for g in range(4):
                P.op("dve", lambda h, g=g: h.tensor_scalar_mul(self.x_tm.t[:, g * 512:(g + 1) * 512], self.big1.t[:, g * 512:(g + 1) * 512], self.grs.t[:, g:g + 1]), reads=[self.big1.d, self.grs.d], writes=[self.x_tm.d])
            for g in range(4):
                pb = self.PB()
                for m in range(4):
                    mm = g * 4 + m
                    P.op("pe", lambda h, m=m, mm=mm, pb=pb: h.transpose(pb.t[:, m * 128:(m + 1) * 128], self.x_tm.t[:, mm * 128:(mm + 1) * 128], self.ident_b.t[:]),
                         reads=[self.x_tm.d, self.ident_b.d], writes=[pb.d])
                P.op("dve", lambda h, g=g, pb=pb: h.tensor_tensor(self.ynT.t[:, g * 4:(g + 1) * 4, :], pb.t[:, 0:512].rearrange("p (m t) -> p m t", t=128),
                                                                   self.gate_g.t[:, g * 4:(g + 1) * 4].unsqueeze(2).to_broadcast([128, 4, 128]), ALU.mult),
                     reads=[pb.d, self.gate_g.d], writes=[self.ynT.d])
        if "C" in ph:
            self.dense_fm(("w_out", lambda w: w), 2048, 1024, self.ynT, self.resid_add)
            self.ffn(0)

        if "D" in ph:
            cur, prv = self.pbuf, 1 - self.pbuf
            kTo, vbo = self.kT[cur], self.v_b[cur]
            self.rms(2, self.uT)
            def ev_k(ps):
                P.op("act", lambda h: h.copy(self.k_tm.t[:], ps.t[:, 0:256]), reads=[ps.d], writes=[self.k_tm.d])
                P.op("dve", lambda h: h.tensor_copy(self.k_rot.t[:], ps.t[:, 0:256]), reads=[ps.d], writes=[self.k_rot.d])
            self.dense_tm(("w_k", lambda w: w), 256, self.uT, ev_k)
            self.rope_apply(self.k_tm, self.k_rot, 4)
            P.op("act", lambda h: h.copy(self.k_b.t[:], self.k_rot.t[:]), reads=[self.k_rot.d], writes=[self.k_b.d])
            pb = self.PB()
            for k in range(4):
                P.op("pe", lambda h, k=k, pb=pb: h.transpose(pb.t[0:64, k * 128:(k + 1) * 128], self.k_b.t[:, k * 64:(k + 1) * 64], self.ident_b.t[:]),
                     reads=[self.k_b.d, self.ident_b.d], writes=[pb.d])
            P.op("dve", lambda h, pb=pb: h.tensor_copy(kTo.t[:, :, :], pb.t[0:64, 0:512].rearrange("p (k t) -> p k t", t=128)), reads=[pb.d], writes=[kTo.d])
            def ev_v(ps):
                P.op("act", lambda h: h.copy(self.v_tm.t[:], ps.t[:, 0:256]), reads=[ps.d], writes=[self.v_tm.d])
                P.op("dve", lambda h: h.tensor_copy(vbo.t[:], ps.t[:, 0:256]), reads=[ps.d], writes=[vbo.d])
            self.dense_tm(("w_v", lambda w: w), 256, self.uT, ev_v)
            if ty == "S":
                for q in range(NSQ):
                    P.dma("pool", self.ds_out, A["k_s"][q, 120:128, :], self.k_rot.t[8 * q:8 * q + 8, :], reads=[self.k_rot.d])
                    P.dma("pool", self.ds_out, A["v_s"][q, 120:128, :], self.v_tm.t[8 * q:8 * q + 8, :], reads=[self.v_tm.d])
                P.dma("pool", self.ds_cp, A["k_s"][:, 0:120, :], A["st_k"][:, 8:128, :])
                P.dma("pool", self.ds_cp, A["v_s"][:, 0:120, :], A["st_v"][:, 8:128, :])
            if last:
                P.dma("pool", self.ds_out, A["k_p"][b], self.k_rot.t[:], reads=[self.k_rot.d])
                P.dma("pool", self.ds_out, A["v_p"][b], self.v_tm.t[:], reads=[self.v_tm.d])
            if ty == "M":
                P.op("pool", lambda h: h.tensor_copy(self.kT_meta.t[:], kTo.t[:]), reads=[kTo.d], writes=[self.kT_meta.d])
                P.op("pool", lambda h: h.tensor_copy(self.v_meta.t[:], vbo.t[:]), reads=[vbo.d], writes=[self.v_meta.d])
            self.rms(3, self.uT)
            for blk in range(2):
                def ev_q(ps, blk=blk):
                    P.op("act", lambda h: h.copy(self.q_tm.t[:, blk * 512:(blk + 1) * 512], ps.t[:, :]), reads=[ps.d], writes=[self.q_tm.d])
                    P.op("dve", lambda h: h.tensor_copy(self.q_rot.t[:, blk * 512:(blk + 1) * 512], ps.t[:, :]), reads=[ps.d], writes=[self.q_rot.d])
                self.dense_tm(("w_q", lambda w, blk=blk: w[:, blk * 512:(blk + 1) * 512]), 512, self.uT, ev_q)
            self.rope_apply(self.q_tm, self.q_rot, 16)
            P.op("act", lambda h: h.copy(self.q_b.t[:], self.q_rot.t[:]), reads=[self.q_rot.d], writes=[self.q_b.d])
            for half in range(2):
                pb = self.PB()
                for hh in range(8):
                    hd = half * 8 + hh
                    P.op("pe", lambda h, hh=hh, hd=hd, pb=pb: h.transpose(pb.t[0:64, hh * 128:(hh + 1) * 128], self.q_b.t[:, hd * 64:(hd + 1) * 64], self.ident_b.t[:]),
                         reads=[self.q_b.d, self.ident_b.d], writes=[pb.d])
                P.op("dve", lambda h, half=half, pb=pb: h.tensor_copy(self.qT.t[:, half * 8:(half + 1) * 8, :], pb.t[0:64, :].rearrange("p (k t) -> p k t", t=128)),
                     reads=[pb.d], writes=[self.qT.d])
            if ty == "S":
                for q in range(NSQ):
                    P.dma("sp", self.ds_kv, self.sk32.t[:], A["st_k"][q], writes=[self.sk32.d])
                    P.dma("sp", self.ds_kv, self.sv32.t[:], A["st_v"][q], writes=[self.sv32.d])
                    P.op("act", lambda h: h.copy(self.skb.t[:], self.sk32.t[:]), reads=[self.sk32.d], writes=[self.skb.d])
                    P.op("dve", lambda h: h.tensor_copy(self.svb.t[:], self.sv32.t[:]), reads=[self.sv32.d], writes=[self.svb.d])
                    pb = self.PB()
                    for k in range(4):
                        P.op("pe", lambda h, k=k, pb=pb: h.transpose(pb.t[0:64, k * 128:(k + 1) * 128], self.skb.t[:, k * 64:(k + 1) * 64], self.ident_b.t[:]),
                             reads=[self.skb.d, self.ident_b.d], writes=[pb.d])
                    P.op("dve", lambda h, pb=pb: h.tensor_copy(self.kTs.t[:, :, :], pb.t[0:64, 0:512].rearrange("p (k t) -> p k t", t=128)), reads=[pb.d], writes=[self.kTs.d])
                    ps = self.PF()
                    for k in range(4):
                        o = ps.t[:, k * 32:(k + 1) * 32]
                        P.op("pe", lambda h, k=k, o=o, q=q: h.matmul(o.rearrange("p (j t) -> p j t", t=8), lhsT=self.kTs.t[:, k, :], rhs=self.qT.t[:, 4 * k:4 * k + 4, 8 * q:8 * q + 8], start=True, stop=False),
                             reads=[self.kTs.d, self.qT.d], writes=[ps.d])
                        P.op("pe", lambda h, k=k, o=o: h.matmul(o, lhsT=self.ident_b.t[:], rhs=self.mbstate.t[:, k * 32:(k + 1) * 32], start=False, stop=True),
                             reads=[self.ident_b.d, self.mbstate.d], writes=[ps.d])
                    P.op("act", lambda h, ps=ps: h.activation(self.PTs.t[:], ps.t[:, 0:128], AF.Exp, scale=0.125), reads=[ps.d], writes=[self.PTs.d])
                    ps2 = self.PF()
                    for k in range(4):
                        P.op("pe", lambda h, k=k, ps2=ps2: h.matmul(ps2.t[0:64, k * 32:(k + 1) * 32], lhsT=self.svb.t[:, k * 64:(k + 1) * 64], rhs=self.PTs.t[:, k * 32:(k + 1) * 32], start=True, stop=True),
                             reads=[self.svb.d, self.PTs.d], writes=[ps2.d])
                    P.op("pe", lambda h, ps2=ps2: h.matmul(ps2.t[0:64, 128:256], lhsT=self.ones_b.t[:, 0:64], rhs=self.PTs.t[:], start=True, stop=True),
                         reads=[self.ones_b.d, self.PTs.d], writes=[ps2.d])
                    P.op("dve", lambda h, q=q, ps2=ps2: h.tensor_copy(self.big2.t[0:64, 0, :, 8 * q:8 * q + 8], ps2.t[0:64, 0:128].rearrange("p (h t) -> p h t", t=8)), reads=[ps2.d], writes=[self.big2_d0])
                    P.op("dve", lambda h, q=q, ps2=ps2: h.tensor_copy(self.big2.t[0:64, 1, :, 8 * q:8 * q + 8], ps2.t[0:64, 128:256].rearrange("p (h t) -> p h t", t=8)), reads=[ps2.d], writes=[self.big2_d1])
            use_prev = ty == "P"
            if first:
                kTp, vbp = self.kT_meta, self.v_meta
            else:
                kTp, vbp = self.kT[prv], self.v_b[prv]
            mpi = 1 if first else 0
            sc = {}

            def emit_scores(hd):
                k = hd // 4
                PT = self.PT[hd % 2]
                ps = self.PF()
                if use_prev:
                    P.op("pe", lambda h, ps=ps, k=k, hd=hd: h.matmul(ps.t[:, 0:128], lhsT=kTp.t[:, k, :], rhs=self.qT.t[:, hd, :], start=True, stop=False),
                         reads=[kTp.d, self.qT.d], writes=[ps.d])
                    P.op("pe", lambda h, ps=ps: h.matmul(ps.t[:, 0:128], lhsT=self.ident_b.t[:], rhs=self.mbprev.t[:, mpi, :], start=False, stop=True),
                         reads=[self.ident_b.d, self.mbprev.d], writes=[ps.d])
                P.op("pe", lambda h, ps=ps, k=k, hd=hd: h.matmul(ps.t[:, 128:256], lhsT=kTo.t[:, k, :], rhs=self.qT.t[:, hd, :], start=True, stop=False),
                     reads=[kTo.d, self.qT.d], writes=[ps.d])
                P.op("pe", lambda h, ps=ps: h.matmul(ps.t[:, 128:256], lhsT=self.ident_b.t[:], rhs=self.mb.t[:, ti, :], start=False, stop=True),
                     reads=[self.ident_b.d, self.mb.d], writes=[ps.d])
                lo = 0 if use_prev else 128
                P.op("act", lambda h, ps=ps, PT=PT, lo=lo: h.activation(PT.t[:, lo:256], ps.t[:, lo:256], AF.Exp, scale=0.125), reads=[ps.d], writes=[PT.d])

            def emit_pv(hd, psO, psD):
                k = hd // 4
                hh = hd % 4
                PT = self.PT[hd % 2]
                oo = psO.t[0:64, hh * 128:(hh + 1) * 128]
                od = psD.t[0:64, hh * 128:(hh + 1) * 128]
                if use_prev:
                    P.op("pe", lambda h, oo=oo, k=k, PT=PT: h.matmul(oo, lhsT=vbp.t[:, k * 64:(k + 1) * 64], rhs=PT.t[:, 0:128], start=True, stop=False), reads=[vbp.d, PT.d], writes=[psO.d])
                P.op("pe", lambda h, oo=oo, k=k, PT=PT: h.matmul(oo, lhsT=vbo.t[:, k * 64:(k + 1) * 64], rhs=PT.t[:, 128:256], start=(not use_prev), stop=True), reads=[vbo.d, PT.d], writes=[psO.d])
                if use_prev:
                    P.op("pe", lambda h, od=od, PT=PT: h.matmul(od, lhsT=self.ones_b.t[:, 0:64], rhs=PT.t[:, 0:128], start=True, stop=False), reads=[self.ones_b.d, PT.d], writes=[psD.d])
                P.op("pe", lambda h, od=od, PT=PT: h.matmul(od, lhsT=self.ones_b.t[:, 0:64], rhs=PT.t[:, 128:256], start=(not use_prev), stop=True), reads=[self.ones_b.d, PT.d], writes=[psD.d])

            emit_scores(0)
            for hq in range(4):
                psO = self.PF()
                psD = self.PF()
                for hh in range(4):
                    hd = hq * 4 + hh
                    if hd + 1 < 16:
                        emit_scores(hd + 1)
                    emit_pv(hd, psO, psD)
                h0 = hq * 4
                den = self.den
                P.op("dve", lambda h, psD=psD, h0=h0: h.tensor_tensor(den.t[:, :, :], psD.t[0:64, :].rearrange("p (h t) -> p h t", t=128),
                                                                      self.esink.t[0:64, h0:h0 + 4].unsqueeze(2).to_broadcast([64, 4, 128]), ALU.add),
                     reads=[psD.d, self.esink.d], writes=[den.d])
                if ty == "S":
                    P.op("dve", lambda h, h0=h0: h.tensor_tensor(den.t[:, :, :], den.t[:, :, :], self.big2.t[0:64, 1, h0:h0 + 4, :], ALU.add), reads=[den.d, self.big2_d1], writes=[den.d])
                    P.op("dve", lambda h, h0=h0, psO=psO: h.tensor_tensor(self.big2.t[0:64, 0, h0:h0 + 4, :], self.big2.t[0:64, 0, h0:h0 + 4, :], psO.t[0:64, :].rearrange("p (h t) -> p h t", t=128), ALU.add),
                         reads=[psO.d, self.big2_d0], writes=[self.big2_d0])
                P.op("dve", lambda h: h.reciprocal(den.t[:, :, :], den.t[:, :, :]), reads=[den.d], writes=[den.d])
                if ty == "S":
                    P.op("dve", lambda h, h0=h0: h.tensor_tensor(self.oT.t[:, h0:h0 + 4, :], self.big2.t[0:64, 0, h0:h0 + 4, :], den.t[:, :, :], ALU.mult),
                         reads=[self.big2_d0, den.d], writes=[self.oT.d])
                else:
                    P.op("dve", lambda h, h0=h0, psO=psO: h.tensor_tensor(self.oT.t[:, h0:h0 + 4, :], psO.t[0:64, :].rearrange("p (h t) -> p h t", t=128), den.t[:, :, :], ALU.mult),
                         reads=[psO.d, den.d], writes=[self.oT.d])
            self.pbuf = 1 - self.pbuf
        if "E" in ph:
            self.dense_fm(("w_o", lambda w: w), 1024, 1024, self.oT, self.resid_add, part=64)
            self.ffn(1)
            if ty != "M":
                self.rms(5, self.q_tm, view3=True)
                for half in range(2):
                    ps = self.PF()
                    for m in range(4):
                        mm = half * 4 + m
                        P.op("pe", lambda h, m=m, mm=mm, ps=ps: h.transpose(ps.t[:, m * 128:(m + 1) * 128], self.q_tm.t[:, mm * 128:(mm + 1) * 128], self.ident_f.t[:]),
                             reads=[self.q_tm.d, self.ident_f.d], writes=[ps.d])
                    P.op("act", lambda h, half=half, ps=ps: h.copy(self.y_st.t[:, half * 512:(half + 1) * 512], ps.t[:, :]), reads=[ps.d], writes=[self.y_st.d])
                dst = A["y_s"] if ty == "S" else A["y_p"][b, c * 128:(c + 1) * 128, :]
                P.dma("pool", self.ds_y, dst, self.y_st.t[:], reads=[self.y_st.d])


_CACHE = {}


def _get_nc():
    if "nc" not in _CACHE:
        k = Kern()
        _CACHE["nc"] = k.build()
        _CACHE["outs"] = k.out_names
    return _CACHE["nc"]


def _col(v, nchunk):
    return np.ascontiguousarray(np.asarray(v, np.float32).reshape(nchunk, 128).T)


def kernel(x_prompt, x_sample, state_ssm, state_conv, state_k, state_v, meta_tokens,
           ssm_norm_g, ssm_w_in, ssm_conv_w, ssm_conv_b, ssm_dt_bias, ssm_A_log, ssm_D,
           ssm_gate_norm_g, ssm_w_out, kv_norm_g, w_k, w_v, attn_norm_g, w_q, attn_sinks, w_o,
           ffn_norm_g, ffn_w_gate, ffn_w_up, ffn_w_down, final_norm_g):
    f = lambda a: np.ascontiguousarray(np.asarray(a, dtype=np.float32))
    nc = _get_nc()
    cst = make_consts()
    gcols = np.stack([_col(ssm_norm_g[0], 8), _col(ffn_norm_g[0], 8), _col(kv_norm_g, 8), _col(attn_norm_g[0], 8),
                      _col(ffn_norm_g[1], 8), _col(final_norm_g, 8)], axis=1)
    gate_g = _col(ssm_gate_norm_g[0], 16)
    conv_w = np.ascontiguousarray(f(ssm_conv_w[0]).reshape(4, 24, 128).transpose(2, 1, 0))
    conv_b = _col(ssm_conv_b[0], 24)
    hp = np.zeros((128, 4, 32), np.float32)
    hp[:, 0, :] = f(ssm_dt_bias[0])[None, :]
    hp[:, 1, :] = f(ssm_A_log[0])[None, :]
    hp[:, 2, :] = f(ssm_D[0])[None, :]
    sinks = np.ascontiguousarray(np.broadcast_to(f(attn_sinks[0])[None, :], (128, 16)))
    shared = {
        "meta": f(meta_tokens), "w_in": f(ssm_w_in[0]), "w_out": f(ssm_w_out[0]), "w_k": f(w_k), "w_v": f(w_v),
        "w_q": f(w_q[0]), "w_o": f(w_o[0]), "w_gate": f(ffn_w_gate), "w_up": f(ffn_w_up), "w_down": f(ffn_w_down),
        "gcols": np.ascontiguousarray(gcols), "gate_g": gate_g, "conv_w": conv_w, "conv_b": conv_b, "hp": hp, "sinks": sinks,
        "c_tri": cst["tri"], "c_mb": cst["mb"], "c_mbprev": cst["mbprev"], "c_mbstate": cst["mbstate"],
        "c_ident_b": cst["ident_b"], "c_ident_f": cst["ident_f"], "c_ones_b": cst["ones_b"], "c_sel": cst["sel"],
        "c_mbq": cst["mbq"], "c_rowm": cst["rowm"], "c_tokmask": cst["tokmask"], "c_rope": cst["rope"],
    }
    xp, xs = f(x_prompt), f(x_sample)
    sssm, sconv, sk, sv = f(state_ssm[0]), f(state_conv[0]), f(state_k), f(state_v)
    in_maps = []
    for i in range(NCORE):
        m = dict(shared)
        m["xs"] = xs[NSQ * i:NSQ * (i + 1)].reshape(128, 1024)
        m["xp"] = xp[NPB * i:NPB * (i + 1)]
        m["st_ssm"] = sssm[NSQ * i:NSQ * (i + 1)]
        m["st_conv"] = sconv[NSQ * i:NSQ * (i + 1)].reshape(NSQ * 3, 3072)
        m["st_k"] = sk[NSQ * i:NSQ * (i + 1)].reshape(NSQ, 128, 256)
        m["st_v"] = sv[NSQ * i:NSQ * (i + 1)].reshape(NSQ, 128, 256)
        in_maps.append(m)
    res = run_bass_kernel_spmd(nc, in_maps, core_ids=list(range(NCORE)))
    R = res.results
    cat = lambda name: np.concatenate([np.asarray(r[name], np.float32) for r in R], axis=0)
    y_prompt = cat("y_p")
    y_sample = cat("y_s").reshape(128, 8, 1024)
    ssm_p = cat("ssm_p")[None]
    conv_p = cat("conv_p")[None]
    k_p = cat("k_p").reshape(16, 128, 4, 64)
    v_p = cat("v_p").reshape(16, 128, 4, 64)
    ssm_s = cat("ssm_s")[None]
    conv_s = cat("conv_s").reshape(128, 3, 3072)[None]
    k_s = cat("k_s").reshape(128, 128, 4, 64)
    v_s = cat("v_s").reshape(128, 128, 4, 64)
    return (y_prompt, y_sample, ssm_p, conv_p, k_p, v_p, ssm_s, conv_s, k_s, v_s)
```

```python
import numpy as np
from contextlib import ExitStack
import ml_dtypes
import concourse.bass as bass
import concourse.mybir as mybir
from concourse.bass_utils import run_bass_kernel_spmd

F32 = mybir.dt.float32
BF16 = mybir.dt.bfloat16
AF = mybir.ActivationFunctionType
ALU = mybir.AluOpType

SEM_CAP = 30000
NEG = -30000.0
NCORE = 8
NSQ = 16
NPB = 2
NCHUNK = 16
DFF = 2816
EPS = 1e-5


class Dep:
    __slots__ = ("name", "w", "rs", "excl")

    def __init__(self, name=""):
        self.name = name
        self.w = None
        self.rs = []
        self.excl = False


class EngS:
    def __init__(self, name, handle):
        self.name = name
        self.h = handle
        self.count = 0
        self.sems = []
        self.seen = {}
        self.ops = []


class DmaSem:
    def __init__(self, sem, name):
        self.sem = sem
        self.total = 0
        self.name = name


class Prog:
    def __init__(self, nc, stack):
        self.nc = nc
        self.stack = stack
        self.E = {
            "pe": EngS("pe", nc.tensor),
            "act": EngS("act", nc.scalar),
            "dve": EngS("dve", nc.vector),
            "pool": EngS("pool", nc.gpsimd),
            "sp": EngS("sp", nc.sync),
        }
        self.dsems = []

    def new_sem(self, name):
        return self.stack.enter_context(self.nc.semaphore(name))

    def dma_sem(self, name):
        d = DmaSem(self.new_sem("d_" + name), name)
        self.dsems.append(d)
        return d

    def _need(self, eng, tok, needs):
        if tok is None:
            return
        if tok[0] == "e":
            if tok[1] == eng.name and eng.name == "pe":
                return
            key = ("e", tok[1])
            if needs.get(key, 0) < tok[2]:
                needs[key] = tok[2]
        else:
            ds = tok[1]
            key = ("d", ds)
            v = ds.total
            if needs.get(key, 0) < v:
                needs[key] = v

    def _waits(self, eng, reads, writes):
        needs = {}
        for d in reads:
            self._need(eng, d.w, needs)
            if d.excl:
                for r in d.rs:
                    if r[0] == "e" and r[1] != eng.name:
                        self._need(eng, r, needs)
        for d in writes:
            self._need(eng, d.w, needs)
            for r in d.rs:
                self._need(eng, r, needs)
        out = []
        for key, v in needs.items():
            if eng.seen.get(key, 0) >= v:
                continue
            eng.seen[key] = v
            out.append((key, v))
        return out

    def _emit_waits(self, eng, waits):
        for key, v in waits:
            if key[0] == "e":
                src = self.E[key[1]]
                si = (v - 1) // SEM_CAP
                val = v - si * SEM_CAP
                sem = src.sems[si]
                eng.ops.append(lambda h=eng.h, sem=sem, val=val: h.wait_ge(sem, val))
            else:
                ds = key[1]
                eng.ops.append(lambda h=eng.h, sem=ds.sem, val=v: h.wait_ge(sem, val))

    def _mark(self, tok, reads, writes):
        for d in reads:
            d.rs.append(tok)
        for d in writes:
            d.w = tok
            d.rs = []

    cap = None

    def op(self, engname, fn, reads=(), writes=()):
        if self.cap is not None:
            self.cap.append(("op", engname, fn, tuple(reads), tuple(writes), None, None, None))
            return
        eng = self.E[engname]
        self._emit_waits(eng, self._waits(eng, reads, writes))
        eng.count += 1
        idx = eng.count
        si = (idx - 1) // SEM_CAP
        while len(eng.sems) <= si:
            eng.sems.append(self.new_sem(f"s_{engname}{len(eng.sems)}"))
        sem = eng.sems[si]
        eng.ops.append(lambda h=eng.h, sem=sem, fn=fn: fn(h).then_inc(sem, 1))
        self._mark(("e", engname, idx), reads, writes)

    def dma(self, qname, dsem, out, in_, reads=(), writes=(), **kw):
        if self.cap is not None:
            self.cap.append(("dma", qname, dsem, tuple(reads), tuple(writes), out, in_, kw))
            return
        eng = self.E[qname]
        self._emit_waits(eng, self._waits(eng, reads, writes))
        dsem.total += 16
        eng.ops.append(lambda h=eng.h, sem=dsem.sem, out=out, in_=in_, kw=kw:
                       h.dma_start(out=out, in_=in_, **kw).then_inc(sem, 16))
        self._mark(("d", dsem, dsem.total), reads, writes)

    def capture(self, gen):
        assert self.cap is None
        self.cap = []
        gen()
        lst, self.cap = self.cap, None
        return lst

    def replay(self, lst):
        for (kind, a, b, reads, writes, out, in_, kw) in lst:
            if kind == "op":
                self.op(a, b, reads, writes)
            else:
                self.dma(a, b, out, in_, reads, writes, **kw)

    def replay_merged(self, l1, l2):
        n1, n2 = len(l1), len(l2)
        i = j = 0
        while i < n1 or j < n2:
            if j >= n2 or (i < n1 and (i + 0.5) * n2 <= (j + 0.5) * n1):
                self.replay(l1[i:i + 1]); i += 1
            else:
                self.replay(l2[j:j + 1]); j += 1

    def finish(self):
        eng = self.E["sp"]
        for name, e in self.E.items():
            if e.count > 0:
                v = e.count
                si = (v - 1) // SEM_CAP
                eng.ops.append(lambda h=eng.h, sem=e.sems[si], val=v - si * SEM_CAP: h.wait_ge(sem, val))
        for ds in self.dsems:
            if ds.total > 0:
                eng.ops.append(lambda h=eng.h, sem=ds.sem, val=ds.total: h.wait_ge(sem, val))

    def emit(self):
        with self.nc.Block() as block:
            @block.tensor
            def _(e):
                for f in self.E["pe"].ops:
                    f()

            @block.scalar
            def _(e):
                for f in self.E["act"].ops:
                    f()

            @block.vector
            def _(e):
                for f in self.E["dve"].ops:
                    f()

            @block.gpsimd
            def _(e):
                for f in self.E["pool"].ops:
                    f()

            @block.sync
            def _(e):
                for f in self.E["sp"].ops:
                    f()


class Tl:
    __slots__ = ("t", "d")

    def __init__(self, t, name):
        self.t = t
        self.d = Dep(name)


def make_consts():
    bf = ml_dtypes.bfloat16
    i = np.arange(128)
    c = {}
    tri_std = (i[:, None] <= i[None, :]).astype(np.float32)
    same = (i[:, None] // 8 == i[None, :] // 8)
    tri_blk = tri_std * same
    c["tri"] = np.stack([tri_std, tri_blk]).astype(bf)
    mb_std = np.where(tri_std > 0, 0.0, NEG).astype(np.float32)
    mb_blk = np.where(tri_blk > 0, 0.0, NEG).astype(np.float32)
    c["mb"] = np.stack([mb_std, mb_blk]).astype(bf)
    mbp = np.where(i[:, None] > i[None, :], 0.0, NEG).astype(np.float32)
    mbp0 = np.where((i[:, None] > i[None, :]) & (i[:, None] >= 112), 0.0, NEG).astype(np.float32)
    c["mbprev"] = np.stack([mbp, mbp0]).astype(bf)
    t8 = np.arange(128) % 8
    c["mbstate"] = np.where(i[:, None] > t8[None, :], 0.0, NEG).astype(np.float32).astype(bf)
    c["ident_b"] = np.eye(128, dtype=np.float32).astype(bf)
    c["ident_f"] = np.eye(128, dtype=np.float32)
    c["ones_b"] = np.ones((128, 128), np.float32).astype(bf)
    sel = np.zeros((128, NSQ, 128), np.float32)
    mbq = np.full((128, NSQ), NEG, np.float32)
    rowm = np.zeros((128, NSQ, 128), np.float32)
    for q in range(NSQ):
        sel[8 * q:8 * q + 8, q, :] = 1.0
        mbq[8 * q:8 * q + 8, q] = 0.0
        rowm[:, q, 8 * q:8 * q + 8] = 1.0
    c["sel"] = sel.astype(bf)
    c["mbq"] = mbq
    c["rowm"] = rowm.astype(bf)
    tm = np.ones((128, 2), np.float32)
    tm[:112, 1] = 0.0
    c["tokmask"] = tm
    inv = (np.float32(500000.0) ** (-np.arange(0, 16, 2, dtype=np.float32) / np.float32(16))).astype(np.float32)
    rope = np.zeros((18, 128, 16), np.float32)
    for ty in range(18):
        if ty < 16:
            pos = 16 + 128 * ty + i
        elif ty == 16:
            pos = i - 112
        else:
            pos = 16384 + (i % 8)
        ang = pos.astype(np.float32)[:, None] * inv[None, :]
        rope[ty, :, 0:8] = np.cos(ang)
        rope[ty, :, 8:16] = np.sin(ang)
    c["rope"] = rope
    return c


class Kern:
    def __init__(self):
        self.nc = bass.Bass("TRN2", target_bir_lowering=False)
        self.in_names = []
        self.out_names = []

    def din(self, name, shape, dt=F32):
        self.in_names.append(name)
        return self.nc.dram_tensor(name, list(shape), dt, kind="ExternalInput").ap()

    def dout(self, name, shape, dt=F32):
        self.out_names.append(name)
        return self.nc.dram_tensor(name, list(shape), dt, kind="ExternalOutput").ap()

    def sb(self, name, shape, dt):
        return Tl(self.st.enter_context(self.nc.sbuf_tensor("sb_" + name, list(shape), dt)), name)

    def build(self):
        nc = self.nc
        A = {}
        A["xs"] = self.din("xs", [128, 1024])
        A["xp"] = self.din("xp", [NPB, 2048, 1024])
        A["meta"] = self.din("meta", [16, 1024])
        A["st_ssm"] = self.din("st_ssm", [NSQ, 32, 64, 128])
        A["st_conv"] = self.din("st_conv", [NSQ * 3, 3072])
        A["st_k"] = self.din("st_k", [NSQ, 128, 256])
        A["st_v"] = self.din("st_v", [NSQ, 128, 256])
        A["w_in"] = self.din("w_in", [1024, 5152])
        A["w_out"] = self.din("w_out", [2048, 1024])
        A["w_k"] = self.din("w_k", [1024, 256])
        A["w_v"] = self.din("w_v", [1024, 256])
        A["w_q"] = self.din("w_q", [1024, 1024])
        A["w_o"] = self.din("w_o", [1024, 1024])
        A["w_gate"] = self.din("w_gate", [2, 1024, DFF])
        A["w_up"] = self.din("w_up", [2, 1024, DFF])
        A["w_down"] = self.din("w_down", [2, DFF, 1024])
        A["gcols"] = self.din("gcols", [128, 6, 8])
        A["gate_g"] = self.din("gate_g", [128, 16])
        A["conv_w"] = self.din("conv_w", [128, 24, 4])
        A["conv_b"] = self.din("conv_b", [128, 24])
        A["hp"] = self.din("hp", [128, 4, 32])
        A["sinks"] = self.din("sinks", [128, 16])
        A["c_tri"] = self.din("c_tri", [2, 128, 128], BF16)
        A["c_mb"] = self.din("c_mb", [2, 128, 128], BF16)
        A["c_mbprev"] = self.din("c_mbprev", [2, 128, 128], BF16)
        A["c_mbstate"] = self.din("c_mbstate", [128, 128], BF16)
        A["c_ident_b"] = self.din("c_ident_b", [128, 128], BF16)
        A["c_ident_f"] = self.din("c_ident_f", [128, 128])
        A["c_ones_b"] = self.din("c_ones_b", [128, 128], BF16)
        A["c_sel"] = self.din("c_sel", [128, NSQ, 128], BF16)
        A["c_mbq"] = self.din("c_mbq", [128, NSQ])
        A["c_rowm"] = self.din("c_rowm", [128, NSQ, 128], BF16)
        A["c_tokmask"] = self.din("c_tokmask", [128, 2])
        A["c_rope"] = self.din("c_rope", [18, 128, 16])
        A["y_p"] = self.dout("y_p", [NPB, 2048, 1024])
        A["y_s"] = self.dout("y_s", [128, 1024])
        A["ssm_p"] = self.dout("ssm_p", [NPB, 32, 64, 128])
        A["conv_p"] = self.dout("conv_p", [NPB, 3, 3072])
        A["k_p"] = self.dout("k_p", [NPB, 128, 256])
        A["v_p"] = self.dout("v_p", [NPB, 128, 256])
        A["ssm_s"] = self.dout("ssm_s", [NSQ, 32, 64, 128])
        A["conv_s"] = self.dout("conv_s", [NSQ * 3, 3072])
        A["k_s"] = self.dout("k_s", [NSQ, 128, 256])
        A["v_s"] = self.dout("v_s", [NSQ, 128, 256])
        self.A = A
        self.WB = {}
        for wn in ["w_in", "w_out", "w_k", "w_v", "w_q", "w_o", "w_gate", "w_up", "w_down"]:
            shp = list(A[wn].shape)
            self.WB[wn] = self.nc.dram_tensor(wn + "_bf", shp, BF16, kind="Internal").ap()
        with ExitStack() as st:
            self.st = st
            self.P = Prog(nc, st)
            self.alloc()
            self.setup()
            import os
            plan = os.environ.get("KPLAN", "SMP")
            if "S" in plan:
                self.chunk("S", None, 0)
            if "M" in plan:
                self.chunk("M", None, 0)
            if "P" in plan:
                self.run_prompt(int(os.environ.get("KNCH", NCHUNK)))
            self.P.finish()
            self.P.emit()
        return nc

    def alloc(self):
        sb = self.sb
        nc = self.nc
        self.tri = sb("tri", [128, 2, 128], BF16)
        self.mb = sb("mb", [128, 2, 128], BF16)
        self.mbprev = sb("mbprev", [128, 2, 128], BF16)
        self.mbstate = sb("mbstate", [128, 128], BF16)
        self.ident_b = sb("ident_b", [128, 128], BF16)
        self.ident_f = sb("ident_f", [128, 128], F32)
        self.ones_b = sb("ones_b", [128, 128], BF16)
        self.sel = sb("sel", [128, NSQ, 128], BF16)
        self.mbq = sb("mbq", [128, NSQ], F32)
        self.rowm = sb("rowm", [128, NSQ, 128], BF16)
        self.tokmask = sb("tokmask", [128, 2], F32)
        self.zero1 = sb("zero1", [128, 1], F32)
        self.gcols = sb("gcols", [128, 6, 8], F32)
        self.gate_g = sb("gate_g", [128, 16], F32)
        self.conv_w = sb("conv_w", [128, 24, 4], F32)
        self.conv_b = sb("conv_b", [128, 24], F32)
        self.hp = sb("hp", [128, 4, 32], F32)
        self.negA = sb("negA", [128, 32], F32)
        self.esink = sb("esink", [128, 16], F32)
        self.sinks = sb("sinks", [128, 16], F32)
        self.ropes = [sb(f"rope{i}", [128, 16], F32) for i in range(2)]
        self.rope_i = 0
        self.cacc = [sb(f"cacc{i}", [128, 128], F32) for i in range(4)]
        self.cth = [sb(f"cth{i}", [128, 128], F32) for i in range(4)]
        self.hTs = [sb(f"hT{i}", [128, 8, 128], F32) for i in range(2)]
        self.hT = self.hTs[0]
        self.uT = sb("uT", [128, 8, 128], BF16)
        self.rstd = sb("rstd", [128, 128], F32)
        self.xin = sb("xin", [128, 1024], F32)
        self.z_tm = sb("z_tm", [128, 2048], BF16)
        self.xpre = sb("xpre", [128, 24, 176], BF16)
        self.uni = sb("uni", [128, 1408], F32)
        self.big1 = sb("big1", [128, 3072], F32)
        self.dtraw = sb("dtraw", [128, 32], F32)
        self.xbcT = sb("xbcT", [128, 24, 128], BF16)
        self.x_tm = sb("x_tm", [128, 2048], BF16)
        self.xdt = sb("xdt", [128, 2048], BF16)
        self.xw = sb("xw", [128, 2048], BF16)
        self.B_tm = sb("B_tm", [128, 512], BF16)
        self.dt = sb("dt", [128, 32], F32)
        self.a32 = sb("a32", [128, 32], F32)
        self.ahi = sb("ahi", [128, 32], BF16)
        self.alo = sb("alo", [128, 32], BF16)
        self.negcs = sb("negcs", [128, 32], F32)
        self.ecs = sb("ecs", [128, 32], F32)
        self.dA = sb("dA", [128, 32], F32)
        self.wdec = sb("wdec", [128, 32], F32)
        self.cbT = sb("cbT", [128, 4, 128], F32)
        self.dec = [sb(f"dec{i}", [128, 4, 128], BF16) for i in range(2)]
        self.MT = [sb(f"MT{i}", [128, 4, 128], BF16) for i in range(2)]
        self.tmpg = [sb(f"tmpg{i}", [128, 512], F32) for i in range(2)]
        self.CTq = sb("CTq", [128, 4, 128], BF16)
        self.ST = sb("ST", [128, 2048], F32)
        self.STb = sb("STb", [128, 2048], BF16)
        self.ST_meta = sb("ST_meta", [128, 2048], F32)
        self.big2 = sb("big2", [128, 2, 16, 128], F32)
        self.big2_d0 = Dep("big2_in")
        self.big2_d1 = Dep("big2_out")
        self.sz = sb("sz", [128, 512], F32)
        self.ssq = sb("ssq", [128, 4], F32)
        self.grs = sb("grs", [128, 4], F32)
        self.ynT = sb("ynT", [128, 16, 128], BF16)
        self.fth = [sb(f"fth{i}", [128, 256], F32) for i in range(2)]
        self.actT = sb("actT", [128, 22, 128], BF16)
        self.k_tm = sb("k_tm", [128, 256], F32)
        self.k_rot = sb("k_rot", [128, 256], F32)
        self.k_b = sb("k_b", [128, 256], BF16)
        self.v_tm = sb("v_tm", [128, 256], F32)
        self.v_b = [sb(f"v_b{i}", [128, 256], BF16) for i in range(2)]
        self.kT = [sb(f"kT{i}", [64, 4, 128], BF16) for i in range(2)]
        self.kT_meta = sb("kT_meta", [64, 4, 128], BF16)
        self.v_meta = sb("v_meta", [128, 256], BF16)
        self.cvtail_meta = sb("cvtail_meta", [128, 24, 3], BF16)
        self.q_tm = sb("q_tm", [128, 1024], F32)
        self.q_rot = sb("q_rot", [128, 1024], F32)
        self.q_b = sb("q_b", [128, 1024], BF16)
        self.qT = sb("qT", [64, 16, 128], BF16)
        self.rtmp = sb("rtmp", [128, 4, 16, 8], F32)
        self.PT = [sb(f"PT{i}", [128, 256], BF16) for i in range(2)]
        self.oT = sb("oT", [64, 16, 128], BF16)
        self.den = sb("den", [64, 4, 128], F32)
        class _V:
            def __init__(s_, ap, name):
                s_.t, s_.d = ap, Dep(name)
        self.u2 = sb("u2", [128, 1088], F32)
        u2 = self.u2.t
        self.sk32 = _V(u2[:, 0:256], "sk32")
        self.sv32 = _V(u2[:, 256:512], "sv32")
        self.skb = _V(u2[:, 512:640].bitcast(BF16), "skb")
        self.svb = _V(u2[:, 640:768].bitcast(BF16), "svb")
        self.kTs = _V(u2[0:64, 768:1024].bitcast(BF16).rearrange("p (k t) -> p k t", t=128), "kTs")
        self.PTs = _V(u2[:, 1024:1088].bitcast(BF16), "PTs")
        self.y_st = self.xin
        import os
        self.NW = int(os.environ.get("KNW", "7"))
        self.wslot = [sb(f"wslot{i}", [128, 2048], BF16) for i in range(self.NW)]
        self.wsem = [self.P.dma_sem(f"w{i}") for i in range(self.NW)]
        self.wi = 0
        self.nw_cur = self.NW

        class _SV:
            def __init__(s_, ap, name, first):
                s_.t, s_.d, s_.first = ap, Dep(name), list(first)
        b2 = self.big2.t[:, :, :, :].rearrange("p a b c -> p (a b c)").bitcast(BF16)
        self.wslot_extra = [_SV(b2[:, i * 2048:(i + 1) * 2048], f"wx{i}", [self.big2_d0]) for i in range(2)]
        self.wslot_extra.append(_SV(self.sel.t[:, :, :].rearrange("p a b -> p (a b)"), "wx4", [self.sel.d]))
        self.wslot_extra.append(_SV(self.rowm.t[:, :, :].rearrange("p a b -> p (a b)"), "wx5", [self.rowm.d]))
        self.wslot_extra.append(_SV(self.u2.t[:, 0:1024].bitcast(BF16), "wx6", [self.sk32.d, self.sv32.d, self.skb.d, self.svb.d, self.kTs.d]))
        self.wsem_extra = [self.P.dma_sem(f"wx{i}") for i in range(len(self.wslot_extra))]
        self.psf = [Tl(self.st.enter_context(nc.psum_tensor(f"psf{i}", [128, 512], F32)), f"psf{i}") for i in range(6)]
        self.psb = [Tl(self.st.enter_context(nc.psum_tensor(f"psb{i}", [128, 1024], BF16)), f"psb{i}") for i in range(2)]
        self.pfi = 0
        self.pbi = 0
        for t in self.psf + self.psb:
            t.d.excl = True
        self.ds_in = self.P.dma_sem("in")
        self.ds_x = self.P.dma_sem("x")
        self.ds_stg = self.P.dma_sem("stg")
        self.ds_stgo = self.P.dma_sem("stgo")
        self.ds_kv = self.P.dma_sem("kv")
        self.ds_out = self.P.dma_sem("out")
        self.ds_y = self.P.dma_sem("y")
        self.ds_cp = self.P.dma_sem("cp")
        self.ds_rope = self.P.dma_sem("rope")
        self.pbuf = 0

    def stage(self, n):
        if n > self.kstage:
            raise StopIteration

    ps_pool = "all"

    def PF(self):
        if self.ps_pool == "all":
            t = self.psf[self.pfi % 6]
        elif self.ps_pool == "lo":
            t = self.psf[self.pfi % 3]
        else:
            t = self.psf[3 + self.pfi % 3]
        self.pfi += 1
        return t

    def run_prompt(self, nch):
        self.nw_cur = self.NW + len(self.wslot_extra)
        P = self.P
        seq = [(b, c) for b in range(NPB) for c in range(nch)]
        prevE = None
        for i, (b, c) in enumerate(seq):
            hs = i % 2
            self.chunk("P", b, c, ph="A", hsel=hs)
            if prevE is None:
                self.chunk("P", b, c, ph="B", hsel=hs)
            else:
                pb_, pc_, phs = prevE
                self.ps_pool = "lo"
                lE = P.capture(lambda: self.chunk("P", pb_, pc_, ph="E", hsel=phs))
                self.ps_pool = "hi"
                lB = P.capture(lambda: self.chunk("P", b, c, ph="B", hsel=hs))
                self.ps_pool = "all"
                P.replay_merged(lE, lB)
            self.chunk("P", b, c, ph="C", hsel=hs)
            self.chunk("P", b, c, ph="D", hsel=hs)
            prevE = (b, c, hs)
        pb_, pc_, phs = prevE
        self.chunk("P", pb_, pc_, ph="E", hsel=phs)

    def PB(self):
        t = self.psb[self.pbi % 2]
        self.pbi += 1
        return t

    def setup(self):
        P, A = self.P, self.A
        ld = lambda tl, src: P.dma("sp", self.ds_in, tl.t[:], src, writes=[tl.d])
        ld(self.tri, A["c_tri"].rearrange("a p n -> p a n"))
        ld(self.mb, A["c_mb"].rearrange("a p n -> p a n"))
        ld(self.mbprev, A["c_mbprev"].rearrange("a p n -> p a n"))
        ld(self.mbstate, A["c_mbstate"])
        ld(self.ident_b, A["c_ident_b"])
        ld(self.ident_f, A["c_ident_f"])
        ld(self.ones_b, A["c_ones_b"])
        ld(self.sel, A["c_sel"])
        ld(self.mbq, A["c_mbq"])
        ld(self.rowm, A["c_rowm"])
        ld(self.tokmask, A["c_tokmask"])
        ld(self.gcols, A["gcols"])
        ld(self.gate_g, A["gate_g"])
        ld(self.conv_w, A["conv_w"])
        ld(self.conv_b, A["conv_b"])
        ld(self.hp, A["hp"])
        ld(self.sinks, A["sinks"])
        P.op("pool", lambda h: h.memset(self.zero1.t[:], 0.0), writes=[self.zero1.d])
        self.wdep = {}
        for wn in ["w_in", "w_out", "w_gate", "w_up", "w_down", "w_k", "w_v", "w_q", "w_o"]:
            self.wdep[wn] = Dep("wb_" + wn)
            self.ds_cast = P.dma_sem("cast_" + wn)
            src, dst = A[wn], self.WB[wn]
            if len(src.shape) == 3:
                for l in range(2):
                    P.dma("pool", self.ds_cast, dst[l].rearrange("(p r) n -> p r n", p=128), src[l].rearrange("(p r) n -> p r n", p=128), writes=[self.wdep[wn]])
            else:
                P.dma("pool", self.ds_cast, dst.rearrange("(p r) n -> p r n", p=128), src.rearrange("(p r) n -> p r n", p=128), writes=[self.wdep[wn]])
        P.op("act", lambda h: h.activation(self.negA.t[:], self.hp.t[:, 1, :], AF.Exp), reads=[self.hp.d], writes=[self.negA.d])
        P.op("dve", lambda h: h.tensor_scalar_mul(self.negA.t[:], self.negA.t[:], -1.0), reads=[self.negA.d], writes=[self.negA.d])
        P.op("act", lambda h: h.activation(self.esink.t[:], self.sinks.t[:], AF.Exp), reads=[self.sinks.d], writes=[self.esink.d])
        P.op("dve", lambda h: h.tensor_scalar_mul(self.conv_w.t[:], self.conv_w.t[:], 0.5), reads=[self.conv_w.d], writes=[self.conv_w.d])
        P.op("dve", lambda h: h.tensor_scalar_mul(self.conv_b.t[:], self.conv_b.t[:], 0.5), reads=[self.conv_b.d], writes=[self.conv_b.d])

    def wload(self, srcspec, r0, r1, ncols, part=128):
        P = self.P
        i = self.wi % self.nw_cur
        self.wi += 1
        if i < self.NW:
            sl, wsem, extra = self.wslot[i], self.wsem[i], []
        else:
            sl, wsem = self.wslot_extra[i - self.NW], self.wsem_extra[i - self.NW]
            extra, sl.first = sl.first, []
        kc = (r1 - r0) // part
        assert kc * ncols <= 2048
        view = sl.t[0:part, 0:kc * ncols].rearrange("p (c n) -> p c n", n=ncols)
        wn, sel = srcspec
        src = sel(self.WB[wn])[r0:r1, :]
        P.dma("sp", wsem, view, src.rearrange("(c p) n -> p c n", p=part), reads=[self.wdep[wn]], writes=[sl.d] + extra)
        return view, sl

    def rms(self, gi, out_tl, view3=False):
        P = self.P
        hT, sq, rstd = self.hT, self.actT, self.rstd
        P.op("act", lambda h: h.activation(sq.t[:, 0:8, :], hT.t[:], AF.Square), reads=[hT.d], writes=[sq.d])
        ps = self.PF()
        for c in range(8):
            P.op("pe", lambda h, c=c: h.matmul(ps.t[:, 0:128], lhsT=self.ones_b.t[:], rhs=sq.t[:, c, :], start=(c == 0), stop=(c == 7)),
                 reads=[sq.d, self.ones_b.d], writes=[ps.d])
        P.op("act", lambda h: h.activation(rstd.t[:], ps.t[:, 0:128], AF.Ln, bias=EPS, scale=1.0 / 1024.0), reads=[ps.d], writes=[rstd.d])
        P.op("act", lambda h: h.activation(rstd.t[:], rstd.t[:], AF.Exp, scale=-0.5), reads=[rstd.d], writes=[rstd.d])
        for c in range(8):
            oc = out_tl.t[:, c * 128:(c + 1) * 128] if view3 else out_tl.t[:, c, :]
            P.op("dve", lambda h, c=c, oc=oc: h.scalar_tensor_tensor(oc, hT.t[:, c, :], self.gcols.t[:, gi, c:c + 1], rstd.t[:], ALU.mult, ALU.mult),
                 reads=[hT.d, rstd.d, self.gcols.d], writes=[out_tl.d])

    def dense_fm(self, src, K, ncols_total, xT_tl, evac, blk=None, part=128):
        P = self.P
        kc = K // part
        nb0 = 256
        ksubs = [(k0, min(k0 + 8, kc)) for k0 in range(0, kc, 8)]
        for c0 in range(0, ncols_total, nb0):
            nb = min(nb0, ncols_total - c0)
            loaded = []
            for (k0, k1) in ksubs:
                wv, wsl = self.wload((src[0], lambda w, c0=c0, nb=nb, f=src[1]: f(w)[:, c0:c0 + nb]), k0 * part, k1 * part, nb, part=part)
                loaded.append((k0, k1, wv, wsl))
            for m in range(nb // 128):
                ps = self.PF()
                for (k0, k1, wv, wsl) in loaded:
                    for k in range(k0, k1):
                        rhs = xT_tl.t[0:part, k, :]
                        P.op("pe", lambda h, k=k, k0=k0, m=m, rhs=rhs, wv=wv, ps=ps: h.matmul(ps.t[:, 0:128], lhsT=wv[:, k - k0, m * 128:(m + 1) * 128], rhs=rhs, start=(k == 0), stop=(k == kc - 1)),
                             reads=[wsl.d, xT_tl.d], writes=[ps.d])
                evac((c0 // 128) + m, ps)

    def dense_tm(self, src, ncols, xT_tl, evac):
        P = self.P
        ps = self.PF()
        for c0 in range(0, ncols, 256):
            nb = min(256, ncols - c0)
            wv, wsl = self.wload((src[0], lambda w, c0=c0, nb=nb, f=src[1]: f(w)[:, c0:c0 + nb]), 0, 1024, nb)
            for k in range(8):
                P.op("pe", lambda h, k=k, c0=c0, nb=nb, wv=wv: h.matmul(ps.t[:, c0:c0 + nb], lhsT=xT_tl.t[:, k, :], rhs=wv[:, k, :], start=(k == 0), stop=(k == 7)),
                     reads=[wsl.d, xT_tl.d], writes=[ps.d])
        evac(ps)

    def resid_add(self, m, ps):
        hT = self.hT
        self.P.op("dve", lambda h: h.tensor_tensor(hT.t[:, m, :], hT.t[:, m, :], ps.t[:, 0:128], ALU.add), reads=[ps.d, hT.d], writes=[hT.d])

    def ffn(self, layer):
        P, A = self.P, self.A
        self.rms(1 if layer == 0 else 4, self.uT)
        wd = ("w_down", lambda w: w[layer])
        act_tm = self.uni.t[:].bitcast(BF16)
        uT = self.uT
        for bi, c0 in enumerate(range(0, DFF, 256)):
            nb = min(256, DFF - c0)
            gv, gsl = self.wload(("w_gate", lambda w, c0=c0, nb=nb: w[layer][:, c0:c0 + nb]), 0, 1024, nb)
            uv, usl = self.wload(("w_up", lambda w, c0=c0, nb=nb: w[layer][:, c0:c0 + nb]), 0, 1024, nb)
            psg = self.PF()
            psu = self.PF()
            for k in range(8):
                P.op("pe", lambda h, k=k, gv=gv, psg=psg, nb=nb: h.matmul(psg.t[:, 0:nb], lhsT=uT.t[:, k, :], rhs=gv[:, k, :], start=(k == 0), stop=(k == 7)),
                     reads=[gsl.d, uT.d], writes=[psg.d])
            for k in range(8):
                P.op("pe", lambda h, k=k, uv=uv, psu=psu, nb=nb: h.matmul(psu.t[:, 0:nb], lhsT=uT.t[:, k, :], rhs=uv[:, k, :], start=(k == 0), stop=(k == 7)),
                     reads=[usl.d, uT.d], writes=[psu.d])
            th = self.fth[bi % 2]
            P.op("act", lambda h, th=th, psg=psg, nb=nb: h.activation(th.t[:, 0:nb], psg.t[:, 0:nb], AF.Tanh, scale=0.5), reads=[psg.d], writes=[th.d])
            P.op("dve", lambda h, th=th, psg=psg, nb=nb: h.scalar_tensor_tensor(th.t[:, 0:nb], th.t[:, 0:nb], 1.0, psg.t[:, 0:nb], ALU.add, ALU.mult), reads=[th.d, psg.d], writes=[th.d])
            P.op("dve", lambda h, th=th, psu=psu, nb=nb, c0=c0: h.scalar_tensor_tensor(act_tm[:, c0:c0 + nb], th.t[:, 0:nb], 0.5, psu.t[:, 0:nb], ALU.mult, ALU.mult),
                 reads=[th.d, psu.d], writes=[self.uni.d])
        for gi_, (m0, n) in enumerate([(0, 8), (8, 8), (16, 6)]):
            ps = self.PF()
            pbv = ps.t[:, :].bitcast(BF16)
            for j in range(n):
                m = m0 + j
                P.op("pe", lambda h, j=j, m=m, pbv=pbv: h.transpose(pbv[:, j * 128:(j + 1) * 128], act_tm[:, m * 128:(m + 1) * 128], self.ident_b.t[:]),
                     reads=[self.uni.d, self.ident_b.d], writes=[ps.d])
            src = pbv[:, 0:n * 128].rearrange("p (m t) -> p m t", t=128)
            if gi_ == 1:
                P.op("act", lambda h, m0=m0, n=n, src=src: h.copy(self.actT.t[:, m0:m0 + n, :], src), reads=[ps.d], writes=[self.actT.d])
            else:
                P.op("dve", lambda h, m0=m0, n=n, src=src: h.tensor_copy(self.actT.t[:, m0:m0 + n, :], src), reads=[ps.d], writes=[self.actT.d])
        self.dense_fm(wd, DFF, 1024, self.actT, self.resid_add)

    def rope_apply(self, src, dst, nh):
        P = self.P
        s3 = src.t[:, :].rearrange("p (h d) -> p h d", d=64)
        d3 = dst.t[:, :].rearrange("p (h d) -> p h d", d=64)
        cos = self.rope.t[:, 0:8].unsqueeze(1).to_broadcast([128, nh, 8])
        sin = self.rope.t[:, 8:16].unsqueeze(1).to_broadcast([128, nh, 8])
        x1, x2 = s3[:, :, 0:8], s3[:, :, 8:16]
        t = [self.rtmp.t[:, i, 0:nh, :] for i in range(4)]
        rd = [src.d, self.rope.d]
        P.op("dve", lambda h: h.tensor_tensor(t[0], x1, cos, ALU.mult), reads=rd, writes=[self.rtmp.d])
        P.op("dve", lambda h: h.tensor_tensor(t[1], x2, sin, ALU.mult), reads=rd, writes=[self.rtmp.d])
        P.op("dve", lambda h: h.tensor_tensor(t[2], x2, cos, ALU.mult), reads=rd, writes=[self.rtmp.d])
        P.op("dve", lambda h: h.tensor_tensor(t[3], x1, sin, ALU.mult), reads=rd, writes=[self.rtmp.d])
        P.op("dve", lambda h: h.tensor_tensor(d3[:, :, 0:8], t[0], t[1], ALU.subtract), reads=[self.rtmp.d], writes=[dst.d])
        P.op("dve", lambda h: h.tensor_tensor(d3[:, :, 8:16], t[2], t[3], ALU.add), reads=[self.rtmp.d], writes=[dst.d])

    def chunk(self, ty, b, c, ph="ABCDE", hsel=0):
        P, A = self.P, self.A
        self.hT = self.hTs[hsel]
        ti = 1 if ty == "S" else 0
        nseq = NSQ if ty == "S" else 1
        first = (ty == "P" and c == 0)
        last = (ty == "P" and c == NCHUNK - 1)
        if "A" in ph:
            xin = self.xin
            if ty == "S":
                P.dma("sp", self.ds_x, xin.t[:], A["xs"], writes=[xin.d])
            elif ty == "M":
                P.op("pool", lambda h: h.memset(xin.t[:], 0.0), writes=[xin.d])
                P.dma("sp", self.ds_x, xin.t[112:128, :], A["meta"], writes=[xin.d])
            else:
                P.dma("sp", self.ds_x, xin.t[:], A["xp"][b, c * 128:(c + 1) * 128, :], writes=[xin.d])
            rty = 17 if ty == "S" else (16 if ty == "M" else c)
            self.rope = self.ropes[self.rope_i % 2]
            self.rope_i += 1
            P.dma("sp", self.ds_rope, self.rope.t[:], A["c_rope"][rty], writes=[self.rope.d])
            for half in range(2):
                ps = self.PF()
                for m in range(4):
                    mm = half * 4 + m
                    P.op("pe", lambda h, m=m, mm=mm, ps=ps: h.transpose(ps.t[:, m * 128:(m + 1) * 128], xin.t[:, mm * 128:(mm + 1) * 128], self.ident_f.t[:]),
                         reads=[xin.d, self.ident_f.d], writes=[ps.d])
                hTc = self.hT
                P.op("act", lambda h, half=half, ps=ps, hTc=hTc: h.copy(hTc.t[:, half * 4:(half + 1) * 4, :], ps.t[:, :].rearrange("p (m t) -> p m t", t=128)),
                     reads=[ps.d], writes=[hTc.d])

            if ty == "M":
                P.op("pool", lambda h: h.memset(self.xpre.t[:, :, 0:3], 0.0), writes=[self.xpre.d])
            if first:
                P.op("pool", lambda h: h.tensor_copy(self.xpre.t[:, :, 0:3], self.cvtail_meta.t[:]), reads=[self.cvtail_meta.d], writes=[self.xpre.d])
            elif ty == "P":
                P.op("pool", lambda h: h.tensor_copy(self.xpre.t[:, :, 0:3], self.xpre.t[:, :, 128:131]), reads=[self.xpre.d], writes=[self.xpre.d])
            if ty == "S":
                P.dma("sp", self.ds_stg, self.big1.t[0:48, :], A["st_conv"], writes=[self.big1.d])
                for m in range(24):
                    ps = self.PF()
                    P.op("pe", lambda h, m=m, ps=ps: h.transpose(ps.t[:, 0:48], self.big1.t[0:48, m * 128:(m + 1) * 128], self.ident_f.t[0:48, 0:48]),
                         reads=[self.big1.d, self.ident_f.d], writes=[ps.d])
                    P.op("dve", lambda h, m=m, ps=ps: h.tensor_copy(self.xpre.t[:, m, :].rearrange("p (q t) -> p q t", t=11)[:, :, 0:3],
                                                                      ps.t[:, 0:48].rearrange("p (q r) -> p q r", r=3)),
                         reads=[ps.d], writes=[self.xpre.d])

            self.rms(0, self.uT)
            w_in = A["w_in"]
            for blk in range(4):
                def ev(ps, blk=blk):
                    P.op("act", lambda h: h.copy(self.z_tm.t[:, blk * 512:(blk + 1) * 512], ps.t[:, :]), reads=[ps.d], writes=[self.z_tm.d])
                self.dense_tm(("w_in", lambda w, blk=blk: w[:, blk * 512:(blk + 1) * 512]), 512, self.uT, ev)
            if ty == "S":
                def ev_x(m, ps):
                    P.op("dve", lambda h: h.tensor_copy(self.xpre.t[:, m, :].rearrange("p (q t) -> p q t", t=11)[:, :, 3:11],
                                                 ps.t[:, 0:128].rearrange("p (q t) -> p q t", t=8)), reads=[ps.d], writes=[self.xpre.d])
                    P.op("dve", lambda h: h.tensor_copy(self.uni.t[:, 0:1152].rearrange("p (m r) -> p m r", r=48)[:, m, :].rearrange("p (q r) -> p q r", r=3),
                                                        ps.t[:, 0:128].rearrange("p (q t) -> p q t", t=8)[:, :, 5:8]), reads=[ps.d], writes=[self.uni.d])
            else:
                def ev_x(m, ps):
                    P.op("dve", lambda h: h.tensor_copy(self.xpre.t[:, m, 3:131], ps.t[:, 0:128]), reads=[ps.d], writes=[self.xpre.d])
                    if last:
                        P.op("dve", lambda h: h.tensor_copy(self.uni.t[:, 0:1152].rearrange("p (m r) -> p m r", r=48)[:, m, 0:3], ps.t[:, 125:128]), reads=[ps.d], writes=[self.uni.d])
            self.dense_fm(("w_in", lambda w: w[:, 2048:5120]), 1024, 3072, self.uT, ev_x)
            def ev_dt(ps):
                P.op("dve", lambda h: h.tensor_copy(self.dtraw.t[:], ps.t[:, 0:32]), reads=[ps.d], writes=[self.dtraw.d])
            self.dense_tm(("w_in", lambda w: w[:, 5120:5152]), 32, self.uT, ev_dt)
            if ty == "M":
                P.op("pool", lambda h: h.tensor_copy(self.cvtail_meta.t[:], self.xpre.t[:, :, 128:131]), reads=[self.xpre.d], writes=[self.cvtail_meta.d])
            if ty == "S" or last:
                nr = 48 if ty == "S" else 3
                for m in range(24):
                    ps = self.PF()
                    P.op("pe", lambda h, m=m, ps=ps: h.transpose(ps.t[0:nr, 0:128], self.uni.t[:, 0:1152].rearrange("p (m r) -> p m r", r=48)[:, m, 0:nr], self.ident_f.t[:]),
                         reads=[self.uni.d, self.ident_f.d], writes=[ps.d])
                    P.op("dve", lambda h, m=m, ps=ps: h.tensor_copy(self.big1.t[0:nr, m * 128:(m + 1) * 128], ps.t[0:nr, 0:128]), reads=[ps.d], writes=[self.big1.d])
                dst = A["conv_s"] if ty == "S" else A["conv_p"][b]
                P.dma("pool", self.ds_out, dst, self.big1.t[0:nr, :], reads=[self.big1.d])

        if "B" in ph:
            if ty == "M":
                P.op("pool", lambda h: h.memset(self.ST.t[:], 0.0), writes=[self.ST.d])
                P.op("pool", lambda h: h.memset(self.STb.t[:], 0.0), writes=[self.STb.d])
            if first:
                P.op("dve", lambda h: h.tensor_copy(self.ST.t[:], self.ST_meta.t[:]), reads=[self.ST_meta.d], writes=[self.ST.d])
                P.op("act", lambda h: h.copy(self.STb.t[:], self.ST_meta.t[:]), reads=[self.ST_meta.d], writes=[self.STb.d])
            for mg in range(6):
                for k in range(4):
                    for mi in range(4):
                        m = mg * 4 + mi
                        acc = self.cacc[mi]
                        if ty == "S":
                            src = self.xpre.t[:, m, :].rearrange("p (q t) -> p q t", t=11)[:, :, k:k + 8]
                            out = acc.t[:, :].rearrange("p (q t) -> p q t", t=8)
                        else:
                            src = self.xpre.t[:, m, k:k + 128]
                            out = acc.t[:, :]
                        wk = self.conv_w.t[:, m, k:k + 1]
                        if k == 0:
                            bk = self.conv_b.t[:, m:m + 1]
                            P.op("dve", lambda h, src=src, out=out, wk=wk, bk=bk: h.tensor_scalar(out, src, wk, bk, ALU.mult, ALU.add), reads=[self.xpre.d, self.conv_w.d, self.conv_b.d], writes=[acc.d])
                        else:
                            P.op("dve", lambda h, src=src, out=out, wk=wk: h.scalar_tensor_tensor(out, src, wk, out, ALU.mult, ALU.add), reads=[self.xpre.d, self.conv_w.d, acc.d], writes=[acc.d])
                for mi in range(4):
                    m = mg * 4 + mi
                    acc = self.cacc[mi]
                    th = self.cth[mi]
                    P.op("act", lambda h, acc=acc, th=th: h.activation(th.t[:, :], acc.t[:, :], AF.Tanh), reads=[acc.d], writes=[th.d])
                    P.op("dve", lambda h, m=m, acc=acc, th=th: h.scalar_tensor_tensor(self.xbcT.t[:, m, :], th.t[:, :], 1.0, acc.t[:, :], ALU.add, ALU.mult),
                         reads=[acc.d, th.d], writes=[self.xbcT.d])
            for half in range(2):
                pb = self.PB()
                for m in range(8):
                    mm = half * 8 + m
                    P.op("pe", lambda h, m=m, mm=mm, pb=pb: h.transpose(pb.t[:, m * 128:(m + 1) * 128], self.xbcT.t[:, mm, :], self.ident_b.t[:]),
                         reads=[self.xbcT.d, self.ident_b.d], writes=[pb.d])
                P.op("act", lambda h, half=half, pb=pb: h.copy(self.x_tm.t[:, half * 1024:(half + 1) * 1024], pb.t[:, :]), reads=[pb.d], writes=[self.x_tm.d])
            pb = self.PB()
            for m in range(4):
                P.op("pe", lambda h, m=m, pb=pb: h.transpose(pb.t[:, m * 128:(m + 1) * 128], self.xbcT.t[:, 16 + m, :], self.ident_b.t[:]),
                     reads=[self.xbcT.d, self.ident_b.d], writes=[pb.d])
            P.op("dve", lambda h, pb=pb: h.tensor_copy(self.B_tm.t[:], pb.t[:, 0:512]), reads=[pb.d], writes=[self.B_tm.d])
            dt, a32 = self.dt, self.a32
            P.op("dve", lambda h: h.tensor_tensor(dt.t[:], self.dtraw.t[:], self.hp.t[:, 0, :], ALU.add), reads=[self.dtraw.d, self.hp.d], writes=[dt.d])
            P.op("act", lambda h: h.activation(dt.t[:], dt.t[:], AF.Exp), reads=[dt.d], writes=[dt.d])
            P.op("act", lambda h: h.activation(dt.t[:], dt.t[:], AF.Ln, bias=1.0), reads=[dt.d], writes=[dt.d])
            tmcol = self.tokmask.t[:, 1:2] if ty == "M" else self.tokmask.t[:, 0:1]
            P.op("dve", lambda h: h.tensor_scalar_mul(dt.t[:], dt.t[:], tmcol), reads=[dt.d, self.tokmask.d], writes=[dt.d])
            P.op("dve", lambda h: h.tensor_tensor(a32.t[:], dt.t[:], self.negA.t[:], ALU.mult), reads=[dt.d, self.negA.d], writes=[a32.d])
            P.op("dve", lambda h: h.tensor_copy(self.ahi.t[:], a32.t[:]), reads=[a32.d], writes=[self.ahi.d])
            P.op("dve", lambda h: h.tensor_tensor(self.alo.t[:], a32.t[:], self.ahi.t[:], ALU.subtract), reads=[a32.d, self.ahi.d], writes=[self.alo.d])
            ps = self.PF()
            P.op("pe", lambda h, ps=ps: h.matmul(ps.t[:, 0:32], lhsT=self.tri.t[:, ti, :], rhs=self.ahi.t[:], start=True, stop=False), reads=[self.tri.d, self.ahi.d], writes=[ps.d])
            P.op("pe", lambda h, ps=ps: h.matmul(ps.t[:, 0:32], lhsT=self.tri.t[:, ti, :], rhs=self.alo.t[:], start=False, stop=True), reads=[self.tri.d, self.alo.d], writes=[ps.d])
            P.op("dve", lambda h, ps=ps: h.tensor_scalar_mul(self.negcs.t[:], ps.t[:, 0:32], -1.0), reads=[ps.d], writes=[self.negcs.d])
            P.op("act", lambda h, ps=ps: h.activation(self.ecs.t[:], ps.t[:, 0:32], AF.Exp), reads=[ps.d], writes=[self.ecs.d])
            x3 = self.x_tm.t[:, :].rearrange("p (h d) -> p h d", d=64)
            P.op("dve", lambda h: h.tensor_tensor(self.xdt.t[:, :].rearrange("p (h d) -> p h d", d=64), x3, dt.t[:].unsqueeze(2).to_broadcast([128, 32, 64]), ALU.mult),
                 reads=[self.x_tm.d, dt.d], writes=[self.xdt.d])
            ps = self.PF()
            for g in range(4):
                P.op("pe", lambda h, g=g, ps=ps: h.matmul(ps.t[:, g * 128:(g + 1) * 128], lhsT=self.xbcT.t[:, 16 + g, :], rhs=self.xbcT.t[:, 20 + g, :], start=True, stop=True),
                     reads=[self.xbcT.d], writes=[ps.d])
            P.op("act", lambda h, ps=ps: h.copy(self.cbT.t[:, :, :], ps.t[:, :].rearrange("p (g l) -> p g l", l=128)), reads=[ps.d], writes=[self.cbT.d])
            psYs = {}

            def emit_decay(hb):
                g = hb // 2
                dec, MT = self.dec[hb % 2], self.MT[hb % 2]
                ps = self.PF()
                for hh in range(4):
                    hd = hb * 4 + hh
                    o = ps.t[:, hh * 128:(hh + 1) * 128]
                    P.op("pe", lambda h, o=o, hd=hd: h.matmul(o, lhsT=self.ahi.t[:, hd:hd + 1].to_broadcast([128, 128]), rhs=self.tri.t[:, ti, :], start=True, stop=False),
                         reads=[self.ahi.d, self.tri.d], writes=[ps.d])
                    P.op("pe", lambda h, o=o, hd=hd: h.matmul(o, lhsT=self.alo.t[:, hd:hd + 1].to_broadcast([128, 128]), rhs=self.tri.t[:, ti, :], start=False, stop=False),
                         reads=[self.alo.d, self.tri.d], writes=[ps.d])
                    P.op("pe", lambda h, o=o: h.matmul(o, lhsT=self.ident_b.t[:], rhs=self.mb.t[:, ti, :], start=False, stop=True),
                         reads=[self.ident_b.d, self.mb.d], writes=[ps.d])
                for hh in range(4):
                    hd = hb * 4 + hh
                    P.op("act", lambda h, hh=hh, hd=hd, ps=ps, dec=dec: h.activation(dec.t[:, hh, :], ps.t[:, hh * 128:(hh + 1) * 128], AF.Exp, bias=self.negcs.t[:, hd:hd + 1]),
                         reads=[ps.d, self.negcs.d], writes=[dec.d])
                P.op("dve", lambda h, g=g, dec=dec, MT=MT: h.tensor_tensor(MT.t[:, :, :], dec.t[:, :, :], self.cbT.t[:, g, :].unsqueeze(1).to_broadcast([128, 4, 128]), ALU.mult),
                     reads=[dec.d, self.cbT.d], writes=[MT.d])

            def emit_ydiag(hb):
                g = hb // 2
                MT = self.MT[hb % 2]
                if hb % 2 == 0:
                    psYs[g] = self.PF()
                psY = psYs[g]
                for hh in range(4):
                    hd = hb * 4 + hh
                    col = (hd % 8) * 64
                    P.op("pe", lambda h, hh=hh, hd=hd, col=col, MT=MT, psY=psY: h.matmul(psY.t[:, col:col + 64], lhsT=MT.t[:, hh, :], rhs=self.xdt.t[:, hd * 64:(hd + 1) * 64], start=True, stop=True),
                         reads=[MT.d, self.xdt.d], writes=[psY.d])
                if hb % 2 == 1:
                    tg = self.tmpg[g % 2]
                    P.op("dve", lambda h, g=g, tg=tg: h.tensor_tensor(tg.t[:, :].rearrange("p (h d) -> p h d", d=64), self.x_tm.t[:, g * 512:(g + 1) * 512].rearrange("p (h d) -> p h d", d=64),
                                                                     self.hp.t[:, 2, g * 8:(g + 1) * 8].unsqueeze(2).to_broadcast([128, 8, 64]), ALU.mult),
                         reads=[self.x_tm.d, self.hp.d], writes=[tg.d])
                    P.op("dve", lambda h, g=g, tg=tg, psY=psY: h.tensor_tensor(self.big1.t[:, g * 512:(g + 1) * 512], psY.t[:, :], tg.t[:], ALU.add),
                         reads=[psY.d, tg.d], writes=[self.big1.d])

            emit_decay(0)
            for hb in range(8):
                if hb + 1 < 8:
                    emit_decay(hb + 1)
                emit_ydiag(hb)
            for q in range(nseq):
                if ty == "S":
                    selq = self.sel.t[:, q, :]
                    mbcol = self.mbq.t[:, q:q + 1]
                else:
                    selq = self.ones_b.t[:]
                    mbcol = self.zero1.t[:, 0:1]
                ps = self.PF()
                P.op("pe", lambda h, ps=ps, selq=selq: h.matmul(ps.t[:, 0:32], lhsT=selq, rhs=self.ahi.t[:], start=True, stop=False), reads=[self.sel.d, self.ones_b.d, self.ahi.d], writes=[ps.d])
                P.op("pe", lambda h, ps=ps, selq=selq: h.matmul(ps.t[:, 0:32], lhsT=selq, rhs=self.alo.t[:], start=False, stop=True), reads=[self.sel.d, self.ones_b.d, self.alo.d], writes=[ps.d])
                P.op("act", lambda h, ps=ps: h.activation(self.dA.t[:], ps.t[:, 0:32], AF.Exp), reads=[ps.d], writes=[self.dA.d])
                P.op("dve", lambda h, ps=ps: h.tensor_tensor(self.wdec.t[:], ps.t[:, 0:32], self.negcs.t[:], ALU.add), reads=[ps.d, self.negcs.d], writes=[self.wdec.d])
                P.op("act", lambda h, mbcol=mbcol: h.activation(self.wdec.t[:], self.wdec.t[:], AF.Exp, bias=mbcol), reads=[self.wdec.d, self.mbq.d, self.zero1.d], writes=[self.wdec.d])
                P.op("dve", lambda h: h.tensor_tensor(self.xw.t[:, :].rearrange("p (h d) -> p h d", d=64), self.xdt.t[:, :].rearrange("p (h d) -> p h d", d=64),
                                                      self.wdec.t[:].unsqueeze(2).to_broadcast([128, 32, 64]), ALU.mult),
                     reads=[self.xdt.d, self.wdec.d], writes=[self.xw.d])
                if ty == "S":
                    P.op("dve", lambda h, q=q: h.tensor_tensor(self.CTq.t[:, :, :], self.xbcT.t[:, 20:24, :], self.rowm.t[:, q, :].unsqueeze(1).to_broadcast([128, 4, 128]), ALU.mult),
                         reads=[self.xbcT.d, self.rowm.d], writes=[self.CTq.d])
                    CT = lambda g: self.CTq.t[:, g, :]
                    ctd = self.CTq.d
                else:
                    CT = lambda g: self.xbcT.t[:, 20 + g, :]
                    ctd = self.xbcT.d
                if ty == "S":
                    for j in range(16):
                        P.dma("sp", self.ds_stg, self.big2.t[:, 0, j, :], A["st_ssm"][q, 2 * j:2 * j + 2].rearrange("h2 p n -> (h2 p) n"), writes=[self.big2_d0])
                    for jb in range(4):
                        ps = self.PF()
                        for jj in range(4):
                            j = jb * 4 + jj
                            P.op("pe", lambda h, j=j, jj=jj, ps=ps: h.transpose(ps.t[:, jj * 128:(jj + 1) * 128], self.big2.t[:, 0, j, :], self.ident_f.t[:]),
                                 reads=[self.big2_d0, self.ident_f.d], writes=[ps.d])
                        P.op("dve", lambda h, jb=jb, ps=ps: h.tensor_copy(self.ST.t[:, jb * 512:(jb + 1) * 512], ps.t[:, :]), reads=[ps.d], writes=[self.ST.d])
                        P.op("act", lambda h, jb=jb, ps=ps: h.copy(self.STb.t[:, jb * 512:(jb + 1) * 512], ps.t[:, :]), reads=[ps.d], writes=[self.STb.d])
                for g in range(4):
                    ps = self.PF()
                    P.op("pe", lambda h, g=g, ps=ps, CT=CT: h.matmul(ps.t[:, :], lhsT=CT(g), rhs=self.STb.t[:, g * 512:(g + 1) * 512], start=True, stop=True),
                         reads=[ctd, self.STb.d], writes=[ps.d])
                    tg = self.tmpg[g % 2]
                    P.op("dve", lambda h, g=g, ps=ps, tg=tg: h.tensor_tensor(tg.t[:, :].rearrange("p (h d) -> p h d", d=64), ps.t[:, :].rearrange("p (h d) -> p h d", d=64),
                                                                           self.ecs.t[:, g * 8:(g + 1) * 8].unsqueeze(2).to_broadcast([128, 8, 64]), ALU.mult),
                         reads=[ps.d, self.ecs.d], writes=[tg.d])
                    P.op("dve", lambda h, g=g, tg=tg: h.tensor_tensor(self.big1.t[:, g * 512:(g + 1) * 512], self.big1.t[:, g * 512:(g + 1) * 512], tg.t[:], ALU.add),
                         reads=[tg.d, self.big1.d], writes=[self.big1.d])
                for g in range(4):
                    ps = self.PF()
                    P.op("pe", lambda h, g=g, ps=ps: h.matmul(ps.t[:, :], lhsT=self.B_tm.t[:, g * 128:(g + 1) * 128], rhs=self.xw.t[:, g * 512:(g + 1) * 512], start=True, stop=True),
                         reads=[self.B_tm.d, self.xw.d], writes=[ps.d])
                    sg3 = self.ST.t[:, g * 512:(g + 1) * 512].rearrange("p (h d) -> p h d", d=64)
                    P.op("dve", lambda h, g=g, sg3=sg3: h.tensor_tensor(sg3, sg3, self.dA.t[:, g * 8:(g + 1) * 8].unsqueeze(2).to_broadcast([128, 8, 64]), ALU.mult),
                         reads=[self.ST.d, self.dA.d], writes=[self.ST.d])
                    P.op("dve", lambda h, g=g, ps=ps: h.tensor_tensor(self.ST.t[:, g * 512:(g + 1) * 512], self.ST.t[:, g * 512:(g + 1) * 512], ps.t[:, :], ALU.add),
                         reads=[self.ST.d, ps.d], writes=[self.ST.d])
                if ty != "S":
                    P.op("act", lambda h: h.copy(self.STb.t[:], self.ST.t[:]), reads=[self.ST.d], writes=[self.STb.d])
                if ty == "M":
                    P.op("pool", lambda h: h.tensor_copy(self.ST_meta.t[:], self.ST.t[:]), reads=[self.ST.d], writes=[self.ST_meta.d])
                if ty == "S" or last:
                    for jb in range(4):
                        ps = self.PF()
                        for jj in range(4):
                            j = jb * 4 + jj
                            P.op("pe", lambda h, j=j, jj=jj, ps=ps: h.transpose(ps.t[:, jj * 128:(jj + 1) * 128], self.ST.t[:, j * 128:(j + 1) * 128], self.ident_f.t[:]),
                                 reads=[self.ST.d, self.ident_f.d], writes=[ps.d])
                        P.op("act", lambda h, jb=jb, ps=ps: h.copy(self.big2.t[:, 1, jb * 4:(jb + 1) * 4, :], ps.t[:, :].rearrange("p (j n) -> p j n", n=128)), reads=[ps.d], writes=[self.big2_d1])
                    dst = A["ssm_s"][q] if ty == "S" else A["ssm_p"][b]
                    for j in range(16):
                        P.dma("pool", self.ds_stgo, dst[2 * j:2 * j + 2].rearrange("h2 p n -> (h2 p) n"), self.big2.t[:, 1, j, :], reads=[self.big2_d1])
            P.op("pool", lambda h: h.memset(self.ssq.t[:], 0.0), writes=[self.ssq.d])
            P.op("act", lambda h: h.activation(self.xw.t[:], self.z_tm.t[:], AF.Tanh, scale=0.5), reads=[self.z_tm.d], writes=[self.xw.d])
            P.op("dve", lambda h: h.scalar_tensor_tensor(self.z_tm.t[:], self.xw.t[:], 1.0, self.z_tm.t[:], ALU.add, ALU.mult), reads=[self.xw.d, self.z_tm.d], writes=[self.z_tm.d])
            P.op("dve", lambda h: h.scalar_tensor_tensor(self.big1.t[:, 0:2048], self.big1.t[:, 0:2048], 0.5, self.z_tm.t[:], ALU.mult, ALU.mult), reads=[self.big1.d, self.z_tm.d], writes=[self.big1.d])
            for g in range(4):
                P.op("act", lambda h, g=g: h.activation(self.sz.t[:], self.big1.t[:, g * 512:(g + 1) * 512], AF.Square, accum_out=self.ssq.t[:, g:g + 1]), reads=[self.big1.d], writes=[self.sz.d, self.ssq.d])
            P.op("act", lambda h: h.activation(self.grs.t[:], self.ssq.t[:], AF.Ln, bias=EPS, scale=1.0 / 512.0), reads=[self.ssq.d], writes=[self.grs.d])
            P.op("act", lambda h: h.activation(self.grs.t[:], self.grs.t[:], AF.Exp, scale=-0.5), reads=[self.grs.d], writes=[self.grs.d])
            for g in range(4):
                P.op("dve", lambda h, g=g: h.tensor_scalar_mul(self.x_tm.t[:, g * 512:(g + 1) * 512], self.big1.t[:, g * 512:(g + 1) * 512], self.grs.t[:, g:g + 1]), reads=[self.big1.d, self.grs.d], writes=[self.x_tm.d])
            for g in range(4):
                pb = self.PB()
                for m in range(4):
                    mm = g * 4 + m
                    P.op("pe", lambda h, m=m, mm=mm, pb=pb: h.transpose(pb.t[:, m * 128:(m + 1) * 128], self.x_tm.t[:, mm * 128:(mm + 1) * 128], self.ident_b.t[:]),
                         reads=[self.x_tm.d, self.ident_b.d], writes=[pb.d])
                P.op("dve", lambda h, g=g, pb=pb: h.tensor_tensor(self.ynT.t[:, g * 4:(g + 1) * 4, :], pb.t[:, 0:512].rearrange("p (m t) -> p m t", t=128),
                                                                   self.gate_g.t[:, g * 4:(g + 1) * 4].unsqueeze(2).to_broadcast([128, 4, 128]), ALU.mult),
                     reads=[pb.d, self.gate_g.d], writes=[self.ynT.d])
        if "C" in ph:
            self.dense_fm(("w_out", lambda w: w), 2048, 1024, self.ynT, self.resid_add)
            self.ffn(0)

        if "D" in ph:
            cur, prv = self.pbuf, 1 - self.pbuf
            kTo, vbo = self.kT[cur], self.v_b[cur]
            self.rms(2, self.uT)
            def ev_k(ps):
                P.op("act", lambda h: h.copy(self.k_tm.t[:], ps.t[:, 0:256]), reads=[ps.d], writes=[self.k_tm.d])
                P.op("dve", lambda h: h.tensor_copy(self.k_rot.t[:], ps.t[:, 0:256]), reads=[ps.d], writes=[self.k_rot.d])
            self.dense_tm(("w_k", lambda w: w), 256, self.uT, ev_k)
            self.rope_apply(self.k_tm, self.k_rot, 4)
            P.op("act", lambda h: h.copy(self.k_b.t[:], self.k_rot.t[:]), reads=[self.k_rot.d], writes=[self.k_b.d])
            pb = self.PB()
            for k in range(4):
                P.op("pe", lambda h, k=k, pb=pb: h.transpose(pb.t[0:64, k * 128:(k + 1) * 128], self.k_b.t[:, k * 64:(k + 1) * 64], self.ident_b.t[:]),
                     reads=[self.k_b.d, self.ident_b.d], writes=[pb.d])
            P.op("dve", lambda h, pb=pb: h.tensor_copy(kTo.t[:, :, :], pb.t[0:64, 0:512].rearrange("p (k t) -> p k t", t=128)), reads=[pb.d], writes=[kTo.d])
            def ev_v(ps):
                P.op("act", lambda h: h.copy(self.v_tm.t[:], ps.t[:, 0:256]), reads=[ps.d], writes=[self.v_tm.d])
                P.op("dve", lambda h: h.tensor_copy(vbo.t[:], ps.t[:, 0:256]), reads=[ps.d], writes=[vbo.d])
            self.dense_tm(("w_v", lambda w: w), 256, self.uT, ev_v)
            if ty == "S":
                for q in range(NSQ):
                    P.dma("pool", self.ds_out, A["k_s"][q, 120:128, :], self.k_rot.t[8 * q:8 * q + 8, :], reads=[self.k_rot.d])
                    P.dma("pool", self.ds_out, A["v_s"][q, 120:128, :], self.v_tm.t[8 * q:8 * q + 8, :], reads=[self.v_tm.d])
                P.dma("pool", self.ds_cp, A["k_s"][:, 0:120, :], A["st_k"][:, 8:128, :])
                P.dma("pool", self.ds_cp, A["v_s"][:, 0:120, :], A["st_v"][:, 8:128, :])
            if last:
                P.dma("pool", self.ds_out, A["k_p"][b], self.k_rot.t[:], reads=[self.k_rot.d])
                P.dma("pool", self.ds_out, A["v_p"][b], self.v_tm.t[:], reads=[self.v_tm.d])
            if ty == "M":
                P.op("pool", lambda h: h.tensor_copy(self.kT_meta.t[:], kTo.t[:]), reads=[kTo.d], writes=[self.kT_meta.d])
                P.op("pool", lambda h: h.tensor_copy(self.v_meta.t[:], vbo.t[:]), reads=[vbo.d], writes=[self.v_meta.d])
            self.rms(3, self.uT)
            for blk in range(2):
                def ev_q(ps, blk=blk):
                    P.op("act", lambda h: h.copy(self.q_tm.t[:, blk * 512:(blk + 1) * 512], ps.t[:, :]), reads=[ps.d], writes=[self.q_tm.d])
                    P.op("dve", lambda h: h.tensor_copy(self.q_rot.t[:, blk * 512:(blk + 1) * 512], ps.t[:, :]), reads=[ps.d], writes=[self.q_rot.d])
                self.dense_tm(("w_q", lambda w, blk=blk: w[:, blk * 512:(blk + 1) * 512]), 512, self.uT, ev_q)
            self.rope_apply(self.q_tm, self.q_rot, 16)
            P.op("act", lambda h: h.copy(self.q_b.t[:], self.q_rot.t[:]), reads=[self.q_rot.d], writes=[self.q_b.d])
            for half in range(2):
                pb = self.PB()
                for hh in range(8):
                    hd = half * 8 + hh
                    P.op("pe", lambda h, hh=hh, hd=hd, pb=pb: h.transpose(pb.t[0:64, hh * 128:(hh + 1) * 128], self.q_b.t[:, hd * 64:(hd + 1) * 64], self.ident_b.t[:]),
                         reads=[self.q_b.d, self.ident_b.d], writes=[pb.d])
                P.op("dve", lambda h, half=half, pb=pb: h.tensor_copy(self.qT.t[:, half * 8:(half + 1) * 8, :], pb.t[0:64, :].rearrange("p (k t) -> p k t", t=128)),
                     reads=[pb.d], writes=[self.qT.d])
            if ty == "S":
                for q in range(NSQ):
                    P.dma("sp", self.ds_kv, self.sk32.t[:], A["st_k"][q], writes=[self.sk32.d])
                    P.dma("sp", self.ds_kv, self.sv32.t[:], A["st_v"][q], writes=[self.sv32.d])
                    P.op("act", lambda h: h.copy(self.skb.t[:], self.sk32.t[:]), reads=[self.sk32.d], writes=[self.skb.d])
                    P.op("dve", lambda h: h.tensor_copy(self.svb.t[:], self.sv32.t[:]), reads=[self.sv32.d], writes=[self.svb.d])
                    pb = self.PB()
                    for k in range(4):
                        P.op("pe", lambda h, k=k, pb=pb: h.transpose(pb.t[0:64, k * 128:(k + 1) * 128], self.skb.t[:, k * 64:(k + 1) * 64], self.ident_b.t[:]),
                             reads=[self.skb.d, self.ident_b.d], writes=[pb.d])
                    P.op("dve", lambda h, pb=pb: h.tensor_copy(self.kTs.t[:, :, :], pb.t[0:64, 0:512].rearrange("p (k t) -> p k t", t=128)), reads=[pb.d], writes=[self.kTs.d])
                    ps = self.PF()
                    for k in range(4):
                        o = ps.t[:, k * 32:(k + 1) * 32]
                        P.op("pe", lambda h, k=k, o=o, q=q: h.matmul(o.rearrange("p (j t) -> p j t", t=8), lhsT=self.kTs.t[:, k, :], rhs=self.qT.t[:, 4 * k:4 * k + 4, 8 * q:8 * q + 8], start=True, stop=False),
                             reads=[self.kTs.d, self.qT.d], writes=[ps.d])
                        P.op("pe", lambda h, k=k, o=o: h.matmul(o, lhsT=self.ident_b.t[:], rhs=self.mbstate.t[:, k * 32:(k + 1) * 32], start=False, stop=True),
                             reads=[self.ident_b.d, self.mbstate.d], writes=[ps.d])
                    P.op("act", lambda h, ps=ps: h.activation(self.PTs.t[:], ps.t[:, 0:128], AF.Exp, scale=0.125), reads=[ps.d], writes=[self.PTs.d])
                    ps2 = self.PF()
                    for k in range(4):
                        P.op("pe", lambda h, k=k, ps2=ps2: h.matmul(ps2.t[0:64, k * 32:(k + 1) * 32], lhsT=self.svb.t[:, k * 64:(k + 1) * 64], rhs=self.PTs.t[:, k * 32:(k + 1) * 32], start=True, stop=True),
                             reads=[self.svb.d, self.PTs.d], writes=[ps2.d])
                    P.op("pe", lambda h, ps2=ps2: h.matmul(ps2.t[0:64, 128:256], lhsT=self.ones_b.t[:, 0:64], rhs=self.PTs.t[:], start=True, stop=True),
                         reads=[self.ones_b.d, self.PTs.d], writes=[ps2.d])
                    P.op("dve", lambda h, q=q, ps2=ps2: h.tensor_copy(self.big2.t[0:64, 0, :, 8 * q:8 * q + 8], ps2.t[0:64, 0:128].rearrange("p (h t) -> p h t", t=8)), reads=[ps2.d], writes=[self.big2_d0])
                    P.op("dve", lambda h, q=q, ps2=ps2: h.tensor_copy(self.big2.t[0:64, 1, :, 8 * q:8 * q + 8], ps2.t[0:64, 128:256].rearrange("p (h t) -> p h t", t=8)), reads=[ps2.d], writes=[self.big2_d1])
            use_prev = ty == "P"
            if first:
                kTp, vbp = self.kT_meta, self.v_meta
            else:
                kTp, vbp = self.kT[prv], self.v_b[prv]
            mpi = 1 if first else 0
            sc = {}

            def emit_scores(hd):
                k = hd // 4
                PT = self.PT[hd % 2]
                ps = self.PF()
                if use_prev:
                    P.op("pe", lambda h, ps=ps, k=k, hd=hd: h.matmul(ps.t[:, 0:128], lhsT=kTp.t[:, k, :], rhs=self.qT.t[:, hd, :], start=True, stop=False),
                         reads=[kTp.d, self.qT.d], writes=[ps.d])
                    P.op("pe", lambda h, ps=ps: h.matmul(ps.t[:, 0:128], lhsT=self.ident_b.t[:], rhs=self.mbprev.t[:, mpi, :], start=False, stop=True),
                         reads=[self.ident_b.d, self.mbprev.d], writes=[ps.d])
                P.op("pe", lambda h, ps=ps, k=k, hd=hd: h.matmul(ps.t[:, 128:256], lhsT=kTo.t[:, k, :], rhs=self.qT.t[:, hd, :], start=True, stop=False),
                     reads=[kTo.d, self.qT.d], writes=[ps.d])
                P.op("pe", lambda h, ps=ps: h.matmul(ps.t[:, 128:256], lhsT=self.ident_b.t[:], rhs=self.mb.t[:, ti, :], start=False, stop=True),
                     reads=[self.ident_b.d, self.mb.d], writes=[ps.d])
                lo = 0 if use_prev else 128
                P.op("act", lambda h, ps=ps, PT=PT, lo=lo: h.activation(PT.t[:, lo:256], ps.t[:, lo:256], AF.Exp, scale=0.125), reads=[ps.d], writes=[PT.d])

            def emit_pv(hd, psO, psD):
                k = hd // 4
                hh = hd % 4
                PT = self.PT[hd % 2]
                oo = psO.t[0:64, hh * 128:(hh + 1) * 128]
                od = psD.t[0:64, hh * 128:(hh + 1) * 128]
                if use_prev:
                    P.op("pe", lambda h, oo=oo, k=k, PT=PT: h.matmul(oo, lhsT=vbp.t[:, k * 64:(k + 1) * 64], rhs=PT.t[:, 0:128], start=True, stop=False), reads=[vbp.d, PT.d], writes=[psO.d])
                P.op("pe", lambda h, oo=oo, k=k, PT=PT: h.matmul(oo, lhsT=vbo.t[:, k * 64:(k + 1) * 64], rhs=PT.t[:, 128:256], start=(not use_prev), stop=True), reads=[vbo.d, PT.d], writes=[psO.d])
                if use_prev:
                    P.op("pe", lambda h, od=od, PT=PT: h.matmul(od, lhsT=self.ones_b.t[:, 0:64], rhs=PT.t[:, 0:128], start=True, stop=False), reads=[self.ones_b.d, PT.d], writes=[psD.d])
                P.op("pe", lambda h, od=od, PT=PT: h.matmul(od, lhsT=self.ones_b.t[:, 0:64], rhs=PT.t[:, 128:256], start=(not use_prev), stop=True), reads=[self.ones_b.d, PT.d], writes=[psD.d])

            emit_scores(0)
            for hq in range(4):
                psO = self.PF()
                psD = self.PF()
                for hh in range(4):
                    hd = hq * 4 + hh
                    if hd + 1 < 16:
                        emit_scores(hd + 1)
                    emit_pv(hd, psO, psD)
                h0 = hq * 4
                den = self.den
                P.op("dve", lambda h, psD=psD, h0=h0: h.tensor_tensor(den.t[:, :, :], psD.t[0:64, :].rearrange("p (h t) -> p h t", t=128),
                                                                      self.esink.t[0:64, h0:h0 + 4].unsqueeze(2).to_broadcast([64, 4, 128]), ALU.add),
                     reads=[psD.d, self.esink.d], writes=[den.d])
                if ty == "S":
                    P.op("dve", lambda h, h0=h0: h.tensor_tensor(den.t[:, :, :], den.t[:, :, :], self.big2.t[0:64, 1, h0:h0 + 4, :], ALU.add), reads=[den.d, self.big2_d1], writes=[den.d])
                    P.op("dve", lambda h, h0=h0, psO=psO: h.tensor_tensor(self.big2.t[0:64, 0, h0:h0 + 4, :], self.big2.t[0:64, 0, h0:h0 + 4, :], psO.t[0:64, :].rearrange("p (h t) -> p h t", t=128), ALU.add),
                         reads=[psO.d, self.big2_d0], writes=[self.big2_d0])
                P.op("dve", lambda h: h.reciprocal(den.t[:, :, :], den.t[:, :, :]), reads=[den.d], writes=[den.d])
                if ty == "S":
                    P.op("dve", lambda h, h0=h0: h.tensor_tensor(self.oT.t[:, h0:h0 + 4, :], self.big2.t[0:64, 0, h0:h0 + 4, :], den.t[:, :, :], ALU.mult),
                         reads=[self.big2_d0, den.d], writes=[self.oT.d])
                else:
                    P.op("dve", lambda h, h0=h0, psO=psO: h.tensor_tensor(self.oT.t[:, h0:h0 + 4, :], psO.t[0:64, :].rearrange("p (h t) -> p h t", t=128), den.t[:, :, :], ALU.mult),
                         reads=[psO.d, den.d], writes=[self.oT.d])
            self.pbuf = 1 - self.pbuf
        if "E" in ph:
            self.dense_fm(("w_o", lambda w: w), 1024, 1024, self.oT, self.resid_add, part=64)
            self.ffn(1)
            if ty != "M":
                self.rms(5, self.q_tm, view3=True)
                for half in range(2):
                    ps = self.PF()
                    for m in range(4):
                        mm = half * 4 + m
                        P.op("pe", lambda h, m=m, mm=mm, ps=ps: h.transpose(ps.t[:, m * 128:(m + 1) * 128], self.q_tm.t[:, mm * 128:(mm + 1) * 128], self.ident_f.t[:]),
                             reads=[self.q_tm.d, self.ident_f.d], writes=[ps.d])
                    P.op("act", lambda h, half=half, ps=ps: h.copy(self.y_st.t[:, half * 512:(half + 1) * 512], ps.t[:, :]), reads=[ps.d], writes=[self.y_st.d])
                dst = A["y_s"] if ty == "S" else A["y_p"][b, c * 128:(c + 1) * 128, :]
                P.dma("pool", self.ds_y, dst, self.y_st.t[:], reads=[self.y_st.d])


_CACHE = {}


def _get_nc():
    if "nc" not in _CACHE:
        k = Kern()
        _CACHE["nc"] = k.build()
        _CACHE["outs"] = k.out_names
    return _CACHE["nc"]


def _col(v, nchunk):
    return np.ascontiguousarray(np.asarray(v, np.float32).reshape(nchunk, 128).T)


def kernel(x_prompt, x_sample, state_ssm, state_conv, state_k, state_v, meta_tokens,
           ssm_norm_g, ssm_w_in, ssm_conv_w, ssm_conv_b, ssm_dt_bias, ssm_A_log, ssm_D,
           ssm_gate_norm_g, ssm_w_out, kv_norm_g, w_k, w_v, attn_norm_g, w_q, attn_sinks, w_o,
           ffn_norm_g, ffn_w_gate, ffn_w_up, ffn_w_down, final_norm_g):
    f = lambda a: np.ascontiguousarray(np.asarray(a, dtype=np.float32))
    nc = _get_nc()
    cst = make_consts()
    gcols = np.stack([_col(ssm_norm_g[0], 8), _col(ffn_norm_g[0], 8), _col(kv_norm_g, 8), _col(attn_norm_g[0], 8),
                      _col(ffn_norm_g[1], 8), _col(final_norm_g, 8)], axis=1)
    gate_g = _col(ssm_gate_norm_g[0], 16)
    conv_w = np.ascontiguousarray(f(ssm_conv_w[0]).reshape(4, 24, 128).transpose(2, 1, 0))
    conv_b = _col(ssm_conv_b[0], 24)
    hp = np.zeros((128, 4, 32), np.float32)
    hp[:, 0, :] = f(ssm_dt_bias[0])[None, :]
    hp[:, 1, :] = f(ssm_A_log[0])[None, :]
    hp[:, 2, :] = f(ssm_D[0])[None, :]
    sinks = np.ascontiguousarray(np.broadcast_to(f(attn_sinks[0])[None, :], (128, 16)))
    shared = {
        "meta": f(meta_tokens), "w_in": f(ssm_w_in[0]), "w_out": f(ssm_w_out[0]), "w_k": f(w_k), "w_v": f(w_v),
        "w_q": f(w_q[0]), "w_o": f(w_o[0]), "w_gate": f(ffn_w_gate), "w_up": f(ffn_w_up), "w_down": f(ffn_w_down),
        "gcols": np.ascontiguousarray(gcols), "gate_g": gate_g, "conv_w": conv_w, "conv_b": conv_b, "hp": hp, "sinks": sinks,
        "c_tri": cst["tri"], "c_mb": cst["mb"], "c_mbprev": cst["mbprev"], "c_mbstate": cst["mbstate"],
        "c_ident_b": cst["ident_b"], "c_ident_f": cst["ident_f"], "c_ones_b": cst["ones_b"], "c_sel": cst["sel"],
        "c_mbq": cst["mbq"], "c_rowm": cst["rowm"], "c_tokmask": cst["tokmask"], "c_rope": cst["rope"],
    }
    xp, xs = f(x_prompt), f(x_sample)
    sssm, sconv, sk, sv = f(state_ssm[0]), f(state_conv[0]), f(state_k), f(state_v)
    in_maps = []
    for i in range(NCORE):
        m = dict(shared)
        m["xs"] = xs[NSQ * i:NSQ * (i + 1)].reshape(128, 1024)
        m["xp"] = xp[NPB * i:NPB * (i + 1)]
        m["st_ssm"] = sssm[NSQ * i:NSQ * (i + 1)]
        m["st_conv"] = sconv[NSQ * i:NSQ * (i + 1)].reshape(NSQ * 3, 3072)
        m["st_k"] = sk[NSQ * i:NSQ * (i + 1)].reshape(NSQ, 128, 256)
        m["st_v"] = sv[NSQ * i:NSQ * (i + 1)].reshape(NSQ, 128, 256)
        in_maps.append(m)
    res = run_bass_kernel_spmd(nc, in_maps, core_ids=list(range(NCORE)))
    R = res.results
    cat = lambda name: np.concatenate([np.asarray(r[name], np.float32) for r in R], axis=0)
    y_prompt = cat("y_p")
    y_sample = cat("y_s").reshape(128, 8, 1024)
    ssm_p = cat("ssm_p")[None]
    conv_p = cat("conv_p")[None]
    k_p = cat("k_p").reshape(16, 128, 4, 64)
    v_p = cat("v_p").reshape(16, 128, 4, 64)
    ssm_s = cat("ssm_s")[None]
    conv_s = cat("conv_s").reshape(128, 3, 3072)[None]
    k_s = cat("k_s").reshape(128, 128, 4, 64)
    v_s = cat("v_s").reshape(128, 128, 4, 64)
    return (y_prompt, y_sample, ssm_p, conv_p, k_p, v_p, ssm_s, conv_s, k_s, v_s)
```

```python
import numpy as np
from contextlib import ExitStack
import ml_dtypes
import concourse.bass as bass
import concourse.mybir as mybir
from concourse.bass_utils import run_bass_kernel_spmd

F32 = mybir.dt.float32
BF16 = mybir.dt.bfloat16
AF = mybir.ActivationFunctionType
ALU = mybir.AluOpType

SEM_CAP = 30000
NEG = -30000.0
NCORE = 8
NSQ = 16
NPB = 2
NCHUNK = 16
DFF = 2816
EPS = 1e-5


class Dep:
    __slots__ = ("name", "w", "rs", "excl")

    def __init__(self, name=""):
        self.name = name
        self.w = None
        self.rs = []
        self.excl = False


class EngS:
    def __init__(self, name, handle):
        self.name = name
        self.h = handle
        self.count = 0
        self.sems = []
        self.seen = {}
        self.ops = []


class DmaSem:
    def __init__(self, sem, name):
        self.sem = sem
        self.total = 0
        self.name = name


class Prog:
    def __init__(self, nc, stack):
        self.nc = nc
        self.stack = stack
        self.E = {
            "pe": EngS("pe", nc.tensor),
            "act": EngS("act", nc.scalar),
            "dve": EngS("dve", nc.vector),
            "pool": EngS("pool", nc.gpsimd),
            "sp": EngS("sp", nc.sync),
        }
        self.dsems = []

    def new_sem(self, name):
        return self.stack.enter_context(self.nc.semaphore(name))

    def dma_sem(self, name):
        d = DmaSem(self.new_sem("d_" + name), name)
        self.dsems.append(d)
        return d

    def _need(self, eng, tok, needs):
        if tok is None:
            return
        if tok[0] == "e":
            if tok[1] == eng.name and eng.name == "pe":
                return
            key = ("e", tok[1])
            if needs.get(key, 0) < tok[2]:
                needs[key] = tok[2]
        else:
            ds = tok[1]
            key = ("d", ds)
            v = ds.total
            if needs.get(key, 0) < v:
                needs[key] = v

    def _waits(self, eng, reads, writes):
        needs = {}
        for d in reads:
            self._need(eng, d.w, needs)
            if d.excl:
                for r in d.rs:
                    if r[0] == "e" and r[1] != eng.name:
                        self._need(eng, r, needs)
        for d in writes:
            self._need(eng, d.w, needs)
            for r in d.rs:
                self._need(eng, r, needs)
        out = []
        for key, v in needs.items():
            if eng.seen.get(key, 0) >= v:
                continue
            eng.seen[key] = v
            out.append((key, v))
        return out

    def _emit_waits(self, eng, waits):
        for key, v in waits:
            if key[0] == "e":
                src = self.E[key[1]]
                si = (v - 1) // SEM_CAP
                val = v - si * SEM_CAP
                sem = src.sems[si]
                eng.ops.append(lambda h=eng.h, sem=sem, val=val: h.wait_ge(sem, val))
            else:
                ds = key[1]
                eng.ops.append(lambda h=eng.h, sem=ds.sem, val=v: h.wait_ge(sem, val))

    def _mark(self, tok, reads, writes):
        for d in reads:
            d.rs.append(tok)
        for d in writes:
            d.w = tok
            d.rs = []

    cap = None

    def op(self, engname, fn, reads=(), writes=()):
        if self.cap is not None:
            self.cap.append(("op", engname, fn, tuple(reads), tuple(writes), None, None, None))
            return
        eng = self.E[engname]
        self._emit_waits(eng, self._waits(eng, reads, writes))
        eng.count += 1
        idx = eng.count
        si = (idx - 1) // SEM_CAP
        while len(eng.sems) <= si:
            eng.sems.append(self.new_sem(f"s_{engname}{len(eng.sems)}"))
        sem = eng.sems[si]
        eng.ops.append(lambda h=eng.h, sem=sem, fn=fn: fn(h).then_inc(sem, 1))
        self._mark(("e", engname, idx), reads, writes)

    def dma(self, qname, dsem, out, in_, reads=(), writes=(), **kw):
        if self.cap is not None:
            self.cap.append(("dma", qname, dsem, tuple(reads), tuple(writes), out, in_, kw))
            return
        eng = self.E[qname]
        self._emit_waits(eng, self._waits(eng, reads, writes))
        dsem.total += 16
        eng.ops.append(lambda h=eng.h, sem=dsem.sem, out=out, in_=in_, kw=kw:
                       h.dma_start(out=out, in_=in_, **kw).then_inc(sem, 16))
        self._mark(("d", dsem, dsem.total), reads, writes)

    def capture(self, gen):
        assert self.cap is None
        self.cap = []
        gen()
        lst, self.cap = self.cap, None
        return lst

    def replay(self, lst):
        for (kind, a, b, reads, writes, out, in_, kw) in lst:
            if kind == "op":
                self.op(a, b, reads, writes)
            else:
                self.dma(a, b, out, in_, reads, writes, **kw)

    def replay_merged(self, l1, l2):
        n1, n2 = len(l1), len(l2)
        i = j = 0
        while i < n1 or j < n2:
            if j >= n2 or (i < n1 and (i + 0.5) * n2 <= (j + 0.5) * n1):
                self.replay(l1[i:i + 1]); i += 1
            else:
                self.replay(l2[j:j + 1]); j += 1

    def finish(self):
        eng = self.E["sp"]
        for name, e in self.E.items():
            if e.count > 0:
                v = e.count
                si = (v - 1) // SEM_CAP
                eng.ops.append(lambda h=eng.h, sem=e.sems[si], val=v - si * SEM_CAP: h.wait_ge(sem, val))
        for ds in self.dsems:
            if ds.total > 0:
                eng.ops.append(lambda h=eng.h, sem=ds.sem, val=ds.total: h.wait_ge(sem, val))

    def emit(self):
        with self.nc.Block() as block:
            @block.tensor
            def _(e):
                for f in self.E["pe"].ops:
                    f()

            @block.scalar
            def _(e):
                for f in self.E["act"].ops:
                    f()

            @block.vector
            def _(e):
                for f in self.E["dve"].ops:
                    f()

            @block.gpsimd
            def _(e):
                for f in self.E["pool"].ops:
                    f()

            @block.sync
            def _(e):
                for f in self.E["sp"].ops:
                    f()


class Tl:
    __slots__ = ("t", "d", "kd")

    def __init__(self, t, name):
        self.t = t
        self.d = Dep(name)
        self.kd = None


def make_consts():
    bf = ml_dtypes.bfloat16
    i = np.arange(128)
    c = {}
    tri_std = (i[:, None] <= i[None, :]).astype(np.float32)
    same = (i[:, None] // 8 == i[None, :] // 8)
    tri_blk = tri_std * same
    c["tri"] = np.stack([tri_std, tri_blk]).astype(bf)
    mb_std = np.where(tri_std > 0, 0.0, NEG).astype(np.float32)
    mb_blk = np.where(tri_blk > 0, 0.0, NEG).astype(np.float32)
    c["mb"] = np.stack([mb_std, mb_blk]).astype(bf)
    mbp = np.where(i[:, None] > i[None, :], 0.0, NEG).astype(np.float32)
    mbp0 = np.where((i[:, None] > i[None, :]) & (i[:, None] >= 112), 0.0, NEG).astype(np.float32)
    c["mbprev"] = np.stack([mbp, mbp0]).astype(bf)
    t8 = np.arange(128) % 8
    c["mbstate"] = np.where(i[:, None] > t8[None, :], 0.0, NEG).astype(np.float32).astype(bf)
    c["ident_b"] = np.eye(128, dtype=np.float32).astype(bf)
    c["ident_f"] = np.eye(128, dtype=np.float32)
    c["ones_b"] = np.ones((128, 128), np.float32).astype(bf)
    sel = np.zeros((128, NSQ, 128), np.float32)
    mbq = np.full((128, NSQ), NEG, np.float32)
    rowm = np.zeros((128, NSQ, 128), np.float32)
    for q in range(NSQ):
        sel[8 * q:8 * q + 8, q, :] = 1.0
        mbq[8 * q:8 * q + 8, q] = 0.0
        rowm[:, q, 8 * q:8 * q + 8] = 1.0
    c["sel"] = sel.astype(bf)
    c["mbq"] = mbq
    c["rowm"] = rowm.astype(bf)
    tm = np.ones((128, 2), np.float32)
    tm[:112, 1] = 0.0
    c["tokmask"] = tm
    inv = (np.float32(500000.0) ** (-np.arange(0, 16, 2, dtype=np.float32) / np.float32(16))).astype(np.float32)
    rope = np.zeros((18, 128, 16), np.float32)
    for ty in range(18):
        if ty < 16:
            pos = 16 + 128 * ty + i
        elif ty == 16:
            pos = i - 112
        else:
            pos = 16384 + (i % 8)
        ang = pos.astype(np.float32)[:, None] * inv[None, :]
        rope[ty, :, 0:8] = np.cos(ang)
        rope[ty, :, 8:16] = np.sin(ang)
    c["rope"] = rope
    return c


class Kern:
    def __init__(self):
        self.nc = bass.Bass("TRN2", target_bir_lowering=False)
        self.in_names = []
        self.out_names = []

    def din(self, name, shape, dt=F32):
        self.in_names.append(name)
        return self.nc.dram_tensor(name, list(shape), dt, kind="ExternalInput").ap()

    def dout(self, name, shape, dt=F32):
        self.out_names.append(name)
        return self.nc.dram_tensor(name, list(shape), dt, kind="ExternalOutput").ap()

    def sb(self, name, shape, dt):
        return Tl(self.st.enter_context(self.nc.sbuf_tensor("sb_" + name, list(shape), dt)), name)

    def build(self):
        nc = self.nc
        A = {}
        A["xs"] = self.din("xs", [128, 1024])
        A["xp"] = self.din("xp", [NPB, 2048, 1024])
        A["meta"] = self.din("meta", [16, 1024])
        A["st_ssm"] = self.din("st_ssm", [NSQ, 32, 64, 128])
        A["st_conv"] = self.din("st_conv", [NSQ * 3, 3072])
        A["st_k"] = self.din("st_k", [NSQ, 128, 256])
        A["st_v"] = self.din("st_v", [NSQ, 128, 256])
        A["w_in"] = self.din("w_in", [1024, 5152])
        A["w_out"] = self.din("w_out", [2048, 1024])
        A["w_k"] = self.din("w_k", [1024, 256])
        A["w_v"] = self.din("w_v", [1024, 256])
        A["w_q"] = self.din("w_q", [1024, 1024])
        A["w_o"] = self.din("w_o", [1024, 1024])
        A["w_gate"] = self.din("w_gate", [2, 1024, DFF])
        A["w_up"] = self.din("w_up", [2, 1024, DFF])
        A["w_down"] = self.din("w_down", [2, DFF, 1024])
        A["gcols"] = self.din("gcols", [128, 6, 8])
        A["gate_g"] = self.din("gate_g", [128, 16])
        A["conv_w"] = self.din("conv_w", [128, 24, 4])
        A["conv_b"] = self.din("conv_b", [128, 24])
        A["hp"] = self.din("hp", [128, 4, 32])
        A["sinks"] = self.din("sinks", [128, 16])
        A["c_tri"] = self.din("c_tri", [2, 128, 128], BF16)
        A["c_mb"] = self.din("c_mb", [2, 128, 128], BF16)
        A["c_mbprev"] = self.din("c_mbprev", [2, 128, 128], BF16)
        A["c_mbstate"] = self.din("c_mbstate", [128, 128], BF16)
        A["c_ident_b"] = self.din("c_ident_b", [128, 128], BF16)
        A["c_ident_f"] = self.din("c_ident_f", [128, 128])
        A["c_ones_b"] = self.din("c_ones_b", [128, 128], BF16)
        A["c_sel"] = self.din("c_sel", [128, NSQ, 128], BF16)
        A["c_mbq"] = self.din("c_mbq", [128, NSQ])
        A["c_rowm"] = self.din("c_rowm", [128, NSQ, 128], BF16)
        A["c_tokmask"] = self.din("c_tokmask", [128, 2])
        A["c_rope"] = self.din("c_rope", [18, 128, 16])
        A["y_p"] = self.dout("y_p", [NPB, 2048, 1024])
        A["y_s"] = self.dout("y_s", [128, 1024])
        A["ssm_p"] = self.dout("ssm_p", [NPB, 32, 64, 128])
        A["conv_p"] = self.dout("conv_p", [NPB, 3, 3072])
        A["k_p"] = self.dout("k_p", [NPB, 128, 256])
        A["v_p"] = self.dout("v_p", [NPB, 128, 256])
        A["ssm_s"] = self.dout("ssm_s", [NSQ, 32, 64, 128])
        A["conv_s"] = self.dout("conv_s", [NSQ * 3, 3072])
        A["k_s"] = self.dout("k_s", [NSQ, 128, 256])
        A["v_s"] = self.dout("v_s", [NSQ, 128, 256])
        self.A = A
        self.WB = {}
        for wn in ["w_in", "w_out", "w_k", "w_v", "w_q", "w_o", "w_gate", "w_up", "w_down"]:
            shp = list(A[wn].shape)
            self.WB[wn] = self.nc.dram_tensor(wn + "_bf", shp, BF16, kind="Internal").ap()
        with ExitStack() as st:
            self.st = st
            self.P = Prog(nc, st)
            self.alloc()
            self.setup()
            import os
            plan = os.environ.get("KPLAN", "SMP")
            if "S" in plan:
                self.chunk("S", None, 0)
            if "M" in plan:
                self.nw_cur = self.NW + len(self.wslot_extra)
                self.chunk("M", None, 0)
            if "P" in plan:
                self.run_prompt(int(os.environ.get("KNCH", NCHUNK)))
            self.P.finish()
            self.P.emit()
        return nc

    def alloc(self):
        sb = self.sb
        nc = self.nc
        self.tri = sb("tri", [128, 2, 128], BF16)
        self.mb = sb("mb", [128, 2, 128], BF16)
        self.mbprev = sb("mbprev", [128, 2, 128], BF16)
        self.mbstate = sb("mbstate", [128, 128], BF16)
        self.ident_b = sb("ident_b", [128, 128], BF16)
        self.ident_f = sb("ident_f", [128, 128], F32)
        self.ones_b = sb("ones_b", [128, 128], BF16)
        self.sel = sb("sel", [128, NSQ, 128], BF16)
        self.mbq = sb("mbq", [128, NSQ], F32)
        self.rowm = sb("rowm", [128, NSQ, 128], BF16)
        self.tokmask = sb("tokmask", [128, 2], F32)
        self.zero1 = sb("zero1", [128, 1], F32)
        self.gcols = sb("gcols", [128, 6, 8], F32)
        self.gate_g = sb("gate_g", [128, 16], F32)
        self.conv_w = sb("conv_w", [128, 24, 4], F32)
        self.conv_b = sb("conv_b", [128, 24], F32)
        self.hp = sb("hp", [128, 4, 32], F32)
        self.negA = sb("negA", [128, 32], F32)
        self.esink = sb("esink", [128, 16], F32)
        self.sinks = sb("sinks", [128, 16], F32)
        self.ropes = [sb(f"rope{i}", [128, 16], F32) for i in range(2)]
        self.rope_i = 0
        self.cacc = [sb(f"cacc{i}", [128, 128], F32) for i in range(4)]
        self.cth = [sb(f"cth{i}", [128, 128], F32) for i in range(4)]
        self.hTs = [sb(f"hT{i}", [128, 8, 128], F32) for i in range(2)]
        self.hT = self.hTs[0]
        self.uT = sb("uT", [128, 8, 128], BF16)
        self.uT.kd = [Dep(f"uT{k}") for k in range(8)]
        self.rstd = sb("rstd", [128, 128], F32)
        self.xin = sb("xin", [128, 1024], F32)
        self.z_tm = sb("z_tm", [128, 2048], BF16)
        self.xpre = sb("xpre", [128, 24, 176], BF16)
        self.uni = sb("uni", [128, 1408], F32)
        self.big1 = sb("big1", [128, 3072], F32)
        self.dtraw = sb("dtraw", [128, 32], F32)
        self.xbcT = sb("xbcT", [128, 24, 128], BF16)
        self.x_tm = sb("x_tm", [128, 2048], BF16)
        self.xdt = sb("xdt", [128, 2048], BF16)
        self.xw = sb("xw", [128, 2048], BF16)
        self.B_tm = sb("B_tm", [128, 512], BF16)
        self.dt = sb("dt", [128, 32], F32)
        self.a32 = sb("a32", [128, 32], F32)
        self.ahi = sb("ahi", [128, 32], BF16)
        self.alo = sb("alo", [128, 32], BF16)
        self.negcs = sb("negcs", [128, 32], F32)
        self.ecs = sb("ecs", [128, 32], F32)
        self.dA = sb("dA", [128, 32], F32)
        self.wdec = sb("wdec", [128, 32], F32)
        self.cbT = sb("cbT", [128, 4, 128], F32)
        self.dec = [sb(f"dec{i}", [128, 4, 128], BF16) for i in range(2)]
        self.MT = [sb(f"MT{i}", [128, 4, 128], BF16) for i in range(2)]
        self.tmpg = [sb(f"tmpg{i}", [128, 512], F32) for i in range(2)]
        self.CTq = sb("CTq", [128, 4, 128], BF16)
        self.ST = sb("ST", [128, 2048], F32)
        self.STb = sb("STb", [128, 2048], BF16)
        self.ST_meta = sb("ST_meta", [128, 2048], F32)
        self.big2 = sb("big2", [128, 2, 16, 128], F32)
        self.big2_d0 = Dep("big2_in")
        self.big2_d1 = Dep("big2_out")
        self.sz = sb("sz", [128, 512], F32)
        self.ssq = sb("ssq", [128, 4], F32)
        self.grs = sb("grs", [128, 4], F32)
        self.ynT = sb("ynT", [128, 16, 128], BF16)
        self.fth = [sb(f"fth{i}", [128, 256], F32) for i in range(2)]
        self.actT = sb("actT", [128, 22, 128], BF16)
        self.k_tm = sb("k_tm", [128, 256], F32)
        self.k_rot = sb("k_rot", [128, 256], F32)
        self.k_b = sb("k_b", [128, 256], BF16)
        self.v_tm = sb("v_tm", [128, 256], F32)
        self.v_b = [sb(f"v_b{i}", [128, 256], BF16) for i in range(2)]
        self.kT = [sb(f"kT{i}", [64, 4, 128], BF16) for i in range(2)]
        self.kT_meta = sb("kT_meta", [64, 4, 128], BF16)
        self.v_meta = sb("v_meta", [128, 256], BF16)
        self.cvtail_meta = sb("cvtail_meta", [128, 24, 3], BF16)
        self.q_tm = sb("q_tm", [128, 1024], F32)
        self.q_rot = sb("q_rot", [128, 1024], F32)
        self.q_b = sb("q_b", [128, 1024], BF16)
        self.qT = sb("qT", [64, 16, 128], BF16)
        self.rtmp = sb("rtmp", [128, 4, 16, 8], F32)
        self.PT = [sb(f"PT{i}", [128, 256], BF16) for i in range(2)]
        self.oT = sb("oT", [64, 16, 128], BF16)
        self.den = sb("den", [64, 4, 128], F32)
        class _V:
            def __init__(s_, ap, name):
                s_.t, s_.d = ap, Dep(name)
        self.u2 = sb("u2", [128, 1088], F32)
        u2 = self.u2.t
        self.sk32 = _V(u2[:, 0:256], "sk32")
        self.sv32 = _V(u2[:, 256:512], "sv32")
        self.skb = _V(u2[:, 512:640].bitcast(BF16), "skb")
        self.svb = _V(u2[:, 640:768].bitcast(BF16), "svb")
        self.kTs = _V(u2[0:64, 768:1024].bitcast(BF16).rearrange("p (k t) -> p k t", t=128), "kTs")
        self.PTs = _V(u2[:, 1024:1088].bitcast(BF16), "PTs")
        self.y_st = self.xin
        import os
        self.NW = int(os.environ.get("KNW", "7"))
        self.wslot = [sb(f"wslot{i}", [128, 2048], BF16) for i in range(self.NW)]
        self.wsem = [self.P.dma_sem(f"w{i}") for i in range(self.NW)]
        self.wi = 0
        self.nw_cur = self.NW

        class _SV:
            def __init__(s_, ap, name, first):
                s_.t, s_.d, s_.first = ap, Dep(name), list(first)
        b2 = self.big2.t[:, :, :, :].rearrange("p a b c -> p (a b c)").bitcast(BF16)
        self.wslot_extra = [_SV(b2[:, i * 2048:(i + 1) * 2048], f"wx{i}", [self.big2_d0]) for i in range(2)]
        self.wslot_extra.append(_SV(self.sel.t[:, :, :].rearrange("p a b -> p (a b)"), "wx4", [self.sel.d]))
        self.wslot_extra.append(_SV(self.rowm.t[:, :, :].rearrange("p a b -> p (a b)"), "wx5", [self.rowm.d]))
        self.wslot_extra.append(_SV(self.u2.t[:, 0:1024].bitcast(BF16), "wx6", [self.sk32.d, self.sv32.d, self.skb.d, self.svb.d, self.kTs.d]))
        self.wsem_extra = [self.P.dma_sem(f"wx{i}") for i in range(len(self.wslot_extra))]
        self.psf = [Tl(self.st.enter_context(nc.psum_tensor(f"psf{i}", [128, 512], F32)), f"psf{i}") for i in range(6)]
        self.psb = [Tl(self.st.enter_context(nc.psum_tensor(f"psb{i}", [128, 1024], BF16)), f"psb{i}") for i in range(2)]
        self.pfi = 0
        self.pbi = 0
        for t in self.psf + self.psb:
            t.d.excl = True
        self.ds_in = self.P.dma_sem("in")
        self.ds_x = self.P.dma_sem("x")
        self.ds_stg = self.P.dma_sem("stg")
        self.ds_stgo = self.P.dma_sem("stgo")
        self.ds_kv = self.P.dma_sem("kv")
        self.ds_out = self.P.dma_sem("out")
        self.ds_y = self.P.dma_sem("y")
        self.ds_cp = self.P.dma_sem("cp")
        self.ds_rope = self.P.dma_sem("rope")
        self.pbuf = 0

    def stage(self, n):
        if n > self.kstage:
            raise StopIteration

    ps_pool = "all"

    def PF(self):
        if self.ps_pool == "all":
            t = self.psf[self.pfi % 6]
        elif self.ps_pool == "lo":
            t = self.psf[self.pfi % 3]
        else:
            t = self.psf[3 + self.pfi % 3]
        self.pfi += 1
        return t

    def run_prompt(self, nch):
        self.nw_cur = self.NW + len(self.wslot_extra)
        P = self.P
        seq = [(b, c) for b in range(NPB) for c in range(nch)]
        prevE = None
        for i, (b, c) in enumerate(seq):
            hs = i % 2
            self.chunk("P", b, c, ph="A", hsel=hs)
            if prevE is None:
                self.chunk("P", b, c, ph="B", hsel=hs)
            else:
                pb_, pc_, phs = prevE
                self.ps_pool = "lo"
                lE = P.capture(lambda: self.chunk("P", pb_, pc_, ph="E", hsel=phs))
                self.ps_pool = "hi"
                lB = P.capture(lambda: self.chunk("P", b, c, ph="B", hsel=hs))
                self.ps_pool = "all"
                P.replay_merged(lE, lB)
            self.chunk("P", b, c, ph="C", hsel=hs)
            self.chunk("P", b, c, ph="D", hsel=hs)
            prevE = (b, c, hs)
        pb_, pc_, phs = prevE
        self.chunk("P", pb_, pc_, ph="E", hsel=phs)

    def PB(self):
        t = self.psb[self.pbi % 2]
        self.pbi += 1
        return t

    def setup(self):
        P, A = self.P, self.A
        ld = lambda tl, src: P.dma("sp", self.ds_in, tl.t[:], src, writes=[tl.d])
        ld(self.tri, A["c_tri"].rearrange("a p n -> p a n"))
        ld(self.mb, A["c_mb"].rearrange("a p n -> p a n"))
        ld(self.mbprev, A["c_mbprev"].rearrange("a p n -> p a n"))
        ld(self.mbstate, A["c_mbstate"])
        ld(self.ident_b, A["c_ident_b"])
        ld(self.ident_f, A["c_ident_f"])
        ld(self.ones_b, A["c_ones_b"])
        ld(self.sel, A["c_sel"])
        ld(self.mbq, A["c_mbq"])
        ld(self.rowm, A["c_rowm"])
        ld(self.tokmask, A["c_tokmask"])
        ld(self.gcols, A["gcols"])
        ld(self.gate_g, A["gate_g"])
        ld(self.conv_w, A["conv_w"])
        ld(self.conv_b, A["conv_b"])
        ld(self.hp, A["hp"])
        ld(self.sinks, A["sinks"])
        P.op("pool", lambda h: h.memset(self.zero1.t[:], 0.0), writes=[self.zero1.d])
        self.wdep = {}
        for wn in ["w_in", "w_out", "w_gate", "w_up", "w_down", "w_k", "w_v", "w_q", "w_o"]:
            self.wdep[wn] = Dep("wb_" + wn)
            self.ds_cast = P.dma_sem("cast_" + wn)
            src, dst = A[wn], self.WB[wn]
            if len(src.shape) == 3:
                for l in range(2):
                    P.dma("pool", self.ds_cast, dst[l].rearrange("(p r) n -> p r n", p=128), src[l].rearrange("(p r) n -> p r n", p=128), writes=[self.wdep[wn]])
            else:
                P.dma("pool", self.ds_cast, dst.rearrange("(p r) n -> p r n", p=128), src.rearrange("(p r) n -> p r n", p=128), writes=[self.wdep[wn]])
        P.op("act", lambda h: h.activation(self.negA.t[:], self.hp.t[:, 1, :], AF.Exp), reads=[self.hp.d], writes=[self.negA.d])
        P.op("dve", lambda h: h.tensor_scalar_mul(self.negA.t[:], self.negA.t[:], -1.0), reads=[self.negA.d], writes=[self.negA.d])
        P.op("act", lambda h: h.activation(self.esink.t[:], self.sinks.t[:], AF.Exp), reads=[self.sinks.d], writes=[self.esink.d])
        P.op("dve", lambda h: h.tensor_scalar_mul(self.conv_w.t[:], self.conv_w.t[:], 0.5), reads=[self.conv_w.d], writes=[self.conv_w.d])
        P.op("dve", lambda h: h.tensor_scalar_mul(self.conv_b.t[:], self.conv_b.t[:], 0.5), reads=[self.conv_b.d], writes=[self.conv_b.d])

    def wload(self, srcspec, r0, r1, ncols, part=128):
        P = self.P
        i = self.wi % self.nw_cur
        self.wi += 1
        if i < self.NW:
            sl, wsem, extra = self.wslot[i], self.wsem[i], []
        else:
            sl, wsem = self.wslot_extra[i - self.NW], self.wsem_extra[i - self.NW]
            extra, sl.first = sl.first, []
        kc = (r1 - r0) // part
        assert kc * ncols <= 2048
        view = sl.t[0:part, 0:kc * ncols].rearrange("p (c n) -> p c n", n=ncols)
        wn, sel = srcspec
        src = sel(self.WB[wn])[r0:r1, :]
        P.dma("sp", wsem, view, src.rearrange("(c p) n -> p c n", p=part), reads=[self.wdep[wn]], writes=[sl.d] + extra)
        return view, sl

    def rms(self, gi, out_tl, view3=False):
        P = self.P
        hT, sq, rstd = self.hT, self.actT, self.rstd
        P.op("act", lambda h: h.activation(sq.t[:, 0:8, :], hT.t[:], AF.Square), reads=[hT.d], writes=[sq.d])
        ps = self.PF()
        for c in range(8):
            P.op("pe", lambda h, c=c: h.matmul(ps.t[:, 0:128], lhsT=self.ones_b.t[:], rhs=sq.t[:, c, :], start=(c == 0), stop=(c == 7)),
                 reads=[sq.d, self.ones_b.d], writes=[ps.d])
        P.op("act", lambda h: h.activation(rstd.t[:], ps.t[:, 0:128], AF.Ln, bias=EPS, scale=1.0 / 1024.0), reads=[ps.d], writes=[rstd.d])
        P.op("act", lambda h: h.activation(rstd.t[:], rstd.t[:], AF.Exp, scale=-0.5), reads=[rstd.d], writes=[rstd.d])
        for c in range(8):
            oc = out_tl.t[:, c * 128:(c + 1) * 128] if view3 else out_tl.t[:, c, :]
            P.op("dve", lambda h, c=c, oc=oc: h.scalar_tensor_tensor(oc, hT.t[:, c, :], self.gcols.t[:, gi, c:c + 1], rstd.t[:], ALU.mult, ALU.mult),
                 reads=[hT.d, rstd.d, self.gcols.d], writes=[out_tl.kd[c] if out_tl.kd else out_tl.d])

    def dense_fm(self, src, K, ncols_total, xT_tl, evac, blk=None, part=128):
        P = self.P
        kc = K // part
        nb0 = 256
        ksubs = [(k0, min(k0 + 8, kc)) for k0 in range(0, kc, 8)]
        for c0 in range(0, ncols_total, nb0):
            nb = min(nb0, ncols_total - c0)
            loaded = []
            for (k0, k1) in ksubs:
                wv, wsl = self.wload((src[0], lambda w, c0=c0, nb=nb, f=src[1]: f(w)[:, c0:c0 + nb]), k0 * part, k1 * part, nb, part=part)
                loaded.append((k0, k1, wv, wsl))
            for m in range(nb // 128):
                ps = self.PF()
                for (k0, k1, wv, wsl) in loaded:
                    for k in range(k0, k1):
                        rhs = xT_tl.t[0:part, k, :]
                        P.op("pe", lambda h, k=k, k0=k0, m=m, rhs=rhs, wv=wv, ps=ps: h.matmul(ps.t[:, 0:128], lhsT=wv[:, k - k0, m * 128:(m + 1) * 128], rhs=rhs, start=(k == 0), stop=(k == kc - 1)),
                             reads=[wsl.d, xT_tl.kd[k] if xT_tl.kd else xT_tl.d], writes=[ps.d])
                evac((c0 // 128) + m, ps)

    def dense_tm(self, src, ncols, xT_tl, evac):
        P = self.P
        ps = self.PF()
        for c0 in range(0, ncols, 256):
            nb = min(256, ncols - c0)
            wv, wsl = self.wload((src[0], lambda w, c0=c0, nb=nb, f=src[1]: f(w)[:, c0:c0 + nb]), 0, 1024, nb)
            for k in range(8):
                P.op("pe", lambda h, k=k, c0=c0, nb=nb, wv=wv: h.matmul(ps.t[:, c0:c0 + nb], lhsT=xT_tl.t[:, k, :], rhs=wv[:, k, :], start=(k == 0), stop=(k == 7)),
                     reads=[wsl.d, xT_tl.kd[k] if xT_tl.kd else xT_tl.d], writes=[ps.d])
        evac(ps)

    def resid_add(self, m, ps):
        hT = self.hT
        self.P.op("dve", lambda h: h.tensor_tensor(hT.t[:, m, :], hT.t[:, m, :], ps.t[:, 0:128], ALU.add), reads=[ps.d, hT.d], writes=[hT.d])

    def ffn(self, layer):
        P, A = self.P, self.A
        self.rms(1 if layer == 0 else 4, self.uT)
        wd = ("w_down", lambda w: w[layer])
        act_tm = self.uni.t[:].bitcast(BF16)
        uT = self.uT
        for bi, c0 in enumerate(range(0, DFF, 256)):
            nb = min(256, DFF - c0)
            gv, gsl = self.wload(("w_gate", lambda w, c0=c0, nb=nb: w[layer][:, c0:c0 + nb]), 0, 1024, nb)
            uv, usl = self.wload(("w_up", lambda w, c0=c0, nb=nb: w[layer][:, c0:c0 + nb]), 0, 1024, nb)
            psg = self.PF()
            psu = self.PF()
            for k in range(8):
                P.op("pe", lambda h, k=k, gv=gv, psg=psg, nb=nb: h.matmul(psg.t[:, 0:nb], lhsT=uT.t[:, k, :], rhs=gv[:, k, :], start=(k == 0), stop=(k == 7)),
                     reads=[gsl.d, uT.kd[k]], writes=[psg.d])
            for k in range(8):
                P.op("pe", lambda h, k=k, uv=uv, psu=psu, nb=nb: h.matmul(psu.t[:, 0:nb], lhsT=uT.t[:, k, :], rhs=uv[:, k, :], start=(k == 0), stop=(k == 7)),
                     reads=[usl.d, uT.kd[k]], writes=[psu.d])
            th = self.fth[bi % 2]
            P.op("act", lambda h, th=th, psg=psg, nb=nb: h.activation(th.t[:, 0:nb], psg.t[:, 0:nb], AF.Tanh, scale=0.5), reads=[psg.d], writes=[th.d])
            P.op("dve", lambda h, th=th, psg=psg, nb=nb: h.scalar_tensor_tensor(th.t[:, 0:nb], th.t[:, 0:nb], 1.0, psg.t[:, 0:nb], ALU.add, ALU.mult), reads=[th.d, psg.d], writes=[th.d])
            P.op("dve", lambda h, th=th, psu=psu, nb=nb, c0=c0: h.scalar_tensor_tensor(act_tm[:, c0:c0 + nb], th.t[:, 0:nb], 0.5, psu.t[:, 0:nb], ALU.mult, ALU.mult),
                 reads=[th.d, psu.d], writes=[self.uni.d])
        for gi_, (m0, n) in enumerate([(0, 8), (8, 8), (16, 6)]):
            ps = self.PF()
            pbv = ps.t[:, :].bitcast(BF16)
            for j in range(n):
                m = m0 + j
                P.op("pe", lambda h, j=j, m=m, pbv=pbv: h.transpose(pbv[:, j * 128:(j + 1) * 128], act_tm[:, m * 128:(m + 1) * 128], self.ident_b.t[:]),
                     reads=[self.uni.d, self.ident_b.d], writes=[ps.d])
            src = pbv[:, 0:n * 128].rearrange("p (m t) -> p m t", t=128)
            if gi_ == 1:
                P.op("act", lambda h, m0=m0, n=n, src=src: h.copy(self.actT.t[:, m0:m0 + n, :], src), reads=[ps.d], writes=[self.actT.d])
            else:
                P.op("dve", lambda h, m0=m0, n=n, src=src: h.tensor_copy(self.actT.t[:, m0:m0 + n, :], src), reads=[ps.d], writes=[self.actT.d])
        self.dense_fm(wd, DFF, 1024, self.actT, self.resid_add)

    def rope_apply(self, src, dst, nh):
        P = self.P
        s3 = src.t[:, :].rearrange("p (h d) -> p h d", d=64)
        d3 = dst.t[:, :].rearrange("p (h d) -> p h d", d=64)
        cos = self.rope.t[:, 0:8].unsqueeze(1).to_broadcast([128, nh, 8])
        sin = self.rope.t[:, 8:16].unsqueeze(1).to_broadcast([128, nh, 8])
        x1, x2 = s3[:, :, 0:8], s3[:, :, 8:16]
        t = [self.rtmp.t[:, i, 0:nh, :] for i in range(4)]
        rd = [src.d, self.rope.d]
        P.op("dve", lambda h: h.tensor_tensor(t[0], x1, cos, ALU.mult), reads=rd, writes=[self.rtmp.d])
        P.op("dve", lambda h: h.tensor_tensor(t[1], x2, sin, ALU.mult), reads=rd, writes=[self.rtmp.d])
        P.op("dve", lambda h: h.tensor_tensor(t[2], x2, cos, ALU.mult), reads=rd, writes=[self.rtmp.d])
        P.op("dve", lambda h: h.tensor_tensor(t[3], x1, sin, ALU.mult), reads=rd, writes=[self.rtmp.d])
        P.op("dve", lambda h: h.tensor_tensor(d3[:, :, 0:8], t[0], t[1], ALU.subtract), reads=[self.rtmp.d], writes=[dst.d])
        P.op("dve", lambda h: h.tensor_tensor(d3[:, :, 8:16], t[2], t[3], ALU.add), reads=[self.rtmp.d], writes=[dst.d])

    def chunk(self, ty, b, c, ph="ABCDE", hsel=0):
        P, A = self.P, self.A
        self.hT = self.hTs[hsel]
        ti = 1 if ty == "S" else 0
        nseq = NSQ if ty == "S" else 1
        first = (ty == "P" and c == 0)
        last = (ty == "P" and c == NCHUNK - 1)
        if "A" in ph:
            xin = self.xin
            if ty == "S":
                P.dma("sp", self.ds_x, xin.t[:], A["xs"], writes=[xin.d])
            elif ty == "M":
                P.op("pool", lambda h: h.memset(xin.t[:], 0.0), writes=[xin.d])
                P.dma("sp", self.ds_x, xin.t[112:128, :], A["meta"], writes=[xin.d])
            else:
                P.dma("sp", self.ds_x, xin.t[:], A["xp"][b, c * 128:(c + 1) * 128, :], writes=[xin.d])
            rty = 17 if ty == "S" else (16 if ty == "M" else c)
            self.rope = self.ropes[self.rope_i % 2]
            self.rope_i += 1
            P.dma("sp", self.ds_rope, self.rope.t[:], A["c_rope"][rty], writes=[self.rope.d])
            for half in range(2):
                ps = self.PF()
                for m in range(4):
                    mm = half * 4 + m
                    P.op("pe", lambda h, m=m, mm=mm, ps=ps: h.transpose(ps.t[:, m * 128:(m + 1) * 128], xin.t[:, mm * 128:(mm + 1) * 128], self.ident_f.t[:]),
                         reads=[xin.d, self.ident_f.d], writes=[ps.d])
                hTc = self.hT
                P.op("act", lambda h, half=half, ps=ps, hTc=hTc: h.copy(hTc.t[:, half * 4:(half + 1) * 4, :], ps.t[:, :].rearrange("p (m t) -> p m t", t=128)),
                     reads=[ps.d], writes=[hTc.d])

            if ty == "M":
                P.op("pool", lambda h: h.memset(self.xpre.t[:, :, 0:3], 0.0), writes=[self.xpre.d])
            if first:
                P.op("pool", lambda h: h.tensor_copy(self.xpre.t[:, :, 0:3], self.cvtail_meta.t[:]), reads=[self.cvtail_meta.d], writes=[self.xpre.d])
            elif ty == "P":
                P.op("pool", lambda h: h.tensor_copy(self.xpre.t[:, :, 0:3], self.xpre.t[:, :, 128:131]), reads=[self.xpre.d], writes=[self.xpre.d])
            if ty == "S":
                P.dma("sp", self.ds_stg, self.big1.t[0:48, :], A["st_conv"], writes=[self.big1.d])
                for m in range(24):
                    ps = self.PF()
                    P.op("pe", lambda h, m=m, ps=ps: h.transpose(ps.t[:, 0:48], self.big1.t[0:48, m * 128:(m + 1) * 128], self.ident_f.t[0:48, 0:48]),
                         reads=[self.big1.d, self.ident_f.d], writes=[ps.d])
                    P.op("dve", lambda h, m=m, ps=ps: h.tensor_copy(self.xpre.t[:, m, :].rearrange("p (q t) -> p q t", t=11)[:, :, 0:3],
                                                                      ps.t[:, 0:48].rearrange("p (q r) -> p q r", r=3)),
                         reads=[ps.d], writes=[self.xpre.d])

            self.rms(0, self.uT)
            w_in = A["w_in"]
            for blk in range(4):
                def ev(ps, blk=blk):
                    P.op("act", lambda h: h.copy(self.z_tm.t[:, blk * 512:(blk + 1) * 512], ps.t[:, :]), reads=[ps.d], writes=[self.z_tm.d])
                self.dense_tm(("w_in", lambda w, blk=blk: w[:, blk * 512:(blk + 1) * 512]), 512, self.uT, ev)
            if ty == "S":
                def ev_x(m, ps):
                    P.op("dve", lambda h: h.tensor_copy(self.xpre.t[:, m, :].rearrange("p (q t) -> p q t", t=11)[:, :, 3:11],
                                                 ps.t[:, 0:128].rearrange("p (q t) -> p q t", t=8)), reads=[ps.d], writes=[self.xpre.d])
                    P.op("dve", lambda h: h.tensor_copy(self.uni.t[:, 0:1152].rearrange("p (m r) -> p m r", r=48)[:, m, :].rearrange("p (q r) -> p q r", r=3),
                                                        ps.t[:, 0:128].rearrange("p (q t) -> p q t", t=8)[:, :, 5:8]), reads=[ps.d], writes=[self.uni.d])
            else:
                def ev_x(m, ps):
                    P.op("dve", lambda h: h.tensor_copy(self.xpre.t[:, m, 3:131], ps.t[:, 0:128]), reads=[ps.d], writes=[self.xpre.d])
                    if last:
                        P.op("dve", lambda h: h.tensor_copy(self.uni.t[:, 0:1152].rearrange("p (m r) -> p m r", r=48)[:, m, 0:3], ps.t[:, 125:128]), reads=[ps.d], writes=[self.uni.d])
            self.dense_fm(("w_in", lambda w: w[:, 2048:5120]), 1024, 3072, self.uT, ev_x)
            def ev_dt(ps):
                P.op("dve", lambda h: h.tensor_copy(self.dtraw.t[:], ps.t[:, 0:32]), reads=[ps.d], writes=[self.dtraw.d])
            self.dense_tm(("w_in", lambda w: w[:, 5120:5152]), 32, self.uT, ev_dt)
            if ty == "M":
                P.op("pool", lambda h: h.tensor_copy(self.cvtail_meta.t[:], self.xpre.t[:, :, 128:131]), reads=[self.xpre.d], writes=[self.cvtail_meta.d])
            if ty == "S" or last:
                nr = 48 if ty == "S" else 3
                for m in range(24):
                    ps = self.PF()
                    P.op("pe", lambda h, m=m, ps=ps: h.transpose(ps.t[0:nr, 0:128], self.uni.t[:, 0:1152].rearrange("p (m r) -> p m r", r=48)[:, m, 0:nr], self.ident_f.t[:]),
                         reads=[self.uni.d, self.ident_f.d], writes=[ps.d])
                    P.op("dve", lambda h, m=m, ps=ps: h.tensor_copy(self.big1.t[0:nr, m * 128:(m + 1) * 128], ps.t[0:nr, 0:128]), reads=[ps.d], writes=[self.big1.d])
                dst = A["conv_s"] if ty == "S" else A["conv_p"][b]
                P.dma("pool", self.ds_out, dst, self.big1.t[0:nr, :], reads=[self.big1.d])

        if "B" in ph:
            if ty == "M":
                P.op("pool", lambda h: h.memset(self.ST.t[:], 0.0), writes=[self.ST.d])
                P.op("pool", lambda h: h.memset(self.STb.t[:], 0.0), writes=[self.STb.d])
            if first:
                P.op("dve", lambda h: h.tensor_copy(self.ST.t[:], self.ST_meta.t[:]), reads=[self.ST_meta.d], writes=[self.ST.d])
                P.op("act", lambda h: h.copy(self.STb.t[:], self.ST_meta.t[:]), reads=[self.ST_meta.d], writes=[self.STb.d])
            for mg in range(6):
                for k in range(4):
                    for mi in range(4):
                        m = mg * 4 + mi
                        acc = self.cacc[mi]
                        if ty == "S":
                            src = self.xpre.t[:, m, :].rearrange("p (q t) -> p q t", t=11)[:, :, k:k + 8]
                            out = acc.t[:, :].rearrange("p (q t) -> p q t", t=8)
                        else:
                            src = self.xpre.t[:, m, k:k + 128]
                            out = acc.t[:, :]
                        wk = self.conv_w.t[:, m, k:k + 1]
                        if k == 0:
                            bk = self.conv_b.t[:, m:m + 1]
                            P.op("dve", lambda h, src=src, out=out, wk=wk, bk=bk: h.tensor_scalar(out, src, wk, bk, ALU.mult, ALU.add), reads=[self.xpre.d, self.conv_w.d, self.conv_b.d], writes=[acc.d])
                        else:
                            P.op("dve", lambda h, src=src, out=out, wk=wk: h.scalar_tensor_tensor(out, src, wk, out, ALU.mult, ALU.add), reads=[self.xpre.d, self.conv_w.d, acc.d], writes=[acc.d])
                for mi in range(4):
                    m = mg * 4 + mi
                    acc = self.cacc[mi]
                    th = self.cth[mi]
                    P.op("act", lambda h, acc=acc, th=th: h.activation(th.t[:, :], acc.t[:, :], AF.Tanh), reads=[acc.d], writes=[th.d])
                    P.op("dve", lambda h, m=m, acc=acc, th=th: h.scalar_tensor_tensor(self.xbcT.t[:, m, :], th.t[:, :], 1.0, acc.t[:, :], ALU.add, ALU.mult),
                         reads=[acc.d, th.d], writes=[self.xbcT.d])
            for half in range(2):
                pb = self.PB()
                for m in range(8):
                    mm = half * 8 + m
                    P.op("pe", lambda h, m=m, mm=mm, pb=pb: h.transpose(pb.t[:, m * 128:(m + 1) * 128], self.xbcT.t[:, mm, :], self.ident_b.t[:]),
                         reads=[self.xbcT.d, self.ident_b.d], writes=[pb.d])
                P.op("act", lambda h, half=half, pb=pb: h.copy(self.x_tm.t[:, half * 1024:(half + 1) * 1024], pb.t[:, :]), reads=[pb.d], writes=[self.x_tm.d])
            pb = self.PB()
            for m in range(4):
                P.op("pe", lambda h, m=m, pb=pb: h.transpose(pb.t[:, m * 128:(m + 1) * 128], self.xbcT.t[:, 16 + m, :], self.ident_b.t[:]),
                     reads=[self.xbcT.d, self.ident_b.d], writes=[pb.d])
            P.op("dve", lambda h, pb=pb: h.tensor_copy(self.B_tm.t[:], pb.t[:, 0:512]), reads=[pb.d], writes=[self.B_tm.d])
            dt, a32 = self.dt, self.a32
            P.op("dve", lambda h: h.tensor_tensor(dt.t[:], self.dtraw.t[:], self.hp.t[:, 0, :], ALU.add), reads=[self.dtraw.d, self.hp.d], writes=[dt.d])
            P.op("act", lambda h: h.activation(dt.t[:], dt.t[:], AF.Exp), reads=[dt.d], writes=[dt.d])
            P.op("act", lambda h: h.activation(dt.t[:], dt.t[:], AF.Ln, bias=1.0), reads=[dt.d], writes=[dt.d])
            tmcol = self.tokmask.t[:, 1:2] if ty == "M" else self.tokmask.t[:, 0:1]
            P.op("dve", lambda h: h.tensor_scalar_mul(dt.t[:], dt.t[:], tmcol), reads=[dt.d, self.tokmask.d], writes=[dt.d])
            P.op("dve", lambda h: h.tensor_tensor(a32.t[:], dt.t[:], self.negA.t[:], ALU.mult), reads=[dt.d, self.negA.d], writes=[a32.d])
            P.op("dve", lambda h: h.tensor_copy(self.ahi.t[:], a32.t[:]), reads=[a32.d], writes=[self.ahi.d])
            P.op("dve", lambda h: h.tensor_tensor(self.alo.t[:], a32.t[:], self.ahi.t[:], ALU.subtract), reads=[a32.d, self.ahi.d], writes=[self.alo.d])
            ps = self.PF()
            P.op("pe", lambda h, ps=ps: h.matmul(ps.t[:, 0:32], lhsT=self.tri.t[:, ti, :], rhs=self.ahi.t[:], start=True, stop=False), reads=[self.tri.d, self.ahi.d], writes=[ps.d])
            P.op("pe", lambda h, ps=ps: h.matmul(ps.t[:, 0:32], lhsT=self.tri.t[:, ti, :], rhs=self.alo.t[:], start=False, stop=True), reads=[self.tri.d, self.alo.d], writes=[ps.d])
            P.op("dve", lambda h, ps=ps: h.tensor_scalar_mul(self.negcs.t[:], ps.t[:, 0:32], -1.0), reads=[ps.d], writes=[self.negcs.d])
            P.op("act", lambda h, ps=ps: h.activation(self.ecs.t[:], ps.t[:, 0:32], AF.Exp), reads=[ps.d], writes=[self.ecs.d])
            x3 = self.x_tm.t[:, :].rearrange("p (h d) -> p h d", d=64)
            P.op("dve", lambda h: h.tensor_tensor(self.xdt.t[:, :].rearrange("p (h d) -> p h d", d=64), x3, dt.t[:].unsqueeze(2).to_broadcast([128, 32, 64]), ALU.mult),
                 reads=[self.x_tm.d, dt.d], writes=[self.xdt.d])
            ps = self.PF()
            for g in range(4):
                P.op("pe", lambda h, g=g, ps=ps: h.matmul(ps.t[:, g * 128:(g + 1) * 128], lhsT=self.xbcT.t[:, 16 + g, :], rhs=self.xbcT.t[:, 20 + g, :], start=True, stop=True),
                     reads=[self.xbcT.d], writes=[ps.d])
            P.op("act", lambda h, ps=ps: h.copy(self.cbT.t[:, :, :], ps.t[:, :].rearrange("p (g l) -> p g l", l=128)), reads=[ps.d], writes=[self.cbT.d])
            psYs = {}

            def emit_decay(hb):
                g = hb // 2
                dec, MT = self.dec[hb % 2], self.MT[hb % 2]
                ps = self.PF()
                for hh in range(4):
                    hd = hb * 4 + hh
                    o = ps.t[:, hh * 128:(hh + 1) * 128]
                    P.op("pe", lambda h, o=o, hd=hd: h.matmul(o, lhsT=self.ahi.t[:, hd:hd + 1].to_broadcast([128, 128]), rhs=self.tri.t[:, ti, :], start=True, stop=False),
                         reads=[self.ahi.d, self.tri.d], writes=[ps.d])
                    P.op("pe", lambda h, o=o, hd=hd: h.matmul(o, lhsT=self.alo.t[:, hd:hd + 1].to_broadcast([128, 128]), rhs=self.tri.t[:, ti, :], start=False, stop=False),
                         reads=[self.alo.d, self.tri.d], writes=[ps.d])
                    P.op("pe", lambda h, o=o: h.matmul(o, lhsT=self.ident_b.t[:], rhs=self.mb.t[:, ti, :], start=False, stop=True),
                         reads=[self.ident_b.d, self.mb.d], writes=[ps.d])
                for hh in range(4):
                    hd = hb * 4 + hh
                    P.op("act", lambda h, hh=hh, hd=hd, ps=ps, dec=dec: h.activation(dec.t[:, hh, :], ps.t[:, hh * 128:(hh + 1) * 128], AF.Exp, bias=self.negcs.t[:, hd:hd + 1]),
                         reads=[ps.d, self.negcs.d], writes=[dec.d])
                P.op("dve", lambda h, g=g, dec=dec, MT=MT: h.tensor_tensor(MT.t[:, :, :], dec.t[:, :, :], self.cbT.t[:, g, :].unsqueeze(1).to_broadcast([128, 4, 128]), ALU.mult),
                     reads=[dec.d, self.cbT.d], writes=[MT.d])

            def emit_ydiag(hb):
                g = hb // 2
                MT = self.MT[hb % 2]
                if hb % 2 == 0:
                    psYs[g] = self.PF()
                psY = psYs[g]
                for hh in range(4):
                    hd = hb * 4 + hh
                    col = (hd % 8) * 64
                    P.op("pe", lambda h, hh=hh, hd=hd, col=col, MT=MT, psY=psY: h.matmul(psY.t[:, col:col + 64], lhsT=MT.t[:, hh, :], rhs=self.xdt.t[:, hd * 64:(hd + 1) * 64], start=True, stop=True),
                         reads=[MT.d, self.xdt.d], writes=[psY.d])
                if hb % 2 == 1:
                    tg = self.tmpg[g % 2]
                    P.op("dve", lambda h, g=g, tg=tg: h.tensor_tensor(tg.t[:, :].rearrange("p (h d) -> p h d", d=64), self.x_tm.t[:, g * 512:(g + 1) * 512].rearrange("p (h d) -> p h d", d=64),
                                                                     self.hp.t[:, 2, g * 8:(g + 1) * 8].unsqueeze(2).to_broadcast([128, 8, 64]), ALU.mult),
                         reads=[self.x_tm.d, self.hp.d], writes=[tg.d])
                    P.op("dve", lambda h, g=g, tg=tg, psY=psY: h.tensor_tensor(self.big1.t[:, g * 512:(g + 1) * 512], psY.t[:, :], tg.t[:], ALU.add),
                         reads=[psY.d, tg.d], writes=[self.big1.d])

            emit_decay(0)
            for hb in range(8):
                if hb + 1 < 8:
                    emit_decay(hb + 1)
                emit_ydiag(hb)
            for q in range(nseq):
                if ty == "S":
                    selq = self.sel.t[:, q, :]
                    mbcol = self.mbq.t[:, q:q + 1]
                else:
                    selq = self.ones_b.t[:]
                    mbcol = self.zero1.t[:, 0:1]
                ps = self.PF()
                P.op("pe", lambda h, ps=ps, selq=selq: h.matmul(ps.t[:, 0:32], lhsT=selq, rhs=self.ahi.t[:], start=True, stop=False), reads=[self.sel.d, self.ones_b.d, self.ahi.d], writes=[ps.d])
                P.op("pe", lambda h, ps=ps, selq=selq: h.matmul(ps.t[:, 0:32], lhsT=selq, rhs=self.alo.t[:], start=False, stop=True), reads=[self.sel.d, self.ones_b.d, self.alo.d], writes=[ps.d])
                P.op("act", lambda h, ps=ps: h.activation(self.dA.t[:], ps.t[:, 0:32], AF.Exp), reads=[ps.d], writes=[self.dA.d])
                P.op("dve", lambda h, ps=ps: h.tensor_tensor(self.wdec.t[:], ps.t[:, 0:32], self.negcs.t[:], ALU.add), reads=[ps.d, self.negcs.d], writes=[self.wdec.d])
                P.op("act", lambda h, mbcol=mbcol: h.activation(self.wdec.t[:], self.wdec.t[:], AF.Exp, bias=mbcol), reads=[self.wdec.d, self.mbq.d, self.zero1.d], writes=[self.wdec.d])
                P.op("dve", lambda h: h.tensor_tensor(self.xw.t[:, :].rearrange("p (h d) -> p h d", d=64), self.xdt.t[:, :].rearrange("p (h d) -> p h d", d=64),
                                                      self.wdec.t[:].unsqueeze(2).to_broadcast([128, 32, 64]), ALU.mult),
                     reads=[self.xdt.d, self.wdec.d], writes=[self.xw.d])
                if ty == "S":
                    P.op("dve", lambda h, q=q: h.tensor_tensor(self.CTq.t[:, :, :], self.xbcT.t[:, 20:24, :], self.rowm.t[:, q, :].unsqueeze(1).to_broadcast([128, 4, 128]), ALU.mult),
                         reads=[self.xbcT.d, self.rowm.d], writes=[self.CTq.d])
                    CT = lambda g: self.CTq.t[:, g, :]
                    ctd = self.CTq.d
                else:
                    CT = lambda g: self.xbcT.t[:, 20 + g, :]
                    ctd = self.xbcT.d
                if ty == "S":
                    for j in range(16):
                        P.dma("sp", self.ds_stg, self.big2.t[:, 0, j, :], A["st_ssm"][q, 2 * j:2 * j + 2].rearrange("h2 p n -> (h2 p) n"), writes=[self.big2_d0])
                    for jb in range(4):
                        ps = self.PF()
                        for jj in range(4):
                            j = jb * 4 + jj
                            P.op("pe", lambda h, j=j, jj=jj, ps=ps: h.transpose(ps.t[:, jj * 128:(jj + 1) * 128], self.big2.t[:, 0, j, :], self.ident_f.t[:]),
                                 reads=[self.big2_d0, self.ident_f.d], writes=[ps.d])
                        P.op("dve", lambda h, jb=jb, ps=ps: h.tensor_copy(self.ST.t[:, jb * 512:(jb + 1) * 512], ps.t[:, :]), reads=[ps.d], writes=[self.ST.d])
                        P.op("act", lambda h, jb=jb, ps=ps: h.copy(self.STb.t[:, jb * 512:(jb + 1) * 512], ps.t[:, :]), reads=[ps.d], writes=[self.STb.d])
                for g in range(4):
                    ps = self.PF()
                    P.op("pe", lambda h, g=g, ps=ps, CT=CT: h.matmul(ps.t[:, :], lhsT=CT(g), rhs=self.STb.t[:, g * 512:(g + 1) * 512], start=True, stop=True),
                         reads=[ctd, self.STb.d], writes=[ps.d])
                    tg = self.tmpg[g % 2]
                    P.op("dve", lambda h, g=g, ps=ps, tg=tg: h.tensor_tensor(tg.t[:, :].rearrange("p (h d) -> p h d", d=64), ps.t[:, :].rearrange("p (h d) -> p h d", d=64),
                                                                           self.ecs.t[:, g * 8:(g + 1) * 8].unsqueeze(2).to_broadcast([128, 8, 64]), ALU.mult),
                         reads=[ps.d, self.ecs.d], writes=[tg.d])
                    P.op("dve", lambda h, g=g, tg=tg: h.tensor_tensor(self.big1.t[:, g * 512:(g + 1) * 512], self.big1.t[:, g * 512:(g + 1) * 512], tg.t[:], ALU.add),
                         reads=[tg.d, self.big1.d], writes=[self.big1.d])
                for g in range(4):
                    ps = self.PF()
                    P.op("pe", lambda h, g=g, ps=ps: h.matmul(ps.t[:, :], lhsT=self.B_tm.t[:, g * 128:(g + 1) * 128], rhs=self.xw.t[:, g * 512:(g + 1) * 512], start=True, stop=True),
                         reads=[self.B_tm.d, self.xw.d], writes=[ps.d])
                    sg3 = self.ST.t[:, g * 512:(g + 1) * 512].rearrange("p (h d) -> p h d", d=64)
                    P.op("dve", lambda h, g=g, sg3=sg3: h.tensor_tensor(sg3, sg3, self.dA.t[:, g * 8:(g + 1) * 8].unsqueeze(2).to_broadcast([128, 8, 64]), ALU.mult),
                         reads=[self.ST.d, self.dA.d], writes=[self.ST.d])
                    P.op("dve", lambda h, g=g, ps=ps: h.tensor_tensor(self.ST.t[:, g * 512:(g + 1) * 512], self.ST.t[:, g * 512:(g + 1) * 512], ps.t[:, :], ALU.add),
                         reads=[self.ST.d, ps.d], writes=[self.ST.d])
                if ty != "S":
                    P.op("act", lambda h: h.copy(self.STb.t[:], self.ST.t[:]), reads=[self.ST.d], writes=[self.STb.d])
                if ty == "M":
                    P.op("pool", lambda h: h.tensor_copy(self.ST_meta.t[:], self.ST.t[:]), reads=[self.ST.d], writes=[self.ST_meta.d])
                if ty == "S" or last:
                    for jb in range(4):
                        ps = self.PF()
                        for jj in range(4):
                            j = jb * 4 + jj
                            P.op("pe", lambda h, j=j, jj=jj, ps=ps: h.transpose(ps.t[:, jj * 128:(jj + 1) * 128], self.ST.t[:, j * 128:(j + 1) * 128], self.ident_f.t[:]),
                                 reads=[self.ST.d, self.ident_f.d], writes=[ps.d])
                        P.op("act", lambda h, jb=jb, ps=ps: h.copy(self.big2.t[:, 1, jb * 4:(jb + 1) * 4, :], ps.t[:, :].rearrange("p (j n) -> p j n", n=128)), reads=[ps.d], writes=[self.big2_d1])
                    dst = A["ssm_s"][q] if ty == "S" else A["ssm_p"][b]
                    for j in range(16):
                        P.dma("pool", self.ds_stgo, dst[2 * j:2 * j + 2].rearrange("h2 p n -> (h2 p) n"), self.big2.t[:, 1, j, :], reads=[self.big2_d1])
            P.op("pool", lambda h: h.memset(self.ssq.t[:], 0.0), writes=[self.ssq.d])
            P.op("act", lambda h: h.activation(self.xw.t[:], self.z_tm.t[:], AF.Tanh, scale=0.5), reads=[self.z_tm.d], writes=[self.xw.d])
            P.op("dve", lambda h: h.scalar_tensor_tensor(self.z_tm.t[:], self.xw.t[:], 1.0, self.z_tm.t[:], ALU.add, ALU.mult), reads=[self.xw.d, self.z_tm.d], writes=[self.z_tm.d])
            P.op("dve", lambda h: h.scalar_tensor_tensor(self.big1.t[:, 0:2048], self.big1.t[:, 0:2048], 0.5, self.z_tm.t[:], ALU.mult, ALU.mult), reads=[self.big1.d, self.z_tm.d], writes=[self.big1.d])
            for g in range(4):
                P.op("act", lambda h, g=g: h.activation(self.sz.t[:], self.big1.t[:, g * 512:(g + 1) * 512], AF.Square, accum_out=self.ssq.t[:, g:g + 1]), reads=[self.big1.d], writes=[self.sz.d, self.ssq.d])
            P.op("act", lambda h: h.activation(self.grs.t[:], self.ssq.t[:], AF.Ln, bias=EPS, scale=1.0 / 512.0), reads=[self.ssq.d], writes=[self.grs.d])
            P.op("act", lambda h: h.activation(self.grs.t[:], self.grs.t[:], AF.Exp, scale=-0.5), reads=[self.grs.d], writes=[self.grs.d])
            for g in range(4):
                P.op("dve", lambda h, g=g: h.tensor_scalar_mul(self.x_tm.t[:, g * 512:(g + 1) * 512], self.big1.t[:, g * 512:(g + 1) * 512], self.grs.t[:, g:g + 1]), reads=[self.big1.d, self.grs.d], writes=[self.x_tm.d])
            for g in range(4):
                pb = self.PB()
                for m in range(4):
                    mm = g * 4 + m
                    P.op("pe", lambda h, m=m, mm=mm, pb=pb: h.transpose(pb.t[:, m * 128:(m + 1) * 128], self.x_tm.t[:, mm * 128:(mm + 1) * 128], self.ident_b.t[:]),
                         reads=[self.x_tm.d, self.ident_b.d], writes=[pb.d])
                P.op("dve", lambda h, g=g, pb=pb: h.tensor_tensor(self.ynT.t[:, g * 4:(g + 1) * 4, :], pb.t[:, 0:512].rearrange("p (m t) -> p m t", t=128),
                                                                   self.gate_g.t[:, g * 4:(g + 1) * 4].unsqueeze(2).to_broadcast([128, 4, 128]), ALU.mult),
                     reads=[pb.d, self.gate_g.d], writes=[self.ynT.d])
        if "C" in ph:
            self.dense_fm(("w_out", lambda w: w), 2048, 1024, self.ynT, self.resid_add)
            self.ffn(0)

        if "D" in ph:
            cur, prv = self.pbuf, 1 - self.pbuf
            kTo, vbo = self.kT[cur], self.v_b[cur]
            self.rms(2, self.uT)
            def ev_k(ps):
                P.op("act", lambda h: h.copy(self.k_tm.t[:], ps.t[:, 0:256]), reads=[ps.d], writes=[self.k_tm.d])
                P.op("dve", lambda h: h.tensor_copy(self.k_rot.t[:], ps.t[:, 0:256]), reads=[ps.d], writes=[self.k_rot.d])
            self.dense_tm(("w_k", lambda w: w), 256, self.uT, ev_k)
            self.rope_apply(self.k_tm, self.k_rot, 4)
            P.op("act", lambda h: h.copy(self.k_b.t[:], self.k_rot.t[:]), reads=[self.k_rot.d], writes=[self.k_b.d])
            pb = self.PB()
            for k in range(4):
                P.op("pe", lambda h, k=k, pb=pb: h.transpose(pb.t[0:64, k * 128:(k + 1) * 128], self.k_b.t[:, k * 64:(k + 1) * 64], self.ident_b.t[:]),
                     reads=[self.k_b.d, self.ident_b.d], writes=[pb.d])
            P.op("dve", lambda h, pb=pb: h.tensor_copy(kTo.t[:, :, :], pb.t[0:64, 0:512].rearrange("p (k t) -> p k t", t=128)), reads=[pb.d], writes=[kTo.d])
            def ev_v(ps):
                P.op("act", lambda h: h.copy(self.v_tm.t[:], ps.t[:, 0:256]), reads=[ps.d], writes=[self.v_tm.d])
                P.op("dve", lambda h: h.tensor_copy(vbo.t[:], ps.t[:, 0:256]), reads=[ps.d], writes=[vbo.d])
            self.dense_tm(("w_v", lambda w: w), 256, self.uT, ev_v)
            if ty == "S":
                for q in range(NSQ):
                    P.dma("pool", self.ds_out, A["k_s"][q, 120:128, :], self.k_rot.t[8 * q:8 * q + 8, :], reads=[self.k_rot.d])
                    P.dma("pool", self.ds_out, A["v_s"][q, 120:128, :], self.v_tm.t[8 * q:8 * q + 8, :], reads=[self.v_tm.d])
                P.dma("pool", self.ds_cp, A["k_s"][:, 0:120, :], A["st_k"][:, 8:128, :])
                P.dma("pool", self.ds_cp, A["v_s"][:, 0:120, :], A["st_v"][:, 8:128, :])
            if last:
                P.dma("pool", self.ds_out, A["k_p"][b], self.k_rot.t[:], reads=[self.k_rot.d])
                P.dma("pool", self.ds_out, A["v_p"][b], self.v_tm.t[:], reads=[self.v_tm.d])
            if ty == "M":
                P.op("pool", lambda h: h.tensor_copy(self.kT_meta.t[:], kTo.t[:]), reads=[kTo.d], writes=[self.kT_meta.d])
                P.op("pool", lambda h: h.tensor_copy(self.v_meta.t[:], vbo.t[:]), reads=[vbo.d], writes=[self.v_meta.d])
            self.rms(3, self.uT)
            for blk in range(2):
                def ev_q(ps, blk=blk):
                    P.op("act", lambda h: h.copy(self.q_tm.t[:, blk * 512:(blk + 1) * 512], ps.t[:, :]), reads=[ps.d], writes=[self.q_tm.d])
                    P.op("dve", lambda h: h.tensor_copy(self.q_rot.t[:, blk * 512:(blk + 1) * 512], ps.t[:, :]), reads=[ps.d], writes=[self.q_rot.d])
                self.dense_tm(("w_q", lambda w, blk=blk: w[:, blk * 512:(blk + 1) * 512]), 512, self.uT, ev_q)
            self.rope_apply(self.q_tm, self.q_rot, 16)
            P.op("act", lambda h: h.copy(self.q_b.t[:], self.q_rot.t[:]), reads=[self.q_rot.d], writes=[self.q_b.d])
            for half in range(2):
                pb = self.PB()
                for hh in range(8):
                    hd = half * 8 + hh
                    P.op("pe", lambda h, hh=hh, hd=hd, pb=pb: h.transpose(pb.t[0:64, hh * 128:(hh + 1) * 128], self.q_b.t[:, hd * 64:(hd + 1) * 64], self.ident_b.t[:]),
                         reads=[self.q_b.d, self.ident_b.d], writes=[pb.d])
                P.op("dve", lambda h, half=half, pb=pb: h.tensor_copy(self.qT.t[:, half * 8:(half + 1) * 8, :], pb.t[0:64, :].rearrange("p (k t) -> p k t", t=128)),
                     reads=[pb.d], writes=[self.qT.d])
            if ty == "S":
                for q in range(NSQ):
                    P.dma("sp", self.ds_kv, self.sk32.t[:], A["st_k"][q], writes=[self.sk32.d])
                    P.dma("sp", self.ds_kv, self.sv32.t[:], A["st_v"][q], writes=[self.sv32.d])
                    P.op("act", lambda h: h.copy(self.skb.t[:], self.sk32.t[:]), reads=[self.sk32.d], writes=[self.skb.d])
                    P.op("dve", lambda h: h.tensor_copy(self.svb.t[:], self.sv32.t[:]), reads=[self.sv32.d], writes=[self.svb.d])
                    pb = self.PB()
                    for k in range(4):
                        P.op("pe", lambda h, k=k, pb=pb: h.transpose(pb.t[0:64, k * 128:(k + 1) * 128], self.skb.t[:, k * 64:(k + 1) * 64], self.ident_b.t[:]),
                             reads=[self.skb.d, self.ident_b.d], writes=[pb.d])
                    P.op("dve", lambda h, pb=pb: h.tensor_copy(self.kTs.t[:, :, :], pb.t[0:64, 0:512].rearrange("p (k t) -> p k t", t=128)), reads=[pb.d], writes=[self.kTs.d])
                    ps = self.PF()
                    for k in range(4):
                        o = ps.t[:, k * 32:(k + 1) * 32]
                        P.op("pe", lambda h, k=k, o=o, q=q: h.matmul(o.rearrange("p (j t) -> p j t", t=8), lhsT=self.kTs.t[:, k, :], rhs=self.qT.t[:, 4 * k:4 * k + 4, 8 * q:8 * q + 8], start=True, stop=False),
                             reads=[self.kTs.d, self.qT.d], writes=[ps.d])
                        P.op("pe", lambda h, k=k, o=o: h.matmul(o, lhsT=self.ident_b.t[:], rhs=self.mbstate.t[:, k * 32:(k + 1) * 32], start=False, stop=True),
                             reads=[self.ident_b.d, self.mbstate.d], writes=[ps.d])
                    P.op("act", lambda h, ps=ps: h.activation(self.PTs.t[:], ps.t[:, 0:128], AF.Exp, scale=0.125), reads=[ps.d], writes=[self.PTs.d])
                    ps2 = self.PF()
                    for k in range(4):
                        P.op("pe", lambda h, k=k, ps2=ps2: h.matmul(ps2.t[0:64, k * 32:(k + 1) * 32], lhsT=self.svb.t[:, k * 64:(k + 1) * 64], rhs=self.PTs.t[:, k * 32:(k + 1) * 32], start=True, stop=True),
                             reads=[self.svb.d, self.PTs.d], writes=[ps2.d])
                    P.op("pe", lambda h, ps2=ps2: h.matmul(ps2.t[0:64, 128:256], lhsT=self.ones_b.t[:, 0:64], rhs=self.PTs.t[:], start=True, stop=True),
                         reads=[self.ones_b.d, self.PTs.d], writes=[ps2.d])
                    P.op("dve", lambda h, q=q, ps2=ps2: h.tensor_copy(self.big2.t[0:64, 0, :, 8 * q:8 * q + 8], ps2.t[0:64, 0:128].rearrange("p (h t) -> p h t", t=8)), reads=[ps2.d], writes=[self.big2_d0])
                    P.op("dve", lambda h, q=q, ps2=ps2: h.tensor_copy(self.big2.t[0:64, 1, :, 8 * q:8 * q + 8], ps2.t[0:64, 128:256].rearrange("p (h t) -> p h t", t=8)), reads=[ps2.d], writes=[self.big2_d1])
            use_prev = ty == "P"
            if first:
                kTp, vbp = self.kT_meta, self.v_meta
            else:
                kTp, vbp = self.kT[prv], self.v_b[prv]
            mpi = 1 if first else 0
            sc = {}

            def emit_scores(hd):
                k = hd // 4
                PT = self.PT[hd % 2]
                ps = self.PF()
                if use_prev:
                    P.op("pe", lambda h, ps=ps, k=k, hd=hd: h.matmul(ps.t[:, 0:128], lhsT=kTp.t[:, k, :], rhs=self.qT.t[:, hd, :], start=True, stop=False),
                         reads=[kTp.d, self.qT.d], writes=[ps.d])
                    P.op("pe", lambda h, ps=ps: h.matmul(ps.t[:, 0:128], lhsT=self.ident_b.t[:], rhs=self.mbprev.t[:, mpi, :], start=False, stop=True),
                         reads=[self.ident_b.d, self.mbprev.d], writes=[ps.d])
                P.op("pe", lambda h, ps=ps, k=k, hd=hd: h.matmul(ps.t[:, 128:256], lhsT=kTo.t[:, k, :], rhs=self.qT.t[:, hd, :], start=True, stop=False),
                     reads=[kTo.d, self.qT.d], writes=[ps.d])
                P.op("pe", lambda h, ps=ps: h.matmul(ps.t[:, 128:256], lhsT=self.ident_b.t[:], rhs=self.mb.t[:, ti, :], start=False, stop=True),
                     reads=[self.ident_b.d, self.mb.d], writes=[ps.d])
                lo = 0 if use_prev else 128
                P.op("act", lambda h, ps=ps, PT=PT, lo=lo: h.activation(PT.t[:, lo:256], ps.t[:, lo:256], AF.Exp, scale=0.125), reads=[ps.d], writes=[PT.d])

            def emit_pv(hd, psO, psD):
                k = hd // 4
                hh = hd % 4
                PT = self.PT[hd % 2]
                oo = psO.t[0:64, hh * 128:(hh + 1) * 128]
                od = psD.t[0:64, hh * 128:(hh + 1) * 128]
                if use_prev:
                    P.op("pe", lambda h, oo=oo, k=k, PT=PT: h.matmul(oo, lhsT=vbp.t[:, k * 64:(k + 1) * 64], rhs=PT.t[:, 0:128], start=True, stop=False), reads=[vbp.d, PT.d], writes=[psO.d])
                P.op("pe", lambda h, oo=oo, k=k, PT=PT: h.matmul(oo, lhsT=vbo.t[:, k * 64:(k + 1) * 64], rhs=PT.t[:, 128:256], start=(not use_prev), stop=True), reads=[vbo.d, PT.d], writes=[psO.d])
                if use_prev:
                    P.op("pe", lambda h, od=od, PT=PT: h.matmul(od, lhsT=self.ones_b.t[:, 0:64], rhs=PT.t[:, 0:128], start=True, stop=False), reads=[self.ones_b.d, PT.d], writes=[psD.d])
                P.op("pe", lambda h, od=od, PT=PT: h.matmul(od, lhsT=self.ones_b.t[:, 0:64], rhs=PT.t[:, 128:256], start=(not use_prev), stop=True), reads=[self.ones_b.d, PT.d], writes=[psD.d])

            emit_scores(0)
            for hq in range(4):
                psO = self.PF()
                psD = self.PF()
                for hh in range(4):
                    hd = hq * 4 + hh
                    if hd + 1 < 16:
                        emit_scores(hd + 1)
                    emit_pv(hd, psO, psD)
                h0 = hq * 4
                den = self.den
                P.op("dve", lambda h, psD=psD, h0=h0: h.tensor_tensor(den.t[:, :, :], psD.t[0:64, :].rearrange("p (h t) -> p h t", t=128),
                                                                      self.esink.t[0:64, h0:h0 + 4].unsqueeze(2).to_broadcast([64, 4, 128]), ALU.add),
                     reads=[psD.d, self.esink.d], writes=[den.d])
                if ty == "S":
                    P.op("dve", lambda h, h0=h0: h.tensor_tensor(den.t[:, :, :], den.t[:, :, :], self.big2.t[0:64, 1, h0:h0 + 4, :], ALU.add), reads=[den.d, self.big2_d1], writes=[den.d])
                    P.op("dve", lambda h, h0=h0, psO=psO: h.tensor_tensor(self.big2.t[0:64, 0, h0:h0 + 4, :], self.big2.t[0:64, 0, h0:h0 + 4, :], psO.t[0:64, :].rearrange("p (h t) -> p h t", t=128), ALU.add),
                         reads=[psO.d, self.big2_d0], writes=[self.big2_d0])
                P.op("dve", lambda h: h.reciprocal(den.t[:, :, :], den.t[:, :, :]), reads=[den.d], writes=[den.d])
                if ty == "S":
                    P.op("dve", lambda h, h0=h0: h.tensor_tensor(self.oT.t[:, h0:h0 + 4, :], self.big2.t[0:64, 0, h0:h0 + 4, :], den.t[:, :, :], ALU.mult),
                         reads=[self.big2_d0, den.d], writes=[self.oT.d])
                else:
                    P.op("dve", lambda h, h0=h0, psO=psO: h.tensor_tensor(self.oT.t[:, h0:h0 + 4, :], psO.t[0:64, :].rearrange("p (h t) -> p h t", t=128), den.t[:, :, :], ALU.mult),
                         reads=[psO.d, den.d], writes=[self.oT.d])
            self.pbuf = 1 - self.pbuf
        if "E" in ph:
            self.dense_fm(("w_o", lambda w: w), 1024, 1024, self.oT, self.resid_add, part=64)
            self.ffn(1)
            if ty != "M":
                self.rms(5, self.q_tm, view3=True)
                for half in range(2):
                    ps = self.PF()
                    for m in range(4):
                        mm = half * 4 + m
                        P.op("pe", lambda h, m=m, mm=mm, ps=ps: h.transpose(ps.t[:, m * 128:(m + 1) * 128], self.q_tm.t[:, mm * 128:(mm + 1) * 128], self.ident_f.t[:]),
                             reads=[self.q_tm.d, self.ident_f.d], writes=[ps.d])
                    P.op("act", lambda h, half=half, ps=ps: h.copy(self.y_st.t[:, half * 512:(half + 1) * 512], ps.t[:, :]), reads=[ps.d], writes=[self.y_st.d])
                dst = A["y_s"] if ty == "S" else A["y_p"][b, c * 128:(c + 1) * 128, :]
                P.dma("pool", self.ds_y, dst, self.y_st.t[:], reads=[self.y_st.d])


_CACHE = {}


def _get_nc():
    if "nc" not in _CACHE:
        k = Kern()
        _CACHE["nc"] = k.build()
        _CACHE["outs"] = k.out_names
    return _CACHE["nc"]


def _col(v, nchunk):
    return np.ascontiguousarray(np.asarray(v, np.float32).reshape(nchunk, 128).T)


def kernel(x_prompt, x_sample, state_ssm, state_conv, state_k, state_v, meta_tokens,
           ssm_norm_g, ssm_w_in, ssm_conv_w, ssm_conv_b, ssm_dt_bias, ssm_A_log, ssm_D,
           ssm_gate_norm_g, ssm_w_out, kv_norm_g, w_k, w_v, attn_norm_g, w_q, attn_sinks, w_o,
           ffn_norm_g, ffn_w_gate, ffn_w_up, ffn_w_down, final_norm_g):
    f = lambda a: np.ascontiguousarray(np.asarray(a, dtype=np.float32))
    nc = _get_nc()
    cst = make_consts()
    gcols = np.stack([_col(ssm_norm_g[0], 8), _col(ffn_norm_g[0], 8), _col(kv_norm_g, 8), _col(attn_norm_g[0], 8),
                      _col(ffn_norm_g[1], 8), _col(final_norm_g, 8)], axis=1)
    gate_g = _col(ssm_gate_norm_g[0], 16)
    conv_w = np.ascontiguousarray(f(ssm_conv_w[0]).reshape(4, 24, 128).transpose(2, 1, 0))
    conv_b = _col(ssm_conv_b[0], 24)
    hp = np.zeros((128, 4, 32), np.float32)
    hp[:, 0, :] = f(ssm_dt_bias[0])[None, :]
    hp[:, 1, :] = f(ssm_A_log[0])[None, :]
    hp[:, 2, :] = f(ssm_D[0])[None, :]
    sinks = np.ascontiguousarray(np.broadcast_to(f(attn_sinks[0])[None, :], (128, 16)))
    shared = {
        "meta": f(meta_tokens), "w_in": f(ssm_w_in[0]), "w_out": f(ssm_w_out[0]), "w_k": f(w_k), "w_v": f(w_v),
        "w_q": f(w_q[0]), "w_o": f(w_o[0]), "w_gate": f(ffn_w_gate), "w_up": f(ffn_w_up), "w_down": f(ffn_w_down),
        "gcols": np.ascontiguousarray(gcols), "gate_g": gate_g, "conv_w": conv_w, "conv_b": conv_b, "hp": hp, "sinks": sinks,
        "c_tri": cst["tri"], "c_mb": cst["mb"], "c_mbprev": cst["mbprev"], "c_mbstate": cst["mbstate"],
        "c_ident_b": cst["ident_b"], "c_ident_f": cst["ident_f"], "c_ones_b": cst["ones_b"], "c_sel": cst["sel"],
        "c_mbq": cst["mbq"], "c_rowm": cst["rowm"], "c_tokmask": cst["tokmask"], "c_rope": cst["rope"],
    }
    xp, xs = f(x_prompt), f(x_sample)
    sssm, sconv, sk, sv = f(state_ssm[0]), f(state_conv[0]), f(state_k), f(state_v)
    in_maps = []
    for i in range(NCORE):
        m = dict(shared)
        m["xs"] = xs[NSQ * i:NSQ * (i + 1)].reshape(128, 1024)
        m["xp"] = xp[NPB * i:NPB * (i + 1)]
        m["st_ssm"] = sssm[NSQ * i:NSQ * (i + 1)]
        m["st_conv"] = sconv[NSQ * i:NSQ * (i + 1)].reshape(NSQ * 3, 3072)
        m["st_k"] = sk[NSQ * i:NSQ * (i + 1)].reshape(NSQ, 128, 256)
        m["st_v"] = sv[NSQ * i:NSQ * (i + 1)].reshape(NSQ, 128, 256)
        in_maps.append(m)
    res = run_bass_kernel_spmd(nc, in_maps, core_ids=list(range(NCORE)))
    R = res.results
    cat = lambda name: np.concatenate([np.asarray(r[name], np.float32) for r in R], axis=0)
    y_prompt = cat("y_p")
    y_sample = cat("y_s").reshape(128, 8, 1024)
    ssm_p = cat("ssm_p")[None]
    conv_p = cat("conv_p")[None]
    k_p = cat("k_p").reshape(16, 128, 4, 64)
    v_p = cat("v_p").reshape(16, 128, 4, 64)
    ssm_s = cat("ssm_s")[None]
    conv_s = cat("conv_s").reshape(128, 3, 3072)[None]
    k_s = cat("k_s").reshape(128, 128, 4, 64)
    v_s = cat("v_s").reshape(128, 128, 4, 64)
    return (y_prompt, y_sample, ssm_p, conv_p, k_p, v_p, ssm_s, conv_s, k_s, v_s)
```

```python
import numpy as np
from contextlib import ExitStack
import ml_dtypes
import concourse.bass as bass
import concourse.mybir as mybir
from concourse.bass_utils import run_bass_kernel_spmd

F32 = mybir.dt.float32
BF16 = mybir.dt.bfloat16
AF = mybir.ActivationFunctionType
ALU = mybir.AluOpType

SEM_CAP = 30000
NEG = -30000.0
NCORE = 8
NSQ = 16
NPB = 2
NCHUNK = 16
DFF = 2816
EPS = 1e-5


class Dep:
    __slots__ = ("name", "w", "rs", "excl")

    def __init__(self, name=""):
        self.name = name
        self.w = None
        self.rs = []
        self.excl = False


class EngS:
    def __init__(self, name, handle):
        self.name = name
        self.h = handle
        self.count = 0
        self.sems = []
        self.seen = {}
        self.ops = []


class DmaSem:
    def __init__(self, sem, name):
        self.sem = sem
        self.total = 0
        self.name = name


class Prog:
    def __init__(self, nc, stack):
        self.nc = nc
        self.stack = stack
        self.E = {
            "pe": EngS("pe", nc.tensor),
            "act": EngS("act", nc.scalar),
            "dve": EngS("dve", nc.vector),
            "pool": EngS("pool", nc.gpsimd),
            "sp": EngS("sp", nc.sync),
        }
        self.dsems = []

    def new_sem(self, name):
        return self.stack.enter_context(self.nc.semaphore(name))

    def dma_sem(self, name):
        d = DmaSem(self.new_sem("d_" + name), name)
        self.dsems.append(d)
        return d

    def _need(self, eng, tok, needs):
        if tok is None:
            return
        if tok[0] == "e":
            if tok[1] == eng.name and eng.name == "pe":
                return
            key = ("e", tok[1])
            if needs.get(key, 0) < tok[2]:
                needs[key] = tok[2]
        else:
            ds = tok[1]
            key = ("d", ds)
            v = ds.total
            if needs.get(key, 0) < v:
                needs[key] = v

    def _waits(self, eng, reads, writes):
        needs = {}
        for d in reads:
            self._need(eng, d.w, needs)
            if d.excl:
                for r in d.rs:
                    if r[0] == "e" and r[1] != eng.name:
                        self._need(eng, r, needs)
        for d in writes:
            self._need(eng, d.w, needs)
            for r in d.rs:
                self._need(eng, r, needs)
        out = []
        for key, v in needs.items():
            if eng.seen.get(key, 0) >= v:
                continue
            eng.seen[key] = v
            out.append((key, v))
        return out

    def _emit_waits(self, eng, waits):
        for key, v in waits:
            if key[0] == "e":
                src = self.E[key[1]]
                si = (v - 1) // SEM_CAP
                val = v - si * SEM_CAP
                sem = src.sems[si]
                eng.ops.append(lambda h=eng.h, sem=sem, val=val: h.wait_ge(sem, val))
            else:
                ds = key[1]
                eng.ops.append(lambda h=eng.h, sem=ds.sem, val=v: h.wait_ge(sem, val))

    def _mark(self, tok, reads, writes):
        for d in reads:
            d.rs.append(tok)
        for d in writes:
            d.w = tok
            d.rs = []

    cap = None

    def op(self, engname, fn, reads=(), writes=()):
        if self.cap is not None:
            self.cap.append(("op", engname, fn, tuple(reads), tuple(writes), None, None, None))
            return
        eng = self.E[engname]
        self._emit_waits(eng, self._waits(eng, reads, writes))
        eng.count += 1
        idx = eng.count
        si = (idx - 1) // SEM_CAP
        while len(eng.sems) <= si:
            eng.sems.append(self.new_sem(f"s_{engname}{len(eng.sems)}"))
        sem = eng.sems[si]
        eng.ops.append(lambda h=eng.h, sem=sem, fn=fn: fn(h).then_inc(sem, 1))
        self._mark(("e", engname, idx), reads, writes)

    def dma(self, qname, dsem, out, in_, reads=(), writes=(), **kw):
        if self.cap is not None:
            self.cap.append(("dma", qname, dsem, tuple(reads), tuple(writes), out, in_, kw))
            return
        eng = self.E[qname]
        self._emit_waits(eng, self._waits(eng, reads, writes))
        dsem.total += 16
        eng.ops.append(lambda h=eng.h, sem=dsem.sem, out=out, in_=in_, kw=kw:
                       h.dma_start(out=out, in_=in_, **kw).then_inc(sem, 16))
        self._mark(("d", dsem, dsem.total), reads, writes)

    def capture(self, gen):
        assert self.cap is None
        self.cap = []
        gen()
        lst, self.cap = self.cap, None
        return lst

    def replay(self, lst):
        for (kind, a, b, reads, writes, out, in_, kw) in lst:
            if kind == "op":
                self.op(a, b, reads, writes)
            else:
                self.dma(a, b, out, in_, reads, writes, **kw)

    def replay_merged(self, l1, l2):
        n1, n2 = len(l1), len(l2)
        i = j = 0
        while i < n1 or j < n2:
            if j >= n2 or (i < n1 and (i + 0.5) * n2 <= (j + 0.5) * n1):
                self.replay(l1[i:i + 1]); i += 1
            else:
                self.replay(l2[j:j + 1]); j += 1

    def finish(self):
        eng = self.E["sp"]
        for name, e in self.E.items():
            if e.count > 0:
                v = e.count
                si = (v - 1) // SEM_CAP
                eng.ops.append(lambda h=eng.h, sem=e.sems[si], val=v - si * SEM_CAP: h.wait_ge(sem, val))
        for ds in self.dsems:
            if ds.total > 0:
                eng.ops.append(lambda h=eng.h, sem=ds.sem, val=ds.total: h.wait_ge(sem, val))

    def emit(self):
        with self.nc.Block() as block:
            @block.tensor
            def _(e):
                for f in self.E["pe"].ops:
                    f()

            @block.scalar
            def _(e):
                for f in self.E["act"].ops:
                    f()

            @block.vector
            def _(e):
                for f in self.E["dve"].ops:
                    f()

            @block.gpsimd
            def _(e):
                for f in self.E["pool"].ops:
                    f()

            @block.sync
            def _(e):
                for f in self.E["sp"].ops:
                    f()


class Tl:
    __slots__ = ("t", "d", "kd")

    def __init__(self, t, name):
        self.t = t
        self.d = Dep(name)
        self.kd = None


def make_consts():
    bf = ml_dtypes.bfloat16
    i = np.arange(128)
    c = {}
    tri_std = (i[:, None] <= i[None, :]).astype(np.float32)
    same = (i[:, None] // 8 == i[None, :] // 8)
    tri_blk = tri_std * same
    c["tri"] = np.stack([tri_std, tri_blk]).astype(bf)
    mb_std = np.where(tri_std > 0, 0.0, NEG).astype(np.float32)
    mb_blk = np.where(tri_blk > 0, 0.0, NEG).astype(np.float32)
    c["mb"] = np.stack([mb_std, mb_blk]).astype(bf)
    mbp = np.where(i[:, None] > i[None, :], 0.0, NEG).astype(np.float32)
    mbp0 = np.where((i[:, None] > i[None, :]) & (i[:, None] >= 112), 0.0, NEG).astype(np.float32)
    c["mbprev"] = np.stack([mbp, mbp0]).astype(bf)
    t8 = np.arange(128) % 8
    c["mbstate"] = np.where(i[:, None] > t8[None, :], 0.0, NEG).astype(np.float32).astype(bf)
    c["ident_b"] = np.eye(128, dtype=np.float32).astype(bf)
    c["ident_f"] = np.eye(128, dtype=np.float32)
    c["ones_b"] = np.ones((128, 128), np.float32).astype(bf)
    sel = np.zeros((128, NSQ, 128), np.float32)
    mbq = np.full((128, NSQ), NEG, np.float32)
    rowm = np.zeros((128, NSQ, 128), np.float32)
    for q in range(NSQ):
        sel[8 * q:8 * q + 8, q, :] = 1.0
        mbq[8 * q:8 * q + 8, q] = 0.0
        rowm[:, q, 8 * q:8 * q + 8] = 1.0
    c["sel"] = sel.astype(bf)
    c["mbq"] = mbq
    c["rowm"] = rowm.astype(bf)
    tm = np.ones((128, 2), np.float32)
    tm[:112, 1] = 0.0
    c["tokmask"] = tm
    inv = (np.float32(500000.0) ** (-np.arange(0, 16, 2, dtype=np.float32) / np.float32(16))).astype(np.float32)
    rope = np.zeros((18, 128, 16), np.float32)
    for ty in range(18):
        if ty < 16:
            pos = 16 + 128 * ty + i
        elif ty == 16:
            pos = i - 112
        else:
            pos = 16384 + (i % 8)
        ang = pos.astype(np.float32)[:, None] * inv[None, :]
        rope[ty, :, 0:8] = np.cos(ang)
        rope[ty, :, 8:16] = np.sin(ang)
    c["rope"] = rope
    return c


class Kern:
    def __init__(self):
        self.nc = bass.Bass("TRN2", target_bir_lowering=False)
        self.in_names = []
        self.out_names = []

    def din(self, name, shape, dt=F32):
        self.in_names.append(name)
        return self.nc.dram_tensor(name, list(shape), dt, kind="ExternalInput").ap()

    def dout(self, name, shape, dt=F32):
        self.out_names.append(name)
        return self.nc.dram_tensor(name, list(shape), dt, kind="ExternalOutput").ap()

    def sb(self, name, shape, dt):
        return Tl(self.st.enter_context(self.nc.sbuf_tensor("sb_" + name, list(shape), dt)), name)

    def build(self):
        nc = self.nc
        A = {}
        A["xs"] = self.din("xs", [128, 1024])
        A["xp"] = self.din("xp", [NPB, 2048, 1024])
        A["meta"] = self.din("meta", [16, 1024])
        A["st_ssm"] = self.din("st_ssm", [NSQ, 32, 64, 128])
        A["st_conv"] = self.din("st_conv", [NSQ * 3, 3072])
        A["st_k"] = self.din("st_k", [NSQ, 128, 256])
        A["st_v"] = self.din("st_v", [NSQ, 128, 256])
        A["w_in"] = self.din("w_in", [1024, 5152])
        A["w_out"] = self.din("w_out", [2048, 1024])
        A["w_k"] = self.din("w_k", [1024, 256])
        A["w_v"] = self.din("w_v", [1024, 256])
        A["w_q"] = self.din("w_q", [1024, 1024])
        A["w_o"] = self.din("w_o", [1024, 1024])
        A["w_gate"] = self.din("w_gate", [2, 1024, DFF])
        A["w_up"] = self.din("w_up", [2, 1024, DFF])
        A["w_down"] = self.din("w_down", [2, DFF, 1024])
        A["gcols"] = self.din("gcols", [128, 6, 8])
        A["gate_g"] = self.din("gate_g", [128, 16])
        A["conv_w"] = self.din("conv_w", [128, 24, 4])
        A["conv_b"] = self.din("conv_b", [128, 24])
        A["hp"] = self.din("hp", [128, 4, 32])
        A["sinks"] = self.din("sinks", [128, 16])
        A["c_tri"] = self.din("c_tri", [2, 128, 128], BF16)
        A["c_mb"] = self.din("c_mb", [2, 128, 128], BF16)
        A["c_mbprev"] = self.din("c_mbprev", [2, 128, 128], BF16)
        A["c_mbstate"] = self.din("c_mbstate", [128, 128], BF16)
        A["c_ident_b"] = self.din("c_ident_b", [128, 128], BF16)
        A["c_ident_f"] = self.din("c_ident_f", [128, 128])
        A["c_ones_b"] = self.din("c_ones_b", [128, 128], BF16)
        A["c_sel"] = self.din("c_sel", [128, NSQ, 128], BF16)
        A["c_mbq"] = self.din("c_mbq", [128, NSQ])
        A["c_rowm"] = self.din("c_rowm", [128, NSQ, 128], BF16)
        A["c_tokmask"] = self.din("c_tokmask", [128, 2])
        A["c_rope"] = self.din("c_rope", [18, 128, 16])
        A["y_p"] = self.dout("y_p", [NPB, 2048, 1024])
        A["y_s"] = self.dout("y_s", [128, 1024])
        A["ssm_p"] = self.dout("ssm_p", [NPB, 32, 64, 128])
        A["conv_p"] = self.dout("conv_p", [NPB, 3, 3072])
        A["k_p"] = self.dout("k_p", [NPB, 128, 256])
        A["v_p"] = self.dout("v_p", [NPB, 128, 256])
        A["ssm_s"] = self.dout("ssm_s", [NSQ, 32, 64, 128])
        A["conv_s"] = self.dout("conv_s", [NSQ * 3, 3072])
        A["k_s"] = self.dout("k_s", [NSQ, 128, 256])
        A["v_s"] = self.dout("v_s", [NSQ, 128, 256])
        self.A = A
        self.WB = {}
        for wn in ["w_in", "w_out", "w_k", "w_v", "w_q", "w_o", "w_gate", "w_up", "w_down"]:
            shp = list(A[wn].shape)
            self.WB[wn] = self.nc.dram_tensor(wn + "_bf", shp, BF16, kind="Internal").ap()
        with ExitStack() as st:
            self.st = st
            self.P = Prog(nc, st)
            self.alloc()
            self.setup()
            import os
            plan = os.environ.get("KPLAN", "SMP")
            if "S" in plan:
                self.chunk("S", None, 0)
            if "M" in plan:
                self.nw_cur = self.NW + len(self.wslot_extra)
                self.chunk("M", None, 0)
            if "P" in plan:
                self.run_prompt(int(os.environ.get("KNCH", NCHUNK)))
            self.P.finish()
            self.P.emit()
        return nc

    def alloc(self):
        sb = self.sb
        nc = self.nc
        self.tri = sb("tri", [128, 2, 128], BF16)
        self.mb = sb("mb", [128, 2, 128], BF16)
        self.mbprev = sb("mbprev", [128, 2, 128], BF16)
        self.mbstate = sb("mbstate", [128, 128], BF16)
        self.ident_b = sb("ident_b", [128, 128], BF16)
        self.ident_f = sb("ident_f", [128, 128], F32)
        self.ones_b = sb("ones_b", [128, 128], BF16)
        self.sel = sb("sel", [128, NSQ, 128], BF16)
        self.mbq = sb("mbq", [128, NSQ], F32)
        self.rowm = sb("rowm", [128, NSQ, 128], BF16)
        self.tokmask = sb("tokmask", [128, 2], F32)
        self.zero1 = sb("zero1", [128, 1], F32)
        self.gcols = sb("gcols", [128, 6, 8], F32)
        self.gate_g = sb("gate_g", [128, 16], F32)
        self.conv_w = sb("conv_w", [128, 24, 4], F32)
        self.conv_b = sb("conv_b", [128, 24], F32)
        self.hp = sb("hp", [128, 4, 32], F32)
        self.negA = sb("negA", [128, 32], F32)
        self.esink = sb("esink", [128, 16], F32)
        self.sinks = sb("sinks", [128, 16], F32)
        self.ropes = [sb(f"rope{i}", [128, 16], F32) for i in range(2)]
        self.rope_i = 0
        self.cacc = [sb(f"cacc{i}", [128, 128], F32) for i in range(4)]
        self.cth = [sb(f"cth{i}", [128, 128], F32) for i in range(4)]
        self.hTs = [sb(f"hT{i}", [128, 8, 128], F32) for i in range(2)]
        self.hT = self.hTs[0]
        self.uT = sb("uT", [128, 8, 128], BF16)
        self.uT.kd = [Dep(f"uT{k}") for k in range(8)]
        self.act_kd = [Dep(f"act{k}") for k in range(11)]
        self.rstd = sb("rstd", [128, 128], F32)
        self.xin = sb("xin", [128, 1024], F32)
        self.z_tm = sb("z_tm", [128, 2048], BF16)
        self.xpre = sb("xpre", [128, 24, 176], BF16)
        self.uni = sb("uni", [128, 1408], F32)
        self.big1 = sb("big1", [128, 3072], F32)
        self.dtraw = sb("dtraw", [128, 32], F32)
        self.xbcT = sb("xbcT", [128, 24, 128], BF16)
        self.x_tm = sb("x_tm", [128, 2048], BF16)
        self.xdt = sb("xdt", [128, 2048], BF16)
        self.xw = sb("xw", [128, 2048], BF16)
        self.B_tm = sb("B_tm", [128, 512], BF16)
        self.dt = sb("dt", [128, 32], F32)
        self.a32 = sb("a32", [128, 32], F32)
        self.ahi = sb("ahi", [128, 32], BF16)
        self.alo = sb("alo", [128, 32], BF16)
        self.negcs = sb("negcs", [128, 32], F32)
        self.ecs = sb("ecs", [128, 32], F32)
        self.dA = sb("dA", [128, 32], F32)
        self.wdec = sb("wdec", [128, 32], F32)
        self.cbT = sb("cbT", [128, 4, 128], F32)
        self.dec = [sb(f"dec{i}", [128, 4, 128], BF16) for i in range(2)]
        self.MT = [sb(f"MT{i}", [128, 4, 128], BF16) for i in range(2)]
        self.tmpg = [sb(f"tmpg{i}", [128, 512], F32) for i in range(2)]
        self.CTq = sb("CTq", [128, 4, 128], BF16)
        self.ST = sb("ST", [128, 2048], F32)
        self.STb = sb("STb", [128, 2048], BF16)
        self.ST_meta = sb("ST_meta", [128, 2048], F32)
        self.big2 = sb("big2", [128, 2, 16, 128], F32)
        self.big2_d0 = Dep("big2_in")
        self.big2_d1 = Dep("big2_out")
        self.sz = sb("sz", [128, 512], F32)
        self.ssq = sb("ssq", [128, 4], F32)
        self.grs = sb("grs", [128, 4], F32)
        self.ynT = sb("ynT", [128, 16, 128], BF16)
        self.fth = [sb(f"fth{i}", [128, 256], F32) for i in range(2)]
        self.actT = sb("actT", [128, 22, 128], BF16)
        self.k_tm = sb("k_tm", [128, 256], F32)
        self.k_rot = sb("k_rot", [128, 256], F32)
        self.k_b = sb("k_b", [128, 256], BF16)
        self.v_tm = sb("v_tm", [128, 256], F32)
        self.v_b = [sb(f"v_b{i}", [128, 256], BF16) for i in range(2)]
        self.kT = [sb(f"kT{i}", [64, 4, 128], BF16) for i in range(2)]
        self.kT_meta = sb("kT_meta", [64, 4, 128], BF16)
        self.v_meta = sb("v_meta", [128, 256], BF16)
        self.cvtail_meta = sb("cvtail_meta", [128, 24, 3], BF16)
        self.q_tm = sb("q_tm", [128, 1024], F32)
        self.q_rot = sb("q_rot", [128, 1024], F32)
        self.q_b = sb("q_b", [128, 1024], BF16)
        self.qT = sb("qT", [64, 16, 128], BF16)
        self.rtmp = sb("rtmp", [128, 4, 16, 8], F32)
        self.PT = [sb(f"PT{i}", [128, 256], BF16) for i in range(2)]
        self.oT = sb("oT", [64, 16, 128], BF16)
        self.den = sb("den", [64, 4, 128], F32)
        class _V:
            def __init__(s_, ap, name):
                s_.t, s_.d = ap, Dep(name)
        self.u2 = sb("u2", [128, 1088], F32)
        u2 = self.u2.t
        self.sk32 = _V(u2[:, 0:256], "sk32")
        self.sv32 = _V(u2[:, 256:512], "sv32")
        self.skb = _V(u2[:, 512:640].bitcast(BF16), "skb")
        self.svb = _V(u2[:, 640:768].bitcast(BF16), "svb")
        self.kTs = _V(u2[0:64, 768:1024].bitcast(BF16).rearrange("p (k t) -> p k t", t=128), "kTs")
        self.PTs = _V(u2[:, 1024:1088].bitcast(BF16), "PTs")
        self.y_st = self.xin
        import os
        self.NW = int(os.environ.get("KNW", "7"))
        self.wslot = [sb(f"wslot{i}", [128, 2048], BF16) for i in range(self.NW)]
        self.wsem = [self.P.dma_sem(f"w{i}") for i in range(self.NW)]
        self.wi = 0
        self.nw_cur = self.NW

        class _SV:
            def __init__(s_, ap, name, first):
                s_.t, s_.d, s_.first = ap, Dep(name), list(first)
        b2 = self.big2.t[:, :, :, :].rearrange("p a b c -> p (a b c)").bitcast(BF16)
        self.wslot_extra = [_SV(b2[:, i * 2048:(i + 1) * 2048], f"wx{i}", [self.big2_d0]) for i in range(2)]
        self.wslot_extra.append(_SV(self.sel.t[:, :, :].rearrange("p a b -> p (a b)"), "wx4", [self.sel.d]))
        self.wslot_extra.append(_SV(self.rowm.t[:, :, :].rearrange("p a b -> p (a b)"), "wx5", [self.rowm.d]))
        self.wslot_extra.append(_SV(self.u2.t[:, 0:1024].bitcast(BF16), "wx6", [self.sk32.d, self.sv32.d, self.skb.d, self.svb.d, self.kTs.d]))
        self.wsem_extra = [self.P.dma_sem(f"wx{i}") for i in range(len(self.wslot_extra))]
        self.psf = [Tl(self.st.enter_context(nc.psum_tensor(f"psf{i}", [128, 512], F32)), f"psf{i}") for i in range(6)]
        self.psb = [Tl(self.st.enter_context(nc.psum_tensor(f"psb{i}", [128, 1024], BF16)), f"psb{i}") for i in range(2)]
        self.pfi = 0
        self.pbi = 0
        for t in self.psf + self.psb:
            t.d.excl = True
        self.ds_in = self.P.dma_sem("in")
        self.ds_x = self.P.dma_sem("x")
        self.ds_stg = self.P.dma_sem("stg")
        self.ds_stgo = self.P.dma_sem("stgo")
        self.ds_kv = self.P.dma_sem("kv")
        self.ds_out = self.P.dma_sem("out")
        self.ds_y = self.P.dma_sem("y")
        self.ds_cp = self.P.dma_sem("cp")
        self.ds_rope = self.P.dma_sem("rope")
        self.pbuf = 0

    def stage(self, n):
        if n > self.kstage:
            raise StopIteration

    ps_pool = "all"

    def PF(self):
        if self.ps_pool == "all":
            t = self.psf[self.pfi % 6]
        elif self.ps_pool == "lo":
            t = self.psf[self.pfi % 3]
        else:
            t = self.psf[3 + self.pfi % 3]
        self.pfi += 1
        return t

    def run_prompt(self, nch):
        self.nw_cur = self.NW + len(self.wslot_extra)
        P = self.P
        seq = [(b, c) for b in range(NPB) for c in range(nch)]
        prevE = None
        for i, (b, c) in enumerate(seq):
            hs = i % 2
            self.chunk("P", b, c, ph="A", hsel=hs)
            if prevE is None:
                self.chunk("P", b, c, ph="B", hsel=hs)
            else:
                pb_, pc_, phs = prevE
                self.ps_pool = "lo"
                lE = P.capture(lambda: self.chunk("P", pb_, pc_, ph="E", hsel=phs))
                self.ps_pool = "hi"
                lB = P.capture(lambda: self.chunk("P", b, c, ph="B", hsel=hs))
                self.ps_pool = "all"
                P.replay_merged(lE, lB)
            self.chunk("P", b, c, ph="C", hsel=hs)
            self.chunk("P", b, c, ph="D", hsel=hs)
            prevE = (b, c, hs)
        pb_, pc_, phs = prevE
        self.chunk("P", pb_, pc_, ph="E", hsel=phs)

    def PB(self):
        t = self.psb[self.pbi % 2]
        self.pbi += 1
        return t

    def setup(self):
        P, A = self.P, self.A
        ld = lambda tl, src: P.dma("sp", self.ds_in, tl.t[:], src, writes=[tl.d])
        ld(self.tri, A["c_tri"].rearrange("a p n -> p a n"))
        ld(self.mb, A["c_mb"].rearrange("a p n -> p a n"))
        ld(self.mbprev, A["c_mbprev"].rearrange("a p n -> p a n"))
        ld(self.mbstate, A["c_mbstate"])
        ld(self.ident_b, A["c_ident_b"])
        ld(self.ident_f, A["c_ident_f"])
        ld(self.ones_b, A["c_ones_b"])
        ld(self.sel, A["c_sel"])
        ld(self.mbq, A["c_mbq"])
        ld(self.rowm, A["c_rowm"])
        ld(self.tokmask, A["c_tokmask"])
        ld(self.gcols, A["gcols"])
        ld(self.gate_g, A["gate_g"])
        ld(self.conv_w, A["conv_w"])
        ld(self.conv_b, A["conv_b"])
        ld(self.hp, A["hp"])
        ld(self.sinks, A["sinks"])
        P.op("pool", lambda h: h.memset(self.zero1.t[:], 0.0), writes=[self.zero1.d])
        self.wdep = {}
        for wn in ["w_in", "w_out", "w_gate", "w_up", "w_down", "w_k", "w_v", "w_q", "w_o"]:
            self.wdep[wn] = Dep("wb_" + wn)
            self.ds_cast = P.dma_sem("cast_" + wn)
            src, dst = A[wn], self.WB[wn]
            if len(src.shape) == 3:
                for l in range(2):
                    P.dma("pool", self.ds_cast, dst[l].rearrange("(p r) n -> p r n", p=128), src[l].rearrange("(p r) n -> p r n", p=128), writes=[self.wdep[wn]])
            else:
                P.dma("pool", self.ds_cast, dst.rearrange("(p r) n -> p r n", p=128), src.rearrange("(p r) n -> p r n", p=128), writes=[self.wdep[wn]])
        P.op("act", lambda h: h.activation(self.negA.t[:], self.hp.t[:, 1, :], AF.Exp), reads=[self.hp.d], writes=[self.negA.d])
        P.op("dve", lambda h: h.tensor_scalar_mul(self.negA.t[:], self.negA.t[:], -1.0), reads=[self.negA.d], writes=[self.negA.d])
        P.op("act", lambda h: h.activation(self.esink.t[:], self.sinks.t[:], AF.Exp), reads=[self.sinks.d], writes=[self.esink.d])
        P.op("dve", lambda h: h.tensor_scalar_mul(self.conv_w.t[:], self.conv_w.t[:], 0.5), reads=[self.conv_w.d], writes=[self.conv_w.d])
        P.op("dve", lambda h: h.tensor_scalar_mul(self.conv_b.t[:], self.conv_b.t[:], 0.5), reads=[self.conv_b.d], writes=[self.conv_b.d])

    def wload(self, srcspec, r0, r1, ncols, part=128):
        P = self.P
        i = self.wi % self.nw_cur
        self.wi += 1
        if i < self.NW:
            sl, wsem, extra = self.wslot[i], self.wsem[i], []
        else:
            sl, wsem = self.wslot_extra[i - self.NW], self.wsem_extra[i - self.NW]
            extra, sl.first = sl.first, []
        kc = (r1 - r0) // part
        assert kc * ncols <= 2048
        view = sl.t[0:part, 0:kc * ncols].rearrange("p (c n) -> p c n", n=ncols)
        wn, sel = srcspec
        src = sel(self.WB[wn])[r0:r1, :]
        P.dma("sp", wsem, view, src.rearrange("(c p) n -> p c n", p=part), reads=[self.wdep[wn]], writes=[sl.d] + extra)
        return view, sl

    def rms(self, gi, out_tl, view3=False):
        P = self.P
        hT, sq, rstd = self.hT, self.actT, self.rstd
        P.op("act", lambda h: h.activation(sq.t[:, 0:8, :], hT.t[:], AF.Square), reads=[hT.d], writes=[sq.d])
        ps = self.PF()
        for c in range(8):
            P.op("pe", lambda h, c=c: h.matmul(ps.t[:, 0:128], lhsT=self.ones_b.t[:], rhs=sq.t[:, c, :], start=(c == 0), stop=(c == 7)),
                 reads=[sq.d, self.ones_b.d], writes=[ps.d])
        P.op("act", lambda h: h.activation(rstd.t[:], ps.t[:, 0:128], AF.Ln, bias=EPS, scale=1.0 / 1024.0), reads=[ps.d], writes=[rstd.d])
        P.op("act", lambda h: h.activation(rstd.t[:], rstd.t[:], AF.Exp, scale=-0.5), reads=[rstd.d], writes=[rstd.d])
        for c in range(8):
            oc = out_tl.t[:, c * 128:(c + 1) * 128] if view3 else out_tl.t[:, c, :]
            P.op("dve", lambda h, c=c, oc=oc: h.scalar_tensor_tensor(oc, hT.t[:, c, :], self.gcols.t[:, gi, c:c + 1], rstd.t[:], ALU.mult, ALU.mult),
                 reads=[hT.d, rstd.d, self.gcols.d], writes=[out_tl.kd[c] if out_tl.kd else out_tl.d])

    def dense_fm(self, src, K, ncols_total, xT_tl, evac, blk=None, part=128):
        P = self.P
        kc = K // part
        nb0 = 256
        ksubs = [(k0, min(k0 + 8, kc)) for k0 in range(0, kc, 8)]
        for c0 in range(0, ncols_total, nb0):
            nb = min(nb0, ncols_total - c0)
            loaded = []
            for (k0, k1) in ksubs:
                wv, wsl = self.wload((src[0], lambda w, c0=c0, nb=nb, f=src[1]: f(w)[:, c0:c0 + nb]), k0 * part, k1 * part, nb, part=part)
                loaded.append((k0, k1, wv, wsl))
            for m in range(nb // 128):
                ps = self.PF()
                for (k0, k1, wv, wsl) in loaded:
                    for k in range(k0, k1):
                        rhs = xT_tl.t[0:part, k, :]
                        P.op("pe", lambda h, k=k, k0=k0, m=m, rhs=rhs, wv=wv, ps=ps: h.matmul(ps.t[:, 0:128], lhsT=wv[:, k - k0, m * 128:(m + 1) * 128], rhs=rhs, start=(k == 0), stop=(k == kc - 1)),
                             reads=[wsl.d, xT_tl.kd[k] if xT_tl.kd else xT_tl.d], writes=[ps.d])
                evac((c0 // 128) + m, ps)

    def dense_tm(self, src, ncols, xT_tl, evac):
        P = self.P
        ps = self.PF()
        for c0 in range(0, ncols, 256):
            nb = min(256, ncols - c0)
            wv, wsl = self.wload((src[0], lambda w, c0=c0, nb=nb, f=src[1]: f(w)[:, c0:c0 + nb]), 0, 1024, nb)
            for k in range(8):
                P.op("pe", lambda h, k=k, c0=c0, nb=nb, wv=wv: h.matmul(ps.t[:, c0:c0 + nb], lhsT=xT_tl.t[:, k, :], rhs=wv[:, k, :], start=(k == 0), stop=(k == 7)),
                     reads=[wsl.d, xT_tl.kd[k] if xT_tl.kd else xT_tl.d], writes=[ps.d])
        evac(ps)

    def resid_add(self, m, ps):
        hT = self.hT
        self.P.op("dve", lambda h: h.tensor_tensor(hT.t[:, m, :], hT.t[:, m, :], ps.t[:, 0:128], ALU.add), reads=[ps.d, hT.d], writes=[hT.d])

    def ffn(self, layer):
        P, A = self.P, self.A
        self.rms(1 if layer == 0 else 4, self.uT)
        wd = ("w_down", lambda w: w[layer])
        act_tm = self.uni.t[:].bitcast(BF16)
        uT = self.uT
        akd = self.act_kd

        def emit_tr(gi_, m0, n):
            ps = self.PF()
            pbv = ps.t[:, :].bitcast(BF16)
            for j in range(n):
                m = m0 + j
                P.op("pe", lambda h, j=j, m=m, pbv=pbv: h.transpose(pbv[:, j * 128:(j + 1) * 128], act_tm[:, m * 128:(m + 1) * 128], self.ident_b.t[:]),
                     reads=[akd[m // 2], self.uni.d, self.ident_b.d], writes=[ps.d])
            src = pbv[:, 0:n * 128].rearrange("p (m t) -> p m t", t=128)
            if gi_ == 1:
                P.op("act", lambda h, m0=m0, n=n, src=src: h.copy(self.actT.t[:, m0:m0 + n, :], src), reads=[ps.d], writes=[self.actT.d])
            else:
                P.op("dve", lambda h, m0=m0, n=n, src=src: h.tensor_copy(self.actT.t[:, m0:m0 + n, :], src), reads=[ps.d], writes=[self.actT.d])
        for bi, c0 in enumerate(range(0, DFF, 256)):
            nb = min(256, DFF - c0)
            gv, gsl = self.wload(("w_gate", lambda w, c0=c0, nb=nb: w[layer][:, c0:c0 + nb]), 0, 1024, nb)
            uv, usl = self.wload(("w_up", lambda w, c0=c0, nb=nb: w[layer][:, c0:c0 + nb]), 0, 1024, nb)
            psg = self.PF()
            psu = self.PF()
            for k in range(8):
                P.op("pe", lambda h, k=k, gv=gv, psg=psg, nb=nb: h.matmul(psg.t[:, 0:nb], lhsT=uT.t[:, k, :], rhs=gv[:, k, :], start=(k == 0), stop=(k == 7)),
                     reads=[gsl.d, uT.kd[k]], writes=[psg.d])
            for k in range(8):
                P.op("pe", lambda h, k=k, uv=uv, psu=psu, nb=nb: h.matmul(psu.t[:, 0:nb], lhsT=uT.t[:, k, :], rhs=uv[:, k, :], start=(k == 0), stop=(k == 7)),
                     reads=[usl.d, uT.kd[k]], writes=[psu.d])
            th = self.fth[bi % 2]
            P.op("act", lambda h, th=th, psg=psg, nb=nb: h.activation(th.t[:, 0:nb], psg.t[:, 0:nb], AF.Tanh, scale=0.5), reads=[psg.d], writes=[th.d])
            P.op("dve", lambda h, th=th, psg=psg, nb=nb: h.scalar_tensor_tensor(th.t[:, 0:nb], th.t[:, 0:nb], 1.0, psg.t[:, 0:nb], ALU.add, ALU.mult), reads=[th.d, psg.d], writes=[th.d])
            P.op("dve", lambda h, th=th, psu=psu, nb=nb, c0=c0: h.scalar_tensor_tensor(act_tm[:, c0:c0 + nb], th.t[:, 0:nb], 0.5, psu.t[:, 0:nb], ALU.mult, ALU.mult),
                 reads=[th.d, psu.d, self.uni.d], writes=[akd[bi]])
            if bi == 4:
                emit_tr(0, 0, 8)
            elif bi == 8:
                emit_tr(1, 8, 8)
        emit_tr(2, 16, 6)
        self.dense_fm(wd, DFF, 1024, self.actT, self.resid_add)

    def rope_apply(self, src, dst, nh):
        P = self.P
        s3 = src.t[:, :].rearrange("p (h d) -> p h d", d=64)
        d3 = dst.t[:, :].rearrange("p (h d) -> p h d", d=64)
        cos = self.rope.t[:, 0:8].unsqueeze(1).to_broadcast([128, nh, 8])
        sin = self.rope.t[:, 8:16].unsqueeze(1).to_broadcast([128, nh, 8])
        x1, x2 = s3[:, :, 0:8], s3[:, :, 8:16]
        t = [self.rtmp.t[:, i, 0:nh, :] for i in range(4)]
        rd = [src.d, self.rope.d]
        P.op("dve", lambda h: h.tensor_tensor(t[0], x1, cos, ALU.mult), reads=rd, writes=[self.rtmp.d])
        P.op("dve", lambda h: h.tensor_tensor(t[1], x2, sin, ALU.mult), reads=rd, writes=[self.rtmp.d])
        P.op("dve", lambda h: h.tensor_tensor(t[2], x2, cos, ALU.mult), reads=rd, writes=[self.rtmp.d])
        P.op("dve", lambda h: h.tensor_tensor(t[3], x1, sin, ALU.mult), reads=rd, writes=[self.rtmp.d])
        P.op("dve", lambda h: h.tensor_tensor(d3[:, :, 0:8], t[0], t[1], ALU.subtract), reads=[self.rtmp.d], writes=[dst.d])
        P.op("dve", lambda h: h.tensor_tensor(d3[:, :, 8:16], t[2], t[3], ALU.add), reads=[self.rtmp.d], writes=[dst.d])

    def chunk(self, ty, b, c, ph="ABCDE", hsel=0):
        P, A = self.P, self.A
        self.hT = self.hTs[hsel]
        ti = 1 if ty == "S" else 0
        nseq = NSQ if ty == "S" else 1
        first = (ty == "P" and c == 0)
        last = (ty == "P" and c == NCHUNK - 1)
        if "A" in ph:
            xin = self.xin
            if ty == "S":
                P.dma("sp", self.ds_x, xin.t[:], A["xs"], writes=[xin.d])
            elif ty == "M":
                P.op("pool", lambda h: h.memset(xin.t[:], 0.0), writes=[xin.d])
                P.dma("sp", self.ds_x, xin.t[112:128, :], A["meta"], writes=[xin.d])
            else:
                P.dma("sp", self.ds_x, xin.t[:], A["xp"][b, c * 128:(c + 1) * 128, :], writes=[xin.d])
            rty = 17 if ty == "S" else (16 if ty == "M" else c)
            self.rope = self.ropes[self.rope_i % 2]
            self.rope_i += 1
            P.dma("sp", self.ds_rope, self.rope.t[:], A["c_rope"][rty], writes=[self.rope.d])
            for half in range(2):
                ps = self.PF()
                for m in range(4):
                    mm = half * 4 + m
                    P.op("pe", lambda h, m=m, mm=mm, ps=ps: h.transpose(ps.t[:, m * 128:(m + 1) * 128], xin.t[:, mm * 128:(mm + 1) * 128], self.ident_f.t[:]),
                         reads=[xin.d, self.ident_f.d], writes=[ps.d])
                hTc = self.hT
                P.op("act", lambda h, half=half, ps=ps, hTc=hTc: h.copy(hTc.t[:, half * 4:(half + 1) * 4, :], ps.t[:, :].rearrange("p (m t) -> p m t", t=128)),
                     reads=[ps.d], writes=[hTc.d])

            if ty == "M":
                P.op("pool", lambda h: h.memset(self.xpre.t[:, :, 0:3], 0.0), writes=[self.xpre.d])
            if first:
                P.op("pool", lambda h: h.tensor_copy(self.xpre.t[:, :, 0:3], self.cvtail_meta.t[:]), reads=[self.cvtail_meta.d], writes=[self.xpre.d])
            elif ty == "P":
                P.op("pool", lambda h: h.tensor_copy(self.xpre.t[:, :, 0:3], self.xpre.t[:, :, 128:131]), reads=[self.xpre.d], writes=[self.xpre.d])
            if ty == "S":
                P.dma("sp", self.ds_stg, self.big1.t[0:48, :], A["st_conv"], writes=[self.big1.d])
                for m in range(24):
                    ps = self.PF()
                    P.op("pe", lambda h, m=m, ps=ps: h.transpose(ps.t[:, 0:48], self.big1.t[0:48, m * 128:(m + 1) * 128], self.ident_f.t[0:48, 0:48]),
                         reads=[self.big1.d, self.ident_f.d], writes=[ps.d])
                    P.op("dve", lambda h, m=m, ps=ps: h.tensor_copy(self.xpre.t[:, m, :].rearrange("p (q t) -> p q t", t=11)[:, :, 0:3],
                                                                      ps.t[:, 0:48].rearrange("p (q r) -> p q r", r=3)),
                         reads=[ps.d], writes=[self.xpre.d])

            self.rms(0, self.uT)
            w_in = A["w_in"]
            for blk in range(4):
                def ev(ps, blk=blk):
                    P.op("act", lambda h: h.copy(self.z_tm.t[:, blk * 512:(blk + 1) * 512], ps.t[:, :]), reads=[ps.d], writes=[self.z_tm.d])
                self.dense_tm(("w_in", lambda w, blk=blk: w[:, blk * 512:(blk + 1) * 512]), 512, self.uT, ev)
            if ty == "S":
                def ev_x(m, ps):
                    P.op("dve", lambda h: h.tensor_copy(self.xpre.t[:, m, :].rearrange("p (q t) -> p q t", t=11)[:, :, 3:11],
                                                 ps.t[:, 0:128].rearrange("p (q t) -> p q t", t=8)), reads=[ps.d], writes=[self.xpre.d])
                    P.op("dve", lambda h: h.tensor_copy(self.uni.t[:, 0:1152].rearrange("p (m r) -> p m r", r=48)[:, m, :].rearrange("p (q r) -> p q r", r=3),
                                                        ps.t[:, 0:128].rearrange("p (q t) -> p q t", t=8)[:, :, 5:8]), reads=[ps.d], writes=[self.uni.d])
            else:
                def ev_x(m, ps):
                    P.op("dve", lambda h: h.tensor_copy(self.xpre.t[:, m, 3:131], ps.t[:, 0:128]), reads=[ps.d], writes=[self.xpre.d])
                    if last:
                        P.op("dve", lambda h: h.tensor_copy(self.uni.t[:, 0:1152].rearrange("p (m r) -> p m r", r=48)[:, m, 0:3], ps.t[:, 125:128]), reads=[ps.d], writes=[self.uni.d])
            self.dense_fm(("w_in", lambda w: w[:, 2048:5120]), 1024, 3072, self.uT, ev_x)
            def ev_dt(ps):
                P.op("dve", lambda h: h.tensor_copy(self.dtraw.t[:], ps.t[:, 0:32]), reads=[ps.d], writes=[self.dtraw.d])
            self.dense_tm(("w_in", lambda w: w[:, 5120:5152]), 32, self.uT, ev_dt)
            if ty == "M":
                P.op("pool", lambda h: h.tensor_copy(self.cvtail_meta.t[:], self.xpre.t[:, :, 128:131]), reads=[self.xpre.d], writes=[self.cvtail_meta.d])
            if ty == "S" or last:
                nr = 48 if ty == "S" else 3
                for m in range(24):
                    ps = self.PF()
                    P.op("pe", lambda h, m=m, ps=ps: h.transpose(ps.t[0:nr, 0:128], self.uni.t[:, 0:1152].rearrange("p (m r) -> p m r", r=48)[:, m, 0:nr], self.ident_f.t[:]),
                         reads=[self.uni.d, self.ident_f.d], writes=[ps.d])
                    P.op("dve", lambda h, m=m, ps=ps: h.tensor_copy(self.big1.t[0:nr, m * 128:(m + 1) * 128], ps.t[0:nr, 0:128]), reads=[ps.d], writes=[self.big1.d])
                dst = A["conv_s"] if ty == "S" else A["conv_p"][b]
                P.dma("pool", self.ds_out, dst, self.big1.t[0:nr, :], reads=[self.big1.d])

        if "B" in ph:
            if ty == "M":
                P.op("pool", lambda h: h.memset(self.ST.t[:], 0.0), writes=[self.ST.d])
                P.op("pool", lambda h: h.memset(self.STb.t[:], 0.0), writes=[self.STb.d])
            if first:
                P.op("dve", lambda h: h.tensor_copy(self.ST.t[:], self.ST_meta.t[:]), reads=[self.ST_meta.d], writes=[self.ST.d])
                P.op("act", lambda h: h.copy(self.STb.t[:], self.ST_meta.t[:]), reads=[self.ST_meta.d], writes=[self.STb.d])
            for mg in range(6):
                for k in range(4):
                    for mi in range(4):
                        m = mg * 4 + mi
                        acc = self.cacc[mi]
                        if ty == "S":
                            src = self.xpre.t[:, m, :].rearrange("p (q t) -> p q t", t=11)[:, :, k:k + 8]
                            out = acc.t[:, :].rearrange("p (q t) -> p q t", t=8)
                        else:
                            src = self.xpre.t[:, m, k:k + 128]
                            out = acc.t[:, :]
                        wk = self.conv_w.t[:, m, k:k + 1]
                        if k == 0:
                            bk = self.conv_b.t[:, m:m + 1]
                            P.op("dve", lambda h, src=src, out=out, wk=wk, bk=bk: h.tensor_scalar(out, src, wk, bk, ALU.mult, ALU.add), reads=[self.xpre.d, self.conv_w.d, self.conv_b.d], writes=[acc.d])
                        else:
                            P.op("dve", lambda h, src=src, out=out, wk=wk: h.scalar_tensor_tensor(out, src, wk, out, ALU.mult, ALU.add), reads=[self.xpre.d, self.conv_w.d, acc.d], writes=[acc.d])
                for mi in range(4):
                    m = mg * 4 + mi
                    acc = self.cacc[mi]
                    th = self.cth[mi]
                    P.op("act", lambda h, acc=acc, th=th: h.activation(th.t[:, :], acc.t[:, :], AF.Tanh), reads=[acc.d], writes=[th.d])
                    P.op("dve", lambda h, m=m, acc=acc, th=th: h.scalar_tensor_tensor(self.xbcT.t[:, m, :], th.t[:, :], 1.0, acc.t[:, :], ALU.add, ALU.mult),
                         reads=[acc.d, th.d], writes=[self.xbcT.d])
            for half in range(2):
                pb = self.PB()
                for m in range(8):
                    mm = half * 8 + m
                    P.op("pe", lambda h, m=m, mm=mm, pb=pb: h.transpose(pb.t[:, m * 128:(m + 1) * 128], self.xbcT.t[:, mm, :], self.ident_b.t[:]),
                         reads=[self.xbcT.d, self.ident_b.d], writes=[pb.d])
                P.op("act", lambda h, half=half, pb=pb: h.copy(self.x_tm.t[:, half * 1024:(half + 1) * 1024], pb.t[:, :]), reads=[pb.d], writes=[self.x_tm.d])
            pb = self.PB()
            for m in range(4):
                P.op("pe", lambda h, m=m, pb=pb: h.transpose(pb.t[:, m * 128:(m + 1) * 128], self.xbcT.t[:, 16 + m, :], self.ident_b.t[:]),
                     reads=[self.xbcT.d, self.ident_b.d], writes=[pb.d])
            P.op("dve", lambda h, pb=pb: h.tensor_copy(self.B_tm.t[:], pb.t[:, 0:512]), reads=[pb.d], writes=[self.B_tm.d])
            dt, a32 = self.dt, self.a32
            P.op("dve", lambda h: h.tensor_tensor(dt.t[:], self.dtraw.t[:], self.hp.t[:, 0, :], ALU.add), reads=[self.dtraw.d, self.hp.d], writes=[dt.d])
            P.op("act", lambda h: h.activation(dt.t[:], dt.t[:], AF.Exp), reads=[dt.d], writes=[dt.d])
            P.op("act", lambda h: h.activation(dt.t[:], dt.t[:], AF.Ln, bias=1.0), reads=[dt.d], writes=[dt.d])
            tmcol = self.tokmask.t[:, 1:2] if ty == "M" else self.tokmask.t[:, 0:1]
            P.op("dve", lambda h: h.tensor_scalar_mul(dt.t[:], dt.t[:], tmcol), reads=[dt.d, self.tokmask.d], writes=[dt.d])
            P.op("dve", lambda h: h.tensor_tensor(a32.t[:], dt.t[:], self.negA.t[:], ALU.mult), reads=[dt.d, self.negA.d], writes=[a32.d])
            P.op("dve", lambda h: h.tensor_copy(self.ahi.t[:], a32.t[:]), reads=[a32.d], writes=[self.ahi.d])
            P.op("dve", lambda h: h.tensor_tensor(self.alo.t[:], a32.t[:], self.ahi.t[:], ALU.subtract), reads=[a32.d, self.ahi.d], writes=[self.alo.d])
            ps = self.PF()
            P.op("pe", lambda h, ps=ps: h.matmul(ps.t[:, 0:32], lhsT=self.tri.t[:, ti, :], rhs=self.ahi.t[:], start=True, stop=False), reads=[self.tri.d, self.ahi.d], writes=[ps.d])
            P.op("pe", lambda h, ps=ps: h.matmul(ps.t[:, 0:32], lhsT=self.tri.t[:, ti, :], rhs=self.alo.t[:], start=False, stop=True), reads=[self.tri.d, self.alo.d], writes=[ps.d])
            P.op("dve", lambda h, ps=ps: h.tensor_scalar_mul(self.negcs.t[:], ps.t[:, 0:32], -1.0), reads=[ps.d], writes=[self.negcs.d])
            P.op("act", lambda h, ps=ps: h.activation(self.ecs.t[:], ps.t[:, 0:32], AF.Exp), reads=[ps.d], writes=[self.ecs.d])
            x3 = self.x_tm.t[:, :].rearrange("p (h d) -> p h d", d=64)
            P.op("dve", lambda h: h.tensor_tensor(self.xdt.t[:, :].rearrange("p (h d) -> p h d", d=64), x3, dt.t[:].unsqueeze(2).to_broadcast([128, 32, 64]), ALU.mult),
                 reads=[self.x_tm.d, dt.d], writes=[self.xdt.d])
            ps = self.PF()
            for g in range(4):
                P.op("pe", lambda h, g=g, ps=ps: h.matmul(ps.t[:, g * 128:(g + 1) * 128], lhsT=self.xbcT.t[:, 16 + g, :], rhs=self.xbcT.t[:, 20 + g, :], start=True, stop=True),
                     reads=[self.xbcT.d], writes=[ps.d])
            P.op("act", lambda h, ps=ps: h.copy(self.cbT.t[:, :, :], ps.t[:, :].rearrange("p (g l) -> p g l", l=128)), reads=[ps.d], writes=[self.cbT.d])
            psYs = {}

            def emit_decay(hb):
                g = hb // 2
                dec, MT = self.dec[hb % 2], self.MT[hb % 2]
                ps = self.PF()
                for hh in range(4):
                    hd = hb * 4 + hh
                    o = ps.t[:, hh * 128:(hh + 1) * 128]
                    P.op("pe", lambda h, o=o, hd=hd: h.matmul(o, lhsT=self.ahi.t[:, hd:hd + 1].to_broadcast([128, 128]), rhs=self.tri.t[:, ti, :], start=True, stop=False),
                         reads=[self.ahi.d, self.tri.d], writes=[ps.d])
                    P.op("pe", lambda h, o=o, hd=hd: h.matmul(o, lhsT=self.alo.t[:, hd:hd + 1].to_broadcast([128, 128]), rhs=self.tri.t[:, ti, :], start=False, stop=False),
                         reads=[self.alo.d, self.tri.d], writes=[ps.d])
                    P.op("pe", lambda h, o=o: h.matmul(o, lhsT=self.ident_b.t[:], rhs=self.mb.t[:, ti, :], start=False, stop=True),
                         reads=[self.ident_b.d, self.mb.d], writes=[ps.d])
                for hh in range(4):
                    hd = hb * 4 + hh
                    P.op("act", lambda h, hh=hh, hd=hd, ps=ps, dec=dec: h.activation(dec.t[:, hh, :], ps.t[:, hh * 128:(hh + 1) * 128], AF.Exp, bias=self.negcs.t[:, hd:hd + 1]),
                         reads=[ps.d, self.negcs.d], writes=[dec.d])
                P.op("dve", lambda h, g=g, dec=dec, MT=MT: h.tensor_tensor(MT.t[:, :, :], dec.t[:, :, :], self.cbT.t[:, g, :].unsqueeze(1).to_broadcast([128, 4, 128]), ALU.mult),
                     reads=[dec.d, self.cbT.d], writes=[MT.d])

            def emit_ydiag(hb):
                g = hb // 2
                MT = self.MT[hb % 2]
                if hb % 2 == 0:
                    psYs[g] = self.PF()
                psY = psYs[g]
                for hh in range(4):
                    hd = hb * 4 + hh
                    col = (hd % 8) * 64
                    P.op("pe", lambda h, hh=hh, hd=hd, col=col, MT=MT, psY=psY: h.matmul(psY.t[:, col:col + 64], lhsT=MT.t[:, hh, :], rhs=self.xdt.t[:, hd * 64:(hd + 1) * 64], start=True, stop=True),
                         reads=[MT.d, self.xdt.d], writes=[psY.d])
                if hb % 2 == 1:
                    tg = self.tmpg[g % 2]
                    P.op("dve", lambda h, g=g, tg=tg: h.tensor_tensor(tg.t[:, :].rearrange("p (h d) -> p h d", d=64), self.x_tm.t[:, g * 512:(g + 1) * 512].rearrange("p (h d) -> p h d", d=64),
                                                                     self.hp.t[:, 2, g * 8:(g + 1) * 8].unsqueeze(2).to_broadcast([128, 8, 64]), ALU.mult),
                         reads=[self.x_tm.d, self.hp.d], writes=[tg.d])
                    P.op("dve", lambda h, g=g, tg=tg, psY=psY: h.tensor_tensor(self.big1.t[:, g * 512:(g + 1) * 512], psY.t[:, :], tg.t[:], ALU.add),
                         reads=[psY.d, tg.d], writes=[self.big1.d])

            emit_decay(0)
            for hb in range(8):
                if hb + 1 < 8:
                    emit_decay(hb + 1)
                emit_ydiag(hb)
            for q in range(nseq):
                if ty == "S":
                    selq = self.sel.t[:, q, :]
                    mbcol = self.mbq.t[:, q:q + 1]
                else:
                    selq = self.ones_b.t[:]
                    mbcol = self.zero1.t[:, 0:1]
                ps = self.PF()
                P.op("pe", lambda h, ps=ps, selq=selq: h.matmul(ps.t[:, 0:32], lhsT=selq, rhs=self.ahi.t[:], start=True, stop=False), reads=[self.sel.d, self.ones_b.d, self.ahi.d], writes=[ps.d])
                P.op("pe", lambda h, ps=ps, selq=selq: h.matmul(ps.t[:, 0:32], lhsT=selq, rhs=self.alo.t[:], start=False, stop=True), reads=[self.sel.d, self.ones_b.d, self.alo.d], writes=[ps.d])
                P.op("act", lambda h, ps=ps: h.activation(self.dA.t[:], ps.t[:, 0:32], AF.Exp), reads=[ps.d], writes=[self.dA.d])
                P.op("dve", lambda h, ps=ps: h.tensor_tensor(self.wdec.t[:], ps.t[:, 0:32], self.negcs.t[:], ALU.add), reads=[ps.d, self.negcs.d], writes=[self.wdec.d])
                P.op("act", lambda h, mbcol=mbcol: h.activation(self.wdec.t[:], self.wdec.t[:], AF.Exp, bias=mbcol), reads=[self.wdec.d, self.mbq.d, self.zero1.d], writes=[self.wdec.d])
                P.op("dve", lambda h: h.tensor_tensor(self.xw.t[:, :].rearrange("p (h d) -> p h d", d=64), self.xdt.t[:, :].rearrange("p (h d) -> p h d", d=64),
                                                      self.wdec.t[:].unsqueeze(2).to_broadcast([128, 32, 64]), ALU.mult),
                     reads=[self.xdt.d, self.wdec.d], writes=[self.xw.d])
                if ty == "S":
                    P.op("dve", lambda h, q=q: h.tensor_tensor(self.CTq.t[:, :, :], self.xbcT.t[:, 20:24, :], self.rowm.t[:, q, :].unsqueeze(1).to_broadcast([128, 4, 128]), ALU.mult),
                         reads=[self.xbcT.d, self.rowm.d], writes=[self.CTq.d])
                    CT = lambda g: self.CTq.t[:, g, :]
                    ctd = self.CTq.d
                else:
                    CT = lambda g: self.xbcT.t[:, 20 + g, :]
                    ctd = self.xbcT.d
                if ty == "S":
                    for j in range(16):
                        P.dma("sp", self.ds_stg, self.big2.t[:, 0, j, :], A["st_ssm"][q, 2 * j:2 * j + 2].rearrange("h2 p n -> (h2 p) n"), writes=[self.big2_d0])
                    for jb in range(4):
                        ps = self.PF()
                        for jj in range(4):
                            j = jb * 4 + jj
                            P.op("pe", lambda h, j=j, jj=jj, ps=ps: h.transpose(ps.t[:, jj * 128:(jj + 1) * 128], self.big2.t[:, 0, j, :], self.ident_f.t[:]),
                                 reads=[self.big2_d0, self.ident_f.d], writes=[ps.d])
                        P.op("dve", lambda h, jb=jb, ps=ps: h.tensor_copy(self.ST.t[:, jb * 512:(jb + 1) * 512], ps.t[:, :]), reads=[ps.d], writes=[self.ST.d])
                        P.op("act", lambda h, jb=jb, ps=ps: h.copy(self.STb.t[:, jb * 512:(jb + 1) * 512], ps.t[:, :]), reads=[ps.d], writes=[self.STb.d])
                for g in range(4):
                    ps = self.PF()
                    P.op("pe", lambda h, g=g, ps=ps, CT=CT: h.matmul(ps.t[:, :], lhsT=CT(g), rhs=self.STb.t[:, g * 512:(g + 1) * 512], start=True, stop=True),
                         reads=[ctd, self.STb.d], writes=[ps.d])
                    tg = self.tmpg[g % 2]
                    P.op("dve", lambda h, g=g, ps=ps, tg=tg: h.tensor_tensor(tg.t[:, :].rearrange("p (h d) -> p h d", d=64), ps.t[:, :].rearrange("p (h d) -> p h d", d=64),
                                                                           self.ecs.t[:, g * 8:(g + 1) * 8].unsqueeze(2).to_broadcast([128, 8, 64]), ALU.mult),
                         reads=[ps.d, self.ecs.d], writes=[tg.d])
                    P.op("dve", lambda h, g=g, tg=tg: h.tensor_tensor(self.big1.t[:, g * 512:(g + 1) * 512], self.big1.t[:, g * 512:(g + 1) * 512], tg.t[:], ALU.add),
                         reads=[tg.d, self.big1.d], writes=[self.big1.d])
                for g in range(4):
                    ps = self.PF()
                    P.op("pe", lambda h, g=g, ps=ps: h.matmul(ps.t[:, :], lhsT=self.B_tm.t[:, g * 128:(g + 1) * 128], rhs=self.xw.t[:, g * 512:(g + 1) * 512], start=True, stop=True),
                         reads=[self.B_tm.d, self.xw.d], writes=[ps.d])
                    sg3 = self.ST.t[:, g * 512:(g + 1) * 512].rearrange("p (h d) -> p h d", d=64)
                    P.op("dve", lambda h, g=g, sg3=sg3: h.tensor_tensor(sg3, sg3, self.dA.t[:, g * 8:(g + 1) * 8].unsqueeze(2).to_broadcast([128, 8, 64]), ALU.mult),
                         reads=[self.ST.d, self.dA.d], writes=[self.ST.d])
                    P.op("dve", lambda h, g=g, ps=ps: h.tensor_tensor(self.ST.t[:, g * 512:(g + 1) * 512], self.ST.t[:, g * 512:(g + 1) * 512], ps.t[:, :], ALU.add),
                         reads=[self.ST.d, ps.d], writes=[self.ST.d])
                if ty != "S":
                    P.op("act", lambda h: h.copy(self.STb.t[:], self.ST.t[:]), reads=[self.ST.d], writes=[self.STb.d])
                if ty == "M":
                    P.op("pool", lambda h: h.tensor_copy(self.ST_meta.t[:], self.ST.t[:]), reads=[self.ST.d], writes=[self.ST_meta.d])
                if ty == "S" or last:
                    for jb in range(4):
                        ps = self.PF()
                        for jj in range(4):
                            j = jb * 4 + jj
                            P.op("pe", lambda h, j=j, jj=jj, ps=ps: h.transpose(ps.t[:, jj * 128:(jj + 1) * 128], self.ST.t[:, j * 128:(j + 1) * 128], self.ident_f.t[:]),
                                 reads=[self.ST.d, self.ident_f.d], writes=[ps.d])
                        P.op("act", lambda h, jb=jb, ps=ps: h.copy(self.big2.t[:, 1, jb * 4:(jb + 1) * 4, :], ps.t[:, :].rearrange("p (j n) -> p j n", n=128)), reads=[ps.d], writes=[self.big2_d1])
                    dst = A["ssm_s"][q] if ty == "S" else A["ssm_p"][b]
                    for j in range(16):
                        P.dma("pool", self.ds_stgo, dst[2 * j:2 * j + 2].rearrange("h2 p n -> (h2 p) n"), self.big2.t[:, 1, j, :], reads=[self.big2_d1])
            P.op("pool", lambda h: h.memset(self.ssq.t[:], 0.0), writes=[self.ssq.d])
            P.op("act", lambda h: h.activation(self.xw.t[:], self.z_tm.t[:], AF.Tanh, scale=0.5), reads=[self.z_tm.d], writes=[self.xw.d])
            P.op("dve", lambda h: h.scalar_tensor_tensor(self.z_tm.t[:], self.xw.t[:], 1.0, self.z_tm.t[:], ALU.add, ALU.mult), reads=[self.xw.d, self.z_tm.d], writes=[self.z_tm.d])
            P.op("dve", lambda h: h.scalar_tensor_tensor(self.big1.t[:, 0:2048], self.big1.t[:, 0:2048], 0.5, self.z_tm.t[:], ALU.mult, ALU.mult), reads=[self.big1.d, self.z_tm.d], writes=[self.big1.d])
            for g in range(4):
                P.op("act", lambda h, g=g: h.activation(self.sz.t[:], self.big1.t[:, g * 512:(g + 1) * 512], AF.Square, accum_out=self.ssq.t[:, g:g + 1]), reads=[self.big1.d], writes=[self.sz.d, self.ssq.d])
            P.op("act", lambda h: h.activation(self.grs.t[:], self.ssq.t[:], AF.Ln, bias=EPS, scale=1.0 / 512.0), reads=[self.ssq.d], writes=[self.grs.d])
            P.op("act", lambda h: h.activation(self.grs.t[:], self.grs.t[:], AF.Exp, scale=-0.5), reads=[self.grs.d], writes=[self.grs.d])
            for g in range(4):
                P.op("dve", lambda h, g=g: h.tensor_scalar_mul(self.x_tm.t[:, g * 512:(g + 1) * 512], self.big1.t[:, g * 512:(g + 1) * 512], self.grs.t[:, g:g + 1]), reads=[self.big1.d, self.grs.d], writes=[self.x_tm.d])
            for g in range(4):
                pb = self.PB()
                for m in range(4):
                    mm = g * 4 + m
                    P.op("pe", lambda h, m=m, mm=mm, pb=pb: h.transpose(pb.t[:, m * 128:(m + 1) * 128], self.x_tm.t[:, mm * 128:(mm + 1) * 128], self.ident_b.t[:]),
                         reads=[self.x_tm.d, self.ident_b.d], writes=[pb.d])
                P.op("dve", lambda h, g=g, pb=pb: h.tensor_tensor(self.ynT.t[:, g * 4:(g + 1) * 4, :], pb.t[:, 0:512].rearrange("p (m t) -> p m t", t=128),
                                                                   self.gate_g.t[:, g * 4:(g + 1) * 4].unsqueeze(2).to_broadcast([128, 4, 128]), ALU.mult),
                     reads=[pb.d, self.gate_g.d], writes=[self.ynT.d])
        if "C" in ph:
            self.dense_fm(("w_out", lambda w: w), 2048, 1024, self.ynT, self.resid_add)
            self.ffn(0)

        if "D" in ph:
            cur, prv = self.pbuf, 1 - self.pbuf
            kTo, vbo = self.kT[cur], self.v_b[cur]
            self.rms(2, self.uT)
            def ev_k(ps):
                P.op("act", lambda h: h.copy(self.k_tm.t[:], ps.t[:, 0:256]), reads=[ps.d], writes=[self.k_tm.d])
                P.op("dve", lambda h: h.tensor_copy(self.k_rot.t[:], ps.t[:, 0:256]), reads=[ps.d], writes=[self.k_rot.d])
            self.dense_tm(("w_k", lambda w: w), 256, self.uT, ev_k)
            self.rope_apply(self.k_tm, self.k_rot, 4)
            P.op("act", lambda h: h.copy(self.k_b.t[:], self.k_rot.t[:]), reads=[self.k_rot.d], writes=[self.k_b.d])
            pb = self.PB()
            for k in range(4):
                P.op("pe", lambda h, k=k, pb=pb: h.transpose(pb.t[0:64, k * 128:(k + 1) * 128], self.k_b.t[:, k * 64:(k + 1) * 64], self.ident_b.t[:]),
                     reads=[self.k_b.d, self.ident_b.d], writes=[pb.d])
            P.op("dve", lambda h, pb=pb: h.tensor_copy(kTo.t[:, :, :], pb.t[0:64, 0:512].rearrange("p (k t) -> p k t", t=128)), reads=[pb.d], writes=[kTo.d])
            def ev_v(ps):
                P.op("act", lambda h: h.copy(self.v_tm.t[:], ps.t[:, 0:256]), reads=[ps.d], writes=[self.v_tm.d])
                P.op("dve", lambda h: h.tensor_copy(vbo.t[:], ps.t[:, 0:256]), reads=[ps.d], writes=[vbo.d])
            self.dense_tm(("w_v", lambda w: w), 256, self.uT, ev_v)
            if ty == "S":
                for q in range(NSQ):
                    P.dma("pool", self.ds_out, A["k_s"][q, 120:128, :], self.k_rot.t[8 * q:8 * q + 8, :], reads=[self.k_rot.d])
                    P.dma("pool", self.ds_out, A["v_s"][q, 120:128, :], self.v_tm.t[8 * q:8 * q + 8, :], reads=[self.v_tm.d])
                P.dma("pool", self.ds_cp, A["k_s"][:, 0:120, :], A["st_k"][:, 8:128, :])
                P.dma("pool", self.ds_cp, A["v_s"][:, 0:120, :], A["st_v"][:, 8:128, :])
            if last:
                P.dma("pool", self.ds_out, A["k_p"][b], self.k_rot.t[:], reads=[self.k_rot.d])
                P.dma("pool", self.ds_out, A["v_p"][b], self.v_tm.t[:], reads=[self.v_tm.d])
            if ty == "M":
                P.op("pool", lambda h: h.tensor_copy(self.kT_meta.t[:], kTo.t[:]), reads=[kTo.d], writes=[self.kT_meta.d])
                P.op("pool", lambda h: h.tensor_copy(self.v_meta.t[:], vbo.t[:]), reads=[vbo.d], writes=[self.v_meta.d])
            self.rms(3, self.uT)
            for blk in range(2):
                def ev_q(ps, blk=blk):
                    P.op("act", lambda h: h.copy(self.q_tm.t[:, blk * 512:(blk + 1) * 512], ps.t[:, :]), reads=[ps.d], writes=[self.q_tm.d])
                    P.op("dve", lambda h: h.tensor_copy(self.q_rot.t[:, blk * 512:(blk + 1) * 512], ps.t[:, :]), reads=[ps.d], writes=[self.q_rot.d])
                self.dense_tm(("w_q", lambda w, blk=blk: w[:, blk * 512:(blk + 1) * 512]), 512, self.uT, ev_q)
            self.rope_apply(self.q_tm, self.q_rot, 16)
            P.op("act", lambda h: h.copy(self.q_b.t[:], self.q_rot.t[:]), reads=[self.q_rot.d], writes=[self.q_b.d])
            for half in range(2):
                pb = self.PB()
                for hh in range(8):
                    hd = half * 8 + hh
                    P.op("pe", lambda h, hh=hh, hd=hd, pb=pb: h.transpose(pb.t[0:64, hh * 128:(hh + 1) * 128], self.q_b.t[:, hd * 64:(hd + 1) * 64], self.ident_b.t[:]),
                         reads=[self.q_b.d, self.ident_b.d], writes=[pb.d])
                P.op("dve", lambda h, half=half, pb=pb: h.tensor_copy(self.qT.t[:, half * 8:(half + 1) * 8, :], pb.t[0:64, :].rearrange("p (k t) -> p k t", t=128)),
                     reads=[pb.d], writes=[self.qT.d])
            if ty == "S":
                for q in range(NSQ):
                    P.dma("sp", self.ds_kv, self.sk32.t[:], A["st_k"][q], writes=[self.sk32.d])
                    P.dma("sp", self.ds_kv, self.sv32.t[:], A["st_v"][q], writes=[self.sv32.d])
                    P.op("act", lambda h: h.copy(self.skb.t[:], self.sk32.t[:]), reads=[self.sk32.d], writes=[self.skb.d])
                    P.op("dve", lambda h: h.tensor_copy(self.svb.t[:], self.sv32.t[:]), reads=[self.sv32.d], writes=[self.svb.d])
                    pb = self.PB()
                    for k in range(4):
                        P.op("pe", lambda h, k=k, pb=pb: h.transpose(pb.t[0:64, k * 128:(k + 1) * 128], self.skb.t[:, k * 64:(k + 1) * 64], self.ident_b.t[:]),
                             reads=[self.skb.d, self.ident_b.d], writes=[pb.d])
                    P.op("dve", lambda h, pb=pb: h.tensor_copy(self.kTs.t[:, :, :], pb.t[0:64, 0:512].rearrange("p (k t) -> p k t", t=128)), reads=[pb.d], writes=[self.kTs.d])
                    ps = self.PF()
                    for k in range(4):
                        o = ps.t[:, k * 32:(k + 1) * 32]
                        P.op("pe", lambda h, k=k, o=o, q=q: h.matmul(o.rearrange("p (j t) -> p j t", t=8), lhsT=self.kTs.t[:, k, :], rhs=self.qT.t[:, 4 * k:4 * k + 4, 8 * q:8 * q + 8], start=True, stop=False),
                             reads=[self.kTs.d, self.qT.d], writes=[ps.d])
                        P.op("pe", lambda h, k=k, o=o: h.matmul(o, lhsT=self.ident_b.t[:], rhs=self.mbstate.t[:, k * 32:(k + 1) * 32], start=False, stop=True),
                             reads=[self.ident_b.d, self.mbstate.d], writes=[ps.d])
                    P.op("act", lambda h, ps=ps: h.activation(self.PTs.t[:], ps.t[:, 0:128], AF.Exp, scale=0.125), reads=[ps.d], writes=[self.PTs.d])
                    ps2 = self.PF()
                    for k in range(4):
                        P.op("pe", lambda h, k=k, ps2=ps2: h.matmul(ps2.t[0:64, k * 32:(k + 1) * 32], lhsT=self.svb.t[:, k * 64:(k + 1) * 64], rhs=self.PTs.t[:, k * 32:(k + 1) * 32], start=True, stop=True),
                             reads=[self.svb.d, self.PTs.d], writes=[ps2.d])
                    P.op("pe", lambda h, ps2=ps2: h.matmul(ps2.t[0:64, 128:256], lhsT=self.ones_b.t[:, 0:64], rhs=self.PTs.t[:], start=True, stop=True),
                         reads=[self.ones_b.d, self.PTs.d], writes=[ps2.d])
                    P.op("dve", lambda h, q=q, ps2=ps2: h.tensor_copy(self.big2.t[0:64, 0, :, 8 * q:8 * q + 8], ps2.t[0:64, 0:128].rearrange("p (h t) -> p h t", t=8)), reads=[ps2.d], writes=[self.big2_d0])
                    P.op("dve", lambda h, q=q, ps2=ps2: h.tensor_copy(self.big2.t[0:64, 1, :, 8 * q:8 * q + 8], ps2.t[0:64, 128:256].rearrange("p (h t) -> p h t", t=8)), reads=[ps2.d], writes=[self.big2_d1])
            use_prev = ty == "P"
            if first:
                kTp, vbp = self.kT_meta, self.v_meta
            else:
                kTp, vbp = self.kT[prv], self.v_b[prv]
            mpi = 1 if first else 0
            sc = {}

            def emit_scores(hd):
                k = hd // 4
                PT = self.PT[hd % 2]
                ps = self.PF()
                if use_prev:
                    P.op("pe", lambda h, ps=ps, k=k, hd=hd: h.matmul(ps.t[:, 0:128], lhsT=kTp.t[:, k, :], rhs=self.qT.t[:, hd, :], start=True, stop=False),
                         reads=[kTp.d, self.qT.d], writes=[ps.d])
                    P.op("pe", lambda h, ps=ps: h.matmul(ps.t[:, 0:128], lhsT=self.ident_b.t[:], rhs=self.mbprev.t[:, mpi, :], start=False, stop=True),
                         reads=[self.ident_b.d, self.mbprev.d], writes=[ps.d])
                P.op("pe", lambda h, ps=ps, k=k, hd=hd: h.matmul(ps.t[:, 128:256], lhsT=kTo.t[:, k, :], rhs=self.qT.t[:, hd, :], start=True, stop=False),
                     reads=[kTo.d, self.qT.d], writes=[ps.d])
                P.op("pe", lambda h, ps=ps: h.matmul(ps.t[:, 128:256], lhsT=self.ident_b.t[:], rhs=self.mb.t[:, ti, :], start=False, stop=True),
                     reads=[self.ident_b.d, self.mb.d], writes=[ps.d])
                lo = 0 if use_prev else 128
                P.op("act", lambda h, ps=ps, PT=PT, lo=lo: h.activation(PT.t[:, lo:256], ps.t[:, lo:256], AF.Exp, scale=0.125), reads=[ps.d], writes=[PT.d])

            def emit_pv(hd, psO, psD):
                k = hd // 4
                hh = hd % 4
                PT = self.PT[hd % 2]
                oo = psO.t[0:64, hh * 128:(hh + 1) * 128]
                od = psD.t[0:64, hh * 128:(hh + 1) * 128]
                if use_prev:
                    P.op("pe", lambda h, oo=oo, k=k, PT=PT: h.matmul(oo, lhsT=vbp.t[:, k * 64:(k + 1) * 64], rhs=PT.t[:, 0:128], start=True, stop=False), reads=[vbp.d, PT.d], writes=[psO.d])
                P.op("pe", lambda h, oo=oo, k=k, PT=PT: h.matmul(oo, lhsT=vbo.t[:, k * 64:(k + 1) * 64], rhs=PT.t[:, 128:256], start=(not use_prev), stop=True), reads=[vbo.d, PT.d], writes=[psO.d])
                if use_prev:
                    P.op("pe", lambda h, od=od, PT=PT: h.matmul(od, lhsT=self.ones_b.t[:, 0:64], rhs=PT.t[:, 0:128], start=True, stop=False), reads=[self.ones_b.d, PT.d], writes=[psD.d])
                P.op("pe", lambda h, od=od, PT=PT: h.matmul(od, lhsT=self.ones_b.t[:, 0:64], rhs=PT.t[:, 128:256], start=(not use_prev), stop=True), reads=[self.ones_b.d, PT.d], writes=[psD.d])

            emit_scores(0)
            for hq in range(4):
                psO = self.PF()
                psD = self.PF()
                for hh in range(4):
                    hd = hq * 4 + hh
                    if hd + 1 < 16:
                        emit_scores(hd + 1)
                    emit_pv(hd, psO, psD)
                h0 = hq * 4
                den = self.den
                P.op("dve", lambda h, psD=psD, h0=h0: h.tensor_tensor(den.t[:, :, :], psD.t[0:64, :].rearrange("p (h t) -> p h t", t=128),
                                                                      self.esink.t[0:64, h0:h0 + 4].unsqueeze(2).to_broadcast([64, 4, 128]), ALU.add),
                     reads=[psD.d, self.esink.d], writes=[den.d])
                if ty == "S":
                    P.op("dve", lambda h, h0=h0: h.tensor_tensor(den.t[:, :, :], den.t[:, :, :], self.big2.t[0:64, 1, h0:h0 + 4, :], ALU.add), reads=[den.d, self.big2_d1], writes=[den.d])
                    P.op("dve", lambda h, h0=h0, psO=psO: h.tensor_tensor(self.big2.t[0:64, 0, h0:h0 + 4, :], self.big2.t[0:64, 0, h0:h0 + 4, :], psO.t[0:64, :].rearrange("p (h t) -> p h t", t=128), ALU.add),
                         reads=[psO.d, self.big2_d0], writes=[self.big2_d0])
                P.op("dve", lambda h: h.reciprocal(den.t[:, :, :], den.t[:, :, :]), reads=[den.d], writes=[den.d])
                if ty == "S":
                    P.op("dve", lambda h, h0=h0: h.tensor_tensor(self.oT.t[:, h0:h0 + 4, :], self.big2.t[0:64, 0, h0:h0 + 4, :], den.t[:, :, :], ALU.mult),
                         reads=[self.big2_d0, den.d], writes=[self.oT.d])
                else:
                    P.op("dve", lambda h, h0=h0, psO=psO: h.tensor_tensor(self.oT.t[:, h0:h0 + 4, :], psO.t[0:64, :].rearrange("p (h t) -> p h t", t=128), den.t[:, :, :], ALU.mult),
                         reads=[psO.d, den.d], writes=[self.oT.d])
            self.pbuf = 1 - self.pbuf
        if "E" in ph:
            self.dense_fm(("w_o", lambda w: w), 1024, 1024, self.oT, self.resid_add, part=64)
            self.ffn(1)
            if ty != "M":
                self.rms(5, self.q_tm, view3=True)
                for half in range(2):
                    ps = self.PF()
                    for m in range(4):
                        mm = half * 4 + m
                        P.op("pe", lambda h, m=m, mm=mm, ps=ps: h.transpose(ps.t[:, m * 128:(m + 1) * 128], self.q_tm.t[:, mm * 128:(mm + 1) * 128], self.ident_f.t[:]),
                             reads=[self.q_tm.d, self.ident_f.d], writes=[ps.d])
                    P.op("act", lambda h, half=half, ps=ps: h.copy(self.y_st.t[:, half * 512:(half + 1) * 512], ps.t[:, :]), reads=[ps.d], writes=[self.y_st.d])
                dst = A["y_s"] if ty == "S" else A["y_p"][b, c * 128:(c + 1) * 128, :]
                P.dma("pool", self.ds_y, dst, self.y_st.t[:], reads=[self.y_st.d])


_CACHE = {}


def _get_nc():
    if "nc" not in _CACHE:
        k = Kern()
        _CACHE["nc"] = k.build()
        _CACHE["outs"] = k.out_names
    return _CACHE["nc"]


def _col(v, nchunk):
    return np.ascontiguousarray(np.asarray(v, np.float32).reshape(nchunk, 128).T)


def kernel(x_prompt, x_sample, state_ssm, state_conv, state_k, state_v, meta_tokens,
           ssm_norm_g, ssm_w_in, ssm_conv_w, ssm_conv_b, ssm_dt_bias, ssm_A_log, ssm_D,
           ssm_gate_norm_g, ssm_w_out, kv_norm_g, w_k, w_v, attn_norm_g, w_q, attn_sinks, w_o,
           ffn_norm_g, ffn_w_gate, ffn_w_up, ffn_w_down, final_norm_g):
    f = lambda a: np.ascontiguousarray(np.asarray(a, dtype=np.float32))
    nc = _get_nc()
    cst = make_consts()
    gcols = np.stack([_col(ssm_norm_g[0], 8), _col(ffn_norm_g[0], 8), _col(kv_norm_g, 8), _col(attn_norm_g[0], 8),
                      _col(ffn_norm_g[1], 8), _col(final_norm_g, 8)], axis=1)
    gate_g = _col(ssm_gate_norm_g[0], 16)
    conv_w = np.ascontiguousarray(f(ssm_conv_w[0]).reshape(4, 24, 128).transpose(2, 1, 0))
    conv_b = _col(ssm_conv_b[0], 24)
    hp = np.zeros((128, 4, 32), np.float32)
    hp[:, 0, :] = f(ssm_dt_bias[0])[None, :]
    hp[:, 1, :] = f(ssm_A_log[0])[None, :]
    hp[:, 2, :] = f(ssm_D[0])[None, :]
    sinks = np.ascontiguousarray(np.broadcast_to(f(attn_sinks[0])[None, :], (128, 16)))
    shared = {
        "meta": f(meta_tokens), "w_in": f(ssm_w_in[0]), "w_out": f(ssm_w_out[0]), "w_k": f(w_k), "w_v": f(w_v),
        "w_q": f(w_q[0]), "w_o": f(w_o[0]), "w_gate": f(ffn_w_gate), "w_up": f(ffn_w_up), "w_down": f(ffn_w_down),
        "gcols": np.ascontiguousarray(gcols), "gate_g": gate_g, "conv_w": conv_w, "conv_b": conv_b, "hp": hp, "sinks": sinks,
        "c_tri": cst["tri"], "c_mb": cst["mb"], "c_mbprev": cst["mbprev"], "c_mbstate": cst["mbstate"],
        "c_ident_b": cst["ident_b"], "c_ident_f": cst["ident_f"], "c_ones_b": cst["ones_b"], "c_sel": cst["sel"],
        "c_mbq": cst["mbq"], "c_rowm": cst["rowm"], "c_tokmask": cst["tokmask"], "c_rope": cst["rope"],
    }
    xp, xs = f(x_prompt), f(x_sample)
    sssm, sconv, sk, sv = f(state_ssm[0]), f(state_conv[0]), f(state_k), f(state_v)
    in_maps = []
    for i in range(NCORE):
        m = dict(shared)
        m["xs"] = xs[NSQ * i:NSQ * (i + 1)].reshape(128, 1024)
        m["xp"] = xp[NPB * i:NPB * (i + 1)]
        m["st_ssm"] = sssm[NSQ * i:NSQ * (i + 1)]
        m["st_conv"] = sconv[NSQ * i:NSQ * (i + 1)].reshape(NSQ * 3, 3072)
        m["st_k"] = sk[NSQ * i:NSQ * (i + 1)].reshape(NSQ, 128, 256)
        m["st_v"] = sv[NSQ * i:NSQ * (i + 1)].reshape(NSQ, 128, 256)
        in_maps.append(m)
    res = run_bass_kernel_spmd(nc, in_maps, core_ids=list(range(NCORE)))
    R = res.results
    cat = lambda name: np.concatenate([np.asarray(r[name], np.float32) for r in R], axis=0)
    y_prompt = cat("y_p")
    y_sample = cat("y_s").reshape(128, 8, 1024)
    ssm_p = cat("ssm_p")[None]
    conv_p = cat("conv_p")[None]
    k_p = cat("k_p").reshape(16, 128, 4, 64)
    v_p = cat("v_p").reshape(16, 128, 4, 64)
    ssm_s = cat("ssm_s")[None]
    conv_s = cat("conv_s").reshape(128, 3, 3072)[None]
    k_s = cat("k_s").reshape(128, 128, 4, 64)
    v_s = cat("v_s").reshape(128, 128, 4, 64)
    return (y_prompt, y_sample, ssm_p, conv_p, k_p, v_p, ssm_s, conv_s, k_s, v_s)
```
